# Optimizing a Trainium2 kernel written in Bass

```python
import jax, jax.numpy as jnp
from jax import lax
import numpy as np

D_MODEL = 1024
BATCH = 4
SEQ = 4096
DEPTH = 2

CTX_LEN = 256
GRID_W = 64
EPS = 1e-6
HALF = 0.5
D_FF = 2816
N_ADA = 9
CONV_CH = 384
CONV_K = 31
N_HEADS = 8
Q_LORA = 384
KV_LORA = 256
QK_NOPE = 64
QK_ROPE = 32
V_DIM = 64
ROPE_AXIS = QK_ROPE // 2
ROPE_BASE = 10000.0
ATTN_SCALE = (QK_NOPE + QK_ROPE) ** -0.5
Q_BLOCK = 128
FOURIER_GROUPS = 4
FOURIER_GROUP_CH = 128
FOURIER_WIDTH = FOURIER_GROUPS * FOURIER_GROUP_CH
MIX_WIDTH = 2 * CONV_CH + Q_LORA + KV_LORA + QK_ROPE + FOURIER_WIDTH
N_BRANCH = 3

kernel_name = 'hybrid_conv_mla_fourier_dit_block'


def rms_norm(x, g):
    xf = x.astype(jnp.float32)
    y = xf * lax.rsqrt(jnp.mean(xf * xf, axis=-1, keepdims=True) + EPS)
    return (y * g.astype(jnp.float32)).astype(x.dtype)


def layer_norm(x, g, b):
    xf = x.astype(jnp.float32)
    mu = jnp.mean(xf, axis=-1, keepdims=True)
    var = jnp.mean(jnp.square(xf - mu), axis=-1, keepdims=True)
    y = (xf - mu) * lax.rsqrt(var + EPS)
    return (y * g.astype(jnp.float32) + b.astype(jnp.float32)).astype(x.dtype)


def ada_norm(h, g, shift, scale):
    return rms_norm(h, g) * (1 + scale) + shift


def swiglu(x, w1, w3, w2):
    return (jax.nn.silu(x @ w1) * (x @ w3)) @ w2


def axial_rope_tables(n_tok, dtype):
    rows = n_tok // GRID_W
    row = jnp.broadcast_to(jnp.arange(rows, dtype=jnp.float32)[:, None], (rows, GRID_W)).reshape(-1)
    col = jnp.broadcast_to(jnp.arange(GRID_W, dtype=jnp.float32)[None, :], (rows, GRID_W)).reshape(-1)
    inv = 1.0 / (ROPE_BASE ** (jnp.arange(ROPE_AXIS // 2, dtype=jnp.float32) * 2.0 / ROPE_AXIS))
    ar = row[:, None] * inv
    ac = col[:, None] * inv
    ang = jnp.concatenate([ar, ar, ac, ac], axis=-1)
    return jnp.cos(ang).astype(dtype), jnp.sin(ang).astype(dtype)


def rotate_half_axial(x):
    xr = x.reshape(x.shape[:-1] + (2, 2, ROPE_AXIS // 2))
    xr = jnp.concatenate([-xr[..., 1:2, :], xr[..., 0:1, :]], axis=-2)
    return xr.reshape(x.shape)


def apply_rope(x, cos, sin):
    return x * cos + rotate_half_axial(x) * sin


def split_mixing(z):
    o1 = 2 * CONV_CH
    o2 = o1 + Q_LORA
    o3 = o2 + KV_LORA
    o4 = o3 + QK_ROPE
    return z[..., :o1], z[..., o1:o2], z[..., o2:o3], z[..., o3:o4], z[..., o4:]


def conv_module(zc, w_dw, b_dw, ln_g, ln_b, w_pw, b_pw):
    a, gt = jnp.split(zc, 2, axis=-1)
    v = a * jax.nn.sigmoid(gt)
    v = lax.conv_general_dilated(v, w_dw[:, None, :].astype(v.dtype), window_strides=(1,),
                                 padding=[(CONV_K // 2, CONV_K // 2)],
                                 dimension_numbers=('NWC', 'WIO', 'NWC'),
                                 feature_group_count=CONV_CH) + b_dw
    v = layer_norm(v, ln_g, ln_b)
    return jax.nn.silu(v) @ w_pw + b_pw


def fourier_mix(zf, w, b):
    bsz, t, _ = zf.shape
    g = zf.reshape(bsz, t, FOURIER_GROUPS, FOURIER_GROUP_CH).astype(jnp.float32)
    f = jnp.fft.fftn(g, axes=(1, 3), norm='ortho').real.astype(zf.dtype)
    return f.reshape(bsz, t, FOURIER_WIDTH) @ w + b


def mla_q(cq, g_qn, w_uq, cos, sin):
    bsz, t, _ = cq.shape
    q = (rms_norm(cq, g_qn) @ w_uq).reshape(bsz, t, N_HEADS, QK_NOPE + QK_ROPE)
    if cos is None:
        return q
    q_nope, q_rope = q[..., :QK_NOPE], q[..., QK_NOPE:]
    return jnp.concatenate([q_nope, apply_rope(q_rope, cos[:, None, :], sin[:, None, :])], axis=-1)


def mla_kv(ckv, kr, g_kvn, w_ukv, cos, sin):
    bsz, t, _ = ckv.shape
    kv = (rms_norm(ckv, g_kvn) @ w_ukv).reshape(bsz, t, N_HEADS, QK_NOPE + V_DIM)
    k_nope, v = kv[..., :QK_NOPE], kv[..., QK_NOPE:]
    if cos is not None:
        kr = apply_rope(kr, cos, sin)
    k = jnp.concatenate([k_nope, jnp.broadcast_to(kr[:, :, None, :], (bsz, t, N_HEADS, QK_ROPE))], axis=-1)
    return k, v


def attend_dense(q, k, v):
    s = jnp.einsum('bqhd,bkhd->bhqk', q, k, preferred_element_type=jnp.float32) * ATTN_SCALE
    p = jax.nn.softmax(s, axis=-1).astype(v.dtype)
    o = jnp.einsum('bhqk,bkhd->bqhd', p, v)
    return o.reshape(o.shape[0], o.shape[1], N_HEADS * V_DIM)


def attend_blocked(q, k, v):
    bsz, s, _, dk = q.shape
    nb = s // Q_BLOCK
    qb = q.reshape(bsz, nb, Q_BLOCK, N_HEADS, dk).transpose(1, 0, 2, 3, 4)
    o = lax.map(lambda blk: attend_dense(blk, k, v), qb)
    return o.transpose(1, 0, 2, 3).reshape(bsz, s, N_HEADS * V_DIM)


def merge_branches(u, y_conv, y_mla, y_four, w_bg, b_bg, w_out):
    g = jax.nn.sigmoid(u @ w_bg + b_bg)
    g0, g1, g2 = jnp.split(g, N_BRANCH, axis=-1)
    return (g0 * y_conv + g1 * y_mla + g2 * y_four) @ w_out


def setup_inputs(seed: int = 0) -> dict:
    key = jax.random.key(seed)
    ks = jax.random.split(key, 40)
    L, D = DEPTH, D_MODEL

    def nrm(k, shape, std):
        return jax.random.normal(k, shape, jnp.float32) * std

    def gain(k, shape):
        return 1.0 + nrm(k, shape, 0.02)

    return {
        'x': nrm(ks[0], (BATCH, SEQ, D), 1.0),
        'c': nrm(ks[1], (BATCH, D), 1.0),
        'ctx': nrm(ks[2], (BATCH, CTX_LEN, D), 1.0),
        'c_ctx': nrm(ks[3], (D,), 1.0),
        'w_ada': nrm(ks[4], (L, D, N_ADA * D), 0.5 * D ** -0.5),
        'b_ada': nrm(ks[5], (L, N_ADA * D), 0.02),
        'g_ffn1': gain(ks[6], (L, D)),
        'w1_ffn1': nrm(ks[7], (L, D, D_FF), D ** -0.5),
        'w3_ffn1': nrm(ks[8], (L, D, D_FF), D ** -0.5),
        'w2_ffn1': nrm(ks[9], (L, D_FF, D), D_FF ** -0.5),
        'g_mix': gain(ks[10], (L, D)),
        'w_in': nrm(ks[11], (L, D, MIX_WIDTH), D ** -0.5),
        'w_dw': nrm(ks[12], (L, CONV_K, CONV_CH), CONV_K ** -0.5),
        'b_dw': nrm(ks[13], (L, CONV_CH), 0.02),
        'ln_g_conv': gain(ks[14], (L, CONV_CH)),
        'ln_b_conv': nrm(ks[15], (L, CONV_CH), 0.02),
        'w_pw_conv': nrm(ks[16], (L, CONV_CH, D), CONV_CH ** -0.5),
        'b_pw_conv': nrm(ks[17], (L, D), 0.02),
        'g_qnorm': gain(ks[18], (L, Q_LORA)),
        'w_uq': nrm(ks[19], (L, Q_LORA, N_HEADS * (QK_NOPE + QK_ROPE)), Q_LORA ** -0.5),
        'g_kvnorm': gain(ks[20], (L, KV_LORA)),
        'w_ukv': nrm(ks[21], (L, KV_LORA, N_HEADS * (QK_NOPE + V_DIM)), KV_LORA ** -0.5),
        'w_o_mla': nrm(ks[22], (L, N_HEADS * V_DIM, D), (N_HEADS * V_DIM) ** -0.5),
        'w_fourier': nrm(ks[23], (L, FOURIER_WIDTH, D), FOURIER_WIDTH ** -0.5),
        'b_fourier': nrm(ks[24], (L, D), 0.02),
        'w_bgate': nrm(ks[25], (L, D, N_BRANCH * D), D ** -0.5),
        'b_bgate': nrm(ks[26], (L, N_BRANCH * D), 0.02),
        'w_out': nrm(ks[27], (L, D, D), D ** -0.5),
        'g_ffn2': gain(ks[28], (L, D)),
        'w1_ffn2': nrm(ks[29], (L, D, D_FF), D ** -0.5),
        'w3_ffn2': nrm(ks[30], (L, D, D_FF), D ** -0.5),
        'w2_ffn2': nrm(ks[31], (L, D_FF, D), D_FF ** -0.5),
        'g_final': gain(ks[32], (D,)),
    }


def reference(x, c, ctx, c_ctx, w_ada, b_ada, g_ffn1, w1_ffn1, w3_ffn1, w2_ffn1, g_mix, w_in,
              w_dw, b_dw, ln_g_conv, ln_b_conv, w_pw_conv, b_pw_conv, g_qnorm, w_uq, g_kvnorm,
              w_ukv, w_o_mla, w_fourier, b_fourier, w_bgate, b_bgate, w_out, g_ffn2, w1_ffn2,
              w3_ffn2, w2_ffn2, g_final):
    cos, sin = axial_rope_tables(x.shape[1], x.dtype)
    h, hc = x, ctx
    for l in range(DEPTH):
        last = l == DEPTH - 1
        mod = jax.nn.silu(c) @ w_ada[l] + b_ada[l]
        sh1, sc1, gt1, sh2, sc2, gt2, sh3, sc3, gt3 = [m[:, None, :] for m in jnp.split(mod, N_ADA, axis=-1)]
        modc = jax.nn.silu(c_ctx) @ w_ada[l] + b_ada[l]
        csh1, csc1, cgt1, csh2, csc2, cgt2, csh3, csc3, cgt3 = jnp.split(modc, N_ADA, axis=-1)

        h = h + HALF * gt1 * swiglu(ada_norm(h, g_ffn1[l], sh1, sc1), w1_ffn1[l], w3_ffn1[l], w2_ffn1[l])
        hc = hc + HALF * cgt1 * swiglu(ada_norm(hc, g_ffn1[l], csh1, csc1), w1_ffn1[l], w3_ffn1[l], w2_ffn1[l])

        u = ada_norm(h, g_mix[l], sh2, sc2)
        uc = ada_norm(hc, g_mix[l], csh2, csc2)
        z_conv, z_cq, z_ckv, z_kr, z_four = split_mixing(u @ w_in[l])
        zc_conv, zc_cq, zc_ckv, zc_kr, zc_four = split_mixing(uc @ w_in[l])

        k_lat, v_lat = mla_kv(z_ckv, z_kr, g_kvnorm[l], w_ukv[l], cos, sin)
        k_ctx, v_ctx = mla_kv(zc_ckv, zc_kr, g_kvnorm[l], w_ukv[l], None, None)
        q_lat = mla_q(z_cq, g_qnorm[l], w_uq[l], cos, sin)
        attn = attend_blocked(q_lat, jnp.concatenate([k_ctx, k_lat], axis=1),
                              jnp.concatenate([v_ctx, v_lat], axis=1))
        y = merge_branches(u,
                           conv_module(z_conv, w_dw[l], b_dw[l], ln_g_conv[l], ln_b_conv[l], w_pw_conv[l], b_pw_conv[l]),
                           attn @ w_o_mla[l],
                           fourier_mix(z_four, w_fourier[l], b_fourier[l]),
                           w_bgate[l], b_bgate[l], w_out[l])
        h = h + gt2 * y
        if not last:
            q_ctx = mla_q(zc_cq, g_qnorm[l], w_uq[l], None, None)
            attn_c = attend_dense(q_ctx, k_ctx, v_ctx)
            yc = merge_branches(uc,
                                conv_module(zc_conv, w_dw[l], b_dw[l], ln_g_conv[l], ln_b_conv[l], w_pw_conv[l], b_pw_conv[l]),
                                attn_c @ w_o_mla[l],
                                fourier_mix(zc_four, w_fourier[l], b_fourier[l]),
                                w_bgate[l], b_bgate[l], w_out[l])
            hc = hc + cgt2 * yc

        h = h + HALF * gt3 * swiglu(ada_norm(h, g_ffn2[l], sh3, sc3), w1_ffn2[l], w3_ffn2[l], w2_ffn2[l])
        if not last:
            hc = hc + HALF * cgt3 * swiglu(ada_norm(hc, g_ffn2[l], csh3, csc3), w1_ffn2[l], w3_ffn2[l], w2_ffn2[l])
    return rms_norm(h, g_final)
```

```python
import numpy as np
import ml_dtypes
from contextlib import ExitStack
import concourse.bass as bass
import concourse.mybir as mybir
from concourse.bass_utils import run_bass_kernel_spmd

F32 = mybir.dt.float32
BF16 = mybir.dt.bfloat16
ALU = mybir.AluOpType
AF = mybir.ActivationFunctionType

D = 1024
KC = 8
NL = 2048
NCX = 256
NT = NL + NCX
SEQ = 4096
DFF = 2816
NJ = DFF // 128
DEPTH = 2
EPS = 1e-6
ATTN_SCALE = 96.0 ** -0.5
TBS = [(0, 512), (512, 512), (1024, 512), (1536, 512), (2048, 256)]
SAME_ENGINE_SYNC = True


class Res:
    __slots__ = ("w", "r", "name")

    def __init__(self, name=""):
        self.w = None
        self.r = {}
        self.name = name


class Eng:
    def __init__(self, name, sem):
        self.name, self.sem = name, sem
        self.count = 0
        self.seen = {}
        self.q = []


class DmaSem:
    def __init__(self, sem):
        self.sem = sem
        self.count = 0


class _Rec:
    def __init__(self):
        self.call = None

    def __getattr__(self, name):
        def f(*a, **k):
            self.call = (name, a, k)
            return self
        return f


class FW:
    def __init__(self, nc, stack, n_dma_sems=32):
        self.nc = nc
        mk = lambda n: stack.enter_context(nc.semaphore(n))
        self.pe = Eng("pe", mk("s_pe"))
        self.act = Eng("act", mk("s_act"))
        self.dve = Eng("dve", mk("s_dve"))
        self.pool = Eng("pool", mk("s_pool"))
        self.sp = Eng("sp", mk("s_sp"))
        self.dsems = [DmaSem(mk(f"s_dma{i}")) for i in range(n_dma_sems)]
        self.dnext = 0
        self.ccs = DmaSem(mk("s_cc"))

    def _wait(self, eng, sem, val):
        key = id(sem)
        if eng.seen.get(key, 0) >= val:
            return
        eng.q.append(lambda h, sem=sem, val=val: h.wait_ge(sem, val))
        eng.seen[key] = val

    def _deps(self, eng, reads, writes):
        deps = []
        for r in reads:
            if r.w is not None:
                deps.append(r.w)
        for w in writes:
            if w.w is not None:
                deps.append(w.w)
            deps.extend(w.r.values())
        for (sem, val, src) in deps:
            if src is eng and (eng.name == "pe" or not SAME_ENGINE_SYNC):
                continue
            self._wait(eng, sem, val)

    def _commit(self, ev, reads, writes):
        key = id(ev[0])
        for r in reads:
            old = r.r.get(key)
            if old is None or old[1] < ev[1]:
                r.r[key] = ev
        for w in writes:
            w.w = ev
            w.r = {}

    def op(self, eng, fn, reads=(), writes=(), sig=True):
        self._deps(eng, reads, writes)
        rec = _Rec()
        fn(rec)
        name, a, k = rec.call
        if sig:
            eng.count += 1
            eng.q.append(lambda h, name=name, a=a, k=k, sem=eng.sem: getattr(h, name)(*a, **k).then_inc(sem, 1))
            ev = (eng.sem, eng.count, eng)
        else:
            eng.q.append(lambda h, name=name, a=a, k=k: getattr(h, name)(*a, **k))
            ev = (eng.sem, eng.count + 1, eng)
        self._commit(ev, reads, writes)
        return ev

    def dma(self, q, out, in_, reads=(), writes=()):
        self._deps(q, reads, writes)
        ds = self.dsems[self.dnext]
        self.dnext = (self.dnext + 1) % len(self.dsems)
        if ds.count:
            self._wait(q, ds.sem, ds.count)
        ds.count += 16
        q.q.append(lambda h, out=out, in_=in_, sem=ds.sem: h.dma_start(out=out, in_=in_).then_inc(sem, 16))
        ev = (ds.sem, ds.count, None)
        self._commit(ev, reads, writes)
        return ev

    def allgather(self, src, dst, reads, writes):
        q = self.pool
        self._deps(q, reads, writes)
        cs = self.ccs
        if cs.count:
            self._wait(q, cs.sem, cs.count)
        cs.count += 1
        q.q.append(lambda h, src=src, dst=dst, sem=cs.sem: h.collective_compute(
            "AllGather", ALU.bypass, replica_groups=[[0, 1], [2, 3], [4, 5], [6, 7]],
            ins=[src], outs=[dst]).then_inc(sem, 1))
        ev = (cs.sem, cs.count, None)
        self._commit(ev, reads, writes)
        return ev

    def run(self):
        nc = self.nc
        with nc.Block() as block:
            @block.tensor
            def _(e):
                for f in self.pe.q:
                    f(e)

            @block.scalar
            def _(e):
                for f in self.act.q:
                    f(e)

            @block.vector
            def _(e):
                for f in self.dve.q:
                    f(e)

            @block.gpsimd
            def _(e):
                for f in self.pool.q:
                    f(e)

            @block.sync
            def _(e):
                for f in self.sp.q:
                    f(e)


def build(stop=99, dbg=False):
    nc = bass.Bass("TRN2", target_bir_lowering=False)
    di = lambda n, s, d=F32: nc.dram_tensor(n, list(s), d, kind="ExternalInput").ap()
    xin = di("xin", [NT, D])
    vecs = di("vecs", [7, 128, 128])
    rope = di("rope", [2, 32, NL])
    dft = di("dft", [4, 32, 128, 2, 512], BF16)
    dftc = di("dftc", [2, 128, 2, 256], BF16)
    ccsc = di("ccsc", [128, 2, 128])
    maskd = di("maskd", [128, 2])
    w_ada = di("w_ada", [DEPTH, D, 9 * D])
    w1a = di("w1_ffn1", [DEPTH, D, DFF]); w3a = di("w3_ffn1", [DEPTH, D, DFF]); w2a = di("w2_ffn1", [DEPTH, DFF, D])
    w1b = di("w1_ffn2", [DEPTH, D, DFF]); w3b = di("w3_ffn2", [DEPTH, D, DFF]); w2b = di("w2_ffn2", [DEPTH, DFF, D])
    w_in = di("w_in", [DEPTH, D, 1952])
    w_krp = di("w_krp", [DEPTH, D, 192])
    w_pw = di("w_pw_conv", [DEPTH, 384, D])
    w_uq = di("w_uq", [DEPTH, 384, 768])
    w_uqp = di("w_uqp", [DEPTH, 384, 768])
    w_ukv = di("w_ukv", [DEPTH, 256, 1024])
    w_o = di("w_o_mla", [DEPTH, 512, D])
    w_fo = di("w_fourier", [DEPTH, 512, D])
    w_bg = di("w_bgate", [DEPTH, D, 3 * D])
    w_out = di("w_out", [DEPTH, D, D])
    yout = nc.dram_tensor("yout", [NL, D], F32, kind="ExternalOutput").ap()

    dt_ = lambda n, s: nc.dram_tensor(n, list(s), BF16)
    XA = dt_("XA", [288, NT]); GA = dt_("GA", [576, NT])
    XG = dt_("XG", [NL, 512]); GG = dt_("GG", [2 * NL, 512]); XGc = dt_("XGc", [NCX, 512])
    XH = dt_("XH", [128, 90]); GH = dt_("GH", [256, 90])
    dd = (lambda n, s: nc.dram_tensor(n, list(s), BF16, kind="ExternalOutput")) if dbg else dt_
    Dcqn = dd("Dcqn", [3 * 128, NT]); Dsv = dd("Dsv", [3 * 128, NT])
    DF = dd("DF", [4 * 128, NT]); Dat = dd("Dat", [4 * 128, NT])

    with ExitStack() as st:
        fw = FW(nc, st)
        pe, act, dve, pool, sp = fw.pe, fw.act, fw.dve, fw.pool, fw.sp
        sb = lambda n, s, d: st.enter_context(nc.sbuf_tensor(n, list(s), d))

        hT = sb("hT", [128, KC, NT], F32)
        Rh = [Res(f"h{i}") for i in range(5)]
        UR = sb("UR", [128, KC * NT], BF16)
        uT = UR[:, :].rearrange("p (c t) -> p c t", c=KC)
        Ru = [Res(f"u{i}") for i in range(5)]
        AR = sb("AR", [128, KC * NT], BF16)
        RA = Res("A")
        RAB = [Res("A0"), Res("A1")]

        PSn = 8
        PS = [st.enter_context(nc.psum_tensor(f"ps{i}", [128, 512], F32)) for i in range(PSn)]
        RPS = [Res(f"ps{i}") for i in range(PSn)]
        psi = [0]
        NROT = 6

        def ps():
            i = psi[0] % NROT
            psi[0] += 1
            return PS[i], RPS[i]

        T32 = sb("T32", [128, 8, 512], F32)
        RT32 = [Res(f"t32_{i}") for i in range(8)]
        t32i = [0]

        def t32():
            i = t32i[0] % 8
            t32i[0] += 1
            return T32[:, i, :], RT32[i]

        NT16 = 6
        T16 = sb("T16", [128, NT16, 512], BF16)
        RT16 = [Res(f"t16_{i}") for i in range(NT16)]
        t16i = [0]

        def t16():
            i = t16i[0] % NT16
            t16i[0] += 1
            return T16[:, i, :], RT16[i]

        NWS = 6
        WS = sb("WS", [128, NWS, 2048], BF16)
        RWS = [Res(f"ws{i}") for i in range(NWS)]
        wsi = [0]

        def ws():
            i = wsi[0] % NWS
            wsi[0] += 1
            return WS[:, i, :], RWS[i]

        ident = sb("ident", [128, 128], F32)
        ones32 = sb("ones32", [128, 128], F32)
        cst = sb("cst", [128, 4], F32)
        VT = sb("VT", [128, 7, 128], F32)
        scb = sb("scb", [128, KC, 2], BF16)
        MOD = sb("MOD", [128, 72, 2], F32)
        DER = sb("DER", [128, 6, KC, 2], F32)
        CS32 = sb("CS32", [128, 2, 128], F32)
        MSK = sb("MSK", [128, 2], F32)
        TT2 = sb("TT2", [128, 2, 512], F32)
        RTT2 = [Res("tt2_0"), Res("tt2_1")]
        QT = sb("QT", [128, 2, 512], BF16)
        RQT = [Res("qt0"), Res("qt1")]
        vTc = sb("vTc", [128, 3, 286], BF16)
        HT = sb("HT", [128, 2, 90], BF16)
        Rc = {k: Res(k) for k in "ident ones cst cst2 VT scb MOD DER CS32 MSK vTc HT XA GA XG XGc GG XH GH Dcqn Dsv DF Dat".split()}

        def mm(out, lhsT, rhs, start, stop, reads, writes, sig=None):
            fw.op(pe, lambda h: h.matmul(out, lhsT=lhsT, rhs=rhs, start=start, stop=stop),
                  reads=reads, writes=writes, sig=(stop if sig is None else sig))

        def handover(srcs, dsts):
            fw.op(dve, lambda h: h.memset(cst[:, 2:3], 0.0), reads=list(srcs), writes=list(dsts) + [Rc["cst2"]])

        fw.op(pool, lambda h: h.memset(ident[:], 1.0), writes=[Rc["ident"]])
        fw.op(pool, lambda h: h.affine_select(out=ident[:], in_=ident[:], pattern=[[-1, 128]],
                                              compare_op=ALU.is_equal, fill=0.0, base=0, channel_multiplier=1),
              reads=[Rc["ident"]], writes=[Rc["ident"]])
        fw.op(pool, lambda h: h.memset(ones32[:], 1.0), writes=[Rc["ones"]])
        fw.op(pool, lambda h: h.memset(cst[:, 0:1], EPS), writes=[Rc["cst"]])
        fw.op(pool, lambda h: h.memset(cst[:, 1:2], 0.0), writes=[Rc["cst"]])
        fw.op(pool, lambda h: h.memset(vTc[:], 0.0), writes=[Rc["vTc"]])
        fw.dma(sp, CS32[:], ccsc[:, :, :], writes=[Rc["CS32"]])
        fw.dma(sp, MSK[:], maskd[:, :], writes=[Rc["MSK"]])
        for i in range(7):
            t, r = t32()
            fw.dma(sp, t[:, 0:128], vecs[i, :, :], writes=[r])
            p, pr = ps()
            fw.op(pe, lambda h, p=p, t=t: h.transpose(out=p[:, 0:128], in_=t[:, 0:128], identity=ident[:]),
                  reads=[r, Rc["ident"]], writes=[pr])
            fw.op(dve, lambda h, p=p, i=i: h.tensor_copy(out=VT[:, i, :], in_=p[:, 0:128]), reads=[pr], writes=[Rc["VT"]])
        VG = VT[:, 0, :]
        VA = lambda l: VT[:, 1 + 3 * l, :]
        VB = lambda l: VT[:, 2 + 3 * l, :]
        VC = lambda l: VT[:, 3 + 3 * l, :]
        for t in range(2):
            fw.op(act, lambda h, t=t: h.activation(out=scb[:, :, t], in_=VG[:, 8 * t:8 * t + 8], func=AF.Silu),
                  reads=[Rc["VT"]], writes=[Rc["scb"]])

        for ti in range(NT // 128):
            bi = min(ti // 4, 4)
            for half in range(2):
                t, r = t32()
                fw.dma(sp, t, xin[ti * 128:(ti + 1) * 128, half * 512:(half + 1) * 512], writes=[r])
                p, pr = ps()
                for kk in range(4):
                    fw.op(pe, lambda h, p=p, t=t, kk=kk: h.transpose(out=p[:, kk * 128:(kk + 1) * 128],
                                                                     in_=t[:, kk * 128:(kk + 1) * 128], identity=ident[:]),
                          reads=[r, Rc["ident"]], writes=[pr], sig=(kk == 3))
                eng = dve if half == 0 else act
                if half == 0:
                    fw.op(dve, lambda h, p=p, ti=ti, half=half: h.tensor_copy(
                        out=hT[:, half * 4:half * 4 + 4, ti * 128:(ti + 1) * 128],
                        in_=p[:, :].rearrange("p (c t) -> p c t", c=4)), reads=[pr], writes=[Rh[bi]])
                else:
                    fw.op(act, lambda h, p=p, ti=ti, half=half: h.copy(
                        out=hT[:, half * 4:half * 4 + 4, ti * 128:(ti + 1) * 128],
                        in_=p[:, :].rearrange("p (c t) -> p c t", c=4)), reads=[pr], writes=[Rh[bi]])

        def wload(src_ap, shape3):
            w, wr = ws()
            a, b = shape3
            v = w[:, 0:a * b].rearrange("p (a b) -> p a b", a=a)
            fw.dma(pool, v, src_ap, writes=[wr])
            return v, wr

        def kcview(wap, c0, n):
            return wap[:, c0:c0 + n].rearrange("(kc p) n -> p kc n", p=128)

        def rstd_from(pst, n, scale):
            rs, rsr = t32()
            fw.op(act, lambda h: h.activation(out=rs[:, :n], in_=pst[0][:, :n], func=AF.Sqrt, bias=cst[:, 0:1], scale=scale),
                  reads=[pst[1], Rc["cst"]], writes=[rsr])
            fw.op(dve, lambda h: h.reciprocal(out=rs[:, :n], in_=rs[:, :n]), reads=[rsr], writes=[rsr])
            return rs, rsr

        def mod_stage(l):
            pm, pmr = ps()
            for s in range(36):
                wv, wr = wload(kcview(w_ada[l], s * 256, 256), (KC, 256))
                for jj in range(2):
                    j = s * 2 + jj
                    for kc in range(KC):
                        mm(pm[:, j * 2:j * 2 + 2], wv[:, kc, jj * 128:(jj + 1) * 128], scb[:, kc, :], kc == 0, kc == KC - 1,
                           [wr, Rc["scb"]], [pmr])
            pmv = pm[:, 0:144].rearrange("p (j t) -> p j t", t=2)
            va = VA(l)
            for t in range(2):
                fw.op(dve, lambda h, t=t: h.tensor_tensor(out=MOD[:, :, t], in0=pmv[:, :, t], in1=va[:, 0:72], op=ALU.add),
                      reads=[pmr, Rc["VT"]], writes=[Rc["MOD"]])
                for i, (gcol, n) in enumerate([(72, 1), (80, 4), (88, 7)]):
                    fw.op(dve, lambda h, t=t, i=i, n=n: h.tensor_scalar(out=DER[:, i, :, t], in0=MOD[:, n * 8:(n + 1) * 8, t],
                                                                       scalar1=1.0, scalar2=None, op0=ALU.add),
                          reads=[Rc["MOD"]], writes=[Rc["DER"]])
                    fw.op(dve, lambda h, t=t, i=i, gcol=gcol: h.tensor_tensor(out=DER[:, i, :, t], in0=DER[:, i, :, t],
                                                                              in1=va[:, gcol:gcol + 8], op=ALU.mult),
                          reads=[Rc["DER"], Rc["VT"]], writes=[Rc["DER"]])
                for i, (n, f) in enumerate([(2, 0.5), (5, 1.0), (8, 0.5)]):
                    fw.op(dve, lambda h, t=t, i=i, n=n, f=f: h.tensor_scalar(out=DER[:, 3 + i, :, t], in0=MOD[:, n * 8:(n + 1) * 8, t],
                                                                             scalar1=f, scalar2=None, op0=ALU.mult),
                          reads=[Rc["MOD"]], writes=[Rc["DER"]])

        def norm_stage(idx, tbs):
            for bi, (t0, n) in tbs:
                ts = 0 if t0 < NL else 1
                pst, pstr = ps()
                for kc in range(KC):
                    sq, sqr = t32()
                    fw.op(act, lambda h, sq=sq, kc=kc: h.activation(out=sq[:, :n], in_=hT[:, kc, t0:t0 + n], func=AF.Square),
                          reads=[Rh[bi]], writes=[sqr])
                    mm(pst[:, :n], ones32[:], sq[:, :n], kc == 0, kc == KC - 1, [sqr, Rc["ones"]], [pstr])
                rs, rsr = rstd_from((pst, pstr), n, 1.0 / D)
                for kc in range(KC):
                    tt, ttr = TT2[:, kc % 2, :], RTT2[kc % 2]
                    fw.op(dve, lambda h, tt=tt, kc=kc: h.scalar_tensor_tensor(
                        out=tt[:, :n], in0=hT[:, kc, t0:t0 + n], scalar=DER[:, idx, kc, ts:ts + 1], in1=rs[:, :n],
                        op0=ALU.mult, op1=ALU.mult), reads=[Rh[bi], rsr, Rc["DER"]], writes=[ttr])
                    fw.op(act, lambda h, tt=tt, kc=kc: h.activation(
                        out=uT[:, kc, t0:t0 + n], in_=tt[:, :n], func=AF.Identity,
                        bias=MOD[:, 3 * idx * 8 + kc, ts:ts + 1], scale=1.0), reads=[ttr, Rc["MOD"]], writes=[Ru[bi]])

        def ffn_stage(l, w1, w3, w2, gidx, tbs):
            groups = [(0, 4), (4, 4), (8, 4), (12, 4), (16, 4), (20, 2)]
            for gi, (j0, nj) in enumerate(groups):
                half = gi % 2
                ab = AR[:, half * 4 * NT:(half + 1) * 4 * NT].rearrange("p (c t) -> p c t", c=4)
                abr = RAB[half]
                for sub in range(nj // 2):
                    c0 = (j0 + sub * 2) * 128
                    w1v, w1r = wload(kcview(w1[l], c0, 256), (KC, 256))
                    w3v, w3r = wload(kcview(w3[l], c0, 256), (KC, 256))
                    for jj in range(2):
                        ja = sub * 2 + jj
                        for bi, (t0, n) in tbs:
                            p1, p1r = ps()
                            p3, p3r = ps()
                            for kc in range(KC):
                                mm(p1[:, :n], w1v[:, kc, jj * 128:(jj + 1) * 128], uT[:, kc, t0:t0 + n], kc == 0, kc == KC - 1,
                                   [w1r, Ru[bi]], [p1r])
                            for kc in range(KC):
                                mm(p3[:, :n], w3v[:, kc, jj * 128:(jj + 1) * 128], uT[:, kc, t0:t0 + n], kc == 0, kc == KC - 1,
                                   [w3r, Ru[bi]], [p3r])
                            s, sr = t32()
                            fw.op(act, lambda h, s=s, p1=p1, n=n: h.activation(out=s[:, :n], in_=p1[:, :n], func=AF.Silu),
                                  reads=[p1r], writes=[sr])
                            fw.op(dve, lambda h, s=s, p3=p3, n=n, ja=ja, t0=t0, ab=ab: h.tensor_tensor(
                                out=ab[:, ja, t0:t0 + n], in0=s[:, :n], in1=p3[:, :n], op=ALU.mult),
                                reads=[sr, p3r], writes=[abr])
                w2v = []
                for sub in range(nj // 2):
                    r0 = (j0 + sub * 2) * 128
                    w2v.append(wload(w2[l, r0:r0 + 256, :].rearrange("(j p) n -> p j n", p=128), (2, D)))
                for bi, (t0, n) in tbs:
                    ts = 0 if t0 < NL else 1
                    for m in range(KC):
                        po, por = ps()
                        for ja in range(nj):
                            wv, wr = w2v[ja // 2]
                            mm(po[:, :n], wv[:, ja % 2, m * 128:(m + 1) * 128], ab[:, ja, t0:t0 + n], ja == 0, ja == nj - 1,
                               [wr, abr], [por])
                        fw.op(dve, lambda h, po=po, m=m, t0=t0, n=n, ts=ts: h.scalar_tensor_tensor(
                            out=hT[:, m, t0:t0 + n], in0=po[:, :n], scalar=DER[:, 3 + gidx, m, ts:ts + 1],
                            in1=hT[:, m, t0:t0 + n], op0=ALU.mult, op1=ALU.add),
                            reads=[por, Rh[bi], Rc["DER"]], writes=[Rh[bi]])

        def mixer_stage(l, last, mstop):
            tbs_all = list(enumerate(TBS))
            tbs_c = tbs_all if not last else tbs_all[:4]
            vb = VB(l)
            vc = VC(l)
            va = VA(l)
            vTl = AR[:, 0:3 * 2078].rearrange("p (c t) -> p c t", c=3)
            DG = AR[:, 6234:6234 + 93 * 128].rearrange("p (j m) -> p j m", j=93)

            wc = [wload(kcview(w_in[l], s * 256, 256), (KC, 256)) for s in range(3)]
            for bi, (t0, n) in tbs_c:
                for cc in range(3):
                    pa, par = ps()
                    pg, pgr = ps()
                    ca, cg = cc * 128, 384 + cc * 128
                    wa, war = wc[ca // 256]
                    wg, wgr = wc[cg // 256]
                    for kc in range(KC):
                        mm(pa[:, :n], wa[:, kc, ca % 256:ca % 256 + 128], uT[:, kc, t0:t0 + n], kc == 0, kc == KC - 1, [war, Ru[bi]], [par])
                    for kc in range(KC):
                        mm(pg[:, :n], wg[:, kc, cg % 256:cg % 256 + 128], uT[:, kc, t0:t0 + n], kc == 0, kc == KC - 1, [wgr, Ru[bi]], [pgr])
                    s, sr = t32()
                    fw.op(act, lambda h, s=s, pg=pg, n=n: h.activation(out=s[:, :n], in_=pg[:, :n], func=AF.Sigmoid), reads=[pgr], writes=[sr])
                    if t0 < NL:
                        fw.op(dve, lambda h, s=s, pa=pa, n=n, cc=cc, t0=t0: h.tensor_tensor(
                            out=vTl[:, cc, 15 + t0:15 + t0 + n], in0=s[:, :n], in1=pa[:, :n], op=ALU.mult), reads=[sr, par], writes=[RA])
                    else:
                        fw.op(dve, lambda h, s=s, pa=pa, n=n, cc=cc: h.tensor_tensor(
                            out=vTc[:, cc, 15:15 + n], in0=s[:, :n], in1=pa[:, :n], op=ALU.mult), reads=[sr, par], writes=[Rc["vTc"]])
            XHv = XH.ap().rearrange("p (c t) -> p c t", c=3)
            fw.dma(sp, XHv[:, :, 0:15], vTl[:, :, 15:30], reads=[RA], writes=[Rc["XH"]])
            fw.dma(sp, XHv[:, :, 15:30], vTl[:, :, 15 + NL - 15:15 + NL], reads=[RA], writes=[Rc["XH"]])

            wq = [wload(kcview(w_in[l], 768, 256), (KC, 256)), wload(kcview(w_in[l], 1024, 128), (KC, 128))]
            wkv = wload(kcview(w_in[l], 1152, 256), (KC, 256))
            wkr = wload(kcview(w_krp[l], 0, 192), (KC, 192))
            XAa = XA.ap()
            Dcq = Dcqn.ap()
            for bi, (t0, n) in tbs_all:
                lat = t0 < NL
                for (nch, wsel, gcol, dst, need) in ((3, "q", 33, Dcq, (lat or not last)), (2, "kv", 36, XAa, True)):
                    if not need:
                        continue
                    pcs = []
                    for cc in range(nch):
                        pq, pqr = ps()
                        if wsel == "q":
                            wv, wr = wq[0] if cc < 2 else wq[1]
                            col = (cc % 2) * 128 if cc < 2 else 0
                        else:
                            wv, wr = wkv
                            col = cc * 128
                        for kc in range(KC):
                            mm(pq[:, :n], wv[:, kc, col:col + 128], uT[:, kc, t0:t0 + n], kc == 0, kc == KC - 1, [wr, Ru[bi]], [pqr])
                        pcs.append((pq, pqr))
                    pst, pstr = ps()
                    for cc in range(nch):
                        sq, sqr = t32()
                        fw.op(act, lambda h, sq=sq, pq=pcs[cc][0], n=n: h.activation(out=sq[:, :n], in_=pq[:, :n], func=AF.Square),
                              reads=[pcs[cc][1]], writes=[sqr])
                        mm(pst[:, :n], ones32[:], sq[:, :n], cc == 0, cc == nch - 1, [sqr, Rc["ones"]], [pstr])
                    rs, rsr = rstd_from((pst, pstr), n, 1.0 / (128 * nch))
                    for cc in range(nch):
                        o16, o16r = t16()
                        fw.op(dve, lambda h, o16=o16, pq=pcs[cc][0], n=n, cc=cc, gcol=gcol, rs=rs: h.scalar_tensor_tensor(
                            out=o16[:, :n], in0=pq[:, :n], scalar=vb[:, gcol + cc:gcol + cc + 1], in1=rs[:, :n],
                            op0=ALU.mult, op1=ALU.mult), reads=[pcs[cc][1], rsr, Rc["VT"]], writes=[o16r])
                        fw.dma(sp, dst[cc * 128:(cc + 1) * 128, t0:t0 + n], o16[:, :n], reads=[o16r],
                               writes=[Rc["Dcqn"] if wsel == "q" else Rc["XA"]])
                pk, pkr = ps()
                pp, ppr = ps()
                for kc in range(KC):
                    mm(pk[0:96, :n], wkr[0][:, kc, 0:96], uT[:, kc, t0:t0 + n], kc == 0, kc == KC - 1, [wkr[1], Ru[bi]], [pkr])
                for kc in range(KC):
                    mm(pp[0:96, :n], wkr[0][:, kc, 96:192], uT[:, kc, t0:t0 + n], kc == 0, kc == KC - 1, [wkr[1], Ru[bi]], [ppr])
                o16, o16r = t16()
                if lat:
                    a1, a1r = t32()
                    a2, a2r = t32()
                    fw.dma(sp, a1[64:96, :], rope[0, :, t0:t0 + 512], writes=[a1r])
                    fw.dma(sp, a2[64:96, :], rope[1, :, t0:t0 + 512], writes=[a2r])
                    fw.op(dve, lambda h, a1=a1, pk=pk: h.tensor_tensor(out=a1[64:96, :], in0=pk[64:96, :], in1=a1[64:96, :], op=ALU.mult),
                          reads=[pkr, a1r], writes=[a1r])
                    fw.op(dve, lambda h, a2=a2, pp=pp: h.tensor_tensor(out=a2[64:96, :], in0=pp[64:96, :], in1=a2[64:96, :], op=ALU.mult),
                          reads=[ppr, a2r], writes=[a2r])
                    fw.op(dve, lambda h, a1=a1, a2=a2, o16=o16: h.tensor_tensor(out=o16[64:96, :], in0=a1[64:96, :], in1=a2[64:96, :], op=ALU.add),
                          reads=[a1r, a2r], writes=[o16r])
                else:
                    fw.op(dve, lambda h, o16=o16, pk=pk, n=n: h.tensor_copy(out=o16[64:96, :n], in_=pk[64:96, :n]), reads=[pkr], writes=[o16r])
                fw.dma(sp, XAa[256:288, t0:t0 + n], o16[64:96, :n], reads=[o16r], writes=[Rc["XA"]])

            wf = [wload(kcview(w_in[l], 1440 + s * 256, 256), (KC, 256)) for s in range(2)]
            XGa = XG.ap()
            ntt = 18 if not last else 16
            for tt in range(ntt):
                bi = min(tt // 4, 4)
                pgm, pgr = ps()
                for s in range(2):
                    for kc in range(KC):
                        mm(pgm[:, s * 256:(s + 1) * 256], uT[:, kc, tt * 128:(tt + 1) * 128], wf[s][0][:, kc, :], kc == 0, kc == KC - 1,
                           [wf[s][1], Ru[bi]], [pgr])
                o16, o16r = t16()
                if tt % 2 == 0:
                    fw.op(dve, lambda h, o16=o16, pgm=pgm: h.tensor_copy(out=o16[:, :], in_=pgm[:, :]), reads=[pgr], writes=[o16r])
                else:
                    fw.op(act, lambda h, o16=o16, pgm=pgm: h.copy(out=o16[:, :], in_=pgm[:, :]), reads=[pgr], writes=[o16r])
                if tt < 16:
                    fw.dma(sp, XGa[tt * 128:(tt + 1) * 128, :], o16[:, :], reads=[o16r], writes=[Rc["XG"]])
                else:
                    fw.dma(sp, XGc.ap()[(tt - 16) * 128:(tt - 15) * 128, :], o16[:, :], reads=[o16r], writes=[Rc["XGc"]])

            fw.allgather(XH.ap(), GH.ap(), [Rc["XH"]], [Rc["GH"]])
            fw.allgather(XA.ap(), GA.ap(), [Rc["XA"]], [Rc["GA"]])
            fw.allgather(XG.ap(), GG.ap(), [Rc["XG"]], [Rc["GG"]])
            if mstop <= 1:
                return

            GHa = GH.ap()
            fw.dma(sp, HT[:, :, :], GHa.rearrange("(r p) f -> p r f", p=128), reads=[Rc["GH"]], writes=[Rc["HT"]])
            HTv = HT[:, :, :].rearrange("p r (c t) -> p r c t", c=3)
            fw.op(dve, lambda h: h.tensor_scalar(out=vTl[:, :, 0:15], in0=HTv[:, 0, :, 15:30], scalar1=MSK[:, 0:1], scalar2=None, op0=ALU.mult),
                  reads=[Rc["HT"], Rc["MSK"]], writes=[RA])
            fw.op(dve, lambda h: h.tensor_scalar(out=vTl[:, :, 15 + NL:30 + NL], in0=HTv[:, 1, :, 0:15], scalar1=MSK[:, 1:2], scalar2=None, op0=ALU.mult),
                  reads=[Rc["HT"], Rc["MSK"]], writes=[RA])
            for idx in range(93):
                fw.op(dve, lambda h, idx=idx: h.tensor_scalar(out=DG[:, idx, :], in0=ident[:], scalar1=vc[:, idx:idx + 1], scalar2=None, op0=ALU.mult),
                      reads=[Rc["ident"], Rc["VT"]], writes=[RA])
            Dsva = Dsv.ap()
            for bi, (t0, n) in tbs_c:
                lat = t0 < NL
                cos_ = []
                for cc in range(3):
                    pc, pcr = ps()
                    for j in range(31):
                        rhs = vTl[:, cc, t0 + j:t0 + j + n] if lat else vTc[:, cc, j:j + n]
                        mm(pc[:, :n], DG[:, j * 3 + cc, :], rhs, j == 0, j == 30, [RA] if lat else [RA, Rc["vTc"]], [pcr])
                    co, cor = t32()
                    fw.op(act, lambda h, co=co, pc=pc, n=n, cc=cc: h.activation(out=co[:, :n], in_=pc[:, :n], func=AF.Identity,
                                                                              bias=vb[:, 24 + cc:25 + cc], scale=1.0),
                          reads=[pcr, Rc["VT"]], writes=[cor])
                    cos_.append((co, cor))
                pss, pssr = ps()
                psq, psqr = ps()
                for cc in range(3):
                    mm(pss[:, :n], ones32[:], cos_[cc][0][:, :n], cc == 0, cc == 2, [cos_[cc][1], Rc["ones"]], [pssr])
                for cc in range(3):
                    sq, sqr = t32()
                    fw.op(act, lambda h, sq=sq, co=cos_[cc][0], n=n: h.activation(out=sq[:, :n], in_=co[:, :n], func=AF.Square),
                          reads=[cos_[cc][1]], writes=[sqr])
                    mm(psq[:, :n], ones32[:], sq[:, :n], cc == 0, cc == 2, [sqr, Rc["ones"]], [psqr])
                mu, mur = t32()
                fw.op(dve, lambda h, mu=mu, pss=pss, n=n: h.tensor_scalar(out=mu[:, :n], in0=pss[:, :n], scalar1=1.0 / 384, scalar2=None, op0=ALU.mult),
                      reads=[pssr], writes=[mur])
                m2, m2r = t32()
                fw.op(dve, lambda h, mu=mu, m2=m2, n=n: h.tensor_tensor(out=m2[:, :n], in0=mu[:, :n], in1=mu[:, :n], op=ALU.mult), reads=[mur], writes=[m2r])
                fw.op(dve, lambda h, m2=m2, psq=psq, n=n: h.scalar_tensor_tensor(out=m2[:, :n], in0=psq[:, :n], scalar=1.0 / 384, in1=m2[:, :n],
                                                                               op0=ALU.mult, op1=ALU.subtract), reads=[psqr, m2r], writes=[m2r])
                fw.op(act, lambda h, m2=m2, n=n: h.activation(out=m2[:, :n], in_=m2[:, :n], func=AF.Sqrt, bias=cst[:, 0:1], scale=1.0),
                      reads=[m2r, Rc["cst"]], writes=[m2r])
                fw.op(dve, lambda h, m2=m2, n=n: h.reciprocal(out=m2[:, :n], in_=m2[:, :n]), reads=[m2r], writes=[m2r])
                for cc in range(3):
                    co, cor = cos_[cc]
                    fw.op(dve, lambda h, co=co, mu=mu, n=n: h.tensor_tensor(out=co[:, :n], in0=co[:, :n], in1=mu[:, :n], op=ALU.subtract),
                          reads=[cor, mur], writes=[cor])
                    fw.op(dve, lambda h, co=co, m2=m2, n=n: h.tensor_tensor(out=co[:, :n], in0=co[:, :n], in1=m2[:, :n], op=ALU.mult),
                          reads=[cor, m2r], writes=[cor])
                    o16, o16r = t16()
                    fw.op(act, lambda h, co=co, o16=o16, n=n, cc=cc: h.activation(out=o16[:, :n], in_=co[:, :n], func=AF.Silu,
                                                                                bias=vb[:, 30 + cc:31 + cc], scale=vb[:, 27 + cc:28 + cc]),
                          reads=[cor, Rc["VT"]], writes=[o16r])
                    fw.dma(sp, Dsva[cc * 128:(cc + 1) * 128, t0:t0 + n], o16[:, :n], reads=[o16r], writes=[Rc["Dsv"]])
            if mstop <= 2:
                return

            gfull = AR[:, 0:32 * 512].rearrange("p (t c) -> p t c", t=32)
            GGa = GG.ap()
            DFa = DF.ap()
            for r in range(2):
                fw.dma(sp, gfull[:, r * 16:(r + 1) * 16, :], GGa[r * NL:(r + 1) * NL, :].rearrange("(t p) c -> p t c", p=128),
                       reads=[Rc["GG"]], writes=[RA])

            def stage2(P, Pr, Q, Qr, n, dst):
                pS, pSr = t32()
                qS, qSr = t32()
                fw.op(dve, lambda h: h.tensor_copy(out=pS[:, :n], in_=P[:, :n]), reads=[Pr], writes=[pSr])
                fw.op(act, lambda h: h.copy(out=qS[:, :n], in_=Q[:, :n]), reads=[Qr], writes=[qSr])
                mm(P[:, :n], CS32[:, 0, :], pS[:, :n], True, False, [pSr, Rc["CS32"]], [Pr])
                mm(P[:, :n], CS32[:, 1, :], qS[:, :n], False, True, [qSr, Rc["CS32"]], [Pr])
                o16, o16r = t16()
                fw.op(act, lambda h: h.copy(out=o16[:, :n], in_=P[:, :n]), reads=[Pr], writes=[o16r])
                fw.dma(sp, dst, o16[:, :n], reads=[o16r], writes=[Rc["DF"]])

            for kb in range(4):
                for ti in range(32):
                    tab, tabr = ws()
                    tv = tab[:, 0:1024].rearrange("p (s k) -> p s k", s=2)
                    fw.dma(sp, tv, dft[kb, ti, :, :, :], writes=[tabr])
                    for gi in range(4):
                        mm(PS[gi][:, :], gfull[:, ti, gi * 128:(gi + 1) * 128], tv[:, 0, :], ti == 0, ti == 31, [RA, tabr], [RPS[gi]])
                        mm(PS[4 + gi][:, :], gfull[:, ti, gi * 128:(gi + 1) * 128], tv[:, 1, :], ti == 0, ti == 31, [RA, tabr], [RPS[4 + gi]],
                           sig=(True if gi == 3 else None))
                for gi in range(4):
                    stage2(PS[gi], RPS[gi], PS[4 + gi], RPS[4 + gi], 512, DFa[gi * 128:(gi + 1) * 128, kb * 512:(kb + 1) * 512])
            if not last:
                gc, gcr = ws()
                gcv = gc[:, 0:1024].rearrange("p (t c) -> p t c", t=2)
                fw.dma(sp, gcv, XGc.ap().rearrange("(t p) c -> p t c", p=128), reads=[Rc["XGc"]], writes=[gcr])
                tc_, tcr = ws()
                tcv = tc_[:, 0:1024].rearrange("p (t s k) -> p t s k", t=2, s=2)
                fw.dma(sp, tcv, dftc.rearrange("t p s k -> p t s k"), writes=[tcr])
                for gi in range(4):
                    P, Pr = PS[gi], RPS[gi]
                    Q, Qr = PS[4 + gi], RPS[4 + gi]
                    for tl in range(2):
                        mm(P[:, :256], gcv[:, tl, gi * 128:(gi + 1) * 128], tcv[:, tl, 0, :], tl == 0, tl == 1, [gcr, tcr], [Pr])
                    for tl in range(2):
                        mm(Q[:, :256], gcv[:, tl, gi * 128:(gi + 1) * 128], tcv[:, tl, 1, :], tl == 0, tl == 1, [gcr, tcr], [Qr])
                    stage2(P, Pr, Q, Qr, 256, DFa[gi * 128:(gi + 1) * 128, NL:NT])
            if mstop <= 3:
                return

            GAa = GA.ap()
            Data = Dat.ap()
            NK = SEQ + NCX
            KTb = [AR[:, b * 8704:b * 8704 + NK] for b in range(2)]
            VAb = [AR[:, b * 8704 + NK:(b + 1) * 8704].rearrange("p (t c) -> p t c", t=34) for b in range(2)]
            RKV = [Res("kv0"), Res("kv1")]
            fw.op(dve, lambda h: h.memset(VAb[0][:, :, 64:128], 1.0), reads=[RA], writes=[RA, RKV[0]])
            fw.op(dve, lambda h: h.memset(VAb[1][:, :, 0:64], 1.0), reads=[RA], writes=[RA, RKV[1]])
            qbs = tbs_c
            for hd in range(8):
                b = hd % 2
                voff = 0 if b == 0 else 64
                wsl, wslr = ws()
                wkvh = wsl[:, 0:256].rearrange("p (c n) -> p c n", c=2)
                wqh = wsl[:, 256:256 + 288].rearrange("p (c n) -> p c n", c=3)
                wqph = wsl[:, 544:544 + 288].rearrange("p (c n) -> p c n", c=3)
                fw.dma(pool, wkvh, w_ukv[l][:, hd * 128:(hd + 1) * 128].rearrange("(c p) n -> p c n", p=128), writes=[wslr])
                fw.dma(pool, wqh, w_uq[l][:, hd * 96:(hd + 1) * 96].rearrange("(c p) n -> p c n", p=128), writes=[wslr])
                fw.dma(pool, wqph, w_uqp[l][:, hd * 96:(hd + 1) * 96].rearrange("(c p) n -> p c n", p=128), writes=[wslr])
                for kb in range(9):
                    if kb < 8:
                        r, c0, n = kb // 4, (kb % 4) * 512, 512
                    else:
                        r, c0, n = 0, NL, NCX
                    kk0 = kb * 512
                    ck = []
                    for cc in range(2):
                        c16, c16r = t16()
                        fw.dma(sp, c16[:, :n], GAa[r * 288 + cc * 128:r * 288 + (cc + 1) * 128, c0:c0 + n], reads=[Rc["GA"]], writes=[c16r])
                        ck.append((c16, c16r))
                    fw.dma(sp, KTb[b][64:96, kk0:kk0 + n], GAa[r * 288 + 256:r * 288 + 288, c0:c0 + n], reads=[Rc["GA"]], writes=[RKV[b]])
                    pk, pkr = ps()
                    for cc in range(2):
                        mm(pk[0:64, :n], wkvh[:, cc, 0:64], ck[cc][0][:, :n], cc == 0, cc == 1, [wslr, ck[cc][1]], [pkr])
                    fw.op(act, lambda h, pk=pk, n=n, kk0=kk0, b=b: h.copy(out=KTb[b][0:64, kk0:kk0 + n], in_=pk[0:64, :n]),
                          reads=[pkr], writes=[RKV[b]])
                    pv, pvr = ps()
                    nt_ = n // 128
                    for tl in range(nt_):
                        for cc in range(2):
                            mm(pv[:, tl * 64:(tl + 1) * 64], ck[cc][0][:, tl * 128:(tl + 1) * 128], wkvh[:, cc, 64:128], cc == 0, cc == 1,
                               [wslr, ck[cc][1]], [pvr])
                    fw.op(dve, lambda h, pv=pv, nt_=nt_, kb=kb, b=b, voff=voff: h.tensor_copy(
                        out=VAb[b][:, kb * 4:kb * 4 + nt_, voff:voff + 64],
                        in_=pv[:, 0:nt_ * 64].rearrange("p (t c) -> p t c", t=nt_)), reads=[pvr], writes=[RKV[b]])
                for bi, (t0, n) in qbs:
                    lat = t0 < NL
                    cq = []
                    for cc in range(3):
                        c16, c16r = t16()
                        fw.dma(sp, c16[:, :n], Dcq[cc * 128:(cc + 1) * 128, t0:t0 + n], reads=[Rc["Dcqn"]], writes=[c16r])
                        cq.append((c16, c16r))
                    pq, pqr = ps()
                    for cc in range(3):
                        mm(pq[0:96, :n], wqh[:, cc, :], cq[cc][0][:, :n], cc == 0, cc == 2, [wslr, cq[cc][1]], [pqr])
                    q16, q16r = QT[:, qti[0] % 2, :], RQT[qti[0] % 2]
                    qti[0] += 1
                    if lat:
                        pp, ppr = ps()
                        for cc in range(3):
                            mm(pp[0:96, :n], wqph[:, cc, :], cq[cc][0][:, :n], cc == 0, cc == 2, [wslr, cq[cc][1]], [ppr])
                        fw.op(act, lambda h, q16=q16, pq=pq: h.copy(out=q16[0:64, :], in_=pq[0:64, :]), reads=[pqr], writes=[q16r])
                        a1, a1r = t32()
                        a2, a2r = t32()
                        fw.dma(sp, a1[64:96, :], rope[0, :, t0:t0 + 512], writes=[a1r])
                        fw.dma(sp, a2[64:96, :], rope[1, :, t0:t0 + 512], writes=[a2r])
                        fw.op(dve, lambda h, a1=a1, pq=pq: h.tensor_tensor(out=a1[64:96, :], in0=pq[64:96, :], in1=a1[64:96, :], op=ALU.mult),
                              reads=[pqr, a1r], writes=[a1r])
                        fw.op(dve, lambda h, a2=a2, pp=pp: h.tensor_tensor(out=a2[64:96, :], in0=pp[64:96, :], in1=a2[64:96, :], op=ALU.mult),
                              reads=[ppr, a2r], writes=[a2r])
                        fw.op(dve, lambda h, a1=a1, a2=a2, q16=q16: h.tensor_tensor(out=q16[64:96, :], in0=a1[64:96, :], in1=a2[64:96, :], op=ALU.add),
                              reads=[a1r, a2r], writes=[q16r])
                        kts = list(range(34))
                    else:
                        fw.op(act, lambda h, q16=q16, pq=pq, n=n: h.copy(out=q16[0:96, :n], in_=pq[0:96, :n]), reads=[pqr], writes=[q16r])
                        kts = [32, 33]
                    ob = 6 + (obi[0] % 2)
                    obi[0] += 1
                    po, por = PS[ob], RPS[ob]
                    for ki, kt in enumerate(kts):
                        psc, pscr = ps()
                        mm(psc[:, :n], KTb[b][0:96, kt * 128:(kt + 1) * 128], q16[0:96, :n], True, True, [RKV[b], q16r], [pscr])
                        pt, ptr = t16()
                        fw.op(act, lambda h, pt=pt, psc=psc, n=n: h.activation(out=pt[:, :n], in_=psc[:, :n], func=AF.Exp, scale=ATTN_SCALE),
                              reads=[pscr], writes=[ptr])
                        mm(po[:, :n], VAb[b][:, kt, :], pt[:, :n], ki == 0, ki == len(kts) - 1, [RKV[b], ptr], [por])
                    orow = slice(0, 64) if b == 0 else slice(64, 128)
                    drow = slice(64, 128) if b == 0 else slice(0, 64)
                    rd, rdr = t32()
                    fw.op(dve, lambda h, rd=rd, po=po, n=n, drow=drow: h.reciprocal(out=rd[drow, :n], in_=po[drow, :n]), reads=[por], writes=[rdr])
                    rsh, rshr = t32()
                    fw.op(dve, lambda h, rd=rd, rsh=rsh, n=n, drow=drow, orow=orow: h.tensor_copy(out=rsh[orow, :n], in_=rd[drow, :n]),
                          reads=[rdr], writes=[rshr])
                    ao, aor = t16()
                    fw.op(dve, lambda h, ao=ao, po=po, rsh=rsh, n=n, orow=orow: h.tensor_tensor(out=ao[orow, :n], in0=po[orow, :n], in1=rsh[orow, :n], op=ALU.mult),
                          reads=[por, rshr], writes=[aor])
                    pr_ = hd // 2
                    r0 = pr_ * 128 + (0 if b == 0 else 64)
                    fw.dma(sp, Data[r0:r0 + 64, t0:t0 + n], ao[orow, :n], reads=[aor], writes=[Rc["Dat"]])
            if mstop <= 4:
                return

            mixacc = AR[:, :].rearrange("p (c t) -> p c t", c=KC)
            RM = Res("mix")
            handover([RKV[0], RKV[1], RA], [RM, RA, RKV[0], RKV[1]])
            branches = [(Dsv.ap(), Rc["Dsv"], 3, w_pw, 96), (Dat.ap(), Rc["Dat"], 4, w_o, None), (DFa, Rc["DF"], 4, w_fo, 104)]
            first = True
            for r, (Dsrc, Dres, nch, wsrc, bcol) in enumerate(branches):
                wbg = [wload(kcview(w_bg[l], r * D + s * 256, 256), (KC, 256)) for s in range(4)]
                wr_ = [wload(wsrc[l].rearrange("(c p) n -> p c n", p=128)[:, :, s * 512:(s + 1) * 512], (nch, 512)) for s in range(2)]
                for bi, (t0, n) in tbs_c:
                    xin_ = []
                    for c in range(nch):
                        c16, c16r = t16()
                        fw.dma(sp, c16[:, :n], Dsrc[c * 128:(c + 1) * 128, t0:t0 + n], reads=[Dres], writes=[c16r])
                        xin_.append((c16, c16r))
                    for m in range(KC):
                        pgt, pgtr = ps()
                        wv, wr = wbg[m // 2]
                        for kc in range(KC):
                            mm(pgt[:, :n], wv[:, kc, (m % 2) * 128:(m % 2) * 128 + 128], uT[:, kc, t0:t0 + n], kc == 0, kc == KC - 1, [wr, Ru[bi]], [pgtr])
                        s, sr = t32()
                        fw.op(act, lambda h, s=s, pgt=pgt, n=n, r=r, m=m: h.activation(out=s[:, :n], in_=pgt[:, :n], func=AF.Sigmoid,
                                                                                     bias=vb[:, r * 8 + m:r * 8 + m + 1], scale=1.0),
                              reads=[pgtr, Rc["VT"]], writes=[sr])
                        py, pyr = ps()
                        wv2, wr2 = wr_[m // 4]
                        for c in range(nch):
                            mm(py[:, :n], wv2[:, c, (m % 4) * 128:(m % 4) * 128 + 128], xin_[c][0][:, :n], c == 0, c == nch - 1, [wr2, xin_[c][1]], [pyr])
                        bias = va[:, bcol + m:bcol + m + 1] if bcol is not None else 0.0
                        if first:
                            fw.op(dve, lambda h, py=py, s=s, n=n, m=m, t0=t0, bias=bias: h.scalar_tensor_tensor(
                                out=mixacc[:, m, t0:t0 + n], in0=py[:, :n], scalar=bias, in1=s[:, :n], op0=ALU.add, op1=ALU.mult),
                                reads=[pyr, sr, Rc["VT"]], writes=[RM])
                        else:
                            fw.op(dve, lambda h, py=py, s=s, n=n, bias=bias: h.scalar_tensor_tensor(
                                out=s[:, :n], in0=py[:, :n], scalar=bias, in1=s[:, :n], op0=ALU.add, op1=ALU.mult),
                                reads=[pyr, sr, Rc["VT"]], writes=[sr])
                            fw.op(dve, lambda h, s=s, n=n, m=m, t0=t0: h.tensor_tensor(
                                out=mixacc[:, m, t0:t0 + n], in0=mixacc[:, m, t0:t0 + n], in1=s[:, :n], op=ALU.add),
                                reads=[sr, RM], writes=[RM])
                first = False
            for mo in range(KC):
                wv, wr = wload(kcview(w_out[l], mo * 128, 128), (KC, 128))
                for bi, (t0, n) in tbs_c:
                    ts = 0 if t0 < NL else 1
                    po, por = ps()
                    for m in range(KC):
                        mm(po[:, :n], wv[:, m, :], mixacc[:, m, t0:t0 + n], m == 0, m == KC - 1, [wr, RM], [por])
                    fw.op(dve, lambda h, po=po, mo=mo, t0=t0, n=n, ts=ts: h.scalar_tensor_tensor(
                        out=hT[:, mo, t0:t0 + n], in0=po[:, :n], scalar=DER[:, 4, mo, ts:ts + 1],
                        in1=hT[:, mo, t0:t0 + n], op0=ALU.mult, op1=ALU.add),
                        reads=[por, Rh[bi], Rc["DER"]], writes=[Rh[bi]])
            handover([RM, RKV[0], RKV[1], RA], [RA, RAB[0], RAB[1]])

        obi = [0]
        qti = [0]
        tbs_all = list(enumerate(TBS))
        stage = 0
        done = False
        for l in range(DEPTH):
            last = l == DEPTH - 1
            mod_stage(l)
            norm_stage(0, tbs_all)
            ffn_stage(l, w1a, w3a, w2a, 0, tbs_all)
            stage += 1
            if stage >= stop:
                done = True
                break
            norm_stage(1, tbs_all)
            handover([RAB[0], RAB[1], RA], [RA, RAB[0], RAB[1]])
            ms = (stop - stage) if (stop - stage) < 6 else 99
            mixer_stage(l, last, ms)
            stage += 5
            if stage >= stop:
                done = True
                break
            tb2 = tbs_all if not last else tbs_all[:4]
            norm_stage(2, tb2)
            ffn_stage(l, w1b, w3b, w2b, 2, tb2)
            stage += 1
            if stage >= stop:
                done = True
                break

        vg = VT[:, 0, :]
        for bi, (t0, n) in tbs_all[:4]:
            if not done:
                pst, pstr = ps()
                for kc in range(KC):
                    sq, sqr = t32()
                    fw.op(act, lambda h, sq=sq, kc=kc: h.activation(out=sq[:, :n], in_=hT[:, kc, t0:t0 + n], func=AF.Square),
                          reads=[Rh[bi]], writes=[sqr])
                    mm(pst[:, :n], ones32[:], sq[:, :n], kc == 0, kc == KC - 1, [sqr, Rc["ones"]], [pstr])
                rs, rsr = rstd_from((pst, pstr), n, 1.0 / D)
                for kc in range(KC):
                    fw.op(dve, lambda h, kc=kc, rs=rs: h.scalar_tensor_tensor(
                        out=hT[:, kc, t0:t0 + n], in0=hT[:, kc, t0:t0 + n], scalar=vg[:, 16 + kc:17 + kc], in1=rs[:, :n],
                        op0=ALU.mult, op1=ALU.mult), reads=[Rh[bi], rsr, Rc["VT"]], writes=[Rh[bi]])
            for tl in range(4):
                tt = (t0 // 128) + tl
                for half in range(2):
                    p, pr = ps()
                    for kk in range(4):
                        kc = half * 4 + kk
                        fw.op(pe, lambda h, p=p, kk=kk, kc=kc, tt=tt: h.transpose(out=p[:, kk * 128:(kk + 1) * 128],
                                                                                 in_=hT[:, kc, tt * 128:(tt + 1) * 128], identity=ident[:]),
                              reads=[Rh[bi], Rc["ident"]], writes=[pr], sig=(kk == 3))
                    o, orr = t32()
                    if half == 0:
                        fw.op(dve, lambda h, o=o, p=p: h.tensor_copy(out=o[:, :], in_=p[:, :]), reads=[pr], writes=[orr])
                    else:
                        fw.op(act, lambda h, o=o, p=p: h.copy(out=o[:, :], in_=p[:, :]), reads=[pr], writes=[orr])
                    fw.dma(sp, yout[tt * 128:(tt + 1) * 128, half * 512:(half + 1) * 512], o[:, :], reads=[orr], writes=[Res()])
        for ds in fw.dsems:
            if ds.count:
                fw._wait(sp, ds.sem, ds.count)
        fw.run()
    return nc


def _tables(half):
    bf = ml_dtypes.bfloat16
    t = np.arange(SEQ, dtype=np.int64)
    k = np.arange(NL, dtype=np.int64) + half * NL
    ang = 2.0 * np.pi * ((t[:, None] * k[None, :]) % SEQ).astype(np.float64) / SEQ
    tab = np.stack([np.cos(ang) / 64.0, -np.sin(ang) / 64.0], axis=1)
    tab = tab.reshape(32, 128, 2, 4, 512).transpose(3, 0, 1, 2, 4)
    dft = np.ascontiguousarray(tab).astype(bf)
    tc = np.arange(NCX, dtype=np.int64)
    angc = 2.0 * np.pi * ((tc[:, None] * tc[None, :]) % NCX).astype(np.float64) / NCX
    tabc = np.stack([np.cos(angc) / 16.0, -np.sin(angc) / 16.0], axis=1).reshape(2, 128, 2, NCX)
    dftc = np.ascontiguousarray(tabc).astype(bf)
    return dft, dftc


def _rope(half):
    tok = np.arange(NL) + half * NL
    row = (tok // 64).astype(np.float32)
    col = (tok % 64).astype(np.float32)
    inv = (1.0 / (np.float32(10000.0) ** (np.arange(8, dtype=np.float32) * np.float32(2.0) / np.float32(16)))).astype(np.float32)
    ar = row[:, None] * inv
    ac = col[:, None] * inv
    ang = np.concatenate([ar, ar, ac, ac], axis=-1).astype(np.float32)
    cos = np.cos(ang).astype(np.float32).T
    sin = np.sin(ang).astype(np.float32).T
    sign = np.ones(32, np.float32)
    sign[0:8] = -1.0
    sign[16:24] = -1.0
    return np.ascontiguousarray(np.stack([cos, sin * sign[:, None]], 0)).astype(np.float32)


_PERM = np.array([(f + 8) if (f % 16) < 8 else (f - 8) for f in range(32)])


def _prep(inputs):
    f32 = np.float32
    g = {k: np.asarray(v, dtype=f32) for k, v in inputs.items()}
    L = DEPTH
    vecs_l = np.zeros((L, 3, 128, 128), f32)
    for l in range(L):
        A = vecs_l[l, 0]
        A[0:72] = g["b_ada"][l].reshape(72, 128)
        A[72:80] = g["g_ffn1"][l].reshape(8, 128)
        A[80:88] = g["g_mix"][l].reshape(8, 128)
        A[88:96] = g["g_ffn2"][l].reshape(8, 128)
        A[96:104] = g["b_pw_conv"][l].reshape(8, 128)
        A[104:112] = g["b_fourier"][l].reshape(8, 128)
        B = vecs_l[l, 1]
        B[0:24] = g["b_bgate"][l].reshape(24, 128)
        B[24:27] = g["b_dw"][l].reshape(3, 128)
        B[27:30] = g["ln_g_conv"][l].reshape(3, 128)
        B[30:33] = g["ln_b_conv"][l].reshape(3, 128)
        B[33:36] = g["g_qnorm"][l].reshape(3, 128)
        B[36:38] = g["g_kvnorm"][l].reshape(2, 128)
        vecs_l[l, 2, 0:93] = g["w_dw"][l].reshape(31 * 3, 128)
    w_krp = np.zeros((L, D, 192), f32)
    w_krp[:, :, 64:96] = g["w_in"][:, :, 1408:1440]
    w_krp[:, :, 160:192] = g["w_in"][:, :, 1408 + _PERM]
    w_uqp = np.zeros((L, 384, 768), f32)
    for hd in range(8):
        w_uqp[:, :, hd * 96 + 64:hd * 96 + 96] = g["w_uq"][:, :, hd * 96 + 64 + _PERM]
    cm = np.arange(128)
    angc = 2.0 * np.pi * ((cm[:, None] * cm[None, :]) % 128).astype(np.float64) / 128.0
    ccsc = np.stack([np.cos(angc), np.sin(angc)], axis=1) / np.sqrt(128.0)
    ccsc = np.ascontiguousarray(ccsc).astype(f32)
    shared = {k: np.ascontiguousarray(g[k]) for k in
              ("w_ada", "w1_ffn1", "w3_ffn1", "w2_ffn1", "w1_ffn2", "w3_ffn2", "w2_ffn2", "w_in", "w_pw_conv",
               "w_uq", "w_ukv", "w_o_mla", "w_fourier", "w_bgate", "w_out")}
    shared["w_krp"] = w_krp
    shared["w_uqp"] = w_uqp
    shared["ccsc"] = ccsc
    tabs = [_tables(0), _tables(1)]
    ropes = [_rope(0), _rope(1)]
    maps = []
    for c in range(8):
        b, half = c // 2, c % 2
        vecs = np.zeros((7, 128, 128), f32)
        vecs[0, 0:8] = g["c"][b].reshape(8, 128)
        vecs[0, 8:16] = g["c_ctx"].reshape(8, 128)
        vecs[0, 16:24] = g["g_final"].reshape(8, 128)
        vecs[1:4] = vecs_l[0]
        vecs[4:7] = vecs_l[1]
        m = dict(shared)
        m["xin"] = np.ascontiguousarray(np.concatenate([g["x"][b, half * NL:(half + 1) * NL], g["ctx"][b]], 0))
        m["vecs"] = vecs
        m["rope"] = ropes[half]
        m["dft"], m["dftc"] = tabs[half]
        mk = np.zeros((128, 2), f32)
        mk[:, 0] = 1.0 if half == 1 else 0.0
        mk[:, 1] = 1.0 if half == 0 else 0.0
        m["maskd"] = mk
        maps.append(m)
    return maps


def run(inputs, stop=99, cores=8, dbg=False, ret_all=False):
    nc = build(stop, dbg)
    maps = _prep(inputs)
    res = run_bass_kernel_spmd(nc, maps[:cores], core_ids=list(range(cores)))
    if ret_all:
        return res.results
    out = np.zeros((4, SEQ, D), np.float32)
    for c in range(cores):
        b, half = c // 2, c % 2
        out[b, half * NL:(half + 1) * NL] = res.results[c]["yout"]
    return out


def kernel(**inputs):
    return run(inputs)
```

```python
import numpy as np
import ml_dtypes
from contextlib import ExitStack
import concourse.bass as bass
import concourse.mybir as mybir
from concourse.bass_utils import run_bass_kernel_spmd

F32 = mybir.dt.float32
BF16 = mybir.dt.bfloat16
ALU = mybir.AluOpType
AF = mybir.ActivationFunctionType

D = 1024
KC = 8
NL = 2048
NCX = 256
NT = NL + NCX
SEQ = 4096
DFF = 2816
NJ = DFF // 128
DEPTH = 2
EPS = 1e-6
ATTN_SCALE = 96.0 ** -0.5
TBS = [(0, 512), (512, 512), (1024, 512), (1536, 512), (2048, 256)]
SAME_ENGINE_SYNC = True


class Res:
    __slots__ = ("w", "r", "name")

    def __init__(self, name=""):
        self.w = None
        self.r = {}
        self.name = name


class Eng:
    def __init__(self, name, sem):
        self.name, self.sem = name, sem
        self.count = 0
        self.seen = {}
        self.q = []


class DmaSem:
    def __init__(self, sem):
        self.sem = sem
        self.count = 0


class _Rec:
    def __init__(self):
        self.call = None

    def __getattr__(self, name):
        def f(*a, **k):
            self.call = (name, a, k)
            return self
        return f


class FW:
    def __init__(self, nc, stack, n_dma_sems=32):
        self.nc = nc
        mk = lambda n: stack.enter_context(nc.semaphore(n))
        self.pe = Eng("pe", mk("s_pe"))
        self.act = Eng("act", mk("s_act"))
        self.dve = Eng("dve", mk("s_dve"))
        self.pool = Eng("pool", mk("s_pool"))
        self.sp = Eng("sp", mk("s_sp"))
        self.dsems = [DmaSem(mk(f"s_dma{i}")) for i in range(n_dma_sems)]
        self.dnext = 0
        self.ccs = DmaSem(mk("s_cc"))

    def _wait(self, eng, sem, val):
        key = id(sem)
        if eng.seen.get(key, 0) >= val:
            return
        eng.q.append(lambda h, sem=sem, val=val: h.wait_ge(sem, val))
        eng.seen[key] = val

    def _deps(self, eng, reads, writes):
        deps = []
        for r in reads:
            if r.w is not None:
                deps.append(r.w)
        for w in writes:
            if w.w is not None:
                deps.append(w.w)
            deps.extend(w.r.values())
        for (sem, val, src) in deps:
            if src is eng and (eng.name == "pe" or not SAME_ENGINE_SYNC):
                continue
            self._wait(eng, sem, val)

    def _commit(self, ev, reads, writes):
        key = id(ev[0])
        for r in reads:
            old = r.r.get(key)
            if old is None or old[1] < ev[1]:
                r.r[key] = ev
        for w in writes:
            w.w = ev
            w.r = {}

    def op(self, eng, fn, reads=(), writes=(), sig=True):
        self._deps(eng, reads, writes)
        rec = _Rec()
        fn(rec)
        name, a, k = rec.call
        if sig:
            eng.count += 1
            eng.q.append(lambda h, name=name, a=a, k=k, sem=eng.sem: getattr(h, name)(*a, **k).then_inc(sem, 1))
            ev = (eng.sem, eng.count, eng)
        else:
            eng.q.append(lambda h, name=name, a=a, k=k: getattr(h, name)(*a, **k))
            ev = (eng.sem, eng.count + 1, eng)
        self._commit(ev, reads, writes)
        return ev

    def dma(self, q, out, in_, reads=(), writes=()):
        self._deps(q, reads, writes)
        ds = self.dsems[self.dnext]
        self.dnext = (self.dnext + 1) % len(self.dsems)
        if ds.count:
            self._wait(q, ds.sem, ds.count)
        ds.count += 16
        q.q.append(lambda h, out=out, in_=in_, sem=ds.sem: h.dma_start(out=out, in_=in_).then_inc(sem, 16))
        ev = (ds.sem, ds.count, None)
        self._commit(ev, reads, writes)
        return ev

    def allgather(self, src, dst, reads, writes):
        q = self.pool
        self._deps(q, reads, writes)
        cs = self.ccs
        if cs.count:
            self._wait(q, cs.sem, cs.count)
        cs.count += 1
        q.q.append(lambda h, src=src, dst=dst, sem=cs.sem: h.collective_compute(
            "AllGather", ALU.bypass, replica_groups=[[0, 1], [2, 3], [4, 5], [6, 7]],
            ins=[src], outs=[dst]).then_inc(sem, 1))
        ev = (cs.sem, cs.count, None)
        self._commit(ev, reads, writes)
        return ev

    def run(self):
        nc = self.nc
        with nc.Block() as block:
            @block.tensor
            def _(e):
                for f in self.pe.q:
                    f(e)

            @block.scalar
            def _(e):
                for f in self.act.q:
                    f(e)

            @block.vector
            def _(e):
                for f in self.dve.q:
                    f(e)

            @block.gpsimd
            def _(e):
                for f in self.pool.q:
                    f(e)

            @block.sync
            def _(e):
                for f in self.sp.q:
                    f(e)


def build(stop=99, dbg=False):
    nc = bass.Bass("TRN2", target_bir_lowering=False)
    di = lambda n, s, d=F32: nc.dram_tensor(n, list(s), d, kind="ExternalInput").ap()
    xin = di("xin", [NT, D])
    vecs = di("vecs", [7, 128, 128])
    rope = di("rope", [2, 32, NL])
    dft = di("dft", [4, 32, 128, 2, 512], BF16)
    dftc = di("dftc", [2, 128, 2, 256], BF16)
    ccsc = di("ccsc", [128, 2, 128])
    maskd = di("maskd", [128, 2])
    w_ada = di("w_ada", [DEPTH, D, 9 * D])
    w1a = di("w1_ffn1", [DEPTH, D, DFF]); w3a = di("w3_ffn1", [DEPTH, D, DFF]); w2a = di("w2_ffn1", [DEPTH, DFF, D])
    w1b = di("w1_ffn2", [DEPTH, D, DFF]); w3b = di("w3_ffn2", [DEPTH, D, DFF]); w2b = di("w2_ffn2", [DEPTH, DFF, D])
    w_in = di("w_in", [DEPTH, D, 1952])
    w_krp = di("w_krp", [DEPTH, D, 192])
    w_pw = di("w_pw_conv", [DEPTH, 384, D])
    w_uq = di("w_uq", [DEPTH, 384, 768])
    w_uqp = di("w_uqp", [DEPTH, 384, 768])
    w_ukv = di("w_ukv", [DEPTH, 256, 1024])
    w_o = di("w_o_mla", [DEPTH, 512, D])
    w_fo = di("w_fourier", [DEPTH, 512, D])
    w_bg = di("w_bgate", [DEPTH, D, 3 * D])
    w_out = di("w_out", [DEPTH, D, D])
    yout = nc.dram_tensor("yout", [NL, D], F32, kind="ExternalOutput").ap()

    dt_ = lambda n, s: nc.dram_tensor(n, list(s), BF16)
    XA = dt_("XA", [288, NT]); GA = dt_("GA", [576, NT])
    XG = dt_("XG", [NL, 512]); GG = dt_("GG", [2 * NL, 512]); XGc = dt_("XGc", [NCX, 512])
    XH = dt_("XH", [128, 90]); GH = dt_("GH", [256, 90])
    dd = (lambda n, s: nc.dram_tensor(n, list(s), BF16, kind="ExternalOutput")) if dbg else dt_
    Dcqn = dd("Dcqn", [3 * 128, NT]); Dsv = dd("Dsv", [3 * 128, NT])
    DF = dd("DF", [4 * 128, NT]); Dat = dd("Dat", [4 * 128, NT])

    with ExitStack() as st:
        fw = FW(nc, st)
        pe, act, dve, pool, sp = fw.pe, fw.act, fw.dve, fw.pool, fw.sp
        sb = lambda n, s, d: st.enter_context(nc.sbuf_tensor(n, list(s), d))

        hT = sb("hT", [128, KC, NT], F32)
        Rh = [Res(f"h{i}") for i in range(5)]
        UR = sb("UR", [128, KC * NT], BF16)
        uT = UR[:, :].rearrange("p (c t) -> p c t", c=KC)
        Ru = [Res(f"u{i}") for i in range(5)]
        AR = sb("AR", [128, KC * NT], BF16)
        RA = Res("A")
        RAB = [Res("A0"), Res("A1")]

        PSn = 8
        PS = [st.enter_context(nc.psum_tensor(f"ps{i}", [128, 512], F32)) for i in range(PSn)]
        RPS = [Res(f"ps{i}") for i in range(PSn)]
        psi = [0]
        NROT = 6

        def ps():
            i = psi[0] % NROT
            psi[0] += 1
            return PS[i], RPS[i]

        T32 = sb("T32", [128, 8, 512], F32)
        RT32 = [Res(f"t32_{i}") for i in range(8)]
        t32i = [0]

        def t32():
            i = t32i[0] % 8
            t32i[0] += 1
            return T32[:, i, :], RT32[i]

        NT16 = 6
        T16 = sb("T16", [128, NT16, 512], BF16)
        RT16 = [Res(f"t16_{i}") for i in range(NT16)]
        t16i = [0]

        def t16():
            i = t16i[0] % NT16
            t16i[0] += 1
            return T16[:, i, :], RT16[i]

        NWS = 6
        WS = sb("WS", [128, NWS, 2048], BF16)
        RWS = [Res(f"ws{i}") for i in range(NWS)]
        wsi = [0]

        def ws():
            i = wsi[0] % NWS
            wsi[0] += 1
            return WS[:, i, :], RWS[i]

        ident = sb("ident", [128, 128], F32)
        ones32 = sb("ones32", [128, 128], F32)
        cst = sb("cst", [128, 4], F32)
        VT = sb("VT", [128, 7, 128], F32)
        scb = sb("scb", [128, KC, 2], BF16)
        MOD = sb("MOD", [128, 72, 2], F32)
        DER = sb("DER", [128, 6, KC, 2], F32)
        CS32 = sb("CS32", [128, 2, 128], F32)
        MSK = sb("MSK", [128, 2], F32)
        TT2 = sb("TT2", [128, 2, 512], F32)
        RTT2 = [Res("tt2_0"), Res("tt2_1")]
        QT = sb("QT", [128, 2, 512], BF16)
        RQT = [Res("qt0"), Res("qt1")]
        vTc = sb("vTc", [128, 3, 286], BF16)
        HT = sb("HT", [128, 2, 90], BF16)
        Rc = {k: Res(k) for k in "ident ones cst cst2 VT scb MOD DER CS32 MSK vTc HT XA GA XG XGc GG XH GH Dcqn Dsv DF Dat".split()}

        def mm(out, lhsT, rhs, start, stop, reads, writes, sig=None):
            fw.op(pe, lambda h: h.matmul(out, lhsT=lhsT, rhs=rhs, start=start, stop=stop),
                  reads=reads, writes=writes, sig=(stop if sig is None else sig))

        def handover(srcs, dsts):
            fw.op(dve, lambda h: h.memset(cst[:, 2:3], 0.0), reads=list(srcs), writes=list(dsts) + [Rc["cst2"]])

        fw.op(pool, lambda h: h.memset(ident[:], 1.0), writes=[Rc["ident"]])
        fw.op(pool, lambda h: h.affine_select(out=ident[:], in_=ident[:], pattern=[[-1, 128]],
                                              compare_op=ALU.is_equal, fill=0.0, base=0, channel_multiplier=1),
              reads=[Rc["ident"]], writes=[Rc["ident"]])
        fw.op(pool, lambda h: h.memset(ones32[:], 1.0), writes=[Rc["ones"]])
        fw.op(pool, lambda h: h.memset(cst[:, 0:1], EPS), writes=[Rc["cst"]])
        fw.op(pool, lambda h: h.memset(cst[:, 1:2], 0.0), writes=[Rc["cst"]])
        fw.op(pool, lambda h: h.memset(vTc[:], 0.0), writes=[Rc["vTc"]])
        fw.dma(sp, CS32[:], ccsc[:, :, :], writes=[Rc["CS32"]])
        fw.dma(sp, MSK[:], maskd[:, :], writes=[Rc["MSK"]])
        for i in range(7):
            t, r = t32()
            fw.dma(sp, t[:, 0:128], vecs[i, :, :], writes=[r])
            p, pr = ps()
            fw.op(pe, lambda h, p=p, t=t: h.transpose(out=p[:, 0:128], in_=t[:, 0:128], identity=ident[:]),
                  reads=[r, Rc["ident"]], writes=[pr])
            fw.op(dve, lambda h, p=p, i=i: h.tensor_copy(out=VT[:, i, :], in_=p[:, 0:128]), reads=[pr], writes=[Rc["VT"]])
        VG = VT[:, 0, :]
        VA = lambda l: VT[:, 1 + 3 * l, :]
        VB = lambda l: VT[:, 2 + 3 * l, :]
        VC = lambda l: VT[:, 3 + 3 * l, :]
        for t in range(2):
            fw.op(act, lambda h, t=t: h.activation(out=scb[:, :, t], in_=VG[:, 8 * t:8 * t + 8], func=AF.Silu),
                  reads=[Rc["VT"]], writes=[Rc["scb"]])

        for ti in range(NT // 128):
            bi = min(ti // 4, 4)
            for half in range(2):
                t, r = t32()
                fw.dma(sp, t, xin[ti * 128:(ti + 1) * 128, half * 512:(half + 1) * 512], writes=[r])
                p, pr = ps()
                for kk in range(4):
                    fw.op(pe, lambda h, p=p, t=t, kk=kk: h.transpose(out=p[:, kk * 128:(kk + 1) * 128],
                                                                     in_=t[:, kk * 128:(kk + 1) * 128], identity=ident[:]),
                          reads=[r, Rc["ident"]], writes=[pr], sig=(kk == 3))
                eng = dve if half == 0 else act
                if half == 0:
                    fw.op(dve, lambda h, p=p, ti=ti, half=half: h.tensor_copy(
                        out=hT[:, half * 4:half * 4 + 4, ti * 128:(ti + 1) * 128],
                        in_=p[:, :].rearrange("p (c t) -> p c t", c=4)), reads=[pr], writes=[Rh[bi]])
                else:
                    fw.op(act, lambda h, p=p, ti=ti, half=half: h.copy(
                        out=hT[:, half * 4:half * 4 + 4, ti * 128:(ti + 1) * 128],
                        in_=p[:, :].rearrange("p (c t) -> p c t", c=4)), reads=[pr], writes=[Rh[bi]])

        def wload(src_ap, shape3):
            w, wr = ws()
            a, b = shape3
            v = w[:, 0:a * b].rearrange("p (a b) -> p a b", a=a)
            fw.dma(pool, v, src_ap, writes=[wr])
            return v, wr

        def kcview(wap, c0, n):
            return wap[:, c0:c0 + n].rearrange("(kc p) n -> p kc n", p=128)

        def rstd_from(pst, n, scale):
            rs, rsr = t32()
            fw.op(act, lambda h: h.activation(out=rs[:, :n], in_=pst[0][:, :n], func=AF.Sqrt, bias=cst[:, 0:1], scale=scale),
                  reads=[pst[1], Rc["cst"]], writes=[rsr])
            fw.op(dve, lambda h: h.reciprocal(out=rs[:, :n], in_=rs[:, :n]), reads=[rsr], writes=[rsr])
            return rs, rsr

        def mod_stage(l):
            pm, pmr = ps()
            for s in range(36):
                wv, wr = wload(kcview(w_ada[l], s * 256, 256), (KC, 256))
                for jj in range(2):
                    j = s * 2 + jj
                    for kc in range(KC):
                        mm(pm[:, j * 2:j * 2 + 2], wv[:, kc, jj * 128:(jj + 1) * 128], scb[:, kc, :], kc == 0, kc == KC - 1,
                           [wr, Rc["scb"]], [pmr])
            pmv = pm[:, 0:144].rearrange("p (j t) -> p j t", t=2)
            va = VA(l)
            for t in range(2):
                fw.op(dve, lambda h, t=t: h.tensor_tensor(out=MOD[:, :, t], in0=pmv[:, :, t], in1=va[:, 0:72], op=ALU.add),
                      reads=[pmr, Rc["VT"]], writes=[Rc["MOD"]])
                for i, (gcol, n) in enumerate([(72, 1), (80, 4), (88, 7)]):
                    fw.op(dve, lambda h, t=t, i=i, n=n: h.tensor_scalar(out=DER[:, i, :, t], in0=MOD[:, n * 8:(n + 1) * 8, t],
                                                                       scalar1=1.0, scalar2=None, op0=ALU.add),
                          reads=[Rc["MOD"]], writes=[Rc["DER"]])
                    fw.op(dve, lambda h, t=t, i=i, gcol=gcol: h.tensor_tensor(out=DER[:, i, :, t], in0=DER[:, i, :, t],
                                                                              in1=va[:, gcol:gcol + 8], op=ALU.mult),
                          reads=[Rc["DER"], Rc["VT"]], writes=[Rc["DER"]])
                for i, (n, f) in enumerate([(2, 0.5), (5, 1.0), (8, 0.5)]):
                    fw.op(dve, lambda h, t=t, i=i, n=n, f=f: h.tensor_scalar(out=DER[:, 3 + i, :, t], in0=MOD[:, n * 8:(n + 1) * 8, t],
                                                                             scalar1=f, scalar2=None, op0=ALU.mult),
                          reads=[Rc["MOD"]], writes=[Rc["DER"]])

        def norm_stage(idx, tbs):
            for bi, (t0, n) in tbs:
                ts = 0 if t0 < NL else 1
                pst, pstr = ps()
                for kc in range(KC):
                    sq, sqr = t32()
                    fw.op(act, lambda h, sq=sq, kc=kc: h.activation(out=sq[:, :n], in_=hT[:, kc, t0:t0 + n], func=AF.Square),
                          reads=[Rh[bi]], writes=[sqr])
                    mm(pst[:, :n], ones32[:], sq[:, :n], kc == 0, kc == KC - 1, [sqr, Rc["ones"]], [pstr])
                rs, rsr = rstd_from((pst, pstr), n, 1.0 / D)
                for kc in range(KC):
                    tt, ttr = TT2[:, kc % 2, :], RTT2[kc % 2]
                    fw.op(dve, lambda h, tt=tt, kc=kc: h.scalar_tensor_tensor(
                        out=tt[:, :n], in0=hT[:, kc, t0:t0 + n], scalar=DER[:, idx, kc, ts:ts + 1], in1=rs[:, :n],
                        op0=ALU.mult, op1=ALU.mult), reads=[Rh[bi], rsr, Rc["DER"]], writes=[ttr])
                    fw.op(act, lambda h, tt=tt, kc=kc: h.activation(
                        out=uT[:, kc, t0:t0 + n], in_=tt[:, :n], func=AF.Identity,
                        bias=MOD[:, 3 * idx * 8 + kc, ts:ts + 1], scale=1.0), reads=[ttr, Rc["MOD"]], writes=[Ru[bi]])

        def ffn_stage(l, w1, w3, w2, gidx, tbs):
            groups = [(0, 4), (4, 4), (8, 4), (12, 4), (16, 4), (20, 2)]
            for gi, (j0, nj) in enumerate(groups):
                half = gi % 2
                ab = AR[:, half * 4 * NT:(half + 1) * 4 * NT].rearrange("p (c t) -> p c t", c=4)
                abr = RAB[half]
                for sub in range(nj // 2):
                    c0 = (j0 + sub * 2) * 128
                    w1v, w1r = wload(kcview(w1[l], c0, 256), (KC, 256))
                    w3v, w3r = wload(kcview(w3[l], c0, 256), (KC, 256))
                    for jj in range(2):
                        ja = sub * 2 + jj
                        for bi, (t0, n) in tbs:
                            p1, p1r = ps()
                            p3, p3r = ps()
                            for kc in range(KC):
                                mm(p1[:, :n], w1v[:, kc, jj * 128:(jj + 1) * 128], uT[:, kc, t0:t0 + n], kc == 0, kc == KC - 1,
                                   [w1r, Ru[bi]], [p1r])
                            for kc in range(KC):
                                mm(p3[:, :n], w3v[:, kc, jj * 128:(jj + 1) * 128], uT[:, kc, t0:t0 + n], kc == 0, kc == KC - 1,
                                   [w3r, Ru[bi]], [p3r])
                            s, sr = t32()
                            fw.op(act, lambda h, s=s, p1=p1, n=n: h.activation(out=s[:, :n], in_=p1[:, :n], func=AF.Silu),
                                  reads=[p1r], writes=[sr])
                            fw.op(dve, lambda h, s=s, p3=p3, n=n, ja=ja, t0=t0, ab=ab: h.tensor_tensor(
                                out=ab[:, ja, t0:t0 + n], in0=s[:, :n], in1=p3[:, :n], op=ALU.mult),
                                reads=[sr, p3r], writes=[abr])
                w2v = []
                for sub in range(nj // 2):
                    r0 = (j0 + sub * 2) * 128
                    w2v.append(wload(w2[l, r0:r0 + 256, :].rearrange("(j p) n -> p j n", p=128), (2, D)))
                for bi, (t0, n) in tbs:
                    ts = 0 if t0 < NL else 1
                    for m in range(KC):
                        po, por = ps()
                        for ja in range(nj):
                            wv, wr = w2v[ja // 2]
                            mm(po[:, :n], wv[:, ja % 2, m * 128:(m + 1) * 128], ab[:, ja, t0:t0 + n], ja == 0, ja == nj - 1,
                               [wr, abr], [por])
                        fw.op(dve, lambda h, po=po, m=m, t0=t0, n=n, ts=ts: h.scalar_tensor_tensor(
                            out=hT[:, m, t0:t0 + n], in0=po[:, :n], scalar=DER[:, 3 + gidx, m, ts:ts + 1],
                            in1=hT[:, m, t0:t0 + n], op0=ALU.mult, op1=ALU.add),
                            reads=[por, Rh[bi], Rc["DER"]], writes=[Rh[bi]])

        def mixer_stage(l, last, mstop):
            tbs_all = list(enumerate(TBS))
            tbs_c = tbs_all if not last else tbs_all[:4]
            vb = VB(l)
            vc = VC(l)
            va = VA(l)
            vTl = AR[:, 0:3 * 2078].rearrange("p (c t) -> p c t", c=3)
            DG = AR[:, 6234:6234 + 93 * 128].rearrange("p (j m) -> p j m", j=93)

            wc = [wload(kcview(w_in[l], s * 256, 256), (KC, 256)) for s in range(3)]
            for bi, (t0, n) in tbs_c:
                for cc in range(3):
                    pa, par = ps()
                    pg, pgr = ps()
                    ca, cg = cc * 128, 384 + cc * 128
                    wa, war = wc[ca // 256]
                    wg, wgr = wc[cg // 256]
                    for kc in range(KC):
                        mm(pa[:, :n], wa[:, kc, ca % 256:ca % 256 + 128], uT[:, kc, t0:t0 + n], kc == 0, kc == KC - 1, [war, Ru[bi]], [par])
                    for kc in range(KC):
                        mm(pg[:, :n], wg[:, kc, cg % 256:cg % 256 + 128], uT[:, kc, t0:t0 + n], kc == 0, kc == KC - 1, [wgr, Ru[bi]], [pgr])
                    s, sr = t32()
                    fw.op(act, lambda h, s=s, pg=pg, n=n: h.activation(out=s[:, :n], in_=pg[:, :n], func=AF.Sigmoid), reads=[pgr], writes=[sr])
                    if t0 < NL:
                        fw.op(dve, lambda h, s=s, pa=pa, n=n, cc=cc, t0=t0: h.tensor_tensor(
                            out=vTl[:, cc, 15 + t0:15 + t0 + n], in0=s[:, :n], in1=pa[:, :n], op=ALU.mult), reads=[sr, par], writes=[RA])
                    else:
                        fw.op(dve, lambda h, s=s, pa=pa, n=n, cc=cc: h.tensor_tensor(
                            out=vTc[:, cc, 15:15 + n], in0=s[:, :n], in1=pa[:, :n], op=ALU.mult), reads=[sr, par], writes=[Rc["vTc"]])
            XHv = XH.ap().rearrange("p (c t) -> p c t", c=3)
            fw.dma(sp, XHv[:, :, 0:15], vTl[:, :, 15:30], reads=[RA], writes=[Rc["XH"]])
            fw.dma(sp, XHv[:, :, 15:30], vTl[:, :, 15 + NL - 15:15 + NL], reads=[RA], writes=[Rc["XH"]])

            wq = [wload(kcview(w_in[l], 768, 256), (KC, 256)), wload(kcview(w_in[l], 1024, 128), (KC, 128))]
            wkv = wload(kcview(w_in[l], 1152, 256), (KC, 256))
            wkr = wload(kcview(w_krp[l], 0, 192), (KC, 192))
            XAa = XA.ap()
            Dcq = Dcqn.ap()
            for bi, (t0, n) in tbs_all:
                lat = t0 < NL
                for (nch, wsel, gcol, dst, need) in ((3, "q", 33, Dcq, (lat or not last)), (2, "kv", 36, XAa, True)):
                    if not need:
                        continue
                    pcs = []
                    for cc in range(nch):
                        pq, pqr = ps()
                        if wsel == "q":
                            wv, wr = wq[0] if cc < 2 else wq[1]
                            col = (cc % 2) * 128 if cc < 2 else 0
                        else:
                            wv, wr = wkv
                            col = cc * 128
                        for kc in range(KC):
                            mm(pq[:, :n], wv[:, kc, col:col + 128], uT[:, kc, t0:t0 + n], kc == 0, kc == KC - 1, [wr, Ru[bi]], [pqr])
                        pcs.append((pq, pqr))
                    pst, pstr = ps()
                    for cc in range(nch):
                        sq, sqr = t32()
                        fw.op(act, lambda h, sq=sq, pq=pcs[cc][0], n=n: h.activation(out=sq[:, :n], in_=pq[:, :n], func=AF.Square),
                              reads=[pcs[cc][1]], writes=[sqr])
                        mm(pst[:, :n], ones32[:], sq[:, :n], cc == 0, cc == nch - 1, [sqr, Rc["ones"]], [pstr])
                    rs, rsr = rstd_from((pst, pstr), n, 1.0 / (128 * nch))
                    for cc in range(nch):
                        o16, o16r = t16()
                        fw.op(dve, lambda h, o16=o16, pq=pcs[cc][0], n=n, cc=cc, gcol=gcol, rs=rs: h.scalar_tensor_tensor(
                            out=o16[:, :n], in0=pq[:, :n], scalar=vb[:, gcol + cc:gcol + cc + 1], in1=rs[:, :n],
                            op0=ALU.mult, op1=ALU.mult), reads=[pcs[cc][1], rsr, Rc["VT"]], writes=[o16r])
                        fw.dma(sp, dst[cc * 128:(cc + 1) * 128, t0:t0 + n], o16[:, :n], reads=[o16r],
                               writes=[Rc["Dcqn"] if wsel == "q" else Rc["XA"]])
                pk, pkr = ps()
                pp, ppr = ps()
                for kc in range(KC):
                    mm(pk[0:96, :n], wkr[0][:, kc, 0:96], uT[:, kc, t0:t0 + n], kc == 0, kc == KC - 1, [wkr[1], Ru[bi]], [pkr])
                for kc in range(KC):
                    mm(pp[0:96, :n], wkr[0][:, kc, 96:192], uT[:, kc, t0:t0 + n], kc == 0, kc == KC - 1, [wkr[1], Ru[bi]], [ppr])
                o16, o16r = t16()
                if lat:
                    a1, a1r = t32()
                    a2, a2r = t32()
                    fw.dma(sp, a1[64:96, :], rope[0, :, t0:t0 + 512], writes=[a1r])
                    fw.dma(sp, a2[64:96, :], rope[1, :, t0:t0 + 512], writes=[a2r])
                    fw.op(dve, lambda h, a1=a1, pk=pk: h.tensor_tensor(out=a1[64:96, :], in0=pk[64:96, :], in1=a1[64:96, :], op=ALU.mult),
                          reads=[pkr, a1r], writes=[a1r])
                    fw.op(dve, lambda h, a2=a2, pp=pp: h.tensor_tensor(out=a2[64:96, :], in0=pp[64:96, :], in1=a2[64:96, :], op=ALU.mult),
                          reads=[ppr, a2r], writes=[a2r])
                    fw.op(dve, lambda h, a1=a1, a2=a2, o16=o16: h.tensor_tensor(out=o16[64:96, :], in0=a1[64:96, :], in1=a2[64:96, :], op=ALU.add),
                          reads=[a1r, a2r], writes=[o16r])
                else:
                    fw.op(dve, lambda h, o16=o16, pk=pk, n=n: h.tensor_copy(out=o16[64:96, :n], in_=pk[64:96, :n]), reads=[pkr], writes=[o16r])
                fw.dma(sp, XAa[256:288, t0:t0 + n], o16[64:96, :n], reads=[o16r], writes=[Rc["XA"]])

            wf = [wload(kcview(w_in[l], 1440 + s * 256, 256), (KC, 256)) for s in range(2)]
            XGa = XG.ap()
            ntt = 18 if not last else 16
            for tt in range(ntt):
                bi = min(tt // 4, 4)
                pgm, pgr = ps()
                for s in range(2):
                    for kc in range(KC):
                        mm(pgm[:, s * 256:(s + 1) * 256], uT[:, kc, tt * 128:(tt + 1) * 128], wf[s][0][:, kc, :], kc == 0, kc == KC - 1,
                           [wf[s][1], Ru[bi]], [pgr])
                o16, o16r = t16()
                if tt % 2 == 0:
                    fw.op(dve, lambda h, o16=o16, pgm=pgm: h.tensor_copy(out=o16[:, :], in_=pgm[:, :]), reads=[pgr], writes=[o16r])
                else:
                    fw.op(act, lambda h, o16=o16, pgm=pgm: h.copy(out=o16[:, :], in_=pgm[:, :]), reads=[pgr], writes=[o16r])
                if tt < 16:
                    fw.dma(sp, XGa[tt * 128:(tt + 1) * 128, :], o16[:, :], reads=[o16r], writes=[Rc["XG"]])
                else:
                    fw.dma(sp, XGc.ap()[(tt - 16) * 128:(tt - 15) * 128, :], o16[:, :], reads=[o16r], writes=[Rc["XGc"]])

            fw.allgather(XH.ap(), GH.ap(), [Rc["XH"]], [Rc["GH"]])
            fw.allgather(XA.ap(), GA.ap(), [Rc["XA"]], [Rc["GA"]])
            fw.allgather(XG.ap(), GG.ap(), [Rc["XG"]], [Rc["GG"]])
            if mstop <= 1:
                return

            GHa = GH.ap()
            fw.dma(sp, HT[:, :, :], GHa.rearrange("(r p) f -> p r f", p=128), reads=[Rc["GH"]], writes=[Rc["HT"]])
            HTv = HT[:, :, :].rearrange("p r (c t) -> p r c t", c=3)
            fw.op(dve, lambda h: h.tensor_scalar(out=vTl[:, :, 0:15], in0=HTv[:, 0, :, 15:30], scalar1=MSK[:, 0:1], scalar2=None, op0=ALU.mult),
                  reads=[Rc["HT"], Rc["MSK"]], writes=[RA])
            fw.op(dve, lambda h: h.tensor_scalar(out=vTl[:, :, 15 + NL:30 + NL], in0=HTv[:, 1, :, 0:15], scalar1=MSK[:, 1:2], scalar2=None, op0=ALU.mult),
                  reads=[Rc["HT"], Rc["MSK"]], writes=[RA])
            for idx in range(93):
                fw.op(dve, lambda h, idx=idx: h.tensor_scalar(out=DG[:, idx, :], in0=ident[:], scalar1=vc[:, idx:idx + 1], scalar2=None, op0=ALU.mult),
                      reads=[Rc["ident"], Rc["VT"]], writes=[RA])
            Dsva = Dsv.ap()
            for bi, (t0, n) in tbs_c:
                lat = t0 < NL
                cos_ = []
                for cc in range(3):
                    pc, pcr = ps()
                    for j in range(31):
                        rhs = vTl[:, cc, t0 + j:t0 + j + n] if lat else vTc[:, cc, j:j + n]
                        mm(pc[:, :n], DG[:, j * 3 + cc, :], rhs, j == 0, j == 30, [RA] if lat else [RA, Rc["vTc"]], [pcr])
                    co, cor = t32()
                    fw.op(act, lambda h, co=co, pc=pc, n=n, cc=cc: h.activation(out=co[:, :n], in_=pc[:, :n], func=AF.Identity,
                                                                              bias=vb[:, 24 + cc:25 + cc], scale=1.0),
                          reads=[pcr, Rc["VT"]], writes=[cor])
                    cos_.append((co, cor))
                pss, pssr = ps()
                psq, psqr = ps()
                for cc in range(3):
                    mm(pss[:, :n], ones32[:], cos_[cc][0][:, :n], cc == 0, cc == 2, [cos_[cc][1], Rc["ones"]], [pssr])
                for cc in range(3):
                    sq, sqr = t32()
                    fw.op(act, lambda h, sq=sq, co=cos_[cc][0], n=n: h.activation(out=sq[:, :n], in_=co[:, :n], func=AF.Square),
                          reads=[cos_[cc][1]], writes=[sqr])
                    mm(psq[:, :n], ones32[:], sq[:, :n], cc == 0, cc == 2, [sqr, Rc["ones"]], [psqr])
                mu, mur = t32()
                fw.op(dve, lambda h, mu=mu, pss=pss, n=n: h.tensor_scalar(out=mu[:, :n], in0=pss[:, :n], scalar1=1.0 / 384, scalar2=None, op0=ALU.mult),
                      reads=[pssr], writes=[mur])
                m2, m2r = t32()
                fw.op(dve, lambda h, mu=mu, m2=m2, n=n: h.tensor_tensor(out=m2[:, :n], in0=mu[:, :n], in1=mu[:, :n], op=ALU.mult), reads=[mur], writes=[m2r])
                fw.op(dve, lambda h, m2=m2, psq=psq, n=n: h.scalar_tensor_tensor(out=m2[:, :n], in0=psq[:, :n], scalar=1.0 / 384, in1=m2[:, :n],
                                                                               op0=ALU.mult, op1=ALU.subtract), reads=[psqr, m2r], writes=[m2r])
                fw.op(act, lambda h, m2=m2, n=n: h.activation(out=m2[:, :n], in_=m2[:, :n], func=AF.Sqrt, bias=cst[:, 0:1], scale=1.0),
                      reads=[m2r, Rc["cst"]], writes=[m2r])
                fw.op(dve, lambda h, m2=m2, n=n: h.reciprocal(out=m2[:, :n], in_=m2[:, :n]), reads=[m2r], writes=[m2r])
                for cc in range(3):
                    co, cor = cos_[cc]
                    fw.op(dve, lambda h, co=co, mu=mu, n=n: h.tensor_tensor(out=co[:, :n], in0=co[:, :n], in1=mu[:, :n], op=ALU.subtract),
                          reads=[cor, mur], writes=[cor])
                    fw.op(dve, lambda h, co=co, m2=m2, n=n: h.tensor_tensor(out=co[:, :n], in0=co[:, :n], in1=m2[:, :n], op=ALU.mult),
                          reads=[cor, m2r], writes=[cor])
                    o16, o16r = t16()
                    fw.op(act, lambda h, co=co, o16=o16, n=n, cc=cc: h.activation(out=o16[:, :n], in_=co[:, :n], func=AF.Silu,
                                                                                bias=vb[:, 30 + cc:31 + cc], scale=vb[:, 27 + cc:28 + cc]),
                          reads=[cor, Rc["VT"]], writes=[o16r])
                    fw.dma(sp, Dsva[cc * 128:(cc + 1) * 128, t0:t0 + n], o16[:, :n], reads=[o16r], writes=[Rc["Dsv"]])
            if mstop <= 2:
                return

            gfull = AR[:, 0:32 * 512].rearrange("p (t c) -> p t c", t=32)
            GGa = GG.ap()
            DFa = DF.ap()
            for r in range(2):
                fw.dma(sp, gfull[:, r * 16:(r + 1) * 16, :], GGa[r * NL:(r + 1) * NL, :].rearrange("(t p) c -> p t c", p=128),
                       reads=[Rc["GG"]], writes=[RA])

            def stage2(P, Pr, Q, Qr, n, dst):
                pS, pSr = t32()
                qS, qSr = t32()
                fw.op(dve, lambda h: h.tensor_copy(out=pS[:, :n], in_=P[:, :n]), reads=[Pr], writes=[pSr])
                fw.op(act, lambda h: h.copy(out=qS[:, :n], in_=Q[:, :n]), reads=[Qr], writes=[qSr])
                mm(P[:, :n], CS32[:, 0, :], pS[:, :n], True, False, [pSr, Rc["CS32"]], [Pr])
                mm(P[:, :n], CS32[:, 1, :], qS[:, :n], False, True, [qSr, Rc["CS32"]], [Pr])
                o16, o16r = t16()
                fw.op(act, lambda h: h.copy(out=o16[:, :n], in_=P[:, :n]), reads=[Pr], writes=[o16r])
                fw.dma(sp, dst, o16[:, :n], reads=[o16r], writes=[Rc["DF"]])

            for kb in range(4):
                for ti in range(32):
                    tab, tabr = ws()
                    tv = tab[:, 0:1024].rearrange("p (s k) -> p s k", s=2)
                    fw.dma(sp, tv, dft[kb, ti, :, :, :], writes=[tabr])
                    for gi in range(4):
                        mm(PS[gi][:, :], gfull[:, ti, gi * 128:(gi + 1) * 128], tv[:, 0, :], ti == 0, ti == 31, [RA, tabr], [RPS[gi]])
                        mm(PS[4 + gi][:, :], gfull[:, ti, gi * 128:(gi + 1) * 128], tv[:, 1, :], ti == 0, ti == 31, [RA, tabr], [RPS[4 + gi]],
                           sig=(True if gi == 3 else None))
                for gi in range(4):
                    stage2(PS[gi], RPS[gi], PS[4 + gi], RPS[4 + gi], 512, DFa[gi * 128:(gi + 1) * 128, kb * 512:(kb + 1) * 512])
            if not last:
                gc, gcr = ws()
                gcv = gc[:, 0:1024].rearrange("p (t c) -> p t c", t=2)
                fw.dma(sp, gcv, XGc.ap().rearrange("(t p) c -> p t c", p=128), reads=[Rc["XGc"]], writes=[gcr])
                tc_, tcr = ws()
                tcv = tc_[:, 0:1024].rearrange("p (t s k) -> p t s k", t=2, s=2)
                fw.dma(sp, tcv, dftc.rearrange("t p s k -> p t s k"), writes=[tcr])
                for gi in range(4):
                    P, Pr = PS[gi], RPS[gi]
                    Q, Qr = PS[4 + gi], RPS[4 + gi]
                    for tl in range(2):
                        mm(P[:, :256], gcv[:, tl, gi * 128:(gi + 1) * 128], tcv[:, tl, 0, :], tl == 0, tl == 1, [gcr, tcr], [Pr])
                    for tl in range(2):
                        mm(Q[:, :256], gcv[:, tl, gi * 128:(gi + 1) * 128], tcv[:, tl, 1, :], tl == 0, tl == 1, [gcr, tcr], [Qr])
                    stage2(P, Pr, Q, Qr, 256, DFa[gi * 128:(gi + 1) * 128, NL:NT])
            if mstop <= 3:
                return

            GAa = GA.ap()
            Data = Dat.ap()
            NK = SEQ + NCX
            KTb = [AR[:, b * 8704:b * 8704 + NK] for b in range(2)]
            VAb = [AR[:, b * 8704 + NK:(b + 1) * 8704].rearrange("p (t c) -> p t c", t=34) for b in range(2)]
            RKV = [Res("kv0"), Res("kv1")]
            fw.op(dve, lambda h: h.memset(VAb[0][:, :, 64:128], 1.0), reads=[RA], writes=[RA, RKV[0]])
            fw.op(dve, lambda h: h.memset(VAb[1][:, :, 0:64], 1.0), reads=[RA], writes=[RA, RKV[1]])
            qbs = tbs_c
            HW = {}
            LAG = 3

            def kv_build(hd):
                b = hd % 2
                voff = 0 if b == 0 else 64
                wsl, wslr = ws()
                wkvh = wsl[:, 0:256].rearrange("p (c n) -> p c n", c=2)
                wqh = wsl[:, 256:256 + 288].rearrange("p (c n) -> p c n", c=3)
                wqph = wsl[:, 544:544 + 288].rearrange("p (c n) -> p c n", c=3)
                fw.dma(pool, wkvh, w_ukv[l][:, hd * 128:(hd + 1) * 128].rearrange("(c p) n -> p c n", p=128), writes=[wslr])
                fw.dma(pool, wqh, w_uq[l][:, hd * 96:(hd + 1) * 96].rearrange("(c p) n -> p c n", p=128), writes=[wslr])
                fw.dma(pool, wqph, w_uqp[l][:, hd * 96:(hd + 1) * 96].rearrange("(c p) n -> p c n", p=128), writes=[wslr])
                HW[hd] = (wqh, wqph, wslr)
                for kb in range(9):
                    if kb < 8:
                        r, c0, n = kb // 4, (kb % 4) * 512, 512
                    else:
                        r, c0, n = 0, NL, NCX
                    kk0 = kb * 512
                    ck = []
                    for cc in range(2):
                        c16, c16r = t16()
                        fw.dma(sp, c16[:, :n], GAa[r * 288 + cc * 128:r * 288 + (cc + 1) * 128, c0:c0 + n], reads=[Rc["GA"]], writes=[c16r])
                        ck.append((c16, c16r))
                    fw.dma(sp, KTb[b][64:96, kk0:kk0 + n], GAa[r * 288 + 256:r * 288 + 288, c0:c0 + n], reads=[Rc["GA"]], writes=[RKV[b]])
                    pk, pkr = ps()
                    for cc in range(2):
                        mm(pk[0:64, :n], wkvh[:, cc, 0:64], ck[cc][0][:, :n], cc == 0, cc == 1, [wslr, ck[cc][1]], [pkr])
                    fw.op(act, lambda h: h.copy(out=KTb[b][0:64, kk0:kk0 + n], in_=pk[0:64, :n]), reads=[pkr], writes=[RKV[b]])
                    pv, pvr = ps()
                    nt_ = n // 128
                    for tl in range(nt_):
                        for cc in range(2):
                            mm(pv[:, tl * 64:(tl + 1) * 64], ck[cc][0][:, tl * 128:(tl + 1) * 128], wkvh[:, cc, 64:128], cc == 0, cc == 1,
                               [wslr, ck[cc][1]], [pvr])
                    fw.op(dve, lambda h: h.tensor_copy(out=VAb[b][:, kb * 4:kb * 4 + nt_, voff:voff + 64],
                                                       in_=pv[:, 0:nt_ * 64].rearrange("p (t c) -> p t c", t=nt_)),
                          reads=[pvr], writes=[RKV[b]])

            def q_build(hd, qi):
                bi, (t0, n) = qbs[qi]
                wqh, wqph, wslr = HW[hd]
                lat = t0 < NL
                cq = []
                for cc in range(3):
                    c16, c16r = t16()
                    fw.dma(sp, c16[:, :n], Dcq[cc * 128:(cc + 1) * 128, t0:t0 + n], reads=[Rc["Dcqn"]], writes=[c16r])
                    cq.append((c16, c16r))
                pq, pqr = ps()
                for cc in range(3):
                    mm(pq[0:96, :n], wqh[:, cc, :], cq[cc][0][:, :n], cc == 0, cc == 2, [wslr, cq[cc][1]], [pqr])
                q16, q16r = QT[:, qti[0] % 2, :], RQT[qti[0] % 2]
                qti[0] += 1
                if lat:
                    pp, ppr = ps()
                    for cc in range(3):
                        mm(pp[0:96, :n], wqph[:, cc, :], cq[cc][0][:, :n], cc == 0, cc == 2, [wslr, cq[cc][1]], [ppr])
                    fw.op(act, lambda h: h.copy(out=q16[0:64, :], in_=pq[0:64, :]), reads=[pqr], writes=[q16r])
                    a1, a1r = t32()
                    a2, a2r = t32()
                    fw.dma(sp, a1[64:96, :], rope[0, :, t0:t0 + 512], writes=[a1r])
                    fw.dma(sp, a2[64:96, :], rope[1, :, t0:t0 + 512], writes=[a2r])
                    fw.op(dve, lambda h: h.tensor_tensor(out=a1[64:96, :], in0=pq[64:96, :], in1=a1[64:96, :], op=ALU.mult),
                          reads=[pqr, a1r], writes=[a1r])
                    fw.op(dve, lambda h: h.tensor_tensor(out=a2[64:96, :], in0=pp[64:96, :], in1=a2[64:96, :], op=ALU.mult),
                          reads=[ppr, a2r], writes=[a2r])
                    fw.op(dve, lambda h: h.tensor_tensor(out=q16[64:96, :], in0=a1[64:96, :], in1=a2[64:96, :], op=ALU.add),
                          reads=[a1r, a2r], writes=[q16r])
                    kts = list(range(34))
                else:
                    fw.op(act, lambda h: h.copy(out=q16[0:96, :n], in_=pq[0:96, :n]), reads=[pqr], writes=[q16r])
                    kts = [32, 33]
                return (q16, q16r, kts, t0, n)

            def scores(hd, qinfo):
                q16, q16r, kts, t0, n = qinfo
                b = hd % 2
                ob = 6 + (obi[0] % 2)
                obi[0] += 1
                po, por = PS[ob], RPS[ob]
                nk = len(kts)
                pts = {}
                for i in range(nk + LAG):
                    if i < nk:
                        kt = kts[i]
                        psc, pscr = ps()
                        mm(psc[:, :n], KTb[b][0:96, kt * 128:(kt + 1) * 128], q16[0:96, :n], True, True, [RKV[b], q16r], [pscr])
                        pt, ptr = t16()
                        fw.op(act, lambda h: h.activation(out=pt[:, :n], in_=psc[:, :n], func=AF.Exp, scale=ATTN_SCALE),
                              reads=[pscr], writes=[ptr])
                        pts[i] = (pt, ptr)
                    j = i - LAG
                    if j >= 0:
                        pt, ptr = pts.pop(j)
                        mm(po[:, :n], VAb[b][:, kts[j], :], pt[:, :n], j == 0, j == nk - 1, [RKV[b], ptr], [por])
                orow = slice(0, 64) if b == 0 else slice(64, 128)
                drow = slice(64, 128) if b == 0 else slice(0, 64)
                rd, rdr = t32()
                fw.op(dve, lambda h: h.reciprocal(out=rd[drow, :n], in_=po[drow, :n]), reads=[por], writes=[rdr])
                rsh, rshr = t32()
                fw.op(dve, lambda h: h.tensor_copy(out=rsh[orow, :n], in_=rd[drow, :n]), reads=[rdr], writes=[rshr])
                ao, aor = t16()
                fw.op(dve, lambda h: h.tensor_tensor(out=ao[orow, :n], in0=po[orow, :n], in1=rsh[orow, :n], op=ALU.mult),
                      reads=[por, rshr], writes=[aor])
                r0 = (hd // 2) * 128 + (0 if b == 0 else 64)
                fw.dma(sp, Data[r0:r0 + 64, t0:t0 + n], ao[orow, :n], reads=[aor], writes=[Rc["Dat"]])

            kv_build(0)
            for hd in range(8):
                qnext = q_build(hd, 0)
                if hd < 7:
                    kv_build(hd + 1)
                for qi in range(len(qbs)):
                    qcur = qnext
                    if qi + 1 < len(qbs):
                        qnext = q_build(hd, qi + 1)
                    scores(hd, qcur)
            if mstop <= 4:
                return

            mixacc = AR[:, :].rearrange("p (c t) -> p c t", c=KC)
            RM = Res("mix")
            handover([RKV[0], RKV[1], RA], [RM, RA, RKV[0], RKV[1]])
            branches = [(Dsv.ap(), Rc["Dsv"], 3, w_pw, 96), (Dat.ap(), Rc["Dat"], 4, w_o, None), (DFa, Rc["DF"], 4, w_fo, 104)]
            first = True
            for r, (Dsrc, Dres, nch, wsrc, bcol) in enumerate(branches):
                wbg = [wload(kcview(w_bg[l], r * D + s * 256, 256), (KC, 256)) for s in range(4)]
                wr_ = [wload(wsrc[l].rearrange("(c p) n -> p c n", p=128)[:, :, s * 512:(s + 1) * 512], (nch, 512)) for s in range(2)]
                for bi, (t0, n) in tbs_c:
                    xin_ = []
                    for c in range(nch):
                        c16, c16r = t16()
                        fw.dma(sp, c16[:, :n], Dsrc[c * 128:(c + 1) * 128, t0:t0 + n], reads=[Dres], writes=[c16r])
                        xin_.append((c16, c16r))
                    for m in range(KC):
                        pgt, pgtr = ps()
                        wv, wr = wbg[m // 2]
                        for kc in range(KC):
                            mm(pgt[:, :n], wv[:, kc, (m % 2) * 128:(m % 2) * 128 + 128], uT[:, kc, t0:t0 + n], kc == 0, kc == KC - 1, [wr, Ru[bi]], [pgtr])
                        s, sr = t32()
                        fw.op(act, lambda h, s=s, pgt=pgt, n=n, r=r, m=m: h.activation(out=s[:, :n], in_=pgt[:, :n], func=AF.Sigmoid,
                                                                                     bias=vb[:, r * 8 + m:r * 8 + m + 1], scale=1.0),
                              reads=[pgtr, Rc["VT"]], writes=[sr])
                        py, pyr = ps()
                        wv2, wr2 = wr_[m // 4]
                        for c in range(nch):
                            mm(py[:, :n], wv2[:, c, (m % 4) * 128:(m % 4) * 128 + 128], xin_[c][0][:, :n], c == 0, c == nch - 1, [wr2, xin_[c][1]], [pyr])
                        bias = va[:, bcol + m:bcol + m + 1] if bcol is not None else 0.0
                        if first:
                            fw.op(dve, lambda h, py=py, s=s, n=n, m=m, t0=t0, bias=bias: h.scalar_tensor_tensor(
                                out=mixacc[:, m, t0:t0 + n], in0=py[:, :n], scalar=bias, in1=s[:, :n], op0=ALU.add, op1=ALU.mult),
                                reads=[pyr, sr, Rc["VT"]], writes=[RM])
                        else:
                            fw.op(dve, lambda h, py=py, s=s, n=n, bias=bias: h.scalar_tensor_tensor(
                                out=s[:, :n], in0=py[:, :n], scalar=bias, in1=s[:, :n], op0=ALU.add, op1=ALU.mult),
                                reads=[pyr, sr, Rc["VT"]], writes=[sr])
                            fw.op(dve, lambda h, s=s, n=n, m=m, t0=t0: h.tensor_tensor(
                                out=mixacc[:, m, t0:t0 + n], in0=mixacc[:, m, t0:t0 + n], in1=s[:, :n], op=ALU.add),
                                reads=[sr, RM], writes=[RM])
                first = False
            for mo in range(KC):
                wv, wr = wload(kcview(w_out[l], mo * 128, 128), (KC, 128))
                for bi, (t0, n) in tbs_c:
                    ts = 0 if t0 < NL else 1
                    po, por = ps()
                    for m in range(KC):
                        mm(po[:, :n], wv[:, m, :], mixacc[:, m, t0:t0 + n], m == 0, m == KC - 1, [wr, RM], [por])
                    fw.op(dve, lambda h, po=po, mo=mo, t0=t0, n=n, ts=ts: h.scalar_tensor_tensor(
                        out=hT[:, mo, t0:t0 + n], in0=po[:, :n], scalar=DER[:, 4, mo, ts:ts + 1],
                        in1=hT[:, mo, t0:t0 + n], op0=ALU.mult, op1=ALU.add),
                        reads=[por, Rh[bi], Rc["DER"]], writes=[Rh[bi]])
            handover([RM, RKV[0], RKV[1], RA], [RA, RAB[0], RAB[1]])

        obi = [0]
        qti = [0]
        tbs_all = list(enumerate(TBS))
        stage = 0
        done = False
        for l in range(DEPTH):
            last = l == DEPTH - 1
            mod_stage(l)
            norm_stage(0, tbs_all)
            ffn_stage(l, w1a, w3a, w2a, 0, tbs_all)
            stage += 1
            if stage >= stop:
                done = True
                break
            norm_stage(1, tbs_all)
            handover([RAB[0], RAB[1], RA], [RA, RAB[0], RAB[1]])
            ms = (stop - stage) if (stop - stage) < 6 else 99
            mixer_stage(l, last, ms)
            stage += 5
            if stage >= stop:
                done = True
                break
            tb2 = tbs_all if not last else tbs_all[:4]
            norm_stage(2, tb2)
            ffn_stage(l, w1b, w3b, w2b, 2, tb2)
            stage += 1
            if stage >= stop:
                done = True
                break

        vg = VT[:, 0, :]
        for bi, (t0, n) in tbs_all[:4]:
            if not done:
                pst, pstr = ps()
                for kc in range(KC):
                    sq, sqr = t32()
                    fw.op(act, lambda h, sq=sq, kc=kc: h.activation(out=sq[:, :n], in_=hT[:, kc, t0:t0 + n], func=AF.Square),
                          reads=[Rh[bi]], writes=[sqr])
                    mm(pst[:, :n], ones32[:], sq[:, :n], kc == 0, kc == KC - 1, [sqr, Rc["ones"]], [pstr])
                rs, rsr = rstd_from((pst, pstr), n, 1.0 / D)
                for kc in range(KC):
                    fw.op(dve, lambda h, kc=kc, rs=rs: h.scalar_tensor_tensor(
                        out=hT[:, kc, t0:t0 + n], in0=hT[:, kc, t0:t0 + n], scalar=vg[:, 16 + kc:17 + kc], in1=rs[:, :n],
                        op0=ALU.mult, op1=ALU.mult), reads=[Rh[bi], rsr, Rc["VT"]], writes=[Rh[bi]])
            for tl in range(4):
                tt = (t0 // 128) + tl
                for half in range(2):
                    p, pr = ps()
                    for kk in range(4):
                        kc = half * 4 + kk
                        fw.op(pe, lambda h, p=p, kk=kk, kc=kc, tt=tt: h.transpose(out=p[:, kk * 128:(kk + 1) * 128],
                                                                                 in_=hT[:, kc, tt * 128:(tt + 1) * 128], identity=ident[:]),
                              reads=[Rh[bi], Rc["ident"]], writes=[pr], sig=(kk == 3))
                    o, orr = t32()
                    if half == 0:
                        fw.op(dve, lambda h, o=o, p=p: h.tensor_copy(out=o[:, :], in_=p[:, :]), reads=[pr], writes=[orr])
                    else:
                        fw.op(act, lambda h, o=o, p=p: h.copy(out=o[:, :], in_=p[:, :]), reads=[pr], writes=[orr])
                    fw.dma(sp, yout[tt * 128:(tt + 1) * 128, half * 512:(half + 1) * 512], o[:, :], reads=[orr], writes=[Res()])
        for ds in fw.dsems:
            if ds.count:
                fw._wait(sp, ds.sem, ds.count)
        fw.run()
    return nc


def _tables(half):
    bf = ml_dtypes.bfloat16
    t = np.arange(SEQ, dtype=np.int64)
    k = np.arange(NL, dtype=np.int64) + half * NL
    ang = 2.0 * np.pi * ((t[:, None] * k[None, :]) % SEQ).astype(np.float64) / SEQ
    tab = np.stack([np.cos(ang) / 64.0, -np.sin(ang) / 64.0], axis=1)
    tab = tab.reshape(32, 128, 2, 4, 512).transpose(3, 0, 1, 2, 4)
    dft = np.ascontiguousarray(tab).astype(bf)
    tc = np.arange(NCX, dtype=np.int64)
    angc = 2.0 * np.pi * ((tc[:, None] * tc[None, :]) % NCX).astype(np.float64) / NCX
    tabc = np.stack([np.cos(angc) / 16.0, -np.sin(angc) / 16.0], axis=1).reshape(2, 128, 2, NCX)
    dftc = np.ascontiguousarray(tabc).astype(bf)
    return dft, dftc


def _rope(half):
    tok = np.arange(NL) + half * NL
    row = (tok // 64).astype(np.float32)
    col = (tok % 64).astype(np.float32)
    inv = (1.0 / (np.float32(10000.0) ** (np.arange(8, dtype=np.float32) * np.float32(2.0) / np.float32(16)))).astype(np.float32)
    ar = row[:, None] * inv
    ac = col[:, None] * inv
    ang = np.concatenate([ar, ar, ac, ac], axis=-1).astype(np.float32)
    cos = np.cos(ang).astype(np.float32).T
    sin = np.sin(ang).astype(np.float32).T
    sign = np.ones(32, np.float32)
    sign[0:8] = -1.0
    sign[16:24] = -1.0
    return np.ascontiguousarray(np.stack([cos, sin * sign[:, None]], 0)).astype(np.float32)


_PERM = np.array([(f + 8) if (f % 16) < 8 else (f - 8) for f in range(32)])


def _prep(inputs):
    f32 = np.float32
    g = {k: np.asarray(v, dtype=f32) for k, v in inputs.items()}
    L = DEPTH
    vecs_l = np.zeros((L, 3, 128, 128), f32)
    for l in range(L):
        A = vecs_l[l, 0]
        A[0:72] = g["b_ada"][l].reshape(72, 128)
        A[72:80] = g["g_ffn1"][l].reshape(8, 128)
        A[80:88] = g["g_mix"][l].reshape(8, 128)
        A[88:96] = g["g_ffn2"][l].reshape(8, 128)
        A[96:104] = g["b_pw_conv"][l].reshape(8, 128)
        A[104:112] = g["b_fourier"][l].reshape(8, 128)
        B = vecs_l[l, 1]
        B[0:24] = g["b_bgate"][l].reshape(24, 128)
        B[24:27] = g["b_dw"][l].reshape(3, 128)
        B[27:30] = g["ln_g_conv"][l].reshape(3, 128)
        B[30:33] = g["ln_b_conv"][l].reshape(3, 128)
        B[33:36] = g["g_qnorm"][l].reshape(3, 128)
        B[36:38] = g["g_kvnorm"][l].reshape(2, 128)
        vecs_l[l, 2, 0:93] = g["w_dw"][l].reshape(31 * 3, 128)
    w_krp = np.zeros((L, D, 192), f32)
    w_krp[:, :, 64:96] = g["w_in"][:, :, 1408:1440]
    w_krp[:, :, 160:192] = g["w_in"][:, :, 1408 + _PERM]
    w_uqp = np.zeros((L, 384, 768), f32)
    for hd in range(8):
        w_uqp[:, :, hd * 96 + 64:hd * 96 + 96] = g["w_uq"][:, :, hd * 96 + 64 + _PERM]
    cm = np.arange(128)
    angc = 2.0 * np.pi * ((cm[:, None] * cm[None, :]) % 128).astype(np.float64) / 128.0
    ccsc = np.stack([np.cos(angc), np.sin(angc)], axis=1) / np.sqrt(128.0)
    ccsc = np.ascontiguousarray(ccsc).astype(f32)
    shared = {k: np.ascontiguousarray(g[k]) for k in
              ("w_ada", "w1_ffn1", "w3_ffn1", "w2_ffn1", "w1_ffn2", "w3_ffn2", "w2_ffn2", "w_in", "w_pw_conv",
               "w_uq", "w_ukv", "w_o_mla", "w_fourier", "w_bgate", "w_out")}
    shared["w_krp"] = w_krp
    shared["w_uqp"] = w_uqp
    shared["ccsc"] = ccsc
    tabs = [_tables(0), _tables(1)]
    ropes = [_rope(0), _rope(1)]
    maps = []
    for c in range(8):
        b, half = c // 2, c % 2
        vecs = np.zeros((7, 128, 128), f32)
        vecs[0, 0:8] = g["c"][b].reshape(8, 128)
        vecs[0, 8:16] = g["c_ctx"].reshape(8, 128)
        vecs[0, 16:24] = g["g_final"].reshape(8, 128)
        vecs[1:4] = vecs_l[0]
        vecs[4:7] = vecs_l[1]
        m = dict(shared)
        m["xin"] = np.ascontiguousarray(np.concatenate([g["x"][b, half * NL:(half + 1) * NL], g["ctx"][b]], 0))
        m["vecs"] = vecs
        m["rope"] = ropes[half]
        m["dft"], m["dftc"] = tabs[half]
        mk = np.zeros((128, 2), f32)
        mk[:, 0] = 1.0 if half == 1 else 0.0
        mk[:, 1] = 1.0 if half == 0 else 0.0
        m["maskd"] = mk
        maps.append(m)
    return maps


def run(inputs, stop=99, cores=8, dbg=False, ret_all=False):
    nc = build(stop, dbg)
    maps = _prep(inputs)
    res = run_bass_kernel_spmd(nc, maps[:cores], core_ids=list(range(cores)))
    if ret_all:
        return res.results
    out = np.zeros((4, SEQ, D), np.float32)
    for c in range(cores):
        b, half = c // 2, c % 2
        out[b, half * NL:(half + 1) * NL] = res.results[c]["yout"]
    return out


def kernel(**inputs):
    return run(inputs)
```

```python
import numpy as np
import ml_dtypes
from contextlib import ExitStack
import concourse.bass as bass
import concourse.mybir as mybir
from concourse.bass_utils import run_bass_kernel_spmd

F32 = mybir.dt.float32
BF16 = mybir.dt.bfloat16
ALU = mybir.AluOpType
AF = mybir.ActivationFunctionType

D = 1024
KC = 8
NL = 2048
NCX = 256
NT = NL + NCX
SEQ = 4096
DFF = 2816
NJ = DFF // 128
DEPTH = 2
EPS = 1e-6
ATTN_SCALE = 96.0 ** -0.5
TBS = [(0, 512), (512, 512), (1024, 512), (1536, 512), (2048, 256)]
SAME_ENGINE_SYNC = True
FUSE_WAIT = True


class Res:
    __slots__ = ("w", "r", "name")

    def __init__(self, name=""):
        self.w = None
        self.r = {}
        self.name = name


class Eng:
    def __init__(self, name, sem):
        self.name, self.sem = name, sem
        self.count = 0
        self.seen = {}
        self.q = []


class DmaSem:
    def __init__(self, sem):
        self.sem = sem
        self.count = 0


class _Rec:
    def __init__(self):
        self.call = None

    def __getattr__(self, name):
        def f(*a, **k):
            self.call = (name, a, k)
            return self
        return f


class FW:
    def __init__(self, nc, stack, n_dma_sems=32):
        self.nc = nc
        mk = lambda n: stack.enter_context(nc.semaphore(n))
        self.pe = Eng("pe", mk("s_pe"))
        self.act = Eng("act", mk("s_act"))
        self.dve = Eng("dve", mk("s_dve"))
        self.pool = Eng("pool", mk("s_pool"))
        self.sp = Eng("sp", mk("s_sp"))
        self.dsems = [DmaSem(mk(f"s_dma{i}")) for i in range(n_dma_sems)]
        self.dnext = 0
        self.ccs = DmaSem(mk("s_cc"))

    def _wait(self, eng, sem, val):
        key = id(sem)
        if eng.seen.get(key, 0) >= val:
            return
        eng.q.append(lambda h, sem=sem, val=val: h.wait_ge(sem, val))
        eng.seen[key] = val

    def _need(self, eng, sem, val, pend):
        key = id(sem)
        if eng.seen.get(key, 0) >= val:
            return
        eng.seen[key] = val
        for i, (s2, v2) in enumerate(pend):
            if s2 is sem:
                pend[i] = (sem, max(val, v2))
                return
        pend.append((sem, val))

    def _flush(self, eng, pend):
        if not pend:
            return None
        for (sem, val) in pend[:-1]:
            eng.q.append(lambda h, sem=sem, val=val: h.wait_ge(sem, val))
        return pend[-1] if FUSE_WAIT else (eng.q.append(lambda h, sem=pend[-1][0], val=pend[-1][1]: h.wait_ge(sem, val)) or None)

    def _deps(self, eng, reads, writes):
        deps = []
        for r in reads:
            if r.w is not None:
                deps.append(r.w)
        for w in writes:
            if w.w is not None:
                deps.append(w.w)
            deps.extend(w.r.values())
        pend = []
        for (sem, val, src) in deps:
            if src is eng and (eng.name == "pe" or not SAME_ENGINE_SYNC):
                continue
            self._need(eng, sem, val, pend)
        return pend

    def _commit(self, ev, reads, writes):
        key = id(ev[0])
        for r in reads:
            old = r.r.get(key)
            if old is None or old[1] < ev[1]:
                r.r[key] = ev
        for w in writes:
            w.w = ev
            w.r = {}

    def op(self, eng, fn, reads=(), writes=(), sig=True):
        fz = self._flush(eng, self._deps(eng, reads, writes))
        rec = _Rec()
        fn(rec)
        name, a, k = rec.call

        def emit(h, name=name, a=a, k=k, fz=fz, sem=eng.sem, sig=sig):
            inst = getattr(h, name)(*a, **k)
            if fz is not None:
                inst._wait_ge(fz[0], fz[1])
            if sig:
                inst.then_inc(sem, 1)
        eng.q.append(emit)
        if sig:
            eng.count += 1
            ev = (eng.sem, eng.count, eng)
        else:
            ev = (eng.sem, eng.count + 1, eng)
        self._commit(ev, reads, writes)
        return ev

    def dma(self, q, out, in_, reads=(), writes=()):
        pend = self._deps(q, reads, writes)
        ds = self.dsems[self.dnext]
        self.dnext = (self.dnext + 1) % len(self.dsems)
        if ds.count:
            self._need(q, ds.sem, ds.count, pend)
        fz = self._flush(q, pend)
        ds.count += 16

        def emit(h, out=out, in_=in_, sem=ds.sem, fz=fz):
            inst = h.dma_start(out=out, in_=in_)
            if fz is not None:
                inst._wait_ge(fz[0], fz[1])
            inst.then_inc(sem, 16)
        q.q.append(emit)
        ev = (ds.sem, ds.count, None)
        self._commit(ev, reads, writes)
        return ev

    def allgather(self, src, dst, reads, writes):
        q = self.pool
        pend = self._deps(q, reads, writes)
        cs = self.ccs
        if cs.count:
            self._need(q, cs.sem, cs.count, pend)
        for (sem, val) in pend:
            q.q.append(lambda h, sem=sem, val=val: h.wait_ge(sem, val))
        cs.count += 1
        q.q.append(lambda h, src=src, dst=dst, sem=cs.sem: h.collective_compute(
            "AllGather", ALU.bypass, replica_groups=[[0, 1], [2, 3], [4, 5], [6, 7]],
            ins=[src], outs=[dst]).then_inc(sem, 1))
        ev = (cs.sem, cs.count, None)
        self._commit(ev, reads, writes)
        return ev

    def run(self):
        nc = self.nc
        with nc.Block() as block:
            @block.tensor
            def _(e):
                for f in self.pe.q:
                    f(e)

            @block.scalar
            def _(e):
                for f in self.act.q:
                    f(e)

            @block.vector
            def _(e):
                for f in self.dve.q:
                    f(e)

            @block.gpsimd
            def _(e):
                for f in self.pool.q:
                    f(e)

            @block.sync
            def _(e):
                for f in self.sp.q:
                    f(e)


def build(stop=99, dbg=False):
    nc = bass.Bass("TRN2", target_bir_lowering=False)
    di = lambda n, s, d=F32: nc.dram_tensor(n, list(s), d, kind="ExternalInput").ap()
    xin = di("xin", [NT, D])
    vecs = di("vecs", [7, 128, 128])
    rope = di("rope", [2, 32, NL])
    dft = di("dft", [4, 32, 128, 2, 512], BF16)
    dftc = di("dftc", [2, 128, 2, 256], BF16)
    ccsc = di("ccsc", [128, 2, 128])
    maskd = di("maskd", [128, 2])
    w_ada = di("w_ada", [DEPTH, D, 9 * D])
    w1a = di("w1_ffn1", [DEPTH, D, DFF]); w3a = di("w3_ffn1", [DEPTH, D, DFF]); w2a = di("w2_ffn1", [DEPTH, DFF, D])
    w1b = di("w1_ffn2", [DEPTH, D, DFF]); w3b = di("w3_ffn2", [DEPTH, D, DFF]); w2b = di("w2_ffn2", [DEPTH, DFF, D])
    w_in = di("w_in", [DEPTH, D, 1952])
    w_krp = di("w_krp", [DEPTH, D, 192])
    w_pw = di("w_pw_conv", [DEPTH, 384, D])
    w_uq = di("w_uq", [DEPTH, 384, 768])
    w_uqp = di("w_uqp", [DEPTH, 384, 768])
    w_ukv = di("w_ukv", [DEPTH, 256, 1024])
    w_o = di("w_o_mla", [DEPTH, 512, D])
    w_fo = di("w_fourier", [DEPTH, 512, D])
    w_bg = di("w_bgate", [DEPTH, D, 3 * D])
    w_out = di("w_out", [DEPTH, D, D])
    yout = nc.dram_tensor("yout", [NL, D], F32, kind="ExternalOutput").ap()

    dt_ = lambda n, s: nc.dram_tensor(n, list(s), BF16)
    XA = dt_("XA", [288, NT]); GA = dt_("GA", [576, NT])
    XG = dt_("XG", [NL, 512]); GG = dt_("GG", [2 * NL, 512]); XGc = dt_("XGc", [NCX, 512])
    XH = dt_("XH", [128, 90]); GH = dt_("GH", [256, 90])
    dd = (lambda n, s: nc.dram_tensor(n, list(s), BF16, kind="ExternalOutput")) if dbg else dt_
    Dcqn = dd("Dcqn", [3 * 128, NT]); Dsv = dd("Dsv", [3 * 128, NT])
    DF = dd("DF", [4 * 128, NT]); Dat = dd("Dat", [4 * 128, NT])

    with ExitStack() as st:
        fw = FW(nc, st)
        pe, act, dve, pool, sp = fw.pe, fw.act, fw.dve, fw.pool, fw.sp
        sb = lambda n, s, d: st.enter_context(nc.sbuf_tensor(n, list(s), d))

        hT = sb("hT", [128, KC, NT], F32)
        Rh = [Res(f"h{i}") for i in range(5)]
        UR = sb("UR", [128, KC * NT], BF16)
        uT = UR[:, :].rearrange("p (c t) -> p c t", c=KC)
        Ru = [Res(f"u{i}") for i in range(5)]
        AR = sb("AR", [128, KC * NT], BF16)
        RA = Res("A")
        RAB = [Res("A0"), Res("A1")]

        PSn = 8
        PS = [st.enter_context(nc.psum_tensor(f"ps{i}", [128, 512], F32)) for i in range(PSn)]
        RPS = [Res(f"ps{i}") for i in range(PSn)]
        psi = [0]
        NROT = 6

        def ps():
            i = psi[0] % NROT
            psi[0] += 1
            return PS[i], RPS[i]

        T32 = sb("T32", [128, 8, 512], F32)
        RT32 = [Res(f"t32_{i}") for i in range(8)]
        t32i = [0]

        def t32():
            i = t32i[0] % 8
            t32i[0] += 1
            return T32[:, i, :], RT32[i]

        NT16 = 6
        T16 = sb("T16", [128, NT16, 512], BF16)
        RT16 = [Res(f"t16_{i}") for i in range(NT16)]
        t16i = [0]

        def t16():
            i = t16i[0] % NT16
            t16i[0] += 1
            return T16[:, i, :], RT16[i]

        NWS = 6
        WS = sb("WS", [128, NWS, 2048], BF16)
        RWS = [Res(f"ws{i}") for i in range(NWS)]
        wsi = [0]

        def ws():
            i = wsi[0] % NWS
            wsi[0] += 1
            return WS[:, i, :], RWS[i]

        ident = sb("ident", [128, 128], F32)
        ones32 = sb("ones32", [128, 128], F32)
        cst = sb("cst", [128, 4], F32)
        VT = sb("VT", [128, 7, 128], F32)
        scb = sb("scb", [128, KC, 2], BF16)
        MOD = sb("MOD", [128, 72, 2], F32)
        DER = sb("DER", [128, 6, KC, 2], F32)
        CS32 = sb("CS32", [128, 2, 128], F32)
        MSK = sb("MSK", [128, 2], F32)
        TT2 = sb("TT2", [128, 2, 512], F32)
        RTT2 = [Res("tt2_0"), Res("tt2_1")]
        QT = sb("QT", [128, 2, 512], BF16)
        RQT = [Res("qt0"), Res("qt1")]
        vTc = sb("vTc", [128, 3, 286], BF16)
        HT = sb("HT", [128, 2, 90], BF16)
        Rc = {k: Res(k) for k in "ident ones cst cst2 VT scb MOD DER CS32 MSK vTc HT XA GA XG XGc GG XH GH Dcqn Dsv DF Dat".split()}

        def mm(out, lhsT, rhs, start, stop, reads, writes, sig=None):
            fw.op(pe, lambda h: h.matmul(out, lhsT=lhsT, rhs=rhs, start=start, stop=stop),
                  reads=reads, writes=writes, sig=(stop if sig is None else sig))

        def handover(srcs, dsts):
            fw.op(dve, lambda h: h.memset(cst[:, 2:3], 0.0), reads=list(srcs), writes=list(dsts) + [Rc["cst2"]])

        fw.op(pool, lambda h: h.memset(ident[:], 1.0), writes=[Rc["ident"]])
        fw.op(pool, lambda h: h.affine_select(out=ident[:], in_=ident[:], pattern=[[-1, 128]],
                                              compare_op=ALU.is_equal, fill=0.0, base=0, channel_multiplier=1),
              reads=[Rc["ident"]], writes=[Rc["ident"]])
        fw.op(pool, lambda h: h.memset(ones32[:], 1.0), writes=[Rc["ones"]])
        fw.op(pool, lambda h: h.memset(cst[:, 0:1], EPS), writes=[Rc["cst"]])
        fw.op(pool, lambda h: h.memset(cst[:, 1:2], 0.0), writes=[Rc["cst"]])
        fw.op(pool, lambda h: h.memset(vTc[:], 0.0), writes=[Rc["vTc"]])
        fw.dma(sp, CS32[:], ccsc[:, :, :], writes=[Rc["CS32"]])
        fw.dma(sp, MSK[:], maskd[:, :], writes=[Rc["MSK"]])
        for i in range(7):
            t, r = t32()
            fw.dma(sp, t[:, 0:128], vecs[i, :, :], writes=[r])
            p, pr = ps()
            fw.op(pe, lambda h, p=p, t=t: h.transpose(out=p[:, 0:128], in_=t[:, 0:128], identity=ident[:]),
                  reads=[r, Rc["ident"]], writes=[pr])
            fw.op(dve, lambda h, p=p, i=i: h.tensor_copy(out=VT[:, i, :], in_=p[:, 0:128]), reads=[pr], writes=[Rc["VT"]])
        VG = VT[:, 0, :]
        VA = lambda l: VT[:, 1 + 3 * l, :]
        VB = lambda l: VT[:, 2 + 3 * l, :]
        VC = lambda l: VT[:, 3 + 3 * l, :]
        for t in range(2):
            fw.op(act, lambda h, t=t: h.activation(out=scb[:, :, t], in_=VG[:, 8 * t:8 * t + 8], func=AF.Silu),
                  reads=[Rc["VT"]], writes=[Rc["scb"]])

        for ti in range(NT // 128):
            bi = min(ti // 4, 4)
            for half in range(2):
                t, r = t32()
                fw.dma(sp, t, xin[ti * 128:(ti + 1) * 128, half * 512:(half + 1) * 512], writes=[r])
                p, pr = ps()
                for kk in range(4):
                    fw.op(pe, lambda h, p=p, t=t, kk=kk: h.transpose(out=p[:, kk * 128:(kk + 1) * 128],
                                                                     in_=t[:, kk * 128:(kk + 1) * 128], identity=ident[:]),
                          reads=[r, Rc["ident"]], writes=[pr], sig=(kk == 3))
                eng = dve if half == 0 else act
                if half == 0:
                    fw.op(dve, lambda h, p=p, ti=ti, half=half: h.tensor_copy(
                        out=hT[:, half * 4:half * 4 + 4, ti * 128:(ti + 1) * 128],
                        in_=p[:, :].rearrange("p (c t) -> p c t", c=4)), reads=[pr], writes=[Rh[bi]])
                else:
                    fw.op(act, lambda h, p=p, ti=ti, half=half: h.copy(
                        out=hT[:, half * 4:half * 4 + 4, ti * 128:(ti + 1) * 128],
                        in_=p[:, :].rearrange("p (c t) -> p c t", c=4)), reads=[pr], writes=[Rh[bi]])

        def wload(src_ap, shape3):
            w, wr = ws()
            a, b = shape3
            v = w[:, 0:a * b].rearrange("p (a b) -> p a b", a=a)
            fw.dma(pool, v, src_ap, writes=[wr])
            return v, wr

        def kcview(wap, c0, n):
            return wap[:, c0:c0 + n].rearrange("(kc p) n -> p kc n", p=128)

        def rstd_from(pst, n, scale):
            rs, rsr = t32()
            fw.op(act, lambda h: h.activation(out=rs[:, :n], in_=pst[0][:, :n], func=AF.Sqrt, bias=cst[:, 0:1], scale=scale),
                  reads=[pst[1], Rc["cst"]], writes=[rsr])
            fw.op(dve, lambda h: h.reciprocal(out=rs[:, :n], in_=rs[:, :n]), reads=[rsr], writes=[rsr])
            return rs, rsr

        def mod_stage(l):
            pm, pmr = ps()
            for s in range(36):
                wv, wr = wload(kcview(w_ada[l], s * 256, 256), (KC, 256))
                for jj in range(2):
                    j = s * 2 + jj
                    for kc in range(KC):
                        mm(pm[:, j * 2:j * 2 + 2], wv[:, kc, jj * 128:(jj + 1) * 128], scb[:, kc, :], kc == 0, kc == KC - 1,
                           [wr, Rc["scb"]], [pmr])
            pmv = pm[:, 0:144].rearrange("p (j t) -> p j t", t=2)
            va = VA(l)
            for t in range(2):
                fw.op(dve, lambda h, t=t: h.tensor_tensor(out=MOD[:, :, t], in0=pmv[:, :, t], in1=va[:, 0:72], op=ALU.add),
                      reads=[pmr, Rc["VT"]], writes=[Rc["MOD"]])
                for i, (gcol, n) in enumerate([(72, 1), (80, 4), (88, 7)]):
                    fw.op(dve, lambda h, t=t, i=i, n=n: h.tensor_scalar(out=DER[:, i, :, t], in0=MOD[:, n * 8:(n + 1) * 8, t],
                                                                       scalar1=1.0, scalar2=None, op0=ALU.add),
                          reads=[Rc["MOD"]], writes=[Rc["DER"]])
                    fw.op(dve, lambda h, t=t, i=i, gcol=gcol: h.tensor_tensor(out=DER[:, i, :, t], in0=DER[:, i, :, t],
                                                                              in1=va[:, gcol:gcol + 8], op=ALU.mult),
                          reads=[Rc["DER"], Rc["VT"]], writes=[Rc["DER"]])
                for i, (n, f) in enumerate([(2, 0.5), (5, 1.0), (8, 0.5)]):
                    fw.op(dve, lambda h, t=t, i=i, n=n, f=f: h.tensor_scalar(out=DER[:, 3 + i, :, t], in0=MOD[:, n * 8:(n + 1) * 8, t],
                                                                             scalar1=f, scalar2=None, op0=ALU.mult),
                          reads=[Rc["MOD"]], writes=[Rc["DER"]])

        def norm_stage(idx, tbs):
            for bi, (t0, n) in tbs:
                ts = 0 if t0 < NL else 1
                pst, pstr = ps()
                for kc in range(KC):
                    sq, sqr = t32()
                    fw.op(act, lambda h, sq=sq, kc=kc: h.activation(out=sq[:, :n], in_=hT[:, kc, t0:t0 + n], func=AF.Square),
                          reads=[Rh[bi]], writes=[sqr])
                    mm(pst[:, :n], ones32[:], sq[:, :n], kc == 0, kc == KC - 1, [sqr, Rc["ones"]], [pstr])
                rs, rsr = rstd_from((pst, pstr), n, 1.0 / D)
                for kc in range(KC):
                    tt, ttr = TT2[:, kc % 2, :], RTT2[kc % 2]
                    fw.op(dve, lambda h, tt=tt, kc=kc: h.scalar_tensor_tensor(
                        out=tt[:, :n], in0=hT[:, kc, t0:t0 + n], scalar=DER[:, idx, kc, ts:ts + 1], in1=rs[:, :n],
                        op0=ALU.mult, op1=ALU.mult), reads=[Rh[bi], rsr, Rc["DER"]], writes=[ttr])
                    fw.op(act, lambda h, tt=tt, kc=kc: h.activation(
                        out=uT[:, kc, t0:t0 + n], in_=tt[:, :n], func=AF.Identity,
                        bias=MOD[:, 3 * idx * 8 + kc, ts:ts + 1], scale=1.0), reads=[ttr, Rc["MOD"]], writes=[Ru[bi]])

        def ffn_stage(l, w1, w3, w2, gidx, tbs):
            groups = [(0, 4), (4, 4), (8, 4), (12, 4), (16, 4), (20, 2)]
            for gi, (j0, nj) in enumerate(groups):
                half = gi % 2
                ab = AR[:, half * 4 * NT:(half + 1) * 4 * NT].rearrange("p (c t) -> p c t", c=4)
                abr = RAB[half]
                for sub in range(nj // 2):
                    c0 = (j0 + sub * 2) * 128
                    w1v, w1r = wload(kcview(w1[l], c0, 256), (KC, 256))
                    w3v, w3r = wload(kcview(w3[l], c0, 256), (KC, 256))
                    for jj in range(2):
                        ja = sub * 2 + jj
                        for bi, (t0, n) in tbs:
                            p1, p1r = ps()
                            p3, p3r = ps()
                            for kc in range(KC):
                                mm(p1[:, :n], w1v[:, kc, jj * 128:(jj + 1) * 128], uT[:, kc, t0:t0 + n], kc == 0, kc == KC - 1,
                                   [w1r, Ru[bi]], [p1r])
                            for kc in range(KC):
                                mm(p3[:, :n], w3v[:, kc, jj * 128:(jj + 1) * 128], uT[:, kc, t0:t0 + n], kc == 0, kc == KC - 1,
                                   [w3r, Ru[bi]], [p3r])
                            s, sr = t32()
                            fw.op(act, lambda h, s=s, p1=p1, n=n: h.activation(out=s[:, :n], in_=p1[:, :n], func=AF.Silu),
                                  reads=[p1r], writes=[sr])
                            fw.op(dve, lambda h, s=s, p3=p3, n=n, ja=ja, t0=t0, ab=ab: h.tensor_tensor(
                                out=ab[:, ja, t0:t0 + n], in0=s[:, :n], in1=p3[:, :n], op=ALU.mult),
                                reads=[sr, p3r], writes=[abr])
                w2v = []
                for sub in range(nj // 2):
                    r0 = (j0 + sub * 2) * 128
                    w2v.append(wload(w2[l, r0:r0 + 256, :].rearrange("(j p) n -> p j n", p=128), (2, D)))
                for bi, (t0, n) in tbs:
                    ts = 0 if t0 < NL else 1
                    for m in range(KC):
                        po, por = ps()
                        for ja in range(nj):
                            wv, wr = w2v[ja // 2]
                            mm(po[:, :n], wv[:, ja % 2, m * 128:(m + 1) * 128], ab[:, ja, t0:t0 + n], ja == 0, ja == nj - 1,
                               [wr, abr], [por])
                        fw.op(dve, lambda h, po=po, m=m, t0=t0, n=n, ts=ts: h.scalar_tensor_tensor(
                            out=hT[:, m, t0:t0 + n], in0=po[:, :n], scalar=DER[:, 3 + gidx, m, ts:ts + 1],
                            in1=hT[:, m, t0:t0 + n], op0=ALU.mult, op1=ALU.add),
                            reads=[por, Rh[bi], Rc["DER"]], writes=[Rh[bi]])

        def mixer_stage(l, last, mstop):
            tbs_all = list(enumerate(TBS))
            tbs_c = tbs_all if not last else tbs_all[:4]
            vb = VB(l)
            vc = VC(l)
            va = VA(l)
            vTl = AR[:, 0:3 * 2078].rearrange("p (c t) -> p c t", c=3)
            DG = AR[:, 6234:6234 + 93 * 128].rearrange("p (j m) -> p j m", j=93)

            wc = [wload(kcview(w_in[l], s * 256, 256), (KC, 256)) for s in range(3)]
            for bi, (t0, n) in tbs_c:
                for cc in range(3):
                    pa, par = ps()
                    pg, pgr = ps()
                    ca, cg = cc * 128, 384 + cc * 128
                    wa, war = wc[ca // 256]
                    wg, wgr = wc[cg // 256]
                    for kc in range(KC):
                        mm(pa[:, :n], wa[:, kc, ca % 256:ca % 256 + 128], uT[:, kc, t0:t0 + n], kc == 0, kc == KC - 1, [war, Ru[bi]], [par])
                    for kc in range(KC):
                        mm(pg[:, :n], wg[:, kc, cg % 256:cg % 256 + 128], uT[:, kc, t0:t0 + n], kc == 0, kc == KC - 1, [wgr, Ru[bi]], [pgr])
                    s, sr = t32()
                    fw.op(act, lambda h, s=s, pg=pg, n=n: h.activation(out=s[:, :n], in_=pg[:, :n], func=AF.Sigmoid), reads=[pgr], writes=[sr])
                    if t0 < NL:
                        fw.op(dve, lambda h, s=s, pa=pa, n=n, cc=cc, t0=t0: h.tensor_tensor(
                            out=vTl[:, cc, 15 + t0:15 + t0 + n], in0=s[:, :n], in1=pa[:, :n], op=ALU.mult), reads=[sr, par], writes=[RA])
                    else:
                        fw.op(dve, lambda h, s=s, pa=pa, n=n, cc=cc: h.tensor_tensor(
                            out=vTc[:, cc, 15:15 + n], in0=s[:, :n], in1=pa[:, :n], op=ALU.mult), reads=[sr, par], writes=[Rc["vTc"]])
            XHv = XH.ap().rearrange("p (c t) -> p c t", c=3)
            fw.dma(sp, XHv[:, :, 0:15], vTl[:, :, 15:30], reads=[RA], writes=[Rc["XH"]])
            fw.dma(sp, XHv[:, :, 15:30], vTl[:, :, 15 + NL - 15:15 + NL], reads=[RA], writes=[Rc["XH"]])

            wq = [wload(kcview(w_in[l], 768, 256), (KC, 256)), wload(kcview(w_in[l], 1024, 128), (KC, 128))]
            wkv = wload(kcview(w_in[l], 1152, 256), (KC, 256))
            wkr = wload(kcview(w_krp[l], 0, 192), (KC, 192))
            XAa = XA.ap()
            Dcq = Dcqn.ap()
            for bi, (t0, n) in tbs_all:
                lat = t0 < NL
                for (nch, wsel, gcol, dst, need) in ((3, "q", 33, Dcq, (lat or not last)), (2, "kv", 36, XAa, True)):
                    if not need:
                        continue
                    pcs = []
                    for cc in range(nch):
                        pq, pqr = ps()
                        if wsel == "q":
                            wv, wr = wq[0] if cc < 2 else wq[1]
                            col = (cc % 2) * 128 if cc < 2 else 0
                        else:
                            wv, wr = wkv
                            col = cc * 128
                        for kc in range(KC):
                            mm(pq[:, :n], wv[:, kc, col:col + 128], uT[:, kc, t0:t0 + n], kc == 0, kc == KC - 1, [wr, Ru[bi]], [pqr])
                        pcs.append((pq, pqr))
                    pst, pstr = ps()
                    for cc in range(nch):
                        sq, sqr = t32()
                        fw.op(act, lambda h, sq=sq, pq=pcs[cc][0], n=n: h.activation(out=sq[:, :n], in_=pq[:, :n], func=AF.Square),
                              reads=[pcs[cc][1]], writes=[sqr])
                        mm(pst[:, :n], ones32[:], sq[:, :n], cc == 0, cc == nch - 1, [sqr, Rc["ones"]], [pstr])
                    rs, rsr = rstd_from((pst, pstr), n, 1.0 / (128 * nch))
                    for cc in range(nch):
                        o16, o16r = t16()
                        fw.op(dve, lambda h, o16=o16, pq=pcs[cc][0], n=n, cc=cc, gcol=gcol, rs=rs: h.scalar_tensor_tensor(
                            out=o16[:, :n], in0=pq[:, :n], scalar=vb[:, gcol + cc:gcol + cc + 1], in1=rs[:, :n],
                            op0=ALU.mult, op1=ALU.mult), reads=[pcs[cc][1], rsr, Rc["VT"]], writes=[o16r])
                        fw.dma(sp, dst[cc * 128:(cc + 1) * 128, t0:t0 + n], o16[:, :n], reads=[o16r],
                               writes=[Rc["Dcqn"] if wsel == "q" else Rc["XA"]])
                pk, pkr = ps()
                pp, ppr = ps()
                for kc in range(KC):
                    mm(pk[0:96, :n], wkr[0][:, kc, 0:96], uT[:, kc, t0:t0 + n], kc == 0, kc == KC - 1, [wkr[1], Ru[bi]], [pkr])
                for kc in range(KC):
                    mm(pp[0:96, :n], wkr[0][:, kc, 96:192], uT[:, kc, t0:t0 + n], kc == 0, kc == KC - 1, [wkr[1], Ru[bi]], [ppr])
                o16, o16r = t16()
                if lat:
                    a1, a1r = t32()
                    a2, a2r = t32()
                    fw.dma(sp, a1[64:96, :], rope[0, :, t0:t0 + 512], writes=[a1r])
                    fw.dma(sp, a2[64:96, :], rope[1, :, t0:t0 + 512], writes=[a2r])
                    fw.op(dve, lambda h, a1=a1, pk=pk: h.tensor_tensor(out=a1[64:96, :], in0=pk[64:96, :], in1=a1[64:96, :], op=ALU.mult),
                          reads=[pkr, a1r], writes=[a1r])
                    fw.op(dve, lambda h, a2=a2, pp=pp: h.tensor_tensor(out=a2[64:96, :], in0=pp[64:96, :], in1=a2[64:96, :], op=ALU.mult),
                          reads=[ppr, a2r], writes=[a2r])
                    fw.op(dve, lambda h, a1=a1, a2=a2, o16=o16: h.tensor_tensor(out=o16[64:96, :], in0=a1[64:96, :], in1=a2[64:96, :], op=ALU.add),
                          reads=[a1r, a2r], writes=[o16r])
                else:
                    fw.op(dve, lambda h, o16=o16, pk=pk, n=n: h.tensor_copy(out=o16[64:96, :n], in_=pk[64:96, :n]), reads=[pkr], writes=[o16r])
                fw.dma(sp, XAa[256:288, t0:t0 + n], o16[64:96, :n], reads=[o16r], writes=[Rc["XA"]])

            wf = [wload(kcview(w_in[l], 1440 + s * 256, 256), (KC, 256)) for s in range(2)]
            XGa = XG.ap()
            ntt = 18 if not last else 16
            for tt in range(ntt):
                bi = min(tt // 4, 4)
                pgm, pgr = ps()
                for s in range(2):
                    for kc in range(KC):
                        mm(pgm[:, s * 256:(s + 1) * 256], uT[:, kc, tt * 128:(tt + 1) * 128], wf[s][0][:, kc, :], kc == 0, kc == KC - 1,
                           [wf[s][1], Ru[bi]], [pgr])
                o16, o16r = t16()
                if tt % 2 == 0:
                    fw.op(dve, lambda h, o16=o16, pgm=pgm: h.tensor_copy(out=o16[:, :], in_=pgm[:, :]), reads=[pgr], writes=[o16r])
                else:
                    fw.op(act, lambda h, o16=o16, pgm=pgm: h.copy(out=o16[:, :], in_=pgm[:, :]), reads=[pgr], writes=[o16r])
                if tt < 16:
                    fw.dma(sp, XGa[tt * 128:(tt + 1) * 128, :], o16[:, :], reads=[o16r], writes=[Rc["XG"]])
                else:
                    fw.dma(sp, XGc.ap()[(tt - 16) * 128:(tt - 15) * 128, :], o16[:, :], reads=[o16r], writes=[Rc["XGc"]])

            fw.allgather(XH.ap(), GH.ap(), [Rc["XH"]], [Rc["GH"]])
            fw.allgather(XA.ap(), GA.ap(), [Rc["XA"]], [Rc["GA"]])
            fw.allgather(XG.ap(), GG.ap(), [Rc["XG"]], [Rc["GG"]])
            if mstop <= 1:
                return

            GHa = GH.ap()
            fw.dma(sp, HT[:, :, :], GHa.rearrange("(r p) f -> p r f", p=128), reads=[Rc["GH"]], writes=[Rc["HT"]])
            HTv = HT[:, :, :].rearrange("p r (c t) -> p r c t", c=3)
            fw.op(dve, lambda h: h.tensor_scalar(out=vTl[:, :, 0:15], in0=HTv[:, 0, :, 15:30], scalar1=MSK[:, 0:1], scalar2=None, op0=ALU.mult),
                  reads=[Rc["HT"], Rc["MSK"]], writes=[RA])
            fw.op(dve, lambda h: h.tensor_scalar(out=vTl[:, :, 15 + NL:30 + NL], in0=HTv[:, 1, :, 0:15], scalar1=MSK[:, 1:2], scalar2=None, op0=ALU.mult),
                  reads=[Rc["HT"], Rc["MSK"]], writes=[RA])
            for idx in range(93):
                fw.op(dve, lambda h, idx=idx: h.tensor_scalar(out=DG[:, idx, :], in0=ident[:], scalar1=vc[:, idx:idx + 1], scalar2=None, op0=ALU.mult),
                      reads=[Rc["ident"], Rc["VT"]], writes=[RA])
            Dsva = Dsv.ap()
            for bi, (t0, n) in tbs_c:
                lat = t0 < NL
                cos_ = []
                for cc in range(3):
                    pc, pcr = ps()
                    for j in range(31):
                        rhs = vTl[:, cc, t0 + j:t0 + j + n] if lat else vTc[:, cc, j:j + n]
                        mm(pc[:, :n], DG[:, j * 3 + cc, :], rhs, j == 0, j == 30, [RA] if lat else [RA, Rc["vTc"]], [pcr])
                    co, cor = t32()
                    fw.op(act, lambda h, co=co, pc=pc, n=n, cc=cc: h.activation(out=co[:, :n], in_=pc[:, :n], func=AF.Identity,
                                                                              bias=vb[:, 24 + cc:25 + cc], scale=1.0),
                          reads=[pcr, Rc["VT"]], writes=[cor])
                    cos_.append((co, cor))
                pss, pssr = ps()
                psq, psqr = ps()
                for cc in range(3):
                    mm(pss[:, :n], ones32[:], cos_[cc][0][:, :n], cc == 0, cc == 2, [cos_[cc][1], Rc["ones"]], [pssr])
                for cc in range(3):
                    sq, sqr = t32()
                    fw.op(act, lambda h, sq=sq, co=cos_[cc][0], n=n: h.activation(out=sq[:, :n], in_=co[:, :n], func=AF.Square),
                          reads=[cos_[cc][1]], writes=[sqr])
                    mm(psq[:, :n], ones32[:], sq[:, :n], cc == 0, cc == 2, [sqr, Rc["ones"]], [psqr])
                mu, mur = t32()
                fw.op(dve, lambda h, mu=mu, pss=pss, n=n: h.tensor_scalar(out=mu[:, :n], in0=pss[:, :n], scalar1=1.0 / 384, scalar2=None, op0=ALU.mult),
                      reads=[pssr], writes=[mur])
                m2, m2r = t32()
                fw.op(dve, lambda h, mu=mu, m2=m2, n=n: h.tensor_tensor(out=m2[:, :n], in0=mu[:, :n], in1=mu[:, :n], op=ALU.mult), reads=[mur], writes=[m2r])
                fw.op(dve, lambda h, m2=m2, psq=psq, n=n: h.scalar_tensor_tensor(out=m2[:, :n], in0=psq[:, :n], scalar=1.0 / 384, in1=m2[:, :n],
                                                                               op0=ALU.mult, op1=ALU.subtract), reads=[psqr, m2r], writes=[m2r])
                fw.op(act, lambda h, m2=m2, n=n: h.activation(out=m2[:, :n], in_=m2[:, :n], func=AF.Sqrt, bias=cst[:, 0:1], scale=1.0),
                      reads=[m2r, Rc["cst"]], writes=[m2r])
                fw.op(dve, lambda h, m2=m2, n=n: h.reciprocal(out=m2[:, :n], in_=m2[:, :n]), reads=[m2r], writes=[m2r])
                for cc in range(3):
                    co, cor = cos_[cc]
                    fw.op(dve, lambda h, co=co, mu=mu, n=n: h.tensor_tensor(out=co[:, :n], in0=co[:, :n], in1=mu[:, :n], op=ALU.subtract),
                          reads=[cor, mur], writes=[cor])
                    fw.op(dve, lambda h, co=co, m2=m2, n=n: h.tensor_tensor(out=co[:, :n], in0=co[:, :n], in1=m2[:, :n], op=ALU.mult),
                          reads=[cor, m2r], writes=[cor])
                    o16, o16r = t16()
                    fw.op(act, lambda h, co=co, o16=o16, n=n, cc=cc: h.activation(out=o16[:, :n], in_=co[:, :n], func=AF.Silu,
                                                                                bias=vb[:, 30 + cc:31 + cc], scale=vb[:, 27 + cc:28 + cc]),
                          reads=[cor, Rc["VT"]], writes=[o16r])
                    fw.dma(sp, Dsva[cc * 128:(cc + 1) * 128, t0:t0 + n], o16[:, :n], reads=[o16r], writes=[Rc["Dsv"]])
            if mstop <= 2:
                return

            gfull = AR[:, 0:32 * 512].rearrange("p (t c) -> p t c", t=32)
            GGa = GG.ap()
            DFa = DF.ap()
            for r in range(2):
                fw.dma(sp, gfull[:, r * 16:(r + 1) * 16, :], GGa[r * NL:(r + 1) * NL, :].rearrange("(t p) c -> p t c", p=128),
                       reads=[Rc["GG"]], writes=[RA])

            def stage2(P, Pr, Q, Qr, n, dst):
                pS, pSr = t32()
                qS, qSr = t32()
                fw.op(dve, lambda h: h.tensor_copy(out=pS[:, :n], in_=P[:, :n]), reads=[Pr], writes=[pSr])
                fw.op(act, lambda h: h.copy(out=qS[:, :n], in_=Q[:, :n]), reads=[Qr], writes=[qSr])
                mm(P[:, :n], CS32[:, 0, :], pS[:, :n], True, False, [pSr, Rc["CS32"]], [Pr])
                mm(P[:, :n], CS32[:, 1, :], qS[:, :n], False, True, [qSr, Rc["CS32"]], [Pr])
                o16, o16r = t16()
                fw.op(act, lambda h: h.copy(out=o16[:, :n], in_=P[:, :n]), reads=[Pr], writes=[o16r])
                fw.dma(sp, dst, o16[:, :n], reads=[o16r], writes=[Rc["DF"]])

            for kb in range(4):
                for ti in range(32):
                    tab, tabr = ws()
                    tv = tab[:, 0:1024].rearrange("p (s k) -> p s k", s=2)
                    fw.dma(sp, tv, dft[kb, ti, :, :, :], writes=[tabr])
                    for gi in range(4):
                        mm(PS[gi][:, :], gfull[:, ti, gi * 128:(gi + 1) * 128], tv[:, 0, :], ti == 0, ti == 31, [RA, tabr], [RPS[gi]])
                        mm(PS[4 + gi][:, :], gfull[:, ti, gi * 128:(gi + 1) * 128], tv[:, 1, :], ti == 0, ti == 31, [RA, tabr], [RPS[4 + gi]],
                           sig=(True if gi == 3 else None))
                for gi in range(4):
                    stage2(PS[gi], RPS[gi], PS[4 + gi], RPS[4 + gi], 512, DFa[gi * 128:(gi + 1) * 128, kb * 512:(kb + 1) * 512])
            if not last:
                gc, gcr = ws()
                gcv = gc[:, 0:1024].rearrange("p (t c) -> p t c", t=2)
                fw.dma(sp, gcv, XGc.ap().rearrange("(t p) c -> p t c", p=128), reads=[Rc["XGc"]], writes=[gcr])
                tc_, tcr = ws()
                tcv = tc_[:, 0:1024].rearrange("p (t s k) -> p t s k", t=2, s=2)
                fw.dma(sp, tcv, dftc.rearrange("t p s k -> p t s k"), writes=[tcr])
                for gi in range(4):
                    P, Pr = PS[gi], RPS[gi]
                    Q, Qr = PS[4 + gi], RPS[4 + gi]
                    for tl in range(2):
                        mm(P[:, :256], gcv[:, tl, gi * 128:(gi + 1) * 128], tcv[:, tl, 0, :], tl == 0, tl == 1, [gcr, tcr], [Pr])
                    for tl in range(2):
                        mm(Q[:, :256], gcv[:, tl, gi * 128:(gi + 1) * 128], tcv[:, tl, 1, :], tl == 0, tl == 1, [gcr, tcr], [Qr])
                    stage2(P, Pr, Q, Qr, 256, DFa[gi * 128:(gi + 1) * 128, NL:NT])
            if mstop <= 3:
                return

            GAa = GA.ap()
            Data = Dat.ap()
            NK = SEQ + NCX
            KTb = [AR[:, b * 8704:b * 8704 + NK] for b in range(2)]
            VAb = [AR[:, b * 8704 + NK:(b + 1) * 8704].rearrange("p (t c) -> p t c", t=34) for b in range(2)]
            RKV = [Res("kv0"), Res("kv1")]
            fw.op(dve, lambda h: h.memset(VAb[0][:, :, 64:128], 1.0), reads=[RA], writes=[RA, RKV[0]])
            fw.op(dve, lambda h: h.memset(VAb[1][:, :, 0:64], 1.0), reads=[RA], writes=[RA, RKV[1]])
            qbs = tbs_c
            HW = {}
            LAG = 3

            def kv_build(hd):
                b = hd % 2
                voff = 0 if b == 0 else 64
                wsl, wslr = ws()
                wkvh = wsl[:, 0:256].rearrange("p (c n) -> p c n", c=2)
                wqh = wsl[:, 256:256 + 288].rearrange("p (c n) -> p c n", c=3)
                wqph = wsl[:, 544:544 + 288].rearrange("p (c n) -> p c n", c=3)
                fw.dma(pool, wkvh, w_ukv[l][:, hd * 128:(hd + 1) * 128].rearrange("(c p) n -> p c n", p=128), writes=[wslr])
                fw.dma(pool, wqh, w_uq[l][:, hd * 96:(hd + 1) * 96].rearrange("(c p) n -> p c n", p=128), writes=[wslr])
                fw.dma(pool, wqph, w_uqp[l][:, hd * 96:(hd + 1) * 96].rearrange("(c p) n -> p c n", p=128), writes=[wslr])
                HW[hd] = (wqh, wqph, wslr)
                for kb in range(9):
                    if kb < 8:
                        r, c0, n = kb // 4, (kb % 4) * 512, 512
                    else:
                        r, c0, n = 0, NL, NCX
                    kk0 = kb * 512
                    ck = []
                    for cc in range(2):
                        c16, c16r = t16()
                        fw.dma(sp, c16[:, :n], GAa[r * 288 + cc * 128:r * 288 + (cc + 1) * 128, c0:c0 + n], reads=[Rc["GA"]], writes=[c16r])
                        ck.append((c16, c16r))
                    fw.dma(sp, KTb[b][64:96, kk0:kk0 + n], GAa[r * 288 + 256:r * 288 + 288, c0:c0 + n], reads=[Rc["GA"]], writes=[RKV[b]])
                    pk, pkr = ps()
                    for cc in range(2):
                        mm(pk[0:64, :n], wkvh[:, cc, 0:64], ck[cc][0][:, :n], cc == 0, cc == 1, [wslr, ck[cc][1]], [pkr])
                    fw.op(act, lambda h: h.copy(out=KTb[b][0:64, kk0:kk0 + n], in_=pk[0:64, :n]), reads=[pkr], writes=[RKV[b]])
                    pv, pvr = ps()
                    nt_ = n // 128
                    for tl in range(nt_):
                        for cc in range(2):
                            mm(pv[:, tl * 64:(tl + 1) * 64], ck[cc][0][:, tl * 128:(tl + 1) * 128], wkvh[:, cc, 64:128], cc == 0, cc == 1,
                               [wslr, ck[cc][1]], [pvr])
                    fw.op(dve, lambda h: h.tensor_copy(out=VAb[b][:, kb * 4:kb * 4 + nt_, voff:voff + 64],
                                                       in_=pv[:, 0:nt_ * 64].rearrange("p (t c) -> p t c", t=nt_)),
                          reads=[pvr], writes=[RKV[b]])

            def q_build(hd, qi):
                bi, (t0, n) = qbs[qi]
                wqh, wqph, wslr = HW[hd]
                lat = t0 < NL
                cq = []
                for cc in range(3):
                    c16, c16r = t16()
                    fw.dma(sp, c16[:, :n], Dcq[cc * 128:(cc + 1) * 128, t0:t0 + n], reads=[Rc["Dcqn"]], writes=[c16r])
                    cq.append((c16, c16r))
                pq, pqr = ps()
                for cc in range(3):
                    mm(pq[0:96, :n], wqh[:, cc, :], cq[cc][0][:, :n], cc == 0, cc == 2, [wslr, cq[cc][1]], [pqr])
                q16, q16r = QT[:, qti[0] % 2, :], RQT[qti[0] % 2]
                qti[0] += 1
                if lat:
                    pp, ppr = ps()
                    for cc in range(3):
                        mm(pp[0:96, :n], wqph[:, cc, :], cq[cc][0][:, :n], cc == 0, cc == 2, [wslr, cq[cc][1]], [ppr])
                    fw.op(act, lambda h: h.copy(out=q16[0:64, :], in_=pq[0:64, :]), reads=[pqr], writes=[q16r])
                    a1, a1r = t32()
                    a2, a2r = t32()
                    fw.dma(sp, a1[64:96, :], rope[0, :, t0:t0 + 512], writes=[a1r])
                    fw.dma(sp, a2[64:96, :], rope[1, :, t0:t0 + 512], writes=[a2r])
                    fw.op(dve, lambda h: h.tensor_tensor(out=a1[64:96, :], in0=pq[64:96, :], in1=a1[64:96, :], op=ALU.mult),
                          reads=[pqr, a1r], writes=[a1r])
                    fw.op(dve, lambda h: h.tensor_tensor(out=a2[64:96, :], in0=pp[64:96, :], in1=a2[64:96, :], op=ALU.mult),
                          reads=[ppr, a2r], writes=[a2r])
                    fw.op(dve, lambda h: h.tensor_tensor(out=q16[64:96, :], in0=a1[64:96, :], in1=a2[64:96, :], op=ALU.add),
                          reads=[a1r, a2r], writes=[q16r])
                    kts = list(range(34))
                else:
                    fw.op(act, lambda h: h.copy(out=q16[0:96, :n], in_=pq[0:96, :n]), reads=[pqr], writes=[q16r])
                    kts = [32, 33]
                return (q16, q16r, kts, t0, n)

            def scores(hd, qinfo):
                q16, q16r, kts, t0, n = qinfo
                b = hd % 2
                ob = 6 + (obi[0] % 2)
                obi[0] += 1
                po, por = PS[ob], RPS[ob]
                nk = len(kts)
                pts = {}
                for i in range(nk + LAG):
                    if i < nk:
                        kt = kts[i]
                        psc, pscr = ps()
                        mm(psc[:, :n], KTb[b][0:96, kt * 128:(kt + 1) * 128], q16[0:96, :n], True, True, [RKV[b], q16r], [pscr])
                        pt, ptr = t16()
                        fw.op(act, lambda h: h.activation(out=pt[:, :n], in_=psc[:, :n], func=AF.Exp, scale=ATTN_SCALE),
                              reads=[pscr], writes=[ptr])
                        pts[i] = (pt, ptr)
                    j = i - LAG
                    if j >= 0:
                        pt, ptr = pts.pop(j)
                        mm(po[:, :n], VAb[b][:, kts[j], :], pt[:, :n], j == 0, j == nk - 1, [RKV[b], ptr], [por])
                orow = slice(0, 64) if b == 0 else slice(64, 128)
                drow = slice(64, 128) if b == 0 else slice(0, 64)
                rd, rdr = t32()
                fw.op(dve, lambda h: h.reciprocal(out=rd[drow, :n], in_=po[drow, :n]), reads=[por], writes=[rdr])
                rsh, rshr = t32()
                fw.op(dve, lambda h: h.tensor_copy(out=rsh[orow, :n], in_=rd[drow, :n]), reads=[rdr], writes=[rshr])
                ao, aor = t16()
                fw.op(dve, lambda h: h.tensor_tensor(out=ao[orow, :n], in0=po[orow, :n], in1=rsh[orow, :n], op=ALU.mult),
                      reads=[por, rshr], writes=[aor])
                r0 = (hd // 2) * 128 + (0 if b == 0 else 64)
                fw.dma(sp, Data[r0:r0 + 64, t0:t0 + n], ao[orow, :n], reads=[aor], writes=[Rc["Dat"]])

            kv_build(0)
            for hd in range(8):
                qnext = q_build(hd, 0)
                if hd < 7:
                    kv_build(hd + 1)
                for qi in range(len(qbs)):
                    qcur = qnext
                    if qi + 1 < len(qbs):
                        qnext = q_build(hd, qi + 1)
                    scores(hd, qcur)
            if mstop <= 4:
                return

            mixacc = AR[:, :].rearrange("p (c t) -> p c t", c=KC)
            RM = Res("mix")
            handover([RKV[0], RKV[1], RA], [RM, RA, RKV[0], RKV[1]])
            branches = [(Dsv.ap(), Rc["Dsv"], 3, w_pw, 96), (Dat.ap(), Rc["Dat"], 4, w_o, None), (DFa, Rc["DF"], 4, w_fo, 104)]
            first = True
            for r, (Dsrc, Dres, nch, wsrc, bcol) in enumerate(branches):
                wbg = [wload(kcview(w_bg[l], r * D + s * 256, 256), (KC, 256)) for s in range(4)]
                wr_ = [wload(wsrc[l].rearrange("(c p) n -> p c n", p=128)[:, :, s * 512:(s + 1) * 512], (nch, 512)) for s in range(2)]
                for bi, (t0, n) in tbs_c:
                    xin_ = []
                    for c in range(nch):
                        c16, c16r = t16()
                        fw.dma(sp, c16[:, :n], Dsrc[c * 128:(c + 1) * 128, t0:t0 + n], reads=[Dres], writes=[c16r])
                        xin_.append((c16, c16r))
                    for m in range(KC):
                        pgt, pgtr = ps()
                        wv, wr = wbg[m // 2]
                        for kc in range(KC):
                            mm(pgt[:, :n], wv[:, kc, (m % 2) * 128:(m % 2) * 128 + 128], uT[:, kc, t0:t0 + n], kc == 0, kc == KC - 1, [wr, Ru[bi]], [pgtr])
                        s, sr = t32()
                        fw.op(act, lambda h, s=s, pgt=pgt, n=n, r=r, m=m: h.activation(out=s[:, :n], in_=pgt[:, :n], func=AF.Sigmoid,
                                                                                     bias=vb[:, r * 8 + m:r * 8 + m + 1], scale=1.0),
                              reads=[pgtr, Rc["VT"]], writes=[sr])
                        py, pyr = ps()
                        wv2, wr2 = wr_[m // 4]
                        for c in range(nch):
                            mm(py[:, :n], wv2[:, c, (m % 4) * 128:(m % 4) * 128 + 128], xin_[c][0][:, :n], c == 0, c == nch - 1, [wr2, xin_[c][1]], [pyr])
                        bias = va[:, bcol + m:bcol + m + 1] if bcol is not None else 0.0
                        if first:
                            fw.op(dve, lambda h, py=py, s=s, n=n, m=m, t0=t0, bias=bias: h.scalar_tensor_tensor(
                                out=mixacc[:, m, t0:t0 + n], in0=py[:, :n], scalar=bias, in1=s[:, :n], op0=ALU.add, op1=ALU.mult),
                                reads=[pyr, sr, Rc["VT"]], writes=[RM])
                        else:
                            fw.op(dve, lambda h, py=py, s=s, n=n, bias=bias: h.scalar_tensor_tensor(
                                out=s[:, :n], in0=py[:, :n], scalar=bias, in1=s[:, :n], op0=ALU.add, op1=ALU.mult),
                                reads=[pyr, sr, Rc["VT"]], writes=[sr])
                            fw.op(dve, lambda h, s=s, n=n, m=m, t0=t0: h.tensor_tensor(
                                out=mixacc[:, m, t0:t0 + n], in0=mixacc[:, m, t0:t0 + n], in1=s[:, :n], op=ALU.add),
                                reads=[sr, RM], writes=[RM])
                first = False
            for mo in range(KC):
                wv, wr = wload(kcview(w_out[l], mo * 128, 128), (KC, 128))
                for bi, (t0, n) in tbs_c:
                    ts = 0 if t0 < NL else 1
                    po, por = ps()
                    for m in range(KC):
                        mm(po[:, :n], wv[:, m, :], mixacc[:, m, t0:t0 + n], m == 0, m == KC - 1, [wr, RM], [por])
                    fw.op(dve, lambda h, po=po, mo=mo, t0=t0, n=n, ts=ts: h.scalar_tensor_tensor(
                        out=hT[:, mo, t0:t0 + n], in0=po[:, :n], scalar=DER[:, 4, mo, ts:ts + 1],
                        in1=hT[:, mo, t0:t0 + n], op0=ALU.mult, op1=ALU.add),
                        reads=[por, Rh[bi], Rc["DER"]], writes=[Rh[bi]])
            handover([RM, RKV[0], RKV[1], RA], [RA, RAB[0], RAB[1]])

        obi = [0]
        qti = [0]
        tbs_all = list(enumerate(TBS))
        stage = 0
        done = False
        for l in range(DEPTH):
            last = l == DEPTH - 1
            mod_stage(l)
            norm_stage(0, tbs_all)
            ffn_stage(l, w1a, w3a, w2a, 0, tbs_all)
            stage += 1
            if stage >= stop:
                done = True
                break
            norm_stage(1, tbs_all)
            handover([RAB[0], RAB[1], RA], [RA, RAB[0], RAB[1]])
            ms = (stop - stage) if (stop - stage) < 6 else 99
            mixer_stage(l, last, ms)
            stage += 5
            if stage >= stop:
                done = True
                break
            tb2 = tbs_all if not last else tbs_all[:4]
            norm_stage(2, tb2)
            ffn_stage(l, w1b, w3b, w2b, 2, tb2)
            stage += 1
            if stage >= stop:
                done = True
                break

        vg = VT[:, 0, :]
        for bi, (t0, n) in tbs_all[:4]:
            if not done:
                pst, pstr = ps()
                for kc in range(KC):
                    sq, sqr = t32()
                    fw.op(act, lambda h, sq=sq, kc=kc: h.activation(out=sq[:, :n], in_=hT[:, kc, t0:t0 + n], func=AF.Square),
                          reads=[Rh[bi]], writes=[sqr])
                    mm(pst[:, :n], ones32[:], sq[:, :n], kc == 0, kc == KC - 1, [sqr, Rc["ones"]], [pstr])
                rs, rsr = rstd_from((pst, pstr), n, 1.0 / D)
                for kc in range(KC):
                    fw.op(dve, lambda h, kc=kc, rs=rs: h.scalar_tensor_tensor(
                        out=hT[:, kc, t0:t0 + n], in0=hT[:, kc, t0:t0 + n], scalar=vg[:, 16 + kc:17 + kc], in1=rs[:, :n],
                        op0=ALU.mult, op1=ALU.mult), reads=[Rh[bi], rsr, Rc["VT"]], writes=[Rh[bi]])
            for tl in range(4):
                tt = (t0 // 128) + tl
                for half in range(2):
                    p, pr = ps()
                    for kk in range(4):
                        kc = half * 4 + kk
                        fw.op(pe, lambda h, p=p, kk=kk, kc=kc, tt=tt: h.transpose(out=p[:, kk * 128:(kk + 1) * 128],
                                                                                 in_=hT[:, kc, tt * 128:(tt + 1) * 128], identity=ident[:]),
                              reads=[Rh[bi], Rc["ident"]], writes=[pr], sig=(kk == 3))
                    o, orr = t32()
                    if half == 0:
                        fw.op(dve, lambda h, o=o, p=p: h.tensor_copy(out=o[:, :], in_=p[:, :]), reads=[pr], writes=[orr])
                    else:
                        fw.op(act, lambda h, o=o, p=p: h.copy(out=o[:, :], in_=p[:, :]), reads=[pr], writes=[orr])
                    fw.dma(sp, yout[tt * 128:(tt + 1) * 128, half * 512:(half + 1) * 512], o[:, :], reads=[orr], writes=[Res()])
        for ds in fw.dsems:
            if ds.count:
                fw._wait(sp, ds.sem, ds.count)
        fw.run()
    return nc


def _tables(half):
    bf = ml_dtypes.bfloat16
    t = np.arange(SEQ, dtype=np.int64)
    k = np.arange(NL, dtype=np.int64) + half * NL
    ang = 2.0 * np.pi * ((t[:, None] * k[None, :]) % SEQ).astype(np.float64) / SEQ
    tab = np.stack([np.cos(ang) / 64.0, -np.sin(ang) / 64.0], axis=1)
    tab = tab.reshape(32, 128, 2, 4, 512).transpose(3, 0, 1, 2, 4)
    dft = np.ascontiguousarray(tab).astype(bf)
    tc = np.arange(NCX, dtype=np.int64)
    angc = 2.0 * np.pi * ((tc[:, None] * tc[None, :]) % NCX).astype(np.float64) / NCX
    tabc = np.stack([np.cos(angc) / 16.0, -np.sin(angc) / 16.0], axis=1).reshape(2, 128, 2, NCX)
    dftc = np.ascontiguousarray(tabc).astype(bf)
    return dft, dftc


def _rope(half):
    tok = np.arange(NL) + half * NL
    row = (tok // 64).astype(np.float32)
    col = (tok % 64).astype(np.float32)
    inv = (1.0 / (np.float32(10000.0) ** (np.arange(8, dtype=np.float32) * np.float32(2.0) / np.float32(16)))).astype(np.float32)
    ar = row[:, None] * inv
    ac = col[:, None] * inv
    ang = np.concatenate([ar, ar, ac, ac], axis=-1).astype(np.float32)
    cos = np.cos(ang).astype(np.float32).T
    sin = np.sin(ang).astype(np.float32).T
    sign = np.ones(32, np.float32)
    sign[0:8] = -1.0
    sign[16:24] = -1.0
    return np.ascontiguousarray(np.stack([cos, sin * sign[:, None]], 0)).astype(np.float32)


_PERM = np.array([(f + 8) if (f % 16) < 8 else (f - 8) for f in range(32)])


def _prep(inputs):
    f32 = np.float32
    g = {k: np.asarray(v, dtype=f32) for k, v in inputs.items()}
    L = DEPTH
    vecs_l = np.zeros((L, 3, 128, 128), f32)
    for l in range(L):
        A = vecs_l[l, 0]
        A[0:72] = g["b_ada"][l].reshape(72, 128)
        A[72:80] = g["g_ffn1"][l].reshape(8, 128)
        A[80:88] = g["g_mix"][l].reshape(8, 128)
        A[88:96] = g["g_ffn2"][l].reshape(8, 128)
        A[96:104] = g["b_pw_conv"][l].reshape(8, 128)
        A[104:112] = g["b_fourier"][l].reshape(8, 128)
        B = vecs_l[l, 1]
        B[0:24] = g["b_bgate"][l].reshape(24, 128)
        B[24:27] = g["b_dw"][l].reshape(3, 128)
        B[27:30] = g["ln_g_conv"][l].reshape(3, 128)
        B[30:33] = g["ln_b_conv"][l].reshape(3, 128)
        B[33:36] = g["g_qnorm"][l].reshape(3, 128)
        B[36:38] = g["g_kvnorm"][l].reshape(2, 128)
        vecs_l[l, 2, 0:93] = g["w_dw"][l].reshape(31 * 3, 128)
    w_krp = np.zeros((L, D, 192), f32)
    w_krp[:, :, 64:96] = g["w_in"][:, :, 1408:1440]
    w_krp[:, :, 160:192] = g["w_in"][:, :, 1408 + _PERM]
    w_uqp = np.zeros((L, 384, 768), f32)
    for hd in range(8):
        w_uqp[:, :, hd * 96 + 64:hd * 96 + 96] = g["w_uq"][:, :, hd * 96 + 64 + _PERM]
    cm = np.arange(128)
    angc = 2.0 * np.pi * ((cm[:, None] * cm[None, :]) % 128).astype(np.float64) / 128.0
    ccsc = np.stack([np.cos(angc), np.sin(angc)], axis=1) / np.sqrt(128.0)
    ccsc = np.ascontiguousarray(ccsc).astype(f32)
    shared = {k: np.ascontiguousarray(g[k]) for k in
              ("w_ada", "w1_ffn1", "w3_ffn1", "w2_ffn1", "w1_ffn2", "w3_ffn2", "w2_ffn2", "w_in", "w_pw_conv",
               "w_uq", "w_ukv", "w_o_mla", "w_fourier", "w_bgate", "w_out")}
    shared["w_krp"] = w_krp
    shared["w_uqp"] = w_uqp
    shared["ccsc"] = ccsc
    tabs = [_tables(0), _tables(1)]
    ropes = [_rope(0), _rope(1)]
    maps = []
    for c in range(8):
        b, half = c // 2, c % 2
        vecs = np.zeros((7, 128, 128), f32)
        vecs[0, 0:8] = g["c"][b].reshape(8, 128)
        vecs[0, 8:16] = g["c_ctx"].reshape(8, 128)
        vecs[0, 16:24] = g["g_final"].reshape(8, 128)
        vecs[1:4] = vecs_l[0]
        vecs[4:7] = vecs_l[1]
        m = dict(shared)
        m["xin"] = np.ascontiguousarray(np.concatenate([g["x"][b, half * NL:(half + 1) * NL], g["ctx"][b]], 0))
        m["vecs"] = vecs
        m["rope"] = ropes[half]
        m["dft"], m["dftc"] = tabs[half]
        mk = np.zeros((128, 2), f32)
        mk[:, 0] = 1.0 if half == 1 else 0.0
        mk[:, 1] = 1.0 if half == 0 else 0.0
        m["maskd"] = mk
        maps.append(m)
    return maps


def run(inputs, stop=99, cores=8, dbg=False, ret_all=False):
    nc = build(stop, dbg)
    maps = _prep(inputs)
    res = run_bass_kernel_spmd(nc, maps[:cores], core_ids=list(range(cores)))
    if ret_all:
        return res.results
    out = np.zeros((4, SEQ, D), np.float32)
    for c in range(cores):
        b, half = c // 2, c % 2
        out[b, half * NL:(half + 1) * NL] = res.results[c]["yout"]
    return out


def kernel(**inputs):
    return run(inputs)
```

```python
import numpy as np
import ml_dtypes
from contextlib import ExitStack
import concourse.bass as bass
import concourse.mybir as mybir
from concourse.bass_utils import run_bass_kernel_spmd

F32 = mybir.dt.float32
BF16 = mybir.dt.bfloat16
ALU = mybir.AluOpType
AF = mybir.ActivationFunctionType

D = 1024
KC = 8
NL = 2048
NCX = 256
NT = NL + NCX
SEQ = 4096
DFF = 2816
NJ = DFF // 128
DEPTH = 2
EPS = 1e-6
ATTN_SCALE = 96.0 ** -0.5
TBS = [(0, 512), (512, 512), (1024, 512), (1536, 512), (2048, 256)]
SAME_ENGINE_SYNC = True
FUSE_WAIT = True


class Res:
    __slots__ = ("w", "r", "name")

    def __init__(self, name=""):
        self.w = None
        self.r = {}
        self.name = name


class Eng:
    def __init__(self, name, sem):
        self.name, self.sem = name, sem
        self.count = 0
        self.seen = {}
        self.q = []


class DmaSem:
    def __init__(self, sem):
        self.sem = sem
        self.count = 0


class _Rec:
    def __init__(self):
        self.call = None

    def __getattr__(self, name):
        def f(*a, **k):
            self.call = (name, a, k)
            return self
        return f


class FW:
    def __init__(self, nc, stack, n_dma_sems=32):
        self.nc = nc
        mk = lambda n: stack.enter_context(nc.semaphore(n))
        self.pe = Eng("pe", mk("s_pe"))
        self.act = Eng("act", mk("s_act"))
        self.dve = Eng("dve", mk("s_dve"))
        self.pool = Eng("pool", mk("s_pool"))
        self.sp = Eng("sp", mk("s_sp"))
        self.dsems = [DmaSem(mk(f"s_dma{i}")) for i in range(n_dma_sems)]
        self.dnext = 0
        self.ccs = DmaSem(mk("s_cc"))

    def _wait(self, eng, sem, val):
        key = id(sem)
        if eng.seen.get(key, 0) >= val:
            return
        eng.q.append(lambda h, sem=sem, val=val: h.wait_ge(sem, val))
        eng.seen[key] = val

    def _need(self, eng, sem, val, pend):
        key = id(sem)
        if eng.seen.get(key, 0) >= val:
            return
        eng.seen[key] = val
        for i, (s2, v2) in enumerate(pend):
            if s2 is sem:
                pend[i] = (sem, max(val, v2))
                return
        pend.append((sem, val))

    def _flush(self, eng, pend):
        if not pend:
            return None
        for (sem, val) in pend[:-1]:
            eng.q.append(lambda h, sem=sem, val=val: h.wait_ge(sem, val))
        return pend[-1] if FUSE_WAIT else (eng.q.append(lambda h, sem=pend[-1][0], val=pend[-1][1]: h.wait_ge(sem, val)) or None)

    def _deps(self, eng, reads, writes):
        deps = []
        for r in reads:
            if r.w is not None:
                deps.append(r.w)
        for w in writes:
            if w.w is not None:
                deps.append(w.w)
            deps.extend(w.r.values())
        pend = []
        for (sem, val, src) in deps:
            if src is eng and (eng.name == "pe" or not SAME_ENGINE_SYNC):
                continue
            self._need(eng, sem, val, pend)
        return pend

    def _commit(self, ev, reads, writes):
        key = id(ev[0])
        for r in reads:
            old = r.r.get(key)
            if old is None or old[1] < ev[1]:
                r.r[key] = ev
        for w in writes:
            w.w = ev
            w.r = {}

    def op(self, eng, fn, reads=(), writes=(), sig=True):
        fz = self._flush(eng, self._deps(eng, reads, writes))
        rec = _Rec()
        fn(rec)
        name, a, k = rec.call

        def emit(h, name=name, a=a, k=k, fz=fz, sem=eng.sem, sig=sig):
            inst = getattr(h, name)(*a, **k)
            if fz is not None:
                inst._wait_ge(fz[0], fz[1])
            if sig:
                inst.then_inc(sem, 1)
        eng.q.append(emit)
        if sig:
            eng.count += 1
            ev = (eng.sem, eng.count, eng)
        else:
            ev = (eng.sem, eng.count + 1, eng)
        self._commit(ev, reads, writes)
        return ev

    def dma(self, q, out, in_, reads=(), writes=()):
        pend = self._deps(q, reads, writes)
        ds = self.dsems[self.dnext]
        self.dnext = (self.dnext + 1) % len(self.dsems)
        if ds.count:
            self._need(q, ds.sem, ds.count, pend)
        fz = self._flush(q, pend)
        ds.count += 16

        def emit(h, out=out, in_=in_, sem=ds.sem, fz=fz):
            inst = h.dma_start(out=out, in_=in_)
            if fz is not None:
                inst._wait_ge(fz[0], fz[1])
            inst.then_inc(sem, 16)
        q.q.append(emit)
        ev = (ds.sem, ds.count, None)
        self._commit(ev, reads, writes)
        return ev

    def allgather(self, src, dst, reads, writes):
        q = self.pool
        pend = self._deps(q, reads, writes)
        cs = self.ccs
        if cs.count:
            self._need(q, cs.sem, cs.count, pend)
        for (sem, val) in pend:
            q.q.append(lambda h, sem=sem, val=val: h.wait_ge(sem, val))
        cs.count += 1
        q.q.append(lambda h, src=src, dst=dst, sem=cs.sem: h.collective_compute(
            "AllGather", ALU.bypass, replica_groups=[[0, 1], [2, 3], [4, 5], [6, 7]],
            ins=[src], outs=[dst]).then_inc(sem, 1))
        ev = (cs.sem, cs.count, None)
        self._commit(ev, reads, writes)
        return ev

    def run(self):
        nc = self.nc
        with nc.Block() as block:
            @block.tensor
            def _(e):
                for f in self.pe.q:
                    f(e)

            @block.scalar
            def _(e):
                for f in self.act.q:
                    f(e)

            @block.vector
            def _(e):
                for f in self.dve.q:
                    f(e)

            @block.gpsimd
            def _(e):
                for f in self.pool.q:
                    f(e)

            @block.sync
            def _(e):
                for f in self.sp.q:
                    f(e)


def build(stop=99, dbg=False):
    nc = bass.Bass("TRN2", target_bir_lowering=False)
    di = lambda n, s, d=F32: nc.dram_tensor(n, list(s), d, kind="ExternalInput").ap()
    xin = di("xin", [NT, D])
    vecs = di("vecs", [7, 128, 128])
    rope = di("rope", [2, 32, NL])
    dft = di("dft", [4, 32, 128, 2, 512], BF16)
    dftc = di("dftc", [2, 128, 2, 256], BF16)
    ccsc = di("ccsc", [128, 2, 128])
    maskd = di("maskd", [128, 2])
    w_ada = di("w_ada", [DEPTH, D, 9 * D])
    w1a = di("w1_ffn1", [DEPTH, D, DFF]); w3a = di("w3_ffn1", [DEPTH, D, DFF]); w2a = di("w2_ffn1", [DEPTH, DFF, D])
    w1b = di("w1_ffn2", [DEPTH, D, DFF]); w3b = di("w3_ffn2", [DEPTH, D, DFF]); w2b = di("w2_ffn2", [DEPTH, DFF, D])
    w_in = di("w_in", [DEPTH, D, 1952])
    w_krp = di("w_krp", [DEPTH, D, 192])
    w_pw = di("w_pw_conv", [DEPTH, 384, D])
    w_uq = di("w_uq", [DEPTH, 384, 768])
    w_uqp = di("w_uqp", [DEPTH, 384, 768])
    w_ukv = di("w_ukv", [DEPTH, 256, 1024])
    w_o = di("w_o_mla", [DEPTH, 512, D])
    w_fo = di("w_fourier", [DEPTH, 512, D])
    w_bg = di("w_bgate", [DEPTH, D, 3 * D])
    w_out = di("w_out", [DEPTH, D, D])
    yout = nc.dram_tensor("yout", [NL, D], F32, kind="ExternalOutput").ap()

    dt_ = lambda n, s: nc.dram_tensor(n, list(s), BF16)
    XA = dt_("XA", [288, NT]); GA = dt_("GA", [576, NT])
    XG = dt_("XG", [NL, 512]); GG = dt_("GG", [2 * NL, 512]); XGc = dt_("XGc", [NCX, 512])
    XH = dt_("XH", [128, 90]); GH = dt_("GH", [256, 90])
    dd = (lambda n, s: nc.dram_tensor(n, list(s), BF16, kind="ExternalOutput")) if dbg else dt_
    Dcqn = dd("Dcqn", [3 * 128, NT]); Dsv = dd("Dsv", [3 * 128, NT])
    DF = dd("DF", [4 * 128, NT]); Dat = dd("Dat", [4 * 128, NT])

    with ExitStack() as st:
        fw = FW(nc, st)
        pe, act, dve, pool, sp = fw.pe, fw.act, fw.dve, fw.pool, fw.sp
        sb = lambda n, s, d: st.enter_context(nc.sbuf_tensor(n, list(s), d))

        hT = sb("hT", [128, KC, NT], F32)
        Rh = [Res(f"h{i}") for i in range(5)]
        UR = sb("UR", [128, KC * NT], BF16)
        uT = UR[:, :].rearrange("p (c t) -> p c t", c=KC)
        Ru = [Res(f"u{i}") for i in range(5)]
        AR = sb("AR", [128, KC * NT], BF16)
        RA = Res("A")
        RAB = [Res("A0"), Res("A1")]

        PSn = 8
        PSD = [st.enter_context(nc.psum_tensor(f"psd{i}", [128, 1024], F32)) for i in range(PSn // 2)]
        PS = [PSD[i // 2][:, (i % 2) * 512:(i % 2 + 1) * 512] for i in range(PSn)]
        RPS = [Res(f"ps{i}") for i in range(PSn)]
        psdi = [0]

        def psd():
            k = psdi[0] % 3
            psdi[0] += 1
            return PSD[k], [RPS[2 * k], RPS[2 * k + 1]]
        psi = [0]
        NROT = 6

        def ps():
            i = psi[0] % NROT
            psi[0] += 1
            return PS[i], RPS[i]

        T32 = sb("T32", [128, 8, 512], F32)
        RT32 = [Res(f"t32_{i}") for i in range(8)]
        t32i = [0]

        def t32():
            i = t32i[0] % 8
            t32i[0] += 1
            return T32[:, i, :], RT32[i]

        NT16 = 6
        T16 = sb("T16", [128, NT16, 512], BF16)
        RT16 = [Res(f"t16_{i}") for i in range(NT16)]
        t16i = [0]

        def t16():
            i = t16i[0] % NT16
            t16i[0] += 1
            return T16[:, i, :], RT16[i]

        NWS = 6
        WS = sb("WS", [128, NWS, 2048], BF16)
        RWS = [Res(f"ws{i}") for i in range(NWS)]
        wsi = [0]

        wslim = [NWS]

        def ws():
            i = wsi[0] % wslim[0]
            wsi[0] += 1
            return WS[:, i, :], RWS[i]

        PTd = [WS[:, 4 + k // 2, (k % 2) * 1024:(k % 2 + 1) * 1024] for k in range(4)]
        RPT = [Res(f"pt{k}") for k in range(4)]
        pti = [0]

        ident = sb("ident", [128, 128], F32)
        ones32 = sb("ones32", [128, 128], F32)
        cst = sb("cst", [128, 4], F32)
        VT = sb("VT", [128, 7, 128], F32)
        scb = sb("scb", [128, KC, 2], BF16)
        MOD = sb("MOD", [128, 72, 2], F32)
        DER = sb("DER", [128, 6, KC, 2], F32)
        CS32 = sb("CS32", [128, 2, 128], F32)
        MSK = sb("MSK", [128, 2], F32)
        TT2 = sb("TT2", [128, 2, 512], F32)
        RTT2 = [Res("tt2_0"), Res("tt2_1")]
        QT = sb("QT", [128, 2, 512], BF16)
        RQT = [Res("qt0"), Res("qt1")]
        vTc = sb("vTc", [128, 3, 286], BF16)
        HT = sb("HT", [128, 2, 90], BF16)
        Rc = {k: Res(k) for k in "ident ones cst cst2 VT scb MOD DER CS32 MSK vTc HT XA GA XG XGc GG XH GH Dcqn Dsv DF Dat".split()}

        def mm(out, lhsT, rhs, start, stop, reads, writes, sig=None):
            fw.op(pe, lambda h: h.matmul(out, lhsT=lhsT, rhs=rhs, start=start, stop=stop),
                  reads=reads, writes=writes, sig=(stop if sig is None else sig))

        def handover(srcs, dsts):
            fw.op(dve, lambda h: h.memset(cst[:, 2:3], 0.0), reads=list(srcs), writes=list(dsts) + [Rc["cst2"]])

        fw.op(pool, lambda h: h.memset(ident[:], 1.0), writes=[Rc["ident"]])
        fw.op(pool, lambda h: h.affine_select(out=ident[:], in_=ident[:], pattern=[[-1, 128]],
                                              compare_op=ALU.is_equal, fill=0.0, base=0, channel_multiplier=1),
              reads=[Rc["ident"]], writes=[Rc["ident"]])
        fw.op(pool, lambda h: h.memset(ones32[:], 1.0), writes=[Rc["ones"]])
        fw.op(pool, lambda h: h.memset(cst[:, 0:1], EPS), writes=[Rc["cst"]])
        fw.op(pool, lambda h: h.memset(cst[:, 1:2], 0.0), writes=[Rc["cst"]])
        fw.op(pool, lambda h: h.memset(vTc[:], 0.0), writes=[Rc["vTc"]])
        fw.dma(sp, CS32[:], ccsc[:, :, :], writes=[Rc["CS32"]])
        fw.dma(sp, MSK[:], maskd[:, :], writes=[Rc["MSK"]])
        for i in range(7):
            t, r = t32()
            fw.dma(sp, t[:, 0:128], vecs[i, :, :], writes=[r])
            p, pr = ps()
            fw.op(pe, lambda h, p=p, t=t: h.transpose(out=p[:, 0:128], in_=t[:, 0:128], identity=ident[:]),
                  reads=[r, Rc["ident"]], writes=[pr])
            fw.op(dve, lambda h, p=p, i=i: h.tensor_copy(out=VT[:, i, :], in_=p[:, 0:128]), reads=[pr], writes=[Rc["VT"]])
        VG = VT[:, 0, :]
        VA = lambda l: VT[:, 1 + 3 * l, :]
        VB = lambda l: VT[:, 2 + 3 * l, :]
        VC = lambda l: VT[:, 3 + 3 * l, :]
        for t in range(2):
            fw.op(act, lambda h, t=t: h.activation(out=scb[:, :, t], in_=VG[:, 8 * t:8 * t + 8], func=AF.Silu),
                  reads=[Rc["VT"]], writes=[Rc["scb"]])

        for ti in range(NT // 128):
            bi = min(ti // 4, 4)
            for half in range(2):
                t, r = t32()
                fw.dma(sp, t, xin[ti * 128:(ti + 1) * 128, half * 512:(half + 1) * 512], writes=[r])
                p, pr = ps()
                for kk in range(4):
                    fw.op(pe, lambda h, p=p, t=t, kk=kk: h.transpose(out=p[:, kk * 128:(kk + 1) * 128],
                                                                     in_=t[:, kk * 128:(kk + 1) * 128], identity=ident[:]),
                          reads=[r, Rc["ident"]], writes=[pr], sig=(kk == 3))
                eng = dve if half == 0 else act
                if half == 0:
                    fw.op(dve, lambda h, p=p, ti=ti, half=half: h.tensor_copy(
                        out=hT[:, half * 4:half * 4 + 4, ti * 128:(ti + 1) * 128],
                        in_=p[:, :].rearrange("p (c t) -> p c t", c=4)), reads=[pr], writes=[Rh[bi]])
                else:
                    fw.op(act, lambda h, p=p, ti=ti, half=half: h.copy(
                        out=hT[:, half * 4:half * 4 + 4, ti * 128:(ti + 1) * 128],
                        in_=p[:, :].rearrange("p (c t) -> p c t", c=4)), reads=[pr], writes=[Rh[bi]])

        def wload(src_ap, shape3):
            w, wr = ws()
            a, b = shape3
            v = w[:, 0:a * b].rearrange("p (a b) -> p a b", a=a)
            fw.dma(pool, v, src_ap, writes=[wr])
            return v, wr

        def kcview(wap, c0, n):
            return wap[:, c0:c0 + n].rearrange("(kc p) n -> p kc n", p=128)

        def rstd_from(pst, n, scale):
            rs, rsr = t32()
            fw.op(act, lambda h: h.activation(out=rs[:, :n], in_=pst[0][:, :n], func=AF.Sqrt, bias=cst[:, 0:1], scale=scale),
                  reads=[pst[1], Rc["cst"]], writes=[rsr])
            fw.op(dve, lambda h: h.reciprocal(out=rs[:, :n], in_=rs[:, :n]), reads=[rsr], writes=[rsr])
            return rs, rsr

        def mod_stage(l):
            pm, pmr = ps()
            for s in range(36):
                wv, wr = wload(kcview(w_ada[l], s * 256, 256), (KC, 256))
                for jj in range(2):
                    j = s * 2 + jj
                    for kc in range(KC):
                        mm(pm[:, j * 2:j * 2 + 2], wv[:, kc, jj * 128:(jj + 1) * 128], scb[:, kc, :], kc == 0, kc == KC - 1,
                           [wr, Rc["scb"]], [pmr])
            pmv = pm[:, 0:144].rearrange("p (j t) -> p j t", t=2)
            va = VA(l)
            for t in range(2):
                fw.op(dve, lambda h, t=t: h.tensor_tensor(out=MOD[:, :, t], in0=pmv[:, :, t], in1=va[:, 0:72], op=ALU.add),
                      reads=[pmr, Rc["VT"]], writes=[Rc["MOD"]])
                for i, (gcol, n) in enumerate([(72, 1), (80, 4), (88, 7)]):
                    fw.op(dve, lambda h, t=t, i=i, n=n: h.tensor_scalar(out=DER[:, i, :, t], in0=MOD[:, n * 8:(n + 1) * 8, t],
                                                                       scalar1=1.0, scalar2=None, op0=ALU.add),
                          reads=[Rc["MOD"]], writes=[Rc["DER"]])
                    fw.op(dve, lambda h, t=t, i=i, gcol=gcol: h.tensor_tensor(out=DER[:, i, :, t], in0=DER[:, i, :, t],
                                                                              in1=va[:, gcol:gcol + 8], op=ALU.mult),
                          reads=[Rc["DER"], Rc["VT"]], writes=[Rc["DER"]])
                for i, (n, f) in enumerate([(2, 0.5), (5, 1.0), (8, 0.5)]):
                    fw.op(dve, lambda h, t=t, i=i, n=n, f=f: h.tensor_scalar(out=DER[:, 3 + i, :, t], in0=MOD[:, n * 8:(n + 1) * 8, t],
                                                                             scalar1=f, scalar2=None, op0=ALU.mult),
                          reads=[Rc["MOD"]], writes=[Rc["DER"]])

        def norm_stage(idx, tbs):
            for bi, (t0, n) in tbs:
                ts = 0 if t0 < NL else 1
                pst, pstr = ps()
                for kc in range(KC):
                    sq, sqr = t32()
                    fw.op(act, lambda h, sq=sq, kc=kc: h.activation(out=sq[:, :n], in_=hT[:, kc, t0:t0 + n], func=AF.Square),
                          reads=[Rh[bi]], writes=[sqr])
                    mm(pst[:, :n], ones32[:], sq[:, :n], kc == 0, kc == KC - 1, [sqr, Rc["ones"]], [pstr])
                rs, rsr = rstd_from((pst, pstr), n, 1.0 / D)
                for kc in range(KC):
                    tt, ttr = TT2[:, kc % 2, :], RTT2[kc % 2]
                    fw.op(dve, lambda h, tt=tt, kc=kc: h.scalar_tensor_tensor(
                        out=tt[:, :n], in0=hT[:, kc, t0:t0 + n], scalar=DER[:, idx, kc, ts:ts + 1], in1=rs[:, :n],
                        op0=ALU.mult, op1=ALU.mult), reads=[Rh[bi], rsr, Rc["DER"]], writes=[ttr])
                    fw.op(act, lambda h, tt=tt, kc=kc: h.activation(
                        out=uT[:, kc, t0:t0 + n], in_=tt[:, :n], func=AF.Identity,
                        bias=MOD[:, 3 * idx * 8 + kc, ts:ts + 1], scale=1.0), reads=[ttr, Rc["MOD"]], writes=[Ru[bi]])

        def ffn_stage(l, w1, w3, w2, gidx, tbs):
            groups = [(0, 4), (4, 4), (8, 4), (12, 4), (16, 4), (20, 2)]
            for gi, (j0, nj) in enumerate(groups):
                half = gi % 2
                ab = AR[:, half * 4 * NT:(half + 1) * 4 * NT].rearrange("p (c t) -> p c t", c=4)
                abr = RAB[half]
                for sub in range(nj // 2):
                    c0 = (j0 + sub * 2) * 128
                    w1v, w1r = wload(kcview(w1[l], c0, 256), (KC, 256))
                    w3v, w3r = wload(kcview(w3[l], c0, 256), (KC, 256))
                    for jj in range(2):
                        ja = sub * 2 + jj
                        for bi, (t0, n) in tbs:
                            p1, p1r = ps()
                            p3, p3r = ps()
                            for kc in range(KC):
                                mm(p1[:, :n], w1v[:, kc, jj * 128:(jj + 1) * 128], uT[:, kc, t0:t0 + n], kc == 0, kc == KC - 1,
                                   [w1r, Ru[bi]], [p1r])
                            for kc in range(KC):
                                mm(p3[:, :n], w3v[:, kc, jj * 128:(jj + 1) * 128], uT[:, kc, t0:t0 + n], kc == 0, kc == KC - 1,
                                   [w3r, Ru[bi]], [p3r])
                            s, sr = t32()
                            fw.op(act, lambda h, s=s, p1=p1, n=n: h.activation(out=s[:, :n], in_=p1[:, :n], func=AF.Silu),
                                  reads=[p1r], writes=[sr])
                            fw.op(dve, lambda h, s=s, p3=p3, n=n, ja=ja, t0=t0, ab=ab: h.tensor_tensor(
                                out=ab[:, ja, t0:t0 + n], in0=s[:, :n], in1=p3[:, :n], op=ALU.mult),
                                reads=[sr, p3r], writes=[abr])
                w2v = []
                for sub in range(nj // 2):
                    r0 = (j0 + sub * 2) * 128
                    w2v.append(wload(w2[l, r0:r0 + 256, :].rearrange("(j p) n -> p j n", p=128), (2, D)))
                for bi, (t0, n) in tbs:
                    ts = 0 if t0 < NL else 1
                    for m in range(KC):
                        po, por = ps()
                        for ja in range(nj):
                            wv, wr = w2v[ja // 2]
                            mm(po[:, :n], wv[:, ja % 2, m * 128:(m + 1) * 128], ab[:, ja, t0:t0 + n], ja == 0, ja == nj - 1,
                               [wr, abr], [por])
                        fw.op(dve, lambda h, po=po, m=m, t0=t0, n=n, ts=ts: h.scalar_tensor_tensor(
                            out=hT[:, m, t0:t0 + n], in0=po[:, :n], scalar=DER[:, 3 + gidx, m, ts:ts + 1],
                            in1=hT[:, m, t0:t0 + n], op0=ALU.mult, op1=ALU.add),
                            reads=[por, Rh[bi], Rc["DER"]], writes=[Rh[bi]])

        def mixer_stage(l, last, mstop):
            tbs_all = list(enumerate(TBS))
            tbs_c = tbs_all if not last else tbs_all[:4]
            vb = VB(l)
            vc = VC(l)
            va = VA(l)
            vTl = AR[:, 0:3 * 2078].rearrange("p (c t) -> p c t", c=3)
            DG = AR[:, 6234:6234 + 93 * 128].rearrange("p (j m) -> p j m", j=93)

            wc = [wload(kcview(w_in[l], s * 256, 256), (KC, 256)) for s in range(3)]
            for bi, (t0, n) in tbs_c:
                for cc in range(3):
                    pa, par = ps()
                    pg, pgr = ps()
                    ca, cg = cc * 128, 384 + cc * 128
                    wa, war = wc[ca // 256]
                    wg, wgr = wc[cg // 256]
                    for kc in range(KC):
                        mm(pa[:, :n], wa[:, kc, ca % 256:ca % 256 + 128], uT[:, kc, t0:t0 + n], kc == 0, kc == KC - 1, [war, Ru[bi]], [par])
                    for kc in range(KC):
                        mm(pg[:, :n], wg[:, kc, cg % 256:cg % 256 + 128], uT[:, kc, t0:t0 + n], kc == 0, kc == KC - 1, [wgr, Ru[bi]], [pgr])
                    s, sr = t32()
                    fw.op(act, lambda h, s=s, pg=pg, n=n: h.activation(out=s[:, :n], in_=pg[:, :n], func=AF.Sigmoid), reads=[pgr], writes=[sr])
                    if t0 < NL:
                        fw.op(dve, lambda h, s=s, pa=pa, n=n, cc=cc, t0=t0: h.tensor_tensor(
                            out=vTl[:, cc, 15 + t0:15 + t0 + n], in0=s[:, :n], in1=pa[:, :n], op=ALU.mult), reads=[sr, par], writes=[RA])
                    else:
                        fw.op(dve, lambda h, s=s, pa=pa, n=n, cc=cc: h.tensor_tensor(
                            out=vTc[:, cc, 15:15 + n], in0=s[:, :n], in1=pa[:, :n], op=ALU.mult), reads=[sr, par], writes=[Rc["vTc"]])
            XHv = XH.ap().rearrange("p (c t) -> p c t", c=3)
            fw.dma(pool, XHv[:, :, 0:15], vTl[:, :, 15:30], reads=[RA], writes=[Rc["XH"]])
            fw.dma(pool, XHv[:, :, 15:30], vTl[:, :, 15 + NL - 15:15 + NL], reads=[RA], writes=[Rc["XH"]])

            wq = [wload(kcview(w_in[l], 768, 256), (KC, 256)), wload(kcview(w_in[l], 1024, 128), (KC, 128))]
            wkv = wload(kcview(w_in[l], 1152, 256), (KC, 256))
            wkr = wload(kcview(w_krp[l], 0, 192), (KC, 192))
            XAa = XA.ap()
            Dcq = Dcqn.ap()
            for bi, (t0, n) in tbs_all:
                lat = t0 < NL
                for (nch, wsel, gcol, dst, need) in ((3, "q", 33, Dcq, (lat or not last)), (2, "kv", 36, XAa, True)):
                    if not need:
                        continue
                    pcs = []
                    for cc in range(nch):
                        pq, pqr = ps()
                        if wsel == "q":
                            wv, wr = wq[0] if cc < 2 else wq[1]
                            col = (cc % 2) * 128 if cc < 2 else 0
                        else:
                            wv, wr = wkv
                            col = cc * 128
                        for kc in range(KC):
                            mm(pq[:, :n], wv[:, kc, col:col + 128], uT[:, kc, t0:t0 + n], kc == 0, kc == KC - 1, [wr, Ru[bi]], [pqr])
                        pcs.append((pq, pqr))
                    pst, pstr = ps()
                    for cc in range(nch):
                        sq, sqr = t32()
                        fw.op(act, lambda h, sq=sq, pq=pcs[cc][0], n=n: h.activation(out=sq[:, :n], in_=pq[:, :n], func=AF.Square),
                              reads=[pcs[cc][1]], writes=[sqr])
                        mm(pst[:, :n], ones32[:], sq[:, :n], cc == 0, cc == nch - 1, [sqr, Rc["ones"]], [pstr])
                    rs, rsr = rstd_from((pst, pstr), n, 1.0 / (128 * nch))
                    for cc in range(nch):
                        o16, o16r = t16()
                        fw.op(dve, lambda h, o16=o16, pq=pcs[cc][0], n=n, cc=cc, gcol=gcol, rs=rs: h.scalar_tensor_tensor(
                            out=o16[:, :n], in0=pq[:, :n], scalar=vb[:, gcol + cc:gcol + cc + 1], in1=rs[:, :n],
                            op0=ALU.mult, op1=ALU.mult), reads=[pcs[cc][1], rsr, Rc["VT"]], writes=[o16r])
                        fw.dma(pool, dst[cc * 128:(cc + 1) * 128, t0:t0 + n], o16[:, :n], reads=[o16r],
                               writes=[Rc["Dcqn"] if wsel == "q" else Rc["XA"]])
                pk, pkr = ps()
                pp, ppr = ps()
                for kc in range(KC):
                    mm(pk[0:96, :n], wkr[0][:, kc, 0:96], uT[:, kc, t0:t0 + n], kc == 0, kc == KC - 1, [wkr[1], Ru[bi]], [pkr])
                for kc in range(KC):
                    mm(pp[0:96, :n], wkr[0][:, kc, 96:192], uT[:, kc, t0:t0 + n], kc == 0, kc == KC - 1, [wkr[1], Ru[bi]], [ppr])
                o16, o16r = t16()
                if lat:
                    a1, a1r = t32()
                    a2, a2r = t32()
                    fw.dma(sp, a1[64:96, :], rope[0, :, t0:t0 + 512], writes=[a1r])
                    fw.dma(sp, a2[64:96, :], rope[1, :, t0:t0 + 512], writes=[a2r])
                    fw.op(dve, lambda h, a1=a1, pk=pk: h.tensor_tensor(out=a1[64:96, :], in0=pk[64:96, :], in1=a1[64:96, :], op=ALU.mult),
                          reads=[pkr, a1r], writes=[a1r])
                    fw.op(dve, lambda h, a2=a2, pp=pp: h.tensor_tensor(out=a2[64:96, :], in0=pp[64:96, :], in1=a2[64:96, :], op=ALU.mult),
                          reads=[ppr, a2r], writes=[a2r])
                    fw.op(dve, lambda h, a1=a1, a2=a2, o16=o16: h.tensor_tensor(out=o16[64:96, :], in0=a1[64:96, :], in1=a2[64:96, :], op=ALU.add),
                          reads=[a1r, a2r], writes=[o16r])
                else:
                    fw.op(dve, lambda h, o16=o16, pk=pk, n=n: h.tensor_copy(out=o16[64:96, :n], in_=pk[64:96, :n]), reads=[pkr], writes=[o16r])
                fw.dma(pool, XAa[256:288, t0:t0 + n], o16[64:96, :n], reads=[o16r], writes=[Rc["XA"]])

            wf = [wload(kcview(w_in[l], 1440 + s * 256, 256), (KC, 256)) for s in range(2)]
            XGa = XG.ap()
            ntt = 18 if not last else 16
            for tt in range(ntt):
                bi = min(tt // 4, 4)
                pgm, pgr = ps()
                for s in range(2):
                    for kc in range(KC):
                        mm(pgm[:, s * 256:(s + 1) * 256], uT[:, kc, tt * 128:(tt + 1) * 128], wf[s][0][:, kc, :], kc == 0, kc == KC - 1,
                           [wf[s][1], Ru[bi]], [pgr])
                o16, o16r = t16()
                if tt % 2 == 0:
                    fw.op(dve, lambda h, o16=o16, pgm=pgm: h.tensor_copy(out=o16[:, :], in_=pgm[:, :]), reads=[pgr], writes=[o16r])
                else:
                    fw.op(act, lambda h, o16=o16, pgm=pgm: h.copy(out=o16[:, :], in_=pgm[:, :]), reads=[pgr], writes=[o16r])
                if tt < 16:
                    fw.dma(pool, XGa[tt * 128:(tt + 1) * 128, :], o16[:, :], reads=[o16r], writes=[Rc["XG"]])
                else:
                    fw.dma(pool, XGc.ap()[(tt - 16) * 128:(tt - 15) * 128, :], o16[:, :], reads=[o16r], writes=[Rc["XGc"]])

            fw.allgather(XH.ap(), GH.ap(), [Rc["XH"]], [Rc["GH"]])
            fw.allgather(XA.ap(), GA.ap(), [Rc["XA"]], [Rc["GA"]])
            fw.allgather(XG.ap(), GG.ap(), [Rc["XG"]], [Rc["GG"]])
            if mstop <= 1:
                return

            GHa = GH.ap()
            fw.dma(sp, HT[:, :, :], GHa.rearrange("(r p) f -> p r f", p=128), reads=[Rc["GH"]], writes=[Rc["HT"]])
            HTv = HT[:, :, :].rearrange("p r (c t) -> p r c t", c=3)
            fw.op(dve, lambda h: h.tensor_scalar(out=vTl[:, :, 0:15], in0=HTv[:, 0, :, 15:30], scalar1=MSK[:, 0:1], scalar2=None, op0=ALU.mult),
                  reads=[Rc["HT"], Rc["MSK"]], writes=[RA])
            fw.op(dve, lambda h: h.tensor_scalar(out=vTl[:, :, 15 + NL:30 + NL], in0=HTv[:, 1, :, 0:15], scalar1=MSK[:, 1:2], scalar2=None, op0=ALU.mult),
                  reads=[Rc["HT"], Rc["MSK"]], writes=[RA])
            for idx in range(93):
                fw.op(dve, lambda h, idx=idx: h.tensor_scalar(out=DG[:, idx, :], in0=ident[:], scalar1=vc[:, idx:idx + 1], scalar2=None, op0=ALU.mult),
                      reads=[Rc["ident"], Rc["VT"]], writes=[RA])
            Dsva = Dsv.ap()
            for bi, (t0, n) in tbs_c:
                lat = t0 < NL
                cos_ = []
                for cc in range(3):
                    pc, pcr = ps()
                    for j in range(31):
                        rhs = vTl[:, cc, t0 + j:t0 + j + n] if lat else vTc[:, cc, j:j + n]
                        mm(pc[:, :n], DG[:, j * 3 + cc, :], rhs, j == 0, j == 30, [RA] if lat else [RA, Rc["vTc"]], [pcr])
                    co, cor = t32()
                    fw.op(act, lambda h, co=co, pc=pc, n=n, cc=cc: h.activation(out=co[:, :n], in_=pc[:, :n], func=AF.Identity,
                                                                              bias=vb[:, 24 + cc:25 + cc], scale=1.0),
                          reads=[pcr, Rc["VT"]], writes=[cor])
                    cos_.append((co, cor))
                pss, pssr = ps()
                psq, psqr = ps()
                for cc in range(3):
                    mm(pss[:, :n], ones32[:], cos_[cc][0][:, :n], cc == 0, cc == 2, [cos_[cc][1], Rc["ones"]], [pssr])
                for cc in range(3):
                    sq, sqr = t32()
                    fw.op(act, lambda h, sq=sq, co=cos_[cc][0], n=n: h.activation(out=sq[:, :n], in_=co[:, :n], func=AF.Square),
                          reads=[cos_[cc][1]], writes=[sqr])
                    mm(psq[:, :n], ones32[:], sq[:, :n], cc == 0, cc == 2, [sqr, Rc["ones"]], [psqr])
                mu, mur = t32()
                fw.op(dve, lambda h, mu=mu, pss=pss, n=n: h.tensor_scalar(out=mu[:, :n], in0=pss[:, :n], scalar1=1.0 / 384, scalar2=None, op0=ALU.mult),
                      reads=[pssr], writes=[mur])
                m2, m2r = t32()
                fw.op(dve, lambda h, mu=mu, m2=m2, n=n: h.tensor_tensor(out=m2[:, :n], in0=mu[:, :n], in1=mu[:, :n], op=ALU.mult), reads=[mur], writes=[m2r])
                fw.op(dve, lambda h, m2=m2, psq=psq, n=n: h.scalar_tensor_tensor(out=m2[:, :n], in0=psq[:, :n], scalar=1.0 / 384, in1=m2[:, :n],
                                                                               op0=ALU.mult, op1=ALU.subtract), reads=[psqr, m2r], writes=[m2r])
                fw.op(act, lambda h, m2=m2, n=n: h.activation(out=m2[:, :n], in_=m2[:, :n], func=AF.Sqrt, bias=cst[:, 0:1], scale=1.0),
                      reads=[m2r, Rc["cst"]], writes=[m2r])
                fw.op(dve, lambda h, m2=m2, n=n: h.reciprocal(out=m2[:, :n], in_=m2[:, :n]), reads=[m2r], writes=[m2r])
                for cc in range(3):
                    co, cor = cos_[cc]
                    fw.op(dve, lambda h, co=co, mu=mu, n=n: h.tensor_tensor(out=co[:, :n], in0=co[:, :n], in1=mu[:, :n], op=ALU.subtract),
                          reads=[cor, mur], writes=[cor])
                    fw.op(dve, lambda h, co=co, m2=m2, n=n: h.tensor_tensor(out=co[:, :n], in0=co[:, :n], in1=m2[:, :n], op=ALU.mult),
                          reads=[cor, m2r], writes=[cor])
                    o16, o16r = t16()
                    fw.op(act, lambda h, co=co, o16=o16, n=n, cc=cc: h.activation(out=o16[:, :n], in_=co[:, :n], func=AF.Silu,
                                                                                bias=vb[:, 30 + cc:31 + cc], scale=vb[:, 27 + cc:28 + cc]),
                          reads=[cor, Rc["VT"]], writes=[o16r])
                    fw.dma(pool, Dsva[cc * 128:(cc + 1) * 128, t0:t0 + n], o16[:, :n], reads=[o16r], writes=[Rc["Dsv"]])
            if mstop <= 2:
                return

            gfull = AR[:, 0:32 * 512].rearrange("p (t c) -> p t c", t=32)
            GGa = GG.ap()
            DFa = DF.ap()
            for r in range(2):
                fw.dma(sp, gfull[:, r * 16:(r + 1) * 16, :], GGa[r * NL:(r + 1) * NL, :].rearrange("(t p) c -> p t c", p=128),
                       reads=[Rc["GG"]], writes=[RA])

            def stage2(P, Pr, Q, Qr, n, dst):
                pS, pSr = t32()
                qS, qSr = t32()
                fw.op(dve, lambda h: h.tensor_copy(out=pS[:, :n], in_=P[:, :n]), reads=[Pr], writes=[pSr])
                fw.op(act, lambda h: h.copy(out=qS[:, :n], in_=Q[:, :n]), reads=[Qr], writes=[qSr])
                mm(P[:, :n], CS32[:, 0, :], pS[:, :n], True, False, [pSr, Rc["CS32"]], [Pr])
                mm(P[:, :n], CS32[:, 1, :], qS[:, :n], False, True, [qSr, Rc["CS32"]], [Pr])
                o16, o16r = t16()
                fw.op(act, lambda h: h.copy(out=o16[:, :n], in_=P[:, :n]), reads=[Pr], writes=[o16r])
                fw.dma(pool, dst, o16[:, :n], reads=[o16r], writes=[Rc["DF"]])

            for kb in range(4):
                for ti in range(32):
                    tab, tabr = ws()
                    tv = tab[:, 0:1024].rearrange("p (s k) -> p s k", s=2)
                    fw.dma(sp, tv, dft[kb, ti, :, :, :], writes=[tabr])
                    for gi in range(4):
                        mm(PS[gi][:, :], gfull[:, ti, gi * 128:(gi + 1) * 128], tv[:, 0, :], ti == 0, ti == 31, [RA, tabr], [RPS[gi]])
                        mm(PS[4 + gi][:, :], gfull[:, ti, gi * 128:(gi + 1) * 128], tv[:, 1, :], ti == 0, ti == 31, [RA, tabr], [RPS[4 + gi]],
                           sig=(True if gi == 3 else None))
                for gi in range(4):
                    stage2(PS[gi], RPS[gi], PS[4 + gi], RPS[4 + gi], 512, DFa[gi * 128:(gi + 1) * 128, kb * 512:(kb + 1) * 512])
            if not last:
                gc, gcr = ws()
                gcv = gc[:, 0:1024].rearrange("p (t c) -> p t c", t=2)
                fw.dma(sp, gcv, XGc.ap().rearrange("(t p) c -> p t c", p=128), reads=[Rc["XGc"]], writes=[gcr])
                tc_, tcr = ws()
                tcv = tc_[:, 0:1024].rearrange("p (t s k) -> p t s k", t=2, s=2)
                fw.dma(sp, tcv, dftc.rearrange("t p s k -> p t s k"), writes=[tcr])
                for gi in range(4):
                    P, Pr = PS[gi], RPS[gi]
                    Q, Qr = PS[4 + gi], RPS[4 + gi]
                    for tl in range(2):
                        mm(P[:, :256], gcv[:, tl, gi * 128:(gi + 1) * 128], tcv[:, tl, 0, :], tl == 0, tl == 1, [gcr, tcr], [Pr])
                    for tl in range(2):
                        mm(Q[:, :256], gcv[:, tl, gi * 128:(gi + 1) * 128], tcv[:, tl, 1, :], tl == 0, tl == 1, [gcr, tcr], [Qr])
                    stage2(P, Pr, Q, Qr, 256, DFa[gi * 128:(gi + 1) * 128, NL:NT])
            if mstop <= 3:
                return

            GAa = GA.ap()
            Data = Dat.ap()
            NK = SEQ + NCX
            KTb = [AR[:, b * 8704:b * 8704 + NK] for b in range(2)]
            VAb = [AR[:, b * 8704 + NK:(b + 1) * 8704].rearrange("p (t c) -> p t c", t=34) for b in range(2)]
            RKV = [Res("kv0"), Res("kv1")]
            fw.op(dve, lambda h: h.memset(VAb[0][:, :, 64:128], 1.0), reads=[RA], writes=[RA, RKV[0]])
            fw.op(dve, lambda h: h.memset(VAb[1][:, :, 0:64], 1.0), reads=[RA], writes=[RA, RKV[1]])
            qbs = tbs_c
            HW = {}
            LAG = 2

            def kv_build(hd):
                b = hd % 2
                voff = 0 if b == 0 else 64
                wsl, wslr = ws()
                wkvh = wsl[:, 0:256].rearrange("p (c n) -> p c n", c=2)
                wqh = wsl[:, 256:256 + 288].rearrange("p (c n) -> p c n", c=3)
                wqph = wsl[:, 544:544 + 288].rearrange("p (c n) -> p c n", c=3)
                fw.dma(pool, wkvh, w_ukv[l][:, hd * 128:(hd + 1) * 128].rearrange("(c p) n -> p c n", p=128), writes=[wslr])
                fw.dma(pool, wqh, w_uq[l][:, hd * 96:(hd + 1) * 96].rearrange("(c p) n -> p c n", p=128), writes=[wslr])
                fw.dma(pool, wqph, w_uqp[l][:, hd * 96:(hd + 1) * 96].rearrange("(c p) n -> p c n", p=128), writes=[wslr])
                HW[hd] = (wqh, wqph, wslr)
                for kb in range(9):
                    if kb < 8:
                        r, c0, n = kb // 4, (kb % 4) * 512, 512
                    else:
                        r, c0, n = 0, NL, NCX
                    kk0 = kb * 512
                    ck = []
                    for cc in range(2):
                        c16, c16r = t16()
                        fw.dma(sp, c16[:, :n], GAa[r * 288 + cc * 128:r * 288 + (cc + 1) * 128, c0:c0 + n], reads=[Rc["GA"]], writes=[c16r])
                        ck.append((c16, c16r))
                    fw.dma(sp, KTb[b][64:96, kk0:kk0 + n], GAa[r * 288 + 256:r * 288 + 288, c0:c0 + n], reads=[Rc["GA"]], writes=[RKV[b]])
                    pk, pkr = ps()
                    for cc in range(2):
                        mm(pk[0:64, :n], wkvh[:, cc, 0:64], ck[cc][0][:, :n], cc == 0, cc == 1, [wslr, ck[cc][1]], [pkr])
                    fw.op(dve, lambda h: h.tensor_copy(out=KTb[b][0:64, kk0:kk0 + n], in_=pk[0:64, :n]), reads=[pkr], writes=[RKV[b]])
                    pv, pvr = ps()
                    nt_ = n // 128
                    for tl in range(nt_):
                        for cc in range(2):
                            mm(pv[:, tl * 64:(tl + 1) * 64], ck[cc][0][:, tl * 128:(tl + 1) * 128], wkvh[:, cc, 64:128], cc == 0, cc == 1,
                               [wslr, ck[cc][1]], [pvr])
                    fw.op(dve, lambda h: h.tensor_copy(out=VAb[b][:, kb * 4:kb * 4 + nt_, voff:voff + 64],
                                                       in_=pv[:, 0:nt_ * 64].rearrange("p (t c) -> p t c", t=nt_)),
                          reads=[pvr], writes=[RKV[b]])

            def q_build(hd, qi):
                bi, (t0, n) = qbs[qi]
                wqh, wqph, wslr = HW[hd]
                lat = t0 < NL
                cq = []
                for cc in range(3):
                    c16, c16r = t16()
                    fw.dma(sp, c16[:, :n], Dcq[cc * 128:(cc + 1) * 128, t0:t0 + n], reads=[Rc["Dcqn"]], writes=[c16r])
                    cq.append((c16, c16r))
                pq, pqr = ps()
                for cc in range(3):
                    mm(pq[0:96, :n], wqh[:, cc, :], cq[cc][0][:, :n], cc == 0, cc == 2, [wslr, cq[cc][1]], [pqr])
                q16, q16r = QT[:, qti[0] % 2, :], RQT[qti[0] % 2]
                qti[0] += 1
                if lat:
                    pp, ppr = ps()
                    for cc in range(3):
                        mm(pp[0:96, :n], wqph[:, cc, :], cq[cc][0][:, :n], cc == 0, cc == 2, [wslr, cq[cc][1]], [ppr])
                    fw.op(dve, lambda h: h.tensor_copy(out=q16[0:64, :], in_=pq[0:64, :]), reads=[pqr], writes=[q16r])
                    a1, a1r = t32()
                    a2, a2r = t32()
                    fw.dma(sp, a1[64:96, :], rope[0, :, t0:t0 + 512], writes=[a1r])
                    fw.dma(sp, a2[64:96, :], rope[1, :, t0:t0 + 512], writes=[a2r])
                    fw.op(dve, lambda h: h.tensor_tensor(out=a1[64:96, :], in0=pq[64:96, :], in1=a1[64:96, :], op=ALU.mult),
                          reads=[pqr, a1r], writes=[a1r])
                    fw.op(dve, lambda h: h.tensor_tensor(out=a2[64:96, :], in0=pp[64:96, :], in1=a2[64:96, :], op=ALU.mult),
                          reads=[ppr, a2r], writes=[a2r])
                    fw.op(dve, lambda h: h.tensor_tensor(out=q16[64:96, :], in0=a1[64:96, :], in1=a2[64:96, :], op=ALU.add),
                          reads=[a1r, a2r], writes=[q16r])
                    kts = list(range(34))
                else:
                    fw.op(dve, lambda h: h.tensor_copy(out=q16[0:96, :n], in_=pq[0:96, :n]), reads=[pqr], writes=[q16r])
                    kts = [32, 33]
                return (q16, q16r, kts, t0, n)

            def scores(hd, qinfo):
                q16, q16r, kts, t0, n = qinfo
                b = hd % 2
                ob = 6 + (obi[0] % 2)
                obi[0] += 1
                po, por = PS[ob], RPS[ob]
                npair = len(kts) // 2
                pts = {}
                for i in range(npair + LAG):
                    if i < npair:
                        p2, p2r = psd()
                        for a in range(2):
                            kt = kts[2 * i + a]
                            mm(p2[:, a * 512:a * 512 + n], KTb[b][0:96, kt * 128:(kt + 1) * 128], q16[0:96, :n], True, True,
                               [RKV[b], q16r], [p2r[a]])
                        k = pti[0] % 4
                        pti[0] += 1
                        pt2, pt2r = PTd[k], RPT[k]
                        if n == 512:
                            fw.op(act, lambda h: h.activation(out=pt2[:, :], in_=p2[:, :], func=AF.Exp, scale=ATTN_SCALE),
                                  reads=p2r, writes=[pt2r])
                        else:
                            fw.op(act, lambda h: h.activation(out=pt2[:, :].rearrange("p (a c) -> p a c", a=2)[:, :, :n],
                                                              in_=p2[:, :].rearrange("p (a c) -> p a c", a=2)[:, :, :n],
                                                              func=AF.Exp, scale=ATTN_SCALE), reads=p2r, writes=[pt2r])
                        pts[i] = (pt2, pt2r)
                    j = i - LAG
                    if j >= 0:
                        pt2, pt2r = pts.pop(j)
                        for a in range(2):
                            mm(po[:, :n], VAb[b][:, kts[2 * j + a], :], pt2[:, a * 512:a * 512 + n], j == 0 and a == 0,
                               j == npair - 1 and a == 1, [RKV[b], pt2r], [por], sig=(True if a == 1 else None))
                orow = slice(0, 64) if b == 0 else slice(64, 128)
                drow = slice(64, 128) if b == 0 else slice(0, 64)
                rd, rdr = t32()
                fw.op(dve, lambda h: h.reciprocal(out=rd[drow, :n], in_=po[drow, :n]), reads=[por], writes=[rdr])
                rsh, rshr = t32()
                fw.op(dve, lambda h: h.tensor_copy(out=rsh[orow, :n], in_=rd[drow, :n]), reads=[rdr], writes=[rshr])
                ao, aor = t16()
                fw.op(dve, lambda h: h.tensor_tensor(out=ao[orow, :n], in0=po[orow, :n], in1=rsh[orow, :n], op=ALU.mult),
                      reads=[por, rshr], writes=[aor])
                r0 = (hd // 2) * 128 + (0 if b == 0 else 64)
                fw.dma(pool, Data[r0:r0 + 64, t0:t0 + n], ao[orow, :n], reads=[aor], writes=[Rc["Dat"]])

            wslim[0] = 4
            handover([RWS[4], RWS[5]], RPT)
            kv_build(0)
            for hd in range(8):
                qnext = q_build(hd, 0)
                if hd < 7:
                    kv_build(hd + 1)
                for qi in range(len(qbs)):
                    qcur = qnext
                    if qi + 1 < len(qbs):
                        qnext = q_build(hd, qi + 1)
                    scores(hd, qcur)
            handover(RPT, [RWS[4], RWS[5]])
            wslim[0] = NWS
            if mstop <= 4:
                return

            mixacc = AR[:, :].rearrange("p (c t) -> p c t", c=KC)
            RM = Res("mix")
            handover([RKV[0], RKV[1], RA], [RM, RA, RKV[0], RKV[1]])
            branches = [(Dsv.ap(), Rc["Dsv"], 3, w_pw, 96), (Dat.ap(), Rc["Dat"], 4, w_o, None), (DFa, Rc["DF"], 4, w_fo, 104)]
            first = True
            for r, (Dsrc, Dres, nch, wsrc, bcol) in enumerate(branches):
                wbg = [wload(kcview(w_bg[l], r * D + s * 256, 256), (KC, 256)) for s in range(4)]
                wr_ = [wload(wsrc[l].rearrange("(c p) n -> p c n", p=128)[:, :, s * 512:(s + 1) * 512], (nch, 512)) for s in range(2)]
                for bi, (t0, n) in tbs_c:
                    xin_ = []
                    for c in range(nch):
                        c16, c16r = t16()
                        fw.dma(sp, c16[:, :n], Dsrc[c * 128:(c + 1) * 128, t0:t0 + n], reads=[Dres], writes=[c16r])
                        xin_.append((c16, c16r))
                    for m in range(KC):
                        pgt, pgtr = ps()
                        wv, wr = wbg[m // 2]
                        for kc in range(KC):
                            mm(pgt[:, :n], wv[:, kc, (m % 2) * 128:(m % 2) * 128 + 128], uT[:, kc, t0:t0 + n], kc == 0, kc == KC - 1, [wr, Ru[bi]], [pgtr])
                        s, sr = t32()
                        fw.op(act, lambda h, s=s, pgt=pgt, n=n, r=r, m=m: h.activation(out=s[:, :n], in_=pgt[:, :n], func=AF.Sigmoid,
                                                                                     bias=vb[:, r * 8 + m:r * 8 + m + 1], scale=1.0),
                              reads=[pgtr, Rc["VT"]], writes=[sr])
                        py, pyr = ps()
                        wv2, wr2 = wr_[m // 4]
                        for c in range(nch):
                            mm(py[:, :n], wv2[:, c, (m % 4) * 128:(m % 4) * 128 + 128], xin_[c][0][:, :n], c == 0, c == nch - 1, [wr2, xin_[c][1]], [pyr])
                        bias = va[:, bcol + m:bcol + m + 1] if bcol is not None else 0.0
                        if first:
                            fw.op(dve, lambda h, py=py, s=s, n=n, m=m, t0=t0, bias=bias: h.scalar_tensor_tensor(
                                out=mixacc[:, m, t0:t0 + n], in0=py[:, :n], scalar=bias, in1=s[:, :n], op0=ALU.add, op1=ALU.mult),
                                reads=[pyr, sr, Rc["VT"]], writes=[RM])
                        else:
                            fw.op(dve, lambda h, py=py, s=s, n=n, bias=bias: h.scalar_tensor_tensor(
                                out=s[:, :n], in0=py[:, :n], scalar=bias, in1=s[:, :n], op0=ALU.add, op1=ALU.mult),
                                reads=[pyr, sr, Rc["VT"]], writes=[sr])
                            fw.op(dve, lambda h, s=s, n=n, m=m, t0=t0: h.tensor_tensor(
                                out=mixacc[:, m, t0:t0 + n], in0=mixacc[:, m, t0:t0 + n], in1=s[:, :n], op=ALU.add),
                                reads=[sr, RM], writes=[RM])
                first = False
            for mo in range(KC):
                wv, wr = wload(kcview(w_out[l], mo * 128, 128), (KC, 128))
                for bi, (t0, n) in tbs_c:
                    ts = 0 if t0 < NL else 1
                    po, por = ps()
                    for m in range(KC):
                        mm(po[:, :n], wv[:, m, :], mixacc[:, m, t0:t0 + n], m == 0, m == KC - 1, [wr, RM], [por])
                    fw.op(dve, lambda h, po=po, mo=mo, t0=t0, n=n, ts=ts: h.scalar_tensor_tensor(
                        out=hT[:, mo, t0:t0 + n], in0=po[:, :n], scalar=DER[:, 4, mo, ts:ts + 1],
                        in1=hT[:, mo, t0:t0 + n], op0=ALU.mult, op1=ALU.add),
                        reads=[por, Rh[bi], Rc["DER"]], writes=[Rh[bi]])
            handover([RM, RKV[0], RKV[1], RA], [RA, RAB[0], RAB[1]])

        obi = [0]
        qti = [0]
        tbs_all = list(enumerate(TBS))
        stage = 0
        done = False
        for l in range(DEPTH):
            last = l == DEPTH - 1
            mod_stage(l)
            norm_stage(0, tbs_all)
            ffn_stage(l, w1a, w3a, w2a, 0, tbs_all)
            stage += 1
            if stage >= stop:
                done = True
                break
            norm_stage(1, tbs_all)
            handover([RAB[0], RAB[1], RA], [RA, RAB[0], RAB[1]])
            ms = (stop - stage) if (stop - stage) < 6 else 99
            mixer_stage(l, last, ms)
            stage += 5
            if stage >= stop:
                done = True
                break
            tb2 = tbs_all if not last else tbs_all[:4]
            norm_stage(2, tb2)
            ffn_stage(l, w1b, w3b, w2b, 2, tb2)
            stage += 1
            if stage >= stop:
                done = True
                break

        vg = VT[:, 0, :]
        for bi, (t0, n) in tbs_all[:4]:
            if not done:
                pst, pstr = ps()
                for kc in range(KC):
                    sq, sqr = t32()
                    fw.op(act, lambda h, sq=sq, kc=kc: h.activation(out=sq[:, :n], in_=hT[:, kc, t0:t0 + n], func=AF.Square),
                          reads=[Rh[bi]], writes=[sqr])
                    mm(pst[:, :n], ones32[:], sq[:, :n], kc == 0, kc == KC - 1, [sqr, Rc["ones"]], [pstr])
                rs, rsr = rstd_from((pst, pstr), n, 1.0 / D)
                for kc in range(KC):
                    fw.op(dve, lambda h, kc=kc, rs=rs: h.scalar_tensor_tensor(
                        out=hT[:, kc, t0:t0 + n], in0=hT[:, kc, t0:t0 + n], scalar=vg[:, 16 + kc:17 + kc], in1=rs[:, :n],
                        op0=ALU.mult, op1=ALU.mult), reads=[Rh[bi], rsr, Rc["VT"]], writes=[Rh[bi]])
            for tl in range(4):
                tt = (t0 // 128) + tl
                for half in range(2):
                    p, pr = ps()
                    for kk in range(4):
                        kc = half * 4 + kk
                        fw.op(pe, lambda h, p=p, kk=kk, kc=kc, tt=tt: h.transpose(out=p[:, kk * 128:(kk + 1) * 128],
                                                                                 in_=hT[:, kc, tt * 128:(tt + 1) * 128], identity=ident[:]),
                              reads=[Rh[bi], Rc["ident"]], writes=[pr], sig=(kk == 3))
                    o, orr = t32()
                    if half == 0:
                        fw.op(dve, lambda h, o=o, p=p: h.tensor_copy(out=o[:, :], in_=p[:, :]), reads=[pr], writes=[orr])
                    else:
                        fw.op(act, lambda h, o=o, p=p: h.copy(out=o[:, :], in_=p[:, :]), reads=[pr], writes=[orr])
                    fw.dma(sp, yout[tt * 128:(tt + 1) * 128, half * 512:(half + 1) * 512], o[:, :], reads=[orr], writes=[Res()])
        for ds in fw.dsems:
            if ds.count:
                fw._wait(sp, ds.sem, ds.count)
        fw.run()
    return nc


def _tables(half):
    bf = ml_dtypes.bfloat16
    t = np.arange(SEQ, dtype=np.int64)
    k = np.arange(NL, dtype=np.int64) + half * NL
    ang = 2.0 * np.pi * ((t[:, None] * k[None, :]) % SEQ).astype(np.float64) / SEQ
    tab = np.stack([np.cos(ang) / 64.0, -np.sin(ang) / 64.0], axis=1)
    tab = tab.reshape(32, 128, 2, 4, 512).transpose(3, 0, 1, 2, 4)
    dft = np.ascontiguousarray(tab).astype(bf)
    tc = np.arange(NCX, dtype=np.int64)
    angc = 2.0 * np.pi * ((tc[:, None] * tc[None, :]) % NCX).astype(np.float64) / NCX
    tabc = np.stack([np.cos(angc) / 16.0, -np.sin(angc) / 16.0], axis=1).reshape(2, 128, 2, NCX)
    dftc = np.ascontiguousarray(tabc).astype(bf)
    return dft, dftc


def _rope(half):
    tok = np.arange(NL) + half * NL
    row = (tok // 64).astype(np.float32)
    col = (tok % 64).astype(np.float32)
    inv = (1.0 / (np.float32(10000.0) ** (np.arange(8, dtype=np.float32) * np.float32(2.0) / np.float32(16)))).astype(np.float32)
    ar = row[:, None] * inv
    ac = col[:, None] * inv
    ang = np.concatenate([ar, ar, ac, ac], axis=-1).astype(np.float32)
    cos = np.cos(ang).astype(np.float32).T
    sin = np.sin(ang).astype(np.float32).T
    sign = np.ones(32, np.float32)
    sign[0:8] = -1.0
    sign[16:24] = -1.0
    return np.ascontiguousarray(np.stack([cos, sin * sign[:, None]], 0)).astype(np.float32)


_PERM = np.array([(f + 8) if (f % 16) < 8 else (f - 8) for f in range(32)])


def _prep(inputs):
    f32 = np.float32
    g = {k: np.asarray(v, dtype=f32) for k, v in inputs.items()}
    L = DEPTH
    vecs_l = np.zeros((L, 3, 128, 128), f32)
    for l in range(L):
        A = vecs_l[l, 0]
        A[0:72] = g["b_ada"][l].reshape(72, 128)
        A[72:80] = g["g_ffn1"][l].reshape(8, 128)
        A[80:88] = g["g_mix"][l].reshape(8, 128)
        A[88:96] = g["g_ffn2"][l].reshape(8, 128)
        A[96:104] = g["b_pw_conv"][l].reshape(8, 128)
        A[104:112] = g["b_fourier"][l].reshape(8, 128)
        B = vecs_l[l, 1]
        B[0:24] = g["b_bgate"][l].reshape(24, 128)
        B[24:27] = g["b_dw"][l].reshape(3, 128)
        B[27:30] = g["ln_g_conv"][l].reshape(3, 128)
        B[30:33] = g["ln_b_conv"][l].reshape(3, 128)
        B[33:36] = g["g_qnorm"][l].reshape(3, 128)
        B[36:38] = g["g_kvnorm"][l].reshape(2, 128)
        vecs_l[l, 2, 0:93] = g["w_dw"][l].reshape(31 * 3, 128)
    w_krp = np.zeros((L, D, 192), f32)
    w_krp[:, :, 64:96] = g["w_in"][:, :, 1408:1440]
    w_krp[:, :, 160:192] = g["w_in"][:, :, 1408 + _PERM]
    w_uqp = np.zeros((L, 384, 768), f32)
    for hd in range(8):
        w_uqp[:, :, hd * 96 + 64:hd * 96 + 96] = g["w_uq"][:, :, hd * 96 + 64 + _PERM]
    cm = np.arange(128)
    angc = 2.0 * np.pi * ((cm[:, None] * cm[None, :]) % 128).astype(np.float64) / 128.0
    ccsc = np.stack([np.cos(angc), np.sin(angc)], axis=1) / np.sqrt(128.0)
    ccsc = np.ascontiguousarray(ccsc).astype(f32)
    shared = {k: np.ascontiguousarray(g[k]) for k in
              ("w_ada", "w1_ffn1", "w3_ffn1", "w2_ffn1", "w1_ffn2", "w3_ffn2", "w2_ffn2", "w_in", "w_pw_conv",
               "w_uq", "w_ukv", "w_o_mla", "w_fourier", "w_bgate", "w_out")}
    shared["w_krp"] = w_krp
    shared["w_uqp"] = w_uqp
    shared["ccsc"] = ccsc
    tabs = [_tables(0), _tables(1)]
    ropes = [_rope(0), _rope(1)]
    maps = []
    for c in range(8):
        b, half = c // 2, c % 2
        vecs = np.zeros((7, 128, 128), f32)
        vecs[0, 0:8] = g["c"][b].reshape(8, 128)
        vecs[0, 8:16] = g["c_ctx"].reshape(8, 128)
        vecs[0, 16:24] = g["g_final"].reshape(8, 128)
        vecs[1:4] = vecs_l[0]
        vecs[4:7] = vecs_l[1]
        m = dict(shared)
        m["xin"] = np.ascontiguousarray(np.concatenate([g["x"][b, half * NL:(half + 1) * NL], g["ctx"][b]], 0))
        m["vecs"] = vecs
        m["rope"] = ropes[half]
        m["dft"], m["dftc"] = tabs[half]
        mk = np.zeros((128, 2), f32)
        mk[:, 0] = 1.0 if half == 1 else 0.0
        mk[:, 1] = 1.0 if half == 0 else 0.0
        m["maskd"] = mk
        maps.append(m)
    return maps


def run(inputs, stop=99, cores=8, dbg=False, ret_all=False):
    nc = build(stop, dbg)
    maps = _prep(inputs)
    res = run_bass_kernel_spmd(nc, maps[:cores], core_ids=list(range(cores)))
    if ret_all:
        return res.results
    out = np.zeros((4, SEQ, D), np.float32)
    for c in range(cores):
        b, half = c // 2, c % 2
        out[b, half * NL:(half + 1) * NL] = res.results[c]["yout"]
    return out


def kernel(**inputs):
    return run(inputs)
```

```python
import numpy as np
import ml_dtypes
from contextlib import ExitStack
import concourse.bass as bass
import concourse.mybir as mybir
from concourse.bass_utils import run_bass_kernel_spmd

F32 = mybir.dt.float32
BF16 = mybir.dt.bfloat16
ALU = mybir.AluOpType
AF = mybir.ActivationFunctionType

D = 1024
KC = 8
NL = 2048
NCX = 256
NT = NL + NCX
SEQ = 4096
DFF = 2816
NJ = DFF // 128
DEPTH = 2
EPS = 1e-6
ATTN_SCALE = 96.0 ** -0.5
TBS = [(0, 512), (512, 512), (1024, 512), (1536, 512), (2048, 256)]
SAME_ENGINE_SYNC = True
FUSE_WAIT = True


class Res:
    __slots__ = ("w", "r", "name")

    def __init__(self, name=""):
        self.w = None
        self.r = {}
        self.name = name


class Eng:
    def __init__(self, name, sem):
        self.name, self.sem = name, sem
        self.count = 0
        self.seen = {}
        self.q = []


class DmaSem:
    def __init__(self, sem):
        self.sem = sem
        self.count = 0


class _Rec:
    def __init__(self):
        self.call = None

    def __getattr__(self, name):
        def f(*a, **k):
            self.call = (name, a, k)
            return self
        return f


class FW:
    def __init__(self, nc, stack, n_dma_sems=32):
        self.nc = nc
        mk = lambda n: stack.enter_context(nc.semaphore(n))
        self.pe = Eng("pe", mk("s_pe"))
        self.act = Eng("act", mk("s_act"))
        self.dve = Eng("dve", mk("s_dve"))
        self.pool = Eng("pool", mk("s_pool"))
        self.sp = Eng("sp", mk("s_sp"))
        self.dsems = [DmaSem(mk(f"s_dma{i}")) for i in range(n_dma_sems)]
        self.dnext = 0
        self.ccs = DmaSem(mk("s_cc"))

    def _wait(self, eng, sem, val):
        key = id(sem)
        if eng.seen.get(key, 0) >= val:
            return
        eng.q.append(lambda h, sem=sem, val=val: h.wait_ge(sem, val))
        eng.seen[key] = val

    def _need(self, eng, sem, val, pend):
        key = id(sem)
        if eng.seen.get(key, 0) >= val:
            return
        eng.seen[key] = val
        for i, (s2, v2) in enumerate(pend):
            if s2 is sem:
                pend[i] = (sem, max(val, v2))
                return
        pend.append((sem, val))

    def _flush(self, eng, pend):
        if not pend:
            return None
        for (sem, val) in pend[:-1]:
            eng.q.append(lambda h, sem=sem, val=val: h.wait_ge(sem, val))
        return pend[-1] if FUSE_WAIT else (eng.q.append(lambda h, sem=pend[-1][0], val=pend[-1][1]: h.wait_ge(sem, val)) or None)

    def _deps(self, eng, reads, writes):
        deps = []
        for r in reads:
            if r.w is not None:
                deps.append(r.w)
        for w in writes:
            if w.w is not None:
                deps.append(w.w)
            deps.extend(w.r.values())
        pend = []
        for (sem, val, src) in deps:
            if src is eng and (eng.name == "pe" or not SAME_ENGINE_SYNC):
                continue
            self._need(eng, sem, val, pend)
        return pend

    def _commit(self, ev, reads, writes):
        key = id(ev[0])
        for r in reads:
            old = r.r.get(key)
            if old is None or old[1] < ev[1]:
                r.r[key] = ev
        for w in writes:
            w.w = ev
            w.r = {}

    def op(self, eng, fn, reads=(), writes=(), sig=True):
        fz = self._flush(eng, self._deps(eng, reads, writes))
        rec = _Rec()
        fn(rec)
        name, a, k = rec.call

        def emit(h, name=name, a=a, k=k, fz=fz, sem=eng.sem, sig=sig):
            inst = getattr(h, name)(*a, **k)
            if fz is not None:
                inst._wait_ge(fz[0], fz[1])
            if sig:
                inst.then_inc(sem, 1)
        eng.q.append(emit)
        if sig:
            eng.count += 1
            ev = (eng.sem, eng.count, eng)
        else:
            ev = (eng.sem, eng.count + 1, eng)
        self._commit(ev, reads, writes)
        return ev

    def dma(self, q, out, in_, reads=(), writes=()):
        pend = self._deps(q, reads, writes)
        ds = self.dsems[self.dnext]
        self.dnext = (self.dnext + 1) % len(self.dsems)
        if ds.count:
            self._need(q, ds.sem, ds.count, pend)
        fz = self._flush(q, pend)
        ds.count += 16

        def emit(h, out=out, in_=in_, sem=ds.sem, fz=fz):
            inst = h.dma_start(out=out, in_=in_)
            if fz is not None:
                inst._wait_ge(fz[0], fz[1])
            inst.then_inc(sem, 16)
        q.q.append(emit)
        ev = (ds.sem, ds.count, None)
        self._commit(ev, reads, writes)
        return ev

    def allgather(self, src, dst, reads, writes):
        q = self.pool
        pend = self._deps(q, reads, writes)
        cs = self.ccs
        if cs.count:
            self._need(q, cs.sem, cs.count, pend)
        for (sem, val) in pend:
            q.q.append(lambda h, sem=sem, val=val: h.wait_ge(sem, val))
        cs.count += 1
        q.q.append(lambda h, src=src, dst=dst, sem=cs.sem: h.collective_compute(
            "AllGather", ALU.bypass, replica_groups=[[0, 1], [2, 3], [4, 5], [6, 7]],
            ins=[src], outs=[dst]).then_inc(sem, 1))
        ev = (cs.sem, cs.count, None)
        self._commit(ev, reads, writes)
        return ev

    def run(self):
        nc = self.nc
        with nc.Block() as block:
            @block.tensor
            def _(e):
                for f in self.pe.q:
                    f(e)

            @block.scalar
            def _(e):
                for f in self.act.q:
                    f(e)

            @block.vector
            def _(e):
                for f in self.dve.q:
                    f(e)

            @block.gpsimd
            def _(e):
                for f in self.pool.q:
                    f(e)

            @block.sync
            def _(e):
                for f in self.sp.q:
                    f(e)


def build(stop=99, dbg=False):
    nc = bass.Bass("TRN2", target_bir_lowering=False)
    di = lambda n, s, d=F32: nc.dram_tensor(n, list(s), d, kind="ExternalInput").ap()
    xin = di("xin", [NT, D])
    vecs = di("vecs", [7, 128, 128])
    rope = di("rope", [2, 32, NL])
    dft = di("dft", [4, 32, 128, 2, 512], BF16)
    dftc = di("dftc", [2, 128, 2, 256], BF16)
    ccsc = di("ccsc", [128, 2, 128])
    maskd = di("maskd", [128, 2])
    w_ada = di("w_ada", [DEPTH, D, 9 * D])
    w1a = di("w1_ffn1", [DEPTH, D, DFF]); w3a = di("w3_ffn1", [DEPTH, D, DFF]); w2a = di("w2_ffn1", [DEPTH, DFF, D])
    w1b = di("w1_ffn2", [DEPTH, D, DFF]); w3b = di("w3_ffn2", [DEPTH, D, DFF]); w2b = di("w2_ffn2", [DEPTH, DFF, D])
    w_in = di("w_in", [DEPTH, D, 1952])
    w_krp = di("w_krp", [DEPTH, D, 192])
    w_pw = di("w_pw_conv", [DEPTH, 384, D])
    w_uq = di("w_uq", [DEPTH, 384, 768])
    w_uqp = di("w_uqp", [DEPTH, 384, 768])
    w_ukv = di("w_ukv", [DEPTH, 256, 1024])
    w_o = di("w_o_mla", [DEPTH, 512, D])
    w_fo = di("w_fourier", [DEPTH, 512, D])
    w_bg = di("w_bgate", [DEPTH, D, 3 * D])
    w_out = di("w_out", [DEPTH, D, D])
    yout = nc.dram_tensor("yout", [NL, D], F32, kind="ExternalOutput").ap()

    dt_ = lambda n, s: nc.dram_tensor(n, list(s), BF16)
    XA = dt_("XA", [288, NT]); GA = dt_("GA", [576, NT])
    XG = dt_("XG", [NL, 512]); GG = dt_("GG", [2 * NL, 512]); XGc = dt_("XGc", [NCX, 512])
    XH = dt_("XH", [128, 90]); GH = dt_("GH", [256, 90])
    dd = (lambda n, s: nc.dram_tensor(n, list(s), BF16, kind="ExternalOutput")) if dbg else dt_
    Dcqn = dd("Dcqn", [3 * 128, NT]); Dsv = dd("Dsv", [3 * 128, NT])
    DF = dd("DF", [4 * 128, NT]); Dat = dd("Dat", [4 * 128, NT])

    with ExitStack() as st:
        fw = FW(nc, st)
        pe, act, dve, pool, sp = fw.pe, fw.act, fw.dve, fw.pool, fw.sp
        sb = lambda n, s, d: st.enter_context(nc.sbuf_tensor(n, list(s), d))

        hT = sb("hT", [128, KC, NT], F32)
        Rh = [Res(f"h{i}") for i in range(5)]
        UR = sb("UR", [128, KC * NT], BF16)
        uT = UR[:, :].rearrange("p (c t) -> p c t", c=KC)
        Ru = [Res(f"u{i}") for i in range(5)]
        AR = sb("AR", [128, KC * NT], BF16)
        RA = Res("A")
        RAB = [Res("A0"), Res("A1")]

        PSn = 8
        PSD = [st.enter_context(nc.psum_tensor(f"psd{i}", [128, 1024], F32)) for i in range(PSn // 2)]
        PS = [PSD[i // 2][:, (i % 2) * 512:(i % 2 + 1) * 512] for i in range(PSn)]
        RPS = [Res(f"ps{i}") for i in range(PSn)]
        psdi = [0]

        def psd():
            k = psdi[0] % 2
            psdi[0] += 1
            return PSD[k], [RPS[2 * k], RPS[2 * k + 1]]
        psi = [0]
        NROT = 6

        psr = [0, NROT]

        def ps():
            i = psr[0] + psi[0] % psr[1]
            psi[0] += 1
            return PS[i], RPS[i]

        T32 = sb("T32", [128, 8, 512], F32)
        RT32 = [Res(f"t32_{i}") for i in range(8)]
        t32i = [0]

        def t32():
            i = t32i[0] % 8
            t32i[0] += 1
            return T32[:, i, :], RT32[i]

        NT16 = 6
        T16 = sb("T16", [128, NT16, 512], BF16)
        RT16 = [Res(f"t16_{i}") for i in range(NT16)]
        t16i = [0]

        t16lim = [NT16]

        def t16():
            i = t16i[0] % t16lim[0]
            t16i[0] += 1
            return T16[:, i, :], RT16[i]

        NWS = 6
        WS = sb("WS", [128, NWS, 2048], BF16)
        RWS = [Res(f"ws{i}") for i in range(NWS)]
        wsi = [0]

        wslim = [NWS]

        def ws():
            i = wsi[0] % wslim[0]
            wsi[0] += 1
            return WS[:, i, :], RWS[i]

        PTd = [WS[:, 4 + k // 2, (k % 2) * 1024:(k % 2 + 1) * 1024] for k in range(4)]
        RPT = [Res(f"pt{k}") for k in range(4)]
        pti = [0]

        ident = sb("ident", [128, 128], F32)
        ones32 = sb("ones32", [128, 128], F32)
        cst = sb("cst", [128, 4], F32)
        VT = sb("VT", [128, 7, 128], F32)
        scb = sb("scb", [128, KC, 2], BF16)
        MOD = sb("MOD", [128, 72, 2], F32)
        DER = sb("DER", [128, 6, KC, 2], F32)
        CS32 = sb("CS32", [128, 2, 128], F32)
        MSK = sb("MSK", [128, 2], F32)
        TT2 = sb("TT2", [128, 2, 512], F32)
        RTT2 = [Res("tt2_0"), Res("tt2_1")]
        QT = sb("QT", [128, 2, 512], BF16)
        RQT = [Res("qt0"), Res("qt1")]
        vTc = sb("vTc", [128, 3, 286], BF16)
        HT = sb("HT", [128, 2, 90], BF16)
        Rc = {k: Res(k) for k in "ident ones cst cst2 VT scb MOD DER CS32 MSK vTc HT XA GA XG XGc GG XH GH Dcqn Dsv DF Dat".split()}

        def mm(out, lhsT, rhs, start, stop, reads, writes, sig=None):
            fw.op(pe, lambda h: h.matmul(out, lhsT=lhsT, rhs=rhs, start=start, stop=stop),
                  reads=reads, writes=writes, sig=(stop if sig is None else sig))

        def handover(srcs, dsts):
            fw.op(dve, lambda h: h.memset(cst[:, 2:3], 0.0), reads=list(srcs), writes=list(dsts) + [Rc["cst2"]])

        fw.op(pool, lambda h: h.memset(ident[:], 1.0), writes=[Rc["ident"]])
        fw.op(pool, lambda h: h.affine_select(out=ident[:], in_=ident[:], pattern=[[-1, 128]],
                                              compare_op=ALU.is_equal, fill=0.0, base=0, channel_multiplier=1),
              reads=[Rc["ident"]], writes=[Rc["ident"]])
        fw.op(pool, lambda h: h.memset(ones32[:], 1.0), writes=[Rc["ones"]])
        fw.op(pool, lambda h: h.memset(cst[:, 0:1], EPS), writes=[Rc["cst"]])
        fw.op(pool, lambda h: h.memset(cst[:, 1:2], 0.0), writes=[Rc["cst"]])
        fw.op(pool, lambda h: h.memset(vTc[:], 0.0), writes=[Rc["vTc"]])
        fw.dma(sp, CS32[:], ccsc[:, :, :], writes=[Rc["CS32"]])
        fw.dma(sp, MSK[:], maskd[:, :], writes=[Rc["MSK"]])
        for i in range(7):
            t, r = t32()
            fw.dma(sp, t[:, 0:128], vecs[i, :, :], writes=[r])
            p, pr = ps()
            fw.op(pe, lambda h, p=p, t=t: h.transpose(out=p[:, 0:128], in_=t[:, 0:128], identity=ident[:]),
                  reads=[r, Rc["ident"]], writes=[pr])
            fw.op(dve, lambda h, p=p, i=i: h.tensor_copy(out=VT[:, i, :], in_=p[:, 0:128]), reads=[pr], writes=[Rc["VT"]])
        VG = VT[:, 0, :]
        VA = lambda l: VT[:, 1 + 3 * l, :]
        VB = lambda l: VT[:, 2 + 3 * l, :]
        VC = lambda l: VT[:, 3 + 3 * l, :]
        for t in range(2):
            fw.op(act, lambda h, t=t: h.activation(out=scb[:, :, t], in_=VG[:, 8 * t:8 * t + 8], func=AF.Silu),
                  reads=[Rc["VT"]], writes=[Rc["scb"]])

        for ti in range(NT // 128):
            bi = min(ti // 4, 4)
            for half in range(2):
                t, r = t32()
                fw.dma(sp, t, xin[ti * 128:(ti + 1) * 128, half * 512:(half + 1) * 512], writes=[r])
                p, pr = ps()
                for kk in range(4):
                    fw.op(pe, lambda h, p=p, t=t, kk=kk: h.transpose(out=p[:, kk * 128:(kk + 1) * 128],
                                                                     in_=t[:, kk * 128:(kk + 1) * 128], identity=ident[:]),
                          reads=[r, Rc["ident"]], writes=[pr], sig=(kk == 3))
                eng = dve if half == 0 else act
                if half == 0:
                    fw.op(dve, lambda h, p=p, ti=ti, half=half: h.tensor_copy(
                        out=hT[:, half * 4:half * 4 + 4, ti * 128:(ti + 1) * 128],
                        in_=p[:, :].rearrange("p (c t) -> p c t", c=4)), reads=[pr], writes=[Rh[bi]])
                else:
                    fw.op(act, lambda h, p=p, ti=ti, half=half: h.copy(
                        out=hT[:, half * 4:half * 4 + 4, ti * 128:(ti + 1) * 128],
                        in_=p[:, :].rearrange("p (c t) -> p c t", c=4)), reads=[pr], writes=[Rh[bi]])

        def wload(src_ap, shape3):
            w, wr = ws()
            a, b = shape3
            v = w[:, 0:a * b].rearrange("p (a b) -> p a b", a=a)
            fw.dma(pool, v, src_ap, writes=[wr])
            return v, wr

        def kcview(wap, c0, n):
            return wap[:, c0:c0 + n].rearrange("(kc p) n -> p kc n", p=128)

        def rstd_from(pst, n, scale):
            rs, rsr = t32()
            fw.op(act, lambda h: h.activation(out=rs[:, :n], in_=pst[0][:, :n], func=AF.Sqrt, bias=cst[:, 0:1], scale=scale),
                  reads=[pst[1], Rc["cst"]], writes=[rsr])
            fw.op(dve, lambda h: h.reciprocal(out=rs[:, :n], in_=rs[:, :n]), reads=[rsr], writes=[rsr])
            return rs, rsr

        def mod_stage(l):
            pm, pmr = ps()
            for s in range(36):
                wv, wr = wload(kcview(w_ada[l], s * 256, 256), (KC, 256))
                for jj in range(2):
                    j = s * 2 + jj
                    for kc in range(KC):
                        mm(pm[:, j * 2:j * 2 + 2], wv[:, kc, jj * 128:(jj + 1) * 128], scb[:, kc, :], kc == 0, kc == KC - 1,
                           [wr, Rc["scb"]], [pmr])
            pmv = pm[:, 0:144].rearrange("p (j t) -> p j t", t=2)
            va = VA(l)
            for t in range(2):
                fw.op(dve, lambda h, t=t: h.tensor_tensor(out=MOD[:, :, t], in0=pmv[:, :, t], in1=va[:, 0:72], op=ALU.add),
                      reads=[pmr, Rc["VT"]], writes=[Rc["MOD"]])
                for i, (gcol, n) in enumerate([(72, 1), (80, 4), (88, 7)]):
                    fw.op(dve, lambda h, t=t, i=i, n=n: h.tensor_scalar(out=DER[:, i, :, t], in0=MOD[:, n * 8:(n + 1) * 8, t],
                                                                       scalar1=1.0, scalar2=None, op0=ALU.add),
                          reads=[Rc["MOD"]], writes=[Rc["DER"]])
                    fw.op(dve, lambda h, t=t, i=i, gcol=gcol: h.tensor_tensor(out=DER[:, i, :, t], in0=DER[:, i, :, t],
                                                                              in1=va[:, gcol:gcol + 8], op=ALU.mult),
                          reads=[Rc["DER"], Rc["VT"]], writes=[Rc["DER"]])
                for i, (n, f) in enumerate([(2, 0.5), (5, 1.0), (8, 0.5)]):
                    fw.op(dve, lambda h, t=t, i=i, n=n, f=f: h.tensor_scalar(out=DER[:, 3 + i, :, t], in0=MOD[:, n * 8:(n + 1) * 8, t],
                                                                             scalar1=f, scalar2=None, op0=ALU.mult),
                          reads=[Rc["MOD"]], writes=[Rc["DER"]])

        def norm_stage(idx, tbs):
            for bi, (t0, n) in tbs:
                ts = 0 if t0 < NL else 1
                pst, pstr = ps()
                for kc in range(KC):
                    sq, sqr = t32()
                    fw.op(act, lambda h, sq=sq, kc=kc: h.activation(out=sq[:, :n], in_=hT[:, kc, t0:t0 + n], func=AF.Square),
                          reads=[Rh[bi]], writes=[sqr])
                    mm(pst[:, :n], ones32[:], sq[:, :n], kc == 0, kc == KC - 1, [sqr, Rc["ones"]], [pstr])
                rs, rsr = rstd_from((pst, pstr), n, 1.0 / D)
                for kc in range(KC):
                    tt, ttr = TT2[:, kc % 2, :], RTT2[kc % 2]
                    fw.op(dve, lambda h, tt=tt, kc=kc: h.scalar_tensor_tensor(
                        out=tt[:, :n], in0=hT[:, kc, t0:t0 + n], scalar=DER[:, idx, kc, ts:ts + 1], in1=rs[:, :n],
                        op0=ALU.mult, op1=ALU.mult), reads=[Rh[bi], rsr, Rc["DER"]], writes=[ttr])
                    fw.op(act, lambda h, tt=tt, kc=kc: h.activation(
                        out=uT[:, kc, t0:t0 + n], in_=tt[:, :n], func=AF.Identity,
                        bias=MOD[:, 3 * idx * 8 + kc, ts:ts + 1], scale=1.0), reads=[ttr, Rc["MOD"]], writes=[Ru[bi]])

        def ffn_stage(l, w1, w3, w2, gidx, tbs):
            groups = [(0, 4), (4, 4), (8, 4), (12, 4), (16, 4), (20, 2)]
            for gi, (j0, nj) in enumerate(groups):
                half = gi % 2
                ab = AR[:, half * 4 * NT:(half + 1) * 4 * NT].rearrange("p (c t) -> p c t", c=4)
                abr = RAB[half]
                for sub in range(nj // 2):
                    c0 = (j0 + sub * 2) * 128
                    w1v, w1r = wload(kcview(w1[l], c0, 256), (KC, 256))
                    w3v, w3r = wload(kcview(w3[l], c0, 256), (KC, 256))
                    for jj in range(2):
                        ja = sub * 2 + jj
                        for bi, (t0, n) in tbs:
                            p1, p1r = ps()
                            p3, p3r = ps()
                            for kc in range(KC):
                                mm(p1[:, :n], w1v[:, kc, jj * 128:(jj + 1) * 128], uT[:, kc, t0:t0 + n], kc == 0, kc == KC - 1,
                                   [w1r, Ru[bi]], [p1r])
                            for kc in range(KC):
                                mm(p3[:, :n], w3v[:, kc, jj * 128:(jj + 1) * 128], uT[:, kc, t0:t0 + n], kc == 0, kc == KC - 1,
                                   [w3r, Ru[bi]], [p3r])
                            s, sr = t32()
                            fw.op(act, lambda h, s=s, p1=p1, n=n: h.activation(out=s[:, :n], in_=p1[:, :n], func=AF.Silu),
                                  reads=[p1r], writes=[sr])
                            fw.op(dve, lambda h, s=s, p3=p3, n=n, ja=ja, t0=t0, ab=ab: h.tensor_tensor(
                                out=ab[:, ja, t0:t0 + n], in0=s[:, :n], in1=p3[:, :n], op=ALU.mult),
                                reads=[sr, p3r], writes=[abr])
                w2v = []
                for sub in range(nj // 2):
                    r0 = (j0 + sub * 2) * 128
                    w2v.append(wload(w2[l, r0:r0 + 256, :].rearrange("(j p) n -> p j n", p=128), (2, D)))
                for bi, (t0, n) in tbs:
                    ts = 0 if t0 < NL else 1
                    for m in range(KC):
                        po, por = ps()
                        for ja in range(nj):
                            wv, wr = w2v[ja // 2]
                            mm(po[:, :n], wv[:, ja % 2, m * 128:(m + 1) * 128], ab[:, ja, t0:t0 + n], ja == 0, ja == nj - 1,
                               [wr, abr], [por])
                        fw.op(dve, lambda h, po=po, m=m, t0=t0, n=n, ts=ts: h.scalar_tensor_tensor(
                            out=hT[:, m, t0:t0 + n], in0=po[:, :n], scalar=DER[:, 3 + gidx, m, ts:ts + 1],
                            in1=hT[:, m, t0:t0 + n], op0=ALU.mult, op1=ALU.add),
                            reads=[por, Rh[bi], Rc["DER"]], writes=[Rh[bi]])

        def mixer_stage(l, last, mstop):
            tbs_all = list(enumerate(TBS))
            tbs_c = tbs_all if not last else tbs_all[:4]
            vb = VB(l)
            vc = VC(l)
            va = VA(l)
            vTl = AR[:, 0:3 * 2078].rearrange("p (c t) -> p c t", c=3)
            DG = AR[:, 6234:6234 + 93 * 128].rearrange("p (j m) -> p j m", j=93)

            wc = [wload(kcview(w_in[l], s * 256, 256), (KC, 256)) for s in range(3)]
            for bi, (t0, n) in tbs_c:
                for cc in range(3):
                    pa, par = ps()
                    pg, pgr = ps()
                    ca, cg = cc * 128, 384 + cc * 128
                    wa, war = wc[ca // 256]
                    wg, wgr = wc[cg // 256]
                    for kc in range(KC):
                        mm(pa[:, :n], wa[:, kc, ca % 256:ca % 256 + 128], uT[:, kc, t0:t0 + n], kc == 0, kc == KC - 1, [war, Ru[bi]], [par])
                    for kc in range(KC):
                        mm(pg[:, :n], wg[:, kc, cg % 256:cg % 256 + 128], uT[:, kc, t0:t0 + n], kc == 0, kc == KC - 1, [wgr, Ru[bi]], [pgr])
                    s, sr = t32()
                    fw.op(act, lambda h, s=s, pg=pg, n=n: h.activation(out=s[:, :n], in_=pg[:, :n], func=AF.Sigmoid), reads=[pgr], writes=[sr])
                    if t0 < NL:
                        fw.op(dve, lambda h, s=s, pa=pa, n=n, cc=cc, t0=t0: h.tensor_tensor(
                            out=vTl[:, cc, 15 + t0:15 + t0 + n], in0=s[:, :n], in1=pa[:, :n], op=ALU.mult), reads=[sr, par], writes=[RA])
                    else:
                        fw.op(dve, lambda h, s=s, pa=pa, n=n, cc=cc: h.tensor_tensor(
                            out=vTc[:, cc, 15:15 + n], in0=s[:, :n], in1=pa[:, :n], op=ALU.mult), reads=[sr, par], writes=[Rc["vTc"]])
            XHv = XH.ap().rearrange("p (c t) -> p c t", c=3)
            fw.dma(pool, XHv[:, :, 0:15], vTl[:, :, 15:30], reads=[RA], writes=[Rc["XH"]])
            fw.dma(pool, XHv[:, :, 15:30], vTl[:, :, 15 + NL - 15:15 + NL], reads=[RA], writes=[Rc["XH"]])

            wq = [wload(kcview(w_in[l], 768, 256), (KC, 256)), wload(kcview(w_in[l], 1024, 128), (KC, 128))]
            wkv = wload(kcview(w_in[l], 1152, 256), (KC, 256))
            wkr = wload(kcview(w_krp[l], 0, 192), (KC, 192))
            XAa = XA.ap()
            Dcq = Dcqn.ap()
            for bi, (t0, n) in tbs_all:
                lat = t0 < NL
                for (nch, wsel, gcol, dst, need) in ((3, "q", 33, Dcq, (lat or not last)), (2, "kv", 36, XAa, True)):
                    if not need:
                        continue
                    pcs = []
                    for cc in range(nch):
                        pq, pqr = ps()
                        if wsel == "q":
                            wv, wr = wq[0] if cc < 2 else wq[1]
                            col = (cc % 2) * 128 if cc < 2 else 0
                        else:
                            wv, wr = wkv
                            col = cc * 128
                        for kc in range(KC):
                            mm(pq[:, :n], wv[:, kc, col:col + 128], uT[:, kc, t0:t0 + n], kc == 0, kc == KC - 1, [wr, Ru[bi]], [pqr])
                        pcs.append((pq, pqr))
                    pst, pstr = ps()
                    for cc in range(nch):
                        sq, sqr = t32()
                        fw.op(act, lambda h, sq=sq, pq=pcs[cc][0], n=n: h.activation(out=sq[:, :n], in_=pq[:, :n], func=AF.Square),
                              reads=[pcs[cc][1]], writes=[sqr])
                        mm(pst[:, :n], ones32[:], sq[:, :n], cc == 0, cc == nch - 1, [sqr, Rc["ones"]], [pstr])
                    rs, rsr = rstd_from((pst, pstr), n, 1.0 / (128 * nch))
                    for cc in range(nch):
                        o16, o16r = t16()
                        fw.op(dve, lambda h, o16=o16, pq=pcs[cc][0], n=n, cc=cc, gcol=gcol, rs=rs: h.scalar_tensor_tensor(
                            out=o16[:, :n], in0=pq[:, :n], scalar=vb[:, gcol + cc:gcol + cc + 1], in1=rs[:, :n],
                            op0=ALU.mult, op1=ALU.mult), reads=[pcs[cc][1], rsr, Rc["VT"]], writes=[o16r])
                        fw.dma(pool, dst[cc * 128:(cc + 1) * 128, t0:t0 + n], o16[:, :n], reads=[o16r],
                               writes=[Rc["Dcqn"] if wsel == "q" else Rc["XA"]])
                pk, pkr = ps()
                pp, ppr = ps()
                for kc in range(KC):
                    mm(pk[0:96, :n], wkr[0][:, kc, 0:96], uT[:, kc, t0:t0 + n], kc == 0, kc == KC - 1, [wkr[1], Ru[bi]], [pkr])
                for kc in range(KC):
                    mm(pp[0:96, :n], wkr[0][:, kc, 96:192], uT[:, kc, t0:t0 + n], kc == 0, kc == KC - 1, [wkr[1], Ru[bi]], [ppr])
                o16, o16r = t16()
                if lat:
                    a1, a1r = t32()
                    a2, a2r = t32()
                    fw.dma(sp, a1[64:96, :], rope[0, :, t0:t0 + 512], writes=[a1r])
                    fw.dma(sp, a2[64:96, :], rope[1, :, t0:t0 + 512], writes=[a2r])
                    fw.op(dve, lambda h, a1=a1, pk=pk: h.tensor_tensor(out=a1[64:96, :], in0=pk[64:96, :], in1=a1[64:96, :], op=ALU.mult),
                          reads=[pkr, a1r], writes=[a1r])
                    fw.op(dve, lambda h, a2=a2, pp=pp: h.tensor_tensor(out=a2[64:96, :], in0=pp[64:96, :], in1=a2[64:96, :], op=ALU.mult),
                          reads=[ppr, a2r], writes=[a2r])
                    fw.op(dve, lambda h, a1=a1, a2=a2, o16=o16: h.tensor_tensor(out=o16[64:96, :], in0=a1[64:96, :], in1=a2[64:96, :], op=ALU.add),
                          reads=[a1r, a2r], writes=[o16r])
                else:
                    fw.op(dve, lambda h, o16=o16, pk=pk, n=n: h.tensor_copy(out=o16[64:96, :n], in_=pk[64:96, :n]), reads=[pkr], writes=[o16r])
                fw.dma(pool, XAa[256:288, t0:t0 + n], o16[64:96, :n], reads=[o16r], writes=[Rc["XA"]])

            wf = [wload(kcview(w_in[l], 1440 + s * 256, 256), (KC, 256)) for s in range(2)]
            XGa = XG.ap()
            ntt = 18 if not last else 16
            for tt in range(ntt):
                bi = min(tt // 4, 4)
                pgm, pgr = ps()
                for s in range(2):
                    for kc in range(KC):
                        mm(pgm[:, s * 256:(s + 1) * 256], uT[:, kc, tt * 128:(tt + 1) * 128], wf[s][0][:, kc, :], kc == 0, kc == KC - 1,
                           [wf[s][1], Ru[bi]], [pgr])
                o16, o16r = t16()
                if tt % 2 == 0:
                    fw.op(dve, lambda h, o16=o16, pgm=pgm: h.tensor_copy(out=o16[:, :], in_=pgm[:, :]), reads=[pgr], writes=[o16r])
                else:
                    fw.op(act, lambda h, o16=o16, pgm=pgm: h.copy(out=o16[:, :], in_=pgm[:, :]), reads=[pgr], writes=[o16r])
                if tt < 16:
                    fw.dma(pool, XGa[tt * 128:(tt + 1) * 128, :], o16[:, :], reads=[o16r], writes=[Rc["XG"]])
                else:
                    fw.dma(pool, XGc.ap()[(tt - 16) * 128:(tt - 15) * 128, :], o16[:, :], reads=[o16r], writes=[Rc["XGc"]])

            fw.allgather(XH.ap(), GH.ap(), [Rc["XH"]], [Rc["GH"]])
            fw.allgather(XA.ap(), GA.ap(), [Rc["XA"]], [Rc["GA"]])
            fw.allgather(XG.ap(), GG.ap(), [Rc["XG"]], [Rc["GG"]])
            if mstop <= 1:
                return

            GHa = GH.ap()
            fw.dma(sp, HT[:, :, :], GHa.rearrange("(r p) f -> p r f", p=128), reads=[Rc["GH"]], writes=[Rc["HT"]])
            HTv = HT[:, :, :].rearrange("p r (c t) -> p r c t", c=3)
            fw.op(dve, lambda h: h.tensor_scalar(out=vTl[:, :, 0:15], in0=HTv[:, 0, :, 15:30], scalar1=MSK[:, 0:1], scalar2=None, op0=ALU.mult),
                  reads=[Rc["HT"], Rc["MSK"]], writes=[RA])
            fw.op(dve, lambda h: h.tensor_scalar(out=vTl[:, :, 15 + NL:30 + NL], in0=HTv[:, 1, :, 0:15], scalar1=MSK[:, 1:2], scalar2=None, op0=ALU.mult),
                  reads=[Rc["HT"], Rc["MSK"]], writes=[RA])
            for idx in range(93):
                fw.op(dve, lambda h, idx=idx: h.tensor_scalar(out=DG[:, idx, :], in0=ident[:], scalar1=vc[:, idx:idx + 1], scalar2=None, op0=ALU.mult),
                      reads=[Rc["ident"], Rc["VT"]], writes=[RA])
            Dsva = Dsv.ap()
            for bi, (t0, n) in tbs_c:
                lat = t0 < NL
                cos_ = []
                for cc in range(3):
                    pc, pcr = ps()
                    for j in range(31):
                        rhs = vTl[:, cc, t0 + j:t0 + j + n] if lat else vTc[:, cc, j:j + n]
                        mm(pc[:, :n], DG[:, j * 3 + cc, :], rhs, j == 0, j == 30, [RA] if lat else [RA, Rc["vTc"]], [pcr])
                    co, cor = t32()
                    fw.op(act, lambda h, co=co, pc=pc, n=n, cc=cc: h.activation(out=co[:, :n], in_=pc[:, :n], func=AF.Identity,
                                                                              bias=vb[:, 24 + cc:25 + cc], scale=1.0),
                          reads=[pcr, Rc["VT"]], writes=[cor])
                    cos_.append((co, cor))
                pss, pssr = ps()
                psq, psqr = ps()
                for cc in range(3):
                    mm(pss[:, :n], ones32[:], cos_[cc][0][:, :n], cc == 0, cc == 2, [cos_[cc][1], Rc["ones"]], [pssr])
                for cc in range(3):
                    sq, sqr = t32()
                    fw.op(act, lambda h, sq=sq, co=cos_[cc][0], n=n: h.activation(out=sq[:, :n], in_=co[:, :n], func=AF.Square),
                          reads=[cos_[cc][1]], writes=[sqr])
                    mm(psq[:, :n], ones32[:], sq[:, :n], cc == 0, cc == 2, [sqr, Rc["ones"]], [psqr])
                mu, mur = t32()
                fw.op(dve, lambda h, mu=mu, pss=pss, n=n: h.tensor_scalar(out=mu[:, :n], in0=pss[:, :n], scalar1=1.0 / 384, scalar2=None, op0=ALU.mult),
                      reads=[pssr], writes=[mur])
                m2, m2r = t32()
                fw.op(dve, lambda h, mu=mu, m2=m2, n=n: h.tensor_tensor(out=m2[:, :n], in0=mu[:, :n], in1=mu[:, :n], op=ALU.mult), reads=[mur], writes=[m2r])
                fw.op(dve, lambda h, m2=m2, psq=psq, n=n: h.scalar_tensor_tensor(out=m2[:, :n], in0=psq[:, :n], scalar=1.0 / 384, in1=m2[:, :n],
                                                                               op0=ALU.mult, op1=ALU.subtract), reads=[psqr, m2r], writes=[m2r])
                fw.op(act, lambda h, m2=m2, n=n: h.activation(out=m2[:, :n], in_=m2[:, :n], func=AF.Sqrt, bias=cst[:, 0:1], scale=1.0),
                      reads=[m2r, Rc["cst"]], writes=[m2r])
                fw.op(dve, lambda h, m2=m2, n=n: h.reciprocal(out=m2[:, :n], in_=m2[:, :n]), reads=[m2r], writes=[m2r])
                for cc in range(3):
                    co, cor = cos_[cc]
                    fw.op(dve, lambda h, co=co, mu=mu, n=n: h.tensor_tensor(out=co[:, :n], in0=co[:, :n], in1=mu[:, :n], op=ALU.subtract),
                          reads=[cor, mur], writes=[cor])
                    fw.op(dve, lambda h, co=co, m2=m2, n=n: h.tensor_tensor(out=co[:, :n], in0=co[:, :n], in1=m2[:, :n], op=ALU.mult),
                          reads=[cor, m2r], writes=[cor])
                    o16, o16r = t16()
                    fw.op(act, lambda h, co=co, o16=o16, n=n, cc=cc: h.activation(out=o16[:, :n], in_=co[:, :n], func=AF.Silu,
                                                                                bias=vb[:, 30 + cc:31 + cc], scale=vb[:, 27 + cc:28 + cc]),
                          reads=[cor, Rc["VT"]], writes=[o16r])
                    fw.dma(pool, Dsva[cc * 128:(cc + 1) * 128, t0:t0 + n], o16[:, :n], reads=[o16r], writes=[Rc["Dsv"]])
            if mstop <= 2:
                return

            gfull = AR[:, 0:32 * 512].rearrange("p (t c) -> p t c", t=32)
            GGa = GG.ap()
            DFa = DF.ap()
            for r in range(2):
                fw.dma(sp, gfull[:, r * 16:(r + 1) * 16, :], GGa[r * NL:(r + 1) * NL, :].rearrange("(t p) c -> p t c", p=128),
                       reads=[Rc["GG"]], writes=[RA])

            def stage2(P, Pr, Q, Qr, n, dst):
                pS, pSr = t32()
                qS, qSr = t32()
                fw.op(dve, lambda h: h.tensor_copy(out=pS[:, :n], in_=P[:, :n]), reads=[Pr], writes=[pSr])
                fw.op(act, lambda h: h.copy(out=qS[:, :n], in_=Q[:, :n]), reads=[Qr], writes=[qSr])
                mm(P[:, :n], CS32[:, 0, :], pS[:, :n], True, False, [pSr, Rc["CS32"]], [Pr])
                mm(P[:, :n], CS32[:, 1, :], qS[:, :n], False, True, [qSr, Rc["CS32"]], [Pr])
                o16, o16r = t16()
                fw.op(act, lambda h: h.copy(out=o16[:, :n], in_=P[:, :n]), reads=[Pr], writes=[o16r])
                fw.dma(pool, dst, o16[:, :n], reads=[o16r], writes=[Rc["DF"]])

            for kb in range(4):
                for ti in range(32):
                    tab, tabr = ws()
                    tv = tab[:, 0:1024].rearrange("p (s k) -> p s k", s=2)
                    fw.dma(sp, tv, dft[kb, ti, :, :, :], writes=[tabr])
                    for gi in range(4):
                        mm(PS[gi][:, :], gfull[:, ti, gi * 128:(gi + 1) * 128], tv[:, 0, :], ti == 0, ti == 31, [RA, tabr], [RPS[gi]])
                        mm(PS[4 + gi][:, :], gfull[:, ti, gi * 128:(gi + 1) * 128], tv[:, 1, :], ti == 0, ti == 31, [RA, tabr], [RPS[4 + gi]],
                           sig=(True if gi == 3 else None))
                for gi in range(4):
                    stage2(PS[gi], RPS[gi], PS[4 + gi], RPS[4 + gi], 512, DFa[gi * 128:(gi + 1) * 128, kb * 512:(kb + 1) * 512])
            if not last:
                gc, gcr = ws()
                gcv = gc[:, 0:1024].rearrange("p (t c) -> p t c", t=2)
                fw.dma(sp, gcv, XGc.ap().rearrange("(t p) c -> p t c", p=128), reads=[Rc["XGc"]], writes=[gcr])
                tc_, tcr = ws()
                tcv = tc_[:, 0:1024].rearrange("p (t s k) -> p t s k", t=2, s=2)
                fw.dma(sp, tcv, dftc.rearrange("t p s k -> p t s k"), writes=[tcr])
                for gi in range(4):
                    P, Pr = PS[gi], RPS[gi]
                    Q, Qr = PS[4 + gi], RPS[4 + gi]
                    for tl in range(2):
                        mm(P[:, :256], gcv[:, tl, gi * 128:(gi + 1) * 128], tcv[:, tl, 0, :], tl == 0, tl == 1, [gcr, tcr], [Pr])
                    for tl in range(2):
                        mm(Q[:, :256], gcv[:, tl, gi * 128:(gi + 1) * 128], tcv[:, tl, 1, :], tl == 0, tl == 1, [gcr, tcr], [Qr])
                    stage2(P, Pr, Q, Qr, 256, DFa[gi * 128:(gi + 1) * 128, NL:NT])
            if mstop <= 3:
                return

            GAa = GA.ap()
            Data = Dat.ap()
            NK = SEQ + NCX
            KTb = [AR[:, b * 8704:b * 8704 + NK] for b in range(2)]
            VAb = [AR[:, b * 8704 + NK:(b + 1) * 8704].rearrange("p (t c) -> p t c", t=34) for b in range(2)]
            RKV = [Res("kv0"), Res("kv1")]
            fw.op(dve, lambda h: h.memset(VAb[0][:, :, 64:128], 1.0), reads=[RA], writes=[RA, RKV[0]])
            fw.op(dve, lambda h: h.memset(VAb[1][:, :, 0:64], 1.0), reads=[RA], writes=[RA, RKV[1]])
            qbs = tbs_c
            HW = {}
            LAG = 2

            def kv_build(hd):
                b = hd % 2
                voff = 0 if b == 0 else 64
                wsl, wslr = ws()
                wkvh = wsl[:, 0:256].rearrange("p (c n) -> p c n", c=2)
                wqh = wsl[:, 256:256 + 288].rearrange("p (c n) -> p c n", c=3)
                wqph = wsl[:, 544:544 + 288].rearrange("p (c n) -> p c n", c=3)
                fw.dma(pool, wkvh, w_ukv[l][:, hd * 128:(hd + 1) * 128].rearrange("(c p) n -> p c n", p=128), writes=[wslr])
                fw.dma(pool, wqh, w_uq[l][:, hd * 96:(hd + 1) * 96].rearrange("(c p) n -> p c n", p=128), writes=[wslr])
                fw.dma(pool, wqph, w_uqp[l][:, hd * 96:(hd + 1) * 96].rearrange("(c p) n -> p c n", p=128), writes=[wslr])
                HW[hd] = (wqh, wqph, wslr)
                yield
                for kb in range(9):
                    if kb < 8:
                        r, c0, n = kb // 4, (kb % 4) * 512, 512
                    else:
                        r, c0, n = 0, NL, NCX
                    kk0 = kb * 512
                    ck = []
                    for cc in range(2):
                        c16, c16r = t16()
                        fw.dma(sp, c16[:, :n], GAa[r * 288 + cc * 128:r * 288 + (cc + 1) * 128, c0:c0 + n], reads=[Rc["GA"]], writes=[c16r])
                        ck.append((c16, c16r))
                    fw.dma(sp, KTb[b][64:96, kk0:kk0 + n], GAa[r * 288 + 256:r * 288 + 288, c0:c0 + n], reads=[Rc["GA"]], writes=[RKV[b]])
                    pk, pkr = ps()
                    for cc in range(2):
                        mm(pk[0:64, :n], wkvh[:, cc, 0:64], ck[cc][0][:, :n], cc == 0, cc == 1, [wslr, ck[cc][1]], [pkr])
                    fw.op(dve, lambda h: h.tensor_copy(out=KTb[b][0:64, kk0:kk0 + n], in_=pk[0:64, :n]), reads=[pkr], writes=[RKV[b]])
                    pv, pvr = ps()
                    nt_ = n // 128
                    for tl in range(nt_):
                        for cc in range(2):
                            mm(pv[:, tl * 64:(tl + 1) * 64], ck[cc][0][:, tl * 128:(tl + 1) * 128], wkvh[:, cc, 64:128], cc == 0, cc == 1,
                               [wslr, ck[cc][1]], [pvr])
                    fw.op(dve, lambda h: h.tensor_copy(out=VAb[b][:, kb * 4:kb * 4 + nt_, voff:voff + 64],
                                                       in_=pv[:, 0:nt_ * 64].rearrange("p (t c) -> p t c", t=nt_)),
                          reads=[pvr], writes=[RKV[b]])
                    yield

            kvg = [None]

            def kv_step():
                if kvg[0] is not None:
                    try:
                        next(kvg[0])
                    except StopIteration:
                        kvg[0] = None

            CQ = [T16[:, 3 + cc, :] for cc in range(3)]
            RCQ = [RT16[3 + cc] for cc in range(3)]
            ropet = {}

            def q_load(hd, qi):
                bi, (t0, n) = qbs[qi]
                for cc in range(3):
                    fw.dma(sp, CQ[cc][:, :n], Dcq[cc * 128:(cc + 1) * 128, t0:t0 + n], reads=[Rc["Dcqn"]], writes=[RCQ[cc]])
                if t0 < NL:
                    a1, a1r = t32()
                    a2, a2r = t32()
                    fw.dma(sp, a1[64:96, :], rope[0, :, t0:t0 + 512], writes=[a1r])
                    fw.dma(sp, a2[64:96, :], rope[1, :, t0:t0 + 512], writes=[a2r])
                    ropet[(hd, qi)] = (a1, a1r, a2, a2r)

            def q_compute(hd, qi):
                bi, (t0, n) = qbs[qi]
                wqh, wqph, wslr = HW[hd]
                lat = t0 < NL
                pq, pqr = ps()
                for cc in range(3):
                    mm(pq[0:96, :n], wqh[:, cc, :], CQ[cc][:, :n], cc == 0, cc == 2, [wslr, RCQ[cc]], [pqr])
                q16, q16r = QT[:, qti[0] % 2, :], RQT[qti[0] % 2]
                qti[0] += 1
                if lat:
                    pp, ppr = ps()
                    for cc in range(3):
                        mm(pp[0:96, :n], wqph[:, cc, :], CQ[cc][:, :n], cc == 0, cc == 2, [wslr, RCQ[cc]], [ppr])
                    fw.op(dve, lambda h: h.tensor_copy(out=q16[0:64, :], in_=pq[0:64, :]), reads=[pqr], writes=[q16r])
                    a1, a1r, a2, a2r = ropet.pop((hd, qi))
                    fw.op(dve, lambda h: h.tensor_tensor(out=a1[64:96, :], in0=pq[64:96, :], in1=a1[64:96, :], op=ALU.mult),
                          reads=[pqr, a1r], writes=[a1r])
                    fw.op(dve, lambda h: h.tensor_tensor(out=a2[64:96, :], in0=pp[64:96, :], in1=a2[64:96, :], op=ALU.mult),
                          reads=[ppr, a2r], writes=[a2r])
                    fw.op(dve, lambda h: h.tensor_tensor(out=q16[64:96, :], in0=a1[64:96, :], in1=a2[64:96, :], op=ALU.add),
                          reads=[a1r, a2r], writes=[q16r])
                    kts = list(range(34))
                else:
                    fw.op(dve, lambda h: h.tensor_copy(out=q16[0:96, :n], in_=pq[0:96, :n]), reads=[pqr], writes=[q16r])
                    kts = [32, 33]
                return (q16, q16r, kts, t0, n)

            def finish_block(hd, po, por, t0, n):
                b = hd % 2
                orow = slice(0, 64) if b == 0 else slice(64, 128)
                drow = slice(64, 128) if b == 0 else slice(0, 64)
                rd, rdr = t32()
                fw.op(dve, lambda h: h.reciprocal(out=rd[drow, :n], in_=po[drow, :n]), reads=[por], writes=[rdr])
                rsh, rshr = t32()
                fw.op(dve, lambda h: h.tensor_copy(out=rsh[orow, :n], in_=rd[drow, :n]), reads=[rdr], writes=[rshr])
                ao, aor = t16()
                fw.op(dve, lambda h: h.tensor_tensor(out=ao[orow, :n], in0=po[orow, :n], in1=rsh[orow, :n], op=ALU.mult),
                      reads=[por, rshr], writes=[aor])
                r0 = (hd // 2) * 128 + (0 if b == 0 else 64)
                fw.dma(pool, Data[r0:r0 + 64, t0:t0 + n], ao[orow, :n], reads=[aor], writes=[Rc["Dat"]])

            wslim[0] = 4
            handover([RWS[4], RWS[5]], RPT)
            psr[0], psr[1] = 4, 2
            t16lim[0] = 3
            blocks = [(hd, qi) for hd in range(8) for qi in range(len(qbs))]
            items = []
            for bk, (hd, qi) in enumerate(blocks):
                npair = 17 if qbs[qi][1][0] < NL else 1
                items += [(bk, j, npair) for j in range(npair)]
            qinfo = {}
            for _ in kv_build(0):
                pass
            q_load(0, 0)
            qinfo[0] = q_compute(0, 0)
            q_load(*blocks[1])
            pts = {}
            for idx in range(len(items) + LAG):
                if idx < len(items):
                    bk, j, npair = items[idx]
                    hd, qi = blocks[bk]
                    b = hd % 2
                    if j == 0 and qi == 0:
                        while kvg[0] is not None:
                            kv_step()
                        if hd < 7:
                            kvg[0] = kv_build(hd + 1)
                            kv_step()
                    if j == min(2, npair - 1) and bk + 1 < len(blocks):
                        qinfo[bk + 1] = q_compute(*blocks[bk + 1])
                        if bk + 2 < len(blocks):
                            q_load(*blocks[bk + 2])
                    q16, q16r, kts, t0, n = qinfo[bk]
                    p2, p2r = psd()
                    for a in range(2):
                        kt = kts[2 * j + a]
                        mm(p2[:, a * 512:a * 512 + n], KTb[b][0:96, kt * 128:(kt + 1) * 128], q16[0:96, :n], True, True,
                           [RKV[b], q16r], [p2r[a]])
                    k = pti[0] % 4
                    pti[0] += 1
                    pt2, pt2r = PTd[k], RPT[k]
                    if n == 512:
                        fw.op(act, lambda h: h.activation(out=pt2[:, :], in_=p2[:, :], func=AF.Exp, scale=ATTN_SCALE),
                              reads=p2r, writes=[pt2r])
                    else:
                        fw.op(act, lambda h: h.activation(out=pt2[:, :].rearrange("p (a c) -> p a c", a=2)[:, :, :n],
                                                          in_=p2[:, :].rearrange("p (a c) -> p a c", a=2)[:, :, :n],
                                                          func=AF.Exp, scale=ATTN_SCALE), reads=p2r, writes=[pt2r])
                    pts[idx] = (pt2, pt2r)
                    if j % 5 == 3:
                        kv_step()
                jdx = idx - LAG
                if jdx >= 0:
                    bk2, j2, npair2 = items[jdx]
                    hd2, qi2 = blocks[bk2]
                    b2 = hd2 % 2
                    q16_, q16r_, kts2, t02, n2 = qinfo[bk2]
                    po, por = PS[6 + bk2 % 2], RPS[6 + bk2 % 2]
                    pt2, pt2r = pts.pop(jdx)
                    for a in range(2):
                        mm(po[:, :n2], VAb[b2][:, kts2[2 * j2 + a], :], pt2[:, a * 512:a * 512 + n2], j2 == 0 and a == 0,
                           j2 == npair2 - 1 and a == 1, [RKV[b2], pt2r], [por], sig=(True if a == 1 else None))
                    if j2 == npair2 - 1:
                        finish_block(hd2, po, por, t02, n2)
            t16lim[0] = NT16
            psr[0], psr[1] = 0, NROT
            handover(RPT, [RWS[4], RWS[5]])
            wslim[0] = NWS
            if mstop <= 4:
                return

            mixacc = AR[:, :].rearrange("p (c t) -> p c t", c=KC)
            RM = Res("mix")
            handover([RKV[0], RKV[1], RA], [RM, RA, RKV[0], RKV[1]])
            branches = [(Dsv.ap(), Rc["Dsv"], 3, w_pw, 96), (Dat.ap(), Rc["Dat"], 4, w_o, None), (DFa, Rc["DF"], 4, w_fo, 104)]
            first = True
            for r, (Dsrc, Dres, nch, wsrc, bcol) in enumerate(branches):
                wbg = [wload(kcview(w_bg[l], r * D + s * 256, 256), (KC, 256)) for s in range(4)]
                wr_ = [wload(wsrc[l].rearrange("(c p) n -> p c n", p=128)[:, :, s * 512:(s + 1) * 512], (nch, 512)) for s in range(2)]
                for bi, (t0, n) in tbs_c:
                    xin_ = []
                    for c in range(nch):
                        c16, c16r = t16()
                        fw.dma(sp, c16[:, :n], Dsrc[c * 128:(c + 1) * 128, t0:t0 + n], reads=[Dres], writes=[c16r])
                        xin_.append((c16, c16r))
                    for m in range(KC):
                        pgt, pgtr = ps()
                        wv, wr = wbg[m // 2]
                        for kc in range(KC):
                            mm(pgt[:, :n], wv[:, kc, (m % 2) * 128:(m % 2) * 128 + 128], uT[:, kc, t0:t0 + n], kc == 0, kc == KC - 1, [wr, Ru[bi]], [pgtr])
                        s, sr = t32()
                        fw.op(act, lambda h, s=s, pgt=pgt, n=n, r=r, m=m: h.activation(out=s[:, :n], in_=pgt[:, :n], func=AF.Sigmoid,
                                                                                     bias=vb[:, r * 8 + m:r * 8 + m + 1], scale=1.0),
                              reads=[pgtr, Rc["VT"]], writes=[sr])
                        py, pyr = ps()
                        wv2, wr2 = wr_[m // 4]
                        for c in range(nch):
                            mm(py[:, :n], wv2[:, c, (m % 4) * 128:(m % 4) * 128 + 128], xin_[c][0][:, :n], c == 0, c == nch - 1, [wr2, xin_[c][1]], [pyr])
                        bias = va[:, bcol + m:bcol + m + 1] if bcol is not None else 0.0
                        if first:
                            fw.op(dve, lambda h, py=py, s=s, n=n, m=m, t0=t0, bias=bias: h.scalar_tensor_tensor(
                                out=mixacc[:, m, t0:t0 + n], in0=py[:, :n], scalar=bias, in1=s[:, :n], op0=ALU.add, op1=ALU.mult),
                                reads=[pyr, sr, Rc["VT"]], writes=[RM])
                        else:
                            fw.op(dve, lambda h, py=py, s=s, n=n, bias=bias: h.scalar_tensor_tensor(
                                out=s[:, :n], in0=py[:, :n], scalar=bias, in1=s[:, :n], op0=ALU.add, op1=ALU.mult),
                                reads=[pyr, sr, Rc["VT"]], writes=[sr])
                            fw.op(dve, lambda h, s=s, n=n, m=m, t0=t0: h.tensor_tensor(
                                out=mixacc[:, m, t0:t0 + n], in0=mixacc[:, m, t0:t0 + n], in1=s[:, :n], op=ALU.add),
                                reads=[sr, RM], writes=[RM])
                first = False
            for mo in range(KC):
                wv, wr = wload(kcview(w_out[l], mo * 128, 128), (KC, 128))
                for bi, (t0, n) in tbs_c:
                    ts = 0 if t0 < NL else 1
                    po, por = ps()
                    for m in range(KC):
                        mm(po[:, :n], wv[:, m, :], mixacc[:, m, t0:t0 + n], m == 0, m == KC - 1, [wr, RM], [por])
                    fw.op(dve, lambda h, po=po, mo=mo, t0=t0, n=n, ts=ts: h.scalar_tensor_tensor(
                        out=hT[:, mo, t0:t0 + n], in0=po[:, :n], scalar=DER[:, 4, mo, ts:ts + 1],
                        in1=hT[:, mo, t0:t0 + n], op0=ALU.mult, op1=ALU.add),
                        reads=[por, Rh[bi], Rc["DER"]], writes=[Rh[bi]])
            handover([RM, RKV[0], RKV[1], RA], [RA, RAB[0], RAB[1]])

        obi = [0]
        qti = [0]
        tbs_all = list(enumerate(TBS))
        stage = 0
        done = False
        for l in range(DEPTH):
            last = l == DEPTH - 1
            mod_stage(l)
            norm_stage(0, tbs_all)
            ffn_stage(l, w1a, w3a, w2a, 0, tbs_all)
            stage += 1
            if stage >= stop:
                done = True
                break
            norm_stage(1, tbs_all)
            handover([RAB[0], RAB[1], RA], [RA, RAB[0], RAB[1]])
            ms = (stop - stage) if (stop - stage) < 6 else 99
            mixer_stage(l, last, ms)
            stage += 5
            if stage >= stop:
                done = True
                break
            tb2 = tbs_all if not last else tbs_all[:4]
            norm_stage(2, tb2)
            ffn_stage(l, w1b, w3b, w2b, 2, tb2)
            stage += 1
            if stage >= stop:
                done = True
                break

        vg = VT[:, 0, :]
        for bi, (t0, n) in tbs_all[:4]:
            if not done:
                pst, pstr = ps()
                for kc in range(KC):
                    sq, sqr = t32()
                    fw.op(act, lambda h, sq=sq, kc=kc: h.activation(out=sq[:, :n], in_=hT[:, kc, t0:t0 + n], func=AF.Square),
                          reads=[Rh[bi]], writes=[sqr])
                    mm(pst[:, :n], ones32[:], sq[:, :n], kc == 0, kc == KC - 1, [sqr, Rc["ones"]], [pstr])
                rs, rsr = rstd_from((pst, pstr), n, 1.0 / D)
                for kc in range(KC):
                    fw.op(dve, lambda h, kc=kc, rs=rs: h.scalar_tensor_tensor(
                        out=hT[:, kc, t0:t0 + n], in0=hT[:, kc, t0:t0 + n], scalar=vg[:, 16 + kc:17 + kc], in1=rs[:, :n],
                        op0=ALU.mult, op1=ALU.mult), reads=[Rh[bi], rsr, Rc["VT"]], writes=[Rh[bi]])
            for tl in range(4):
                tt = (t0 // 128) + tl
                for half in range(2):
                    p, pr = ps()
                    for kk in range(4):
                        kc = half * 4 + kk
                        fw.op(pe, lambda h, p=p, kk=kk, kc=kc, tt=tt: h.transpose(out=p[:, kk * 128:(kk + 1) * 128],
                                                                                 in_=hT[:, kc, tt * 128:(tt + 1) * 128], identity=ident[:]),
                              reads=[Rh[bi], Rc["ident"]], writes=[pr], sig=(kk == 3))
                    o, orr = t32()
                    if half == 0:
                        fw.op(dve, lambda h, o=o, p=p: h.tensor_copy(out=o[:, :], in_=p[:, :]), reads=[pr], writes=[orr])
                    else:
                        fw.op(act, lambda h, o=o, p=p: h.copy(out=o[:, :], in_=p[:, :]), reads=[pr], writes=[orr])
                    fw.dma(sp, yout[tt * 128:(tt + 1) * 128, half * 512:(half + 1) * 512], o[:, :], reads=[orr], writes=[Res()])
        for ds in fw.dsems:
            if ds.count:
                fw._wait(sp, ds.sem, ds.count)
        fw.run()
    return nc


def _tables(half):
    bf = ml_dtypes.bfloat16
    t = np.arange(SEQ, dtype=np.int64)
    k = np.arange(NL, dtype=np.int64) + half * NL
    ang = 2.0 * np.pi * ((t[:, None] * k[None, :]) % SEQ).astype(np.float64) / SEQ
    tab = np.stack([np.cos(ang) / 64.0, -np.sin(ang) / 64.0], axis=1)
    tab = tab.reshape(32, 128, 2, 4, 512).transpose(3, 0, 1, 2, 4)
    dft = np.ascontiguousarray(tab).astype(bf)
    tc = np.arange(NCX, dtype=np.int64)
    angc = 2.0 * np.pi * ((tc[:, None] * tc[None, :]) % NCX).astype(np.float64) / NCX
    tabc = np.stack([np.cos(angc) / 16.0, -np.sin(angc) / 16.0], axis=1).reshape(2, 128, 2, NCX)
    dftc = np.ascontiguousarray(tabc).astype(bf)
    return dft, dftc


def _rope(half):
    tok = np.arange(NL) + half * NL
    row = (tok // 64).astype(np.float32)
    col = (tok % 64).astype(np.float32)
    inv = (1.0 / (np.float32(10000.0) ** (np.arange(8, dtype=np.float32) * np.float32(2.0) / np.float32(16)))).astype(np.float32)
    ar = row[:, None] * inv
    ac = col[:, None] * inv
    ang = np.concatenate([ar, ar, ac, ac], axis=-1).astype(np.float32)
    cos = np.cos(ang).astype(np.float32).T
    sin = np.sin(ang).astype(np.float32).T
    sign = np.ones(32, np.float32)
    sign[0:8] = -1.0
    sign[16:24] = -1.0
    return np.ascontiguousarray(np.stack([cos, sin * sign[:, None]], 0)).astype(np.float32)


_PERM = np.array([(f + 8) if (f % 16) < 8 else (f - 8) for f in range(32)])


def _prep(inputs):
    f32 = np.float32
    g = {k: np.asarray(v, dtype=f32) for k, v in inputs.items()}
    L = DEPTH
    vecs_l = np.zeros((L, 3, 128, 128), f32)
    for l in range(L):
        A = vecs_l[l, 0]
        A[0:72] = g["b_ada"][l].reshape(72, 128)
        A[72:80] = g["g_ffn1"][l].reshape(8, 128)
        A[80:88] = g["g_mix"][l].reshape(8, 128)
        A[88:96] = g["g_ffn2"][l].reshape(8, 128)
        A[96:104] = g["b_pw_conv"][l].reshape(8, 128)
        A[104:112] = g["b_fourier"][l].reshape(8, 128)
        B = vecs_l[l, 1]
        B[0:24] = g["b_bgate"][l].reshape(24, 128)
        B[24:27] = g["b_dw"][l].reshape(3, 128)
        B[27:30] = g["ln_g_conv"][l].reshape(3, 128)
        B[30:33] = g["ln_b_conv"][l].reshape(3, 128)
        B[33:36] = g["g_qnorm"][l].reshape(3, 128)
        B[36:38] = g["g_kvnorm"][l].reshape(2, 128)
        vecs_l[l, 2, 0:93] = g["w_dw"][l].reshape(31 * 3, 128)
    w_krp = np.zeros((L, D, 192), f32)
    w_krp[:, :, 64:96] = g["w_in"][:, :, 1408:1440]
    w_krp[:, :, 160:192] = g["w_in"][:, :, 1408 + _PERM]
    w_uqp = np.zeros((L, 384, 768), f32)
    for hd in range(8):
        w_uqp[:, :, hd * 96 + 64:hd * 96 + 96] = g["w_uq"][:, :, hd * 96 + 64 + _PERM]
    cm = np.arange(128)
    angc = 2.0 * np.pi * ((cm[:, None] * cm[None, :]) % 128).astype(np.float64) / 128.0
    ccsc = np.stack([np.cos(angc), np.sin(angc)], axis=1) / np.sqrt(128.0)
    ccsc = np.ascontiguousarray(ccsc).astype(f32)
    shared = {k: np.ascontiguousarray(g[k]) for k in
              ("w_ada", "w1_ffn1", "w3_ffn1", "w2_ffn1", "w1_ffn2", "w3_ffn2", "w2_ffn2", "w_in", "w_pw_conv",
               "w_uq", "w_ukv", "w_o_mla", "w_fourier", "w_bgate", "w_out")}
    shared["w_krp"] = w_krp
    shared["w_uqp"] = w_uqp
    shared["ccsc"] = ccsc
    tabs = [_tables(0), _tables(1)]
    ropes = [_rope(0), _rope(1)]
    maps = []
    for c in range(8):
        b, half = c // 2, c % 2
        vecs = np.zeros((7, 128, 128), f32)
        vecs[0, 0:8] = g["c"][b].reshape(8, 128)
        vecs[0, 8:16] = g["c_ctx"].reshape(8, 128)
        vecs[0, 16:24] = g["g_final"].reshape(8, 128)
        vecs[1:4] = vecs_l[0]
        vecs[4:7] = vecs_l[1]
        m = dict(shared)
        m["xin"] = np.ascontiguousarray(np.concatenate([g["x"][b, half * NL:(half + 1) * NL], g["ctx"][b]], 0))
        m["vecs"] = vecs
        m["rope"] = ropes[half]
        m["dft"], m["dftc"] = tabs[half]
        mk = np.zeros((128, 2), f32)
        mk[:, 0] = 1.0 if half == 1 else 0.0
        mk[:, 1] = 1.0 if half == 0 else 0.0
        m["maskd"] = mk
        maps.append(m)
    return maps


def run(inputs, stop=99, cores=8, dbg=False, ret_all=False):
    nc = build(stop, dbg)
    maps = _prep(inputs)
    res = run_bass_kernel_spmd(nc, maps[:cores], core_ids=list(range(cores)))
    if ret_all:
        return res.results
    out = np.zeros((4, SEQ, D), np.float32)
    for c in range(cores):
        b, half = c // 2, c % 2
        out[b, half * NL:(half + 1) * NL] = res.results[c]["yout"]
    return out


def kernel(**inputs):
    return run(inputs)
```

```python
import numpy as np
import ml_dtypes
from contextlib import ExitStack
import concourse.bass as bass
import concourse.mybir as mybir
from concourse.bass_utils import run_bass_kernel_spmd

F32 = mybir.dt.float32
BF16 = mybir.dt.bfloat16
ALU = mybir.AluOpType
AF = mybir.ActivationFunctionType

D = 1024
KC = 8
NL = 2048
NCX = 256
NT = NL + NCX
SEQ = 4096
DFF = 2816
NJ = DFF // 128
DEPTH = 2
EPS = 1e-6
ATTN_SCALE = 96.0 ** -0.5
TBS = [(0, 512), (512, 512), (1024, 512), (1536, 512), (2048, 256)]
SAME_ENGINE_SYNC = True
FUSE_WAIT = True


class Res:
    __slots__ = ("w", "r", "name")

    def __init__(self, name=""):
        self.w = None
        self.r = {}
        self.name = name


class Eng:
    def __init__(self, name, sem):
        self.name, self.sem = name, sem
        self.count = 0
        self.seen = {}
        self.q = []


class DmaSem:
    def __init__(self, sem):
        self.sem = sem
        self.count = 0


class _Rec:
    def __init__(self):
        self.call = None

    def __getattr__(self, name):
        def f(*a, **k):
            self.call = (name, a, k)
            return self
        return f


class FW:
    def __init__(self, nc, stack, n_dma_sems=32):
        self.nc = nc
        mk = lambda n: stack.enter_context(nc.semaphore(n))
        self.pe = Eng("pe", mk("s_pe"))
        self.act = Eng("act", mk("s_act"))
        self.dve = Eng("dve", mk("s_dve"))
        self.pool = Eng("pool", mk("s_pool"))
        self.sp = Eng("sp", mk("s_sp"))
        self.dsems_q = {"sp": [DmaSem(mk(f"s_dma{i}")) for i in range(n_dma_sems // 2)],
                        "pool": [DmaSem(mk(f"s_dmp{i}")) for i in range(n_dma_sems // 2)]}
        self.dsems = self.dsems_q["sp"] + self.dsems_q["pool"]
        self.dnext_q = {"sp": 0, "pool": 0}
        self.ccs = DmaSem(mk("s_cc"))

    def _wait(self, eng, sem, val):
        key = id(sem)
        if eng.seen.get(key, 0) >= val:
            return
        eng.q.append(lambda h, sem=sem, val=val: h.wait_ge(sem, val))
        eng.seen[key] = val

    def _need(self, eng, sem, val, pend):
        key = id(sem)
        if eng.seen.get(key, 0) >= val:
            return
        eng.seen[key] = val
        for i, (s2, v2) in enumerate(pend):
            if s2 is sem:
                pend[i] = (sem, max(val, v2))
                return
        pend.append((sem, val))

    def _flush(self, eng, pend):
        if not pend:
            return None
        for (sem, val) in pend[:-1]:
            eng.q.append(lambda h, sem=sem, val=val: h.wait_ge(sem, val))
        return pend[-1] if FUSE_WAIT else (eng.q.append(lambda h, sem=pend[-1][0], val=pend[-1][1]: h.wait_ge(sem, val)) or None)

    def _deps(self, eng, reads, writes):
        deps = []
        for r in reads:
            if r.w is not None:
                deps.append(r.w)
        for w in writes:
            if w.w is not None:
                deps.append(w.w)
            deps.extend(w.r.values())
        pend = []
        for (sem, val, src) in deps:
            if src is eng and (eng.name == "pe" or not SAME_ENGINE_SYNC):
                continue
            self._need(eng, sem, val, pend)
        return pend

    def _commit(self, ev, reads, writes):
        key = id(ev[0])
        for r in reads:
            old = r.r.get(key)
            if old is None or old[1] < ev[1]:
                r.r[key] = ev
        for w in writes:
            w.w = ev
            w.r = {}

    def op(self, eng, fn, reads=(), writes=(), sig=True):
        fz = self._flush(eng, self._deps(eng, reads, writes))
        rec = _Rec()
        fn(rec)
        name, a, k = rec.call

        def emit(h, name=name, a=a, k=k, fz=fz, sem=eng.sem, sig=sig):
            inst = getattr(h, name)(*a, **k)
            if fz is not None:
                inst._wait_ge(fz[0], fz[1])
            if sig:
                inst.then_inc(sem, 1)
        eng.q.append(emit)
        if sig:
            eng.count += 1
            ev = (eng.sem, eng.count, eng)
        else:
            ev = (eng.sem, eng.count + 1, eng)
        self._commit(ev, reads, writes)
        return ev

    def dma(self, q, out, in_, reads=(), writes=()):
        pend = self._deps(q, reads, writes)
        pool_ = self.dsems_q[q.name]
        ds = pool_[self.dnext_q[q.name]]
        self.dnext_q[q.name] = (self.dnext_q[q.name] + 1) % len(pool_)
        if ds.count:
            self._need(q, ds.sem, ds.count, pend)
        fz = self._flush(q, pend)
        ds.count += 16

        def emit(h, out=out, in_=in_, sem=ds.sem, fz=fz):
            inst = h.dma_start(out=out, in_=in_)
            if fz is not None:
                inst._wait_ge(fz[0], fz[1])
            inst.then_inc(sem, 16)
        q.q.append(emit)
        ev = (ds.sem, ds.count, None)
        self._commit(ev, reads, writes)
        return ev

    def allgather(self, src, dst, reads, writes):
        q = self.pool
        pend = self._deps(q, reads, writes)
        cs = self.ccs
        if cs.count:
            self._need(q, cs.sem, cs.count, pend)
        for (sem, val) in pend:
            q.q.append(lambda h, sem=sem, val=val: h.wait_ge(sem, val))
        cs.count += 1
        q.q.append(lambda h, src=src, dst=dst, sem=cs.sem: h.collective_compute(
            "AllGather", ALU.bypass, replica_groups=[[0, 1], [2, 3], [4, 5], [6, 7]],
            ins=[src], outs=[dst]).then_inc(sem, 1))
        ev = (cs.sem, cs.count, None)
        self._commit(ev, reads, writes)
        return ev

    def run(self):
        nc = self.nc
        with nc.Block() as block:
            @block.tensor
            def _(e):
                for f in self.pe.q:
                    f(e)

            @block.scalar
            def _(e):
                for f in self.act.q:
                    f(e)

            @block.vector
            def _(e):
                for f in self.dve.q:
                    f(e)

            @block.gpsimd
            def _(e):
                for f in self.pool.q:
                    f(e)

            @block.sync
            def _(e):
                for f in self.sp.q:
                    f(e)


def build(stop=99, dbg=False):
    nc = bass.Bass("TRN2", target_bir_lowering=False)
    di = lambda n, s, d=F32: nc.dram_tensor(n, list(s), d, kind="ExternalInput").ap()
    xin = di("xin", [NT, D])
    vecs = di("vecs", [7, 128, 128])
    rope = di("rope", [2, 32, NL])
    dft = di("dft", [4, 32, 128, 2, 512], BF16)
    dftc = di("dftc", [2, 128, 2, 256], BF16)
    ccsc = di("ccsc", [128, 2, 128])
    maskd = di("maskd", [128, 2])
    w_ada = di("w_ada", [DEPTH, D, 9 * D])
    w1a = di("w1_ffn1", [DEPTH, D, DFF]); w3a = di("w3_ffn1", [DEPTH, D, DFF]); w2a = di("w2_ffn1", [DEPTH, DFF, D])
    w1b = di("w1_ffn2", [DEPTH, D, DFF]); w3b = di("w3_ffn2", [DEPTH, D, DFF]); w2b = di("w2_ffn2", [DEPTH, DFF, D])
    w_in = di("w_in", [DEPTH, D, 1952])
    w_krp = di("w_krp", [DEPTH, D, 192])
    w_pw = di("w_pw_conv", [DEPTH, 384, D])
    w_uq = di("w_uq", [DEPTH, 384, 768])
    w_uqp = di("w_uqp", [DEPTH, 384, 768])
    w_ukv = di("w_ukv", [DEPTH, 256, 1024])
    w_o = di("w_o_mla", [DEPTH, 512, D])
    w_fo = di("w_fourier", [DEPTH, 512, D])
    w_bg = di("w_bgate", [DEPTH, D, 3 * D])
    w_out = di("w_out", [DEPTH, D, D])
    yout = nc.dram_tensor("yout", [NL, D], F32, kind="ExternalOutput").ap()

    dt_ = lambda n, s: nc.dram_tensor(n, list(s), BF16)
    XA = dt_("XA", [288, NT]); GA = dt_("GA", [576, NT])
    XG = dt_("XG", [NL, 512]); GG = dt_("GG", [2 * NL, 512]); XGc = dt_("XGc", [NCX, 512])
    XH = dt_("XH", [128, 90]); GH = dt_("GH", [256, 90])
    dd = (lambda n, s: nc.dram_tensor(n, list(s), BF16, kind="ExternalOutput")) if dbg else dt_
    Dcqn = dd("Dcqn", [3 * 128, NT]); Dsv = dd("Dsv", [3 * 128, NT])
    DF = dd("DF", [4 * 128, NT]); Dat = dd("Dat", [4 * 128, NT])

    with ExitStack() as st:
        fw = FW(nc, st)
        pe, act, dve, pool, sp = fw.pe, fw.act, fw.dve, fw.pool, fw.sp
        sb = lambda n, s, d: st.enter_context(nc.sbuf_tensor(n, list(s), d))

        hT = sb("hT", [128, KC, NT], F32)
        Rh = [Res(f"h{i}") for i in range(5)]
        UR = sb("UR", [128, KC * NT], BF16)
        uT = UR[:, :].rearrange("p (c t) -> p c t", c=KC)
        Ru = [Res(f"u{i}") for i in range(5)]
        AR = sb("AR", [128, KC * NT], BF16)
        RA = Res("A")
        RAB = [Res("A0"), Res("A1")]

        PSn = 8
        PSD = [st.enter_context(nc.psum_tensor(f"psd{i}", [128, 1024], F32)) for i in range(PSn // 2)]
        PS = [PSD[i // 2][:, (i % 2) * 512:(i % 2 + 1) * 512] for i in range(PSn)]
        RPS = [Res(f"ps{i}") for i in range(PSn)]
        psdi = [0]

        def psd():
            k = psdi[0] % 2
            psdi[0] += 1
            return PSD[k], [RPS[2 * k], RPS[2 * k + 1]]
        psi = [0]
        NROT = 6

        psr = [0, NROT]

        def ps():
            i = psr[0] + psi[0] % psr[1]
            psi[0] += 1
            return PS[i], RPS[i]

        T32 = sb("T32", [128, 8, 512], F32)
        RT32 = [Res(f"t32_{i}") for i in range(8)]
        t32i = [0]

        def t32():
            i = t32i[0] % 8
            t32i[0] += 1
            return T32[:, i, :], RT32[i]

        NT16 = 6
        T16 = sb("T16", [128, NT16, 512], BF16)
        RT16 = [Res(f"t16_{i}") for i in range(NT16)]
        t16i = [0]

        t16lim = [NT16]

        def t16():
            i = t16i[0] % t16lim[0]
            t16i[0] += 1
            return T16[:, i, :], RT16[i]

        NWS = 6
        WS = sb("WS", [128, NWS, 2048], BF16)
        RWS = [Res(f"ws{i}") for i in range(NWS)]
        wsi = [0]

        wslim = [NWS]

        def ws():
            i = wsi[0] % wslim[0]
            wsi[0] += 1
            return WS[:, i, :], RWS[i]

        PTd = [WS[:, 4 + k // 2, (k % 2) * 1024:(k % 2 + 1) * 1024] for k in range(4)]
        RPT = [Res(f"pt{k}") for k in range(4)]
        pti = [0]

        ident = sb("ident", [128, 128], F32)
        ones32 = sb("ones32", [128, 128], F32)
        ones16 = sb("ones16", [128, 128], BF16)
        cst = sb("cst", [128, 4], F32)
        VT = sb("VT", [128, 7, 128], F32)
        scb = sb("scb", [128, KC, 2], BF16)
        MOD = sb("MOD", [128, 72, 2], F32)
        DER = sb("DER", [128, 6, KC, 2], F32)
        CS32 = sb("CS32", [128, 2, 128], F32)
        MSK = sb("MSK", [128, 2], F32)
        TT2 = sb("TT2", [128, 2, 512], F32)
        RTT2 = [Res("tt2_0"), Res("tt2_1")]
        QT = sb("QT", [128, 2, 512], BF16)
        RQT = [Res("qt0"), Res("qt1")]
        vTc = sb("vTc", [128, 3, 286], BF16)
        HT = sb("HT", [128, 2, 90], BF16)
        Rc = {k: Res(k) for k in "ident ones cst cst2 VT scb MOD DER CS32 MSK vTc HT XA GA XG XGc GG XH GH Dcqn Dsv DF Dat".split()}

        def mm(out, lhsT, rhs, start, stop, reads, writes, sig=None):
            fw.op(pe, lambda h: h.matmul(out, lhsT=lhsT, rhs=rhs, start=start, stop=stop),
                  reads=reads, writes=writes, sig=(stop if sig is None else sig))

        def handover(srcs, dsts):
            fw.op(dve, lambda h: h.memset(cst[:, 2:3], 0.0), reads=list(srcs), writes=list(dsts) + [Rc["cst2"]])

        fw.op(pool, lambda h: h.memset(ident[:], 1.0), writes=[Rc["ident"]])
        fw.op(pool, lambda h: h.affine_select(out=ident[:], in_=ident[:], pattern=[[-1, 128]],
                                              compare_op=ALU.is_equal, fill=0.0, base=0, channel_multiplier=1),
              reads=[Rc["ident"]], writes=[Rc["ident"]])
        fw.op(pool, lambda h: h.memset(ones32[:], 1.0), writes=[Rc["ones"]])
        fw.op(pool, lambda h: h.memset(ones16[:], 1.0), writes=[Rc["ones"]])
        fw.op(pool, lambda h: h.memset(cst[:, 0:1], EPS), writes=[Rc["cst"]])
        fw.op(pool, lambda h: h.memset(cst[:, 1:2], 0.0), writes=[Rc["cst"]])
        fw.op(pool, lambda h: h.memset(vTc[:], 0.0), writes=[Rc["vTc"]])
        fw.dma(sp, CS32[:], ccsc[:, :, :], writes=[Rc["CS32"]])
        fw.dma(sp, MSK[:], maskd[:, :], writes=[Rc["MSK"]])
        for i in range(7):
            t, r = t32()
            fw.dma(sp, t[:, 0:128], vecs[i, :, :], writes=[r])
            p, pr = ps()
            fw.op(pe, lambda h, p=p, t=t: h.transpose(out=p[:, 0:128], in_=t[:, 0:128], identity=ident[:]),
                  reads=[r, Rc["ident"]], writes=[pr])
            fw.op(dve, lambda h, p=p, i=i: h.tensor_copy(out=VT[:, i, :], in_=p[:, 0:128]), reads=[pr], writes=[Rc["VT"]])
        VG = VT[:, 0, :]
        VA = lambda l: VT[:, 1 + 3 * l, :]
        VB = lambda l: VT[:, 2 + 3 * l, :]
        VC = lambda l: VT[:, 3 + 3 * l, :]
        for t in range(2):
            fw.op(act, lambda h, t=t: h.activation(out=scb[:, :, t], in_=VG[:, 8 * t:8 * t + 8], func=AF.Silu),
                  reads=[Rc["VT"]], writes=[Rc["scb"]])

        for ti in range(NT // 128):
            bi = min(ti // 4, 4)
            for half in range(2):
                t, r = t32()
                fw.dma(sp, t, xin[ti * 128:(ti + 1) * 128, half * 512:(half + 1) * 512], writes=[r])
                p, pr = ps()
                for kk in range(4):
                    fw.op(pe, lambda h, p=p, t=t, kk=kk: h.transpose(out=p[:, kk * 128:(kk + 1) * 128],
                                                                     in_=t[:, kk * 128:(kk + 1) * 128], identity=ident[:]),
                          reads=[r, Rc["ident"]], writes=[pr], sig=(kk == 3))
                eng = dve if half == 0 else act
                if half == 0:
                    fw.op(dve, lambda h, p=p, ti=ti, half=half: h.tensor_copy(
                        out=hT[:, half * 4:half * 4 + 4, ti * 128:(ti + 1) * 128],
                        in_=p[:, :].rearrange("p (c t) -> p c t", c=4)), reads=[pr], writes=[Rh[bi]])
                else:
                    fw.op(act, lambda h, p=p, ti=ti, half=half: h.copy(
                        out=hT[:, half * 4:half * 4 + 4, ti * 128:(ti + 1) * 128],
                        in_=p[:, :].rearrange("p (c t) -> p c t", c=4)), reads=[pr], writes=[Rh[bi]])

        def wload(src_ap, shape3):
            w, wr = ws()
            a, b = shape3
            v = w[:, 0:a * b].rearrange("p (a b) -> p a b", a=a)
            fw.dma(pool, v, src_ap, writes=[wr])
            return v, wr

        def kcview(wap, c0, n):
            return wap[:, c0:c0 + n].rearrange("(kc p) n -> p kc n", p=128)

        def rstd_from(pst, n, scale):
            rs, rsr = t32()
            fw.op(act, lambda h: h.activation(out=rs[:, :n], in_=pst[0][:, :n], func=AF.Sqrt, bias=cst[:, 0:1], scale=scale),
                  reads=[pst[1], Rc["cst"]], writes=[rsr])
            fw.op(dve, lambda h: h.reciprocal(out=rs[:, :n], in_=rs[:, :n]), reads=[rsr], writes=[rsr])
            return rs, rsr

        def mod_stage(l):
            pm, pmr = ps()
            for s in range(36):
                wv, wr = wload(kcview(w_ada[l], s * 256, 256), (KC, 256))
                for jj in range(2):
                    j = s * 2 + jj
                    for kc in range(KC):
                        mm(pm[:, j * 2:j * 2 + 2], wv[:, kc, jj * 128:(jj + 1) * 128], scb[:, kc, :], kc == 0, kc == KC - 1,
                           [wr, Rc["scb"]], [pmr])
            pmv = pm[:, 0:144].rearrange("p (j t) -> p j t", t=2)
            va = VA(l)
            for t in range(2):
                fw.op(dve, lambda h, t=t: h.tensor_tensor(out=MOD[:, :, t], in0=pmv[:, :, t], in1=va[:, 0:72], op=ALU.add),
                      reads=[pmr, Rc["VT"]], writes=[Rc["MOD"]])
                for i, (gcol, n) in enumerate([(72, 1), (80, 4), (88, 7)]):
                    fw.op(dve, lambda h, t=t, i=i, n=n: h.tensor_scalar(out=DER[:, i, :, t], in0=MOD[:, n * 8:(n + 1) * 8, t],
                                                                       scalar1=1.0, scalar2=None, op0=ALU.add),
                          reads=[Rc["MOD"]], writes=[Rc["DER"]])
                    fw.op(dve, lambda h, t=t, i=i, gcol=gcol: h.tensor_tensor(out=DER[:, i, :, t], in0=DER[:, i, :, t],
                                                                              in1=va[:, gcol:gcol + 8], op=ALU.mult),
                          reads=[Rc["DER"], Rc["VT"]], writes=[Rc["DER"]])
                for i, (n, f) in enumerate([(2, 0.5), (5, 1.0), (8, 0.5)]):
                    fw.op(dve, lambda h, t=t, i=i, n=n, f=f: h.tensor_scalar(out=DER[:, 3 + i, :, t], in0=MOD[:, n * 8:(n + 1) * 8, t],
                                                                             scalar1=f, scalar2=None, op0=ALU.mult),
                          reads=[Rc["MOD"]], writes=[Rc["DER"]])

        def norm_stage(idx, tbs):
            for bi, (t0, n) in tbs:
                ts = 0 if t0 < NL else 1
                pst, pstr = ps()
                for kc in range(KC):
                    sq, sqr = t16()
                    fw.op(act, lambda h, sq=sq, kc=kc: h.activation(out=sq[:, :n], in_=hT[:, kc, t0:t0 + n], func=AF.Square),
                          reads=[Rh[bi]], writes=[sqr])
                    mm(pst[:, :n], ones16[:], sq[:, :n], kc == 0, kc == KC - 1, [sqr, Rc["ones"]], [pstr], sig=True)
                rs, rsr = rstd_from((pst, pstr), n, 1.0 / D)
                for kc in range(KC):
                    tt, ttr = TT2[:, kc % 2, :], RTT2[kc % 2]
                    fw.op(dve, lambda h, tt=tt, kc=kc: h.scalar_tensor_tensor(
                        out=tt[:, :n], in0=hT[:, kc, t0:t0 + n], scalar=DER[:, idx, kc, ts:ts + 1], in1=rs[:, :n],
                        op0=ALU.mult, op1=ALU.mult), reads=[Rh[bi], rsr, Rc["DER"]], writes=[ttr])
                    fw.op(act, lambda h, tt=tt, kc=kc: h.activation(
                        out=uT[:, kc, t0:t0 + n], in_=tt[:, :n], func=AF.Identity,
                        bias=MOD[:, 3 * idx * 8 + kc, ts:ts + 1], scale=1.0), reads=[ttr, Rc["MOD"]], writes=[Ru[bi]])

        def ffn_stage(l, w1, w3, w2, gidx, tbs):
            groups = [(0, 4), (4, 4), (8, 4), (12, 4), (16, 4), (20, 2)]
            for gi, (j0, nj) in enumerate(groups):
                half = gi % 2
                ab = AR[:, half * 4 * NT:(half + 1) * 4 * NT].rearrange("p (c t) -> p c t", c=4)
                abr = RAB[half]
                for sub in range(nj // 2):
                    c0 = (j0 + sub * 2) * 128
                    w1v, w1r = wload(kcview(w1[l], c0, 256), (KC, 256))
                    w3v, w3r = wload(kcview(w3[l], c0, 256), (KC, 256))
                    for jj in range(2):
                        ja = sub * 2 + jj
                        for bi, (t0, n) in tbs:
                            p1, p1r = ps()
                            p3, p3r = ps()
                            for kc in range(KC):
                                mm(p1[:, :n], w1v[:, kc, jj * 128:(jj + 1) * 128], uT[:, kc, t0:t0 + n], kc == 0, kc == KC - 1,
                                   [w1r, Ru[bi]], [p1r])
                            for kc in range(KC):
                                mm(p3[:, :n], w3v[:, kc, jj * 128:(jj + 1) * 128], uT[:, kc, t0:t0 + n], kc == 0, kc == KC - 1,
                                   [w3r, Ru[bi]], [p3r])
                            s, sr = t32()
                            fw.op(act, lambda h, s=s, p1=p1, n=n: h.activation(out=s[:, :n], in_=p1[:, :n], func=AF.Silu),
                                  reads=[p1r], writes=[sr])
                            fw.op(dve, lambda h, s=s, p3=p3, n=n, ja=ja, t0=t0, ab=ab: h.tensor_tensor(
                                out=ab[:, ja, t0:t0 + n], in0=s[:, :n], in1=p3[:, :n], op=ALU.mult),
                                reads=[sr, p3r], writes=[abr])
                w2v = []
                for sub in range(nj // 2):
                    r0 = (j0 + sub * 2) * 128
                    w2v.append(wload(w2[l, r0:r0 + 256, :].rearrange("(j p) n -> p j n", p=128), (2, D)))
                for bi, (t0, n) in tbs:
                    ts = 0 if t0 < NL else 1
                    for m in range(KC):
                        po, por = ps()
                        for ja in range(nj):
                            wv, wr = w2v[ja // 2]
                            mm(po[:, :n], wv[:, ja % 2, m * 128:(m + 1) * 128], ab[:, ja, t0:t0 + n], ja == 0, ja == nj - 1,
                               [wr, abr], [por])
                        fw.op(dve, lambda h, po=po, m=m, t0=t0, n=n, ts=ts: h.scalar_tensor_tensor(
                            out=hT[:, m, t0:t0 + n], in0=po[:, :n], scalar=DER[:, 3 + gidx, m, ts:ts + 1],
                            in1=hT[:, m, t0:t0 + n], op0=ALU.mult, op1=ALU.add),
                            reads=[por, Rh[bi], Rc["DER"]], writes=[Rh[bi]])

        def mixer_stage(l, last, mstop):
            tbs_all = list(enumerate(TBS))
            tbs_c = tbs_all if not last else tbs_all[:4]
            vb = VB(l)
            vc = VC(l)
            va = VA(l)
            vTl = AR[:, 0:3 * 2078].rearrange("p (c t) -> p c t", c=3)
            DG = AR[:, 6234:6234 + 93 * 128].rearrange("p (j m) -> p j m", j=93)

            wc = [wload(kcview(w_in[l], s * 256, 256), (KC, 256)) for s in range(3)]
            for bi, (t0, n) in tbs_c:
                for cc in range(3):
                    pa, par = ps()
                    pg, pgr = ps()
                    ca, cg = cc * 128, 384 + cc * 128
                    wa, war = wc[ca // 256]
                    wg, wgr = wc[cg // 256]
                    for kc in range(KC):
                        mm(pa[:, :n], wa[:, kc, ca % 256:ca % 256 + 128], uT[:, kc, t0:t0 + n], kc == 0, kc == KC - 1, [war, Ru[bi]], [par])
                    for kc in range(KC):
                        mm(pg[:, :n], wg[:, kc, cg % 256:cg % 256 + 128], uT[:, kc, t0:t0 + n], kc == 0, kc == KC - 1, [wgr, Ru[bi]], [pgr])
                    s, sr = t32()
                    fw.op(act, lambda h, s=s, pg=pg, n=n: h.activation(out=s[:, :n], in_=pg[:, :n], func=AF.Sigmoid), reads=[pgr], writes=[sr])
                    if t0 < NL:
                        fw.op(dve, lambda h, s=s, pa=pa, n=n, cc=cc, t0=t0: h.tensor_tensor(
                            out=vTl[:, cc, 15 + t0:15 + t0 + n], in0=s[:, :n], in1=pa[:, :n], op=ALU.mult), reads=[sr, par], writes=[RA])
                    else:
                        fw.op(dve, lambda h, s=s, pa=pa, n=n, cc=cc: h.tensor_tensor(
                            out=vTc[:, cc, 15:15 + n], in0=s[:, :n], in1=pa[:, :n], op=ALU.mult), reads=[sr, par], writes=[Rc["vTc"]])
            XHv = XH.ap().rearrange("p (c t) -> p c t", c=3)
            fw.dma(pool, XHv[:, :, 0:15], vTl[:, :, 15:30], reads=[RA], writes=[Rc["XH"]])
            fw.dma(pool, XHv[:, :, 15:30], vTl[:, :, 15 + NL - 15:15 + NL], reads=[RA], writes=[Rc["XH"]])

            wq = [wload(kcview(w_in[l], 768, 256), (KC, 256)), wload(kcview(w_in[l], 1024, 128), (KC, 128))]
            wkv = wload(kcview(w_in[l], 1152, 256), (KC, 256))
            wkr = wload(kcview(w_krp[l], 0, 192), (KC, 192))
            XAa = XA.ap()
            Dcq = Dcqn.ap()
            for bi, (t0, n) in tbs_all:
                lat = t0 < NL
                for (nch, wsel, gcol, dst, need) in ((3, "q", 33, Dcq, (lat or not last)), (2, "kv", 36, XAa, True)):
                    if not need:
                        continue
                    pcs = []
                    for cc in range(nch):
                        pq, pqr = ps()
                        if wsel == "q":
                            wv, wr = wq[0] if cc < 2 else wq[1]
                            col = (cc % 2) * 128 if cc < 2 else 0
                        else:
                            wv, wr = wkv
                            col = cc * 128
                        for kc in range(KC):
                            mm(pq[:, :n], wv[:, kc, col:col + 128], uT[:, kc, t0:t0 + n], kc == 0, kc == KC - 1, [wr, Ru[bi]], [pqr])
                        pcs.append((pq, pqr))
                    pst, pstr = ps()
                    for cc in range(nch):
                        sq, sqr = t16()
                        fw.op(act, lambda h, sq=sq, pq=pcs[cc][0], n=n: h.activation(out=sq[:, :n], in_=pq[:, :n], func=AF.Square),
                              reads=[pcs[cc][1]], writes=[sqr])
                        mm(pst[:, :n], ones16[:], sq[:, :n], cc == 0, cc == nch - 1, [sqr, Rc["ones"]], [pstr])
                    rs, rsr = rstd_from((pst, pstr), n, 1.0 / (128 * nch))
                    for cc in range(nch):
                        o16, o16r = t16()
                        fw.op(dve, lambda h, o16=o16, pq=pcs[cc][0], n=n, cc=cc, gcol=gcol, rs=rs: h.scalar_tensor_tensor(
                            out=o16[:, :n], in0=pq[:, :n], scalar=vb[:, gcol + cc:gcol + cc + 1], in1=rs[:, :n],
                            op0=ALU.mult, op1=ALU.mult), reads=[pcs[cc][1], rsr, Rc["VT"]], writes=[o16r])
                        fw.dma(pool, dst[cc * 128:(cc + 1) * 128, t0:t0 + n], o16[:, :n], reads=[o16r],
                               writes=[Rc["Dcqn"] if wsel == "q" else Rc["XA"]])
                pk, pkr = ps()
                pp, ppr = ps()
                for kc in range(KC):
                    mm(pk[0:96, :n], wkr[0][:, kc, 0:96], uT[:, kc, t0:t0 + n], kc == 0, kc == KC - 1, [wkr[1], Ru[bi]], [pkr])
                for kc in range(KC):
                    mm(pp[0:96, :n], wkr[0][:, kc, 96:192], uT[:, kc, t0:t0 + n], kc == 0, kc == KC - 1, [wkr[1], Ru[bi]], [ppr])
                o16, o16r = t16()
                if lat:
                    a1, a1r = t32()
                    a2, a2r = t32()
                    fw.dma(sp, a1[64:96, :], rope[0, :, t0:t0 + 512], writes=[a1r])
                    fw.dma(sp, a2[64:96, :], rope[1, :, t0:t0 + 512], writes=[a2r])
                    fw.op(dve, lambda h, a1=a1, pk=pk: h.tensor_tensor(out=a1[64:96, :], in0=pk[64:96, :], in1=a1[64:96, :], op=ALU.mult),
                          reads=[pkr, a1r], writes=[a1r])
                    fw.op(dve, lambda h, a2=a2, pp=pp: h.tensor_tensor(out=a2[64:96, :], in0=pp[64:96, :], in1=a2[64:96, :], op=ALU.mult),
                          reads=[ppr, a2r], writes=[a2r])
                    fw.op(dve, lambda h, a1=a1, a2=a2, o16=o16: h.tensor_tensor(out=o16[64:96, :], in0=a1[64:96, :], in1=a2[64:96, :], op=ALU.add),
                          reads=[a1r, a2r], writes=[o16r])
                else:
                    fw.op(dve, lambda h, o16=o16, pk=pk, n=n: h.tensor_copy(out=o16[64:96, :n], in_=pk[64:96, :n]), reads=[pkr], writes=[o16r])
                fw.dma(pool, XAa[256:288, t0:t0 + n], o16[64:96, :n], reads=[o16r], writes=[Rc["XA"]])

            wf = [wload(kcview(w_in[l], 1440 + s * 256, 256), (KC, 256)) for s in range(2)]
            XGa = XG.ap()
            ntt = 18 if not last else 16
            for tt in range(ntt):
                bi = min(tt // 4, 4)
                pgm, pgr = ps()
                for s in range(2):
                    for kc in range(KC):
                        mm(pgm[:, s * 256:(s + 1) * 256], uT[:, kc, tt * 128:(tt + 1) * 128], wf[s][0][:, kc, :], kc == 0, kc == KC - 1,
                           [wf[s][1], Ru[bi]], [pgr])
                o16, o16r = t16()
                if tt % 2 == 0:
                    fw.op(dve, lambda h, o16=o16, pgm=pgm: h.tensor_copy(out=o16[:, :], in_=pgm[:, :]), reads=[pgr], writes=[o16r])
                else:
                    fw.op(act, lambda h, o16=o16, pgm=pgm: h.copy(out=o16[:, :], in_=pgm[:, :]), reads=[pgr], writes=[o16r])
                if tt < 16:
                    fw.dma(pool, XGa[tt * 128:(tt + 1) * 128, :], o16[:, :], reads=[o16r], writes=[Rc["XG"]])
                else:
                    fw.dma(pool, XGc.ap()[(tt - 16) * 128:(tt - 15) * 128, :], o16[:, :], reads=[o16r], writes=[Rc["XGc"]])

            fw.allgather(XH.ap(), GH.ap(), [Rc["XH"]], [Rc["GH"]])
            fw.allgather(XA.ap(), GA.ap(), [Rc["XA"]], [Rc["GA"]])
            fw.allgather(XG.ap(), GG.ap(), [Rc["XG"]], [Rc["GG"]])
            if mstop <= 1:
                return

            GHa = GH.ap()
            fw.dma(sp, HT[:, :, :], GHa.rearrange("(r p) f -> p r f", p=128), reads=[Rc["GH"]], writes=[Rc["HT"]])
            HTv = HT[:, :, :].rearrange("p r (c t) -> p r c t", c=3)
            fw.op(dve, lambda h: h.tensor_scalar(out=vTl[:, :, 0:15], in0=HTv[:, 0, :, 15:30], scalar1=MSK[:, 0:1], scalar2=None, op0=ALU.mult),
                  reads=[Rc["HT"], Rc["MSK"]], writes=[RA])
            fw.op(dve, lambda h: h.tensor_scalar(out=vTl[:, :, 15 + NL:30 + NL], in0=HTv[:, 1, :, 0:15], scalar1=MSK[:, 1:2], scalar2=None, op0=ALU.mult),
                  reads=[Rc["HT"], Rc["MSK"]], writes=[RA])
            for idx in range(93):
                fw.op(dve, lambda h, idx=idx: h.tensor_scalar(out=DG[:, idx, :], in0=ident[:], scalar1=vc[:, idx:idx + 1], scalar2=None, op0=ALU.mult),
                      reads=[Rc["ident"], Rc["VT"]], writes=[RA])
            Dsva = Dsv.ap()
            for bi, (t0, n) in tbs_c:
                lat = t0 < NL
                cos_ = []
                for cc in range(3):
                    pc, pcr = ps()
                    for j in range(31):
                        rhs = vTl[:, cc, t0 + j:t0 + j + n] if lat else vTc[:, cc, j:j + n]
                        mm(pc[:, :n], DG[:, j * 3 + cc, :], rhs, j == 0, j == 30, [RA] if lat else [RA, Rc["vTc"]], [pcr])
                    co, cor = t32()
                    fw.op(act, lambda h, co=co, pc=pc, n=n, cc=cc: h.activation(out=co[:, :n], in_=pc[:, :n], func=AF.Identity,
                                                                              bias=vb[:, 24 + cc:25 + cc], scale=1.0),
                          reads=[pcr, Rc["VT"]], writes=[cor])
                    cos_.append((co, cor))
                pss, pssr = ps()
                psq, psqr = ps()
                for cc in range(3):
                    mm(pss[:, :n], ones32[:], cos_[cc][0][:, :n], cc == 0, cc == 2, [cos_[cc][1], Rc["ones"]], [pssr])
                for cc in range(3):
                    sq, sqr = t16()
                    fw.op(act, lambda h, sq=sq, co=cos_[cc][0], n=n: h.activation(out=sq[:, :n], in_=co[:, :n], func=AF.Square),
                          reads=[cos_[cc][1]], writes=[sqr])
                    mm(psq[:, :n], ones16[:], sq[:, :n], cc == 0, cc == 2, [sqr, Rc["ones"]], [psqr])
                mu, mur = t32()
                fw.op(dve, lambda h, mu=mu, pss=pss, n=n: h.tensor_scalar(out=mu[:, :n], in0=pss[:, :n], scalar1=1.0 / 384, scalar2=None, op0=ALU.mult),
                      reads=[pssr], writes=[mur])
                m2, m2r = t32()
                fw.op(dve, lambda h, mu=mu, m2=m2, n=n: h.tensor_tensor(out=m2[:, :n], in0=mu[:, :n], in1=mu[:, :n], op=ALU.mult), reads=[mur], writes=[m2r])
                fw.op(dve, lambda h, m2=m2, psq=psq, n=n: h.scalar_tensor_tensor(out=m2[:, :n], in0=psq[:, :n], scalar=1.0 / 384, in1=m2[:, :n],
                                                                               op0=ALU.mult, op1=ALU.subtract), reads=[psqr, m2r], writes=[m2r])
                fw.op(act, lambda h, m2=m2, n=n: h.activation(out=m2[:, :n], in_=m2[:, :n], func=AF.Sqrt, bias=cst[:, 0:1], scale=1.0),
                      reads=[m2r, Rc["cst"]], writes=[m2r])
                fw.op(dve, lambda h, m2=m2, n=n: h.reciprocal(out=m2[:, :n], in_=m2[:, :n]), reads=[m2r], writes=[m2r])
                for cc in range(3):
                    co, cor = cos_[cc]
                    fw.op(dve, lambda h, co=co, mu=mu, n=n: h.tensor_tensor(out=co[:, :n], in0=co[:, :n], in1=mu[:, :n], op=ALU.subtract),
                          reads=[cor, mur], writes=[cor])
                    fw.op(dve, lambda h, co=co, m2=m2, n=n: h.tensor_tensor(out=co[:, :n], in0=co[:, :n], in1=m2[:, :n], op=ALU.mult),
                          reads=[cor, m2r], writes=[cor])
                    o16, o16r = t16()
                    fw.op(act, lambda h, co=co, o16=o16, n=n, cc=cc: h.activation(out=o16[:, :n], in_=co[:, :n], func=AF.Silu,
                                                                                bias=vb[:, 30 + cc:31 + cc], scale=vb[:, 27 + cc:28 + cc]),
                          reads=[cor, Rc["VT"]], writes=[o16r])
                    fw.dma(pool, Dsva[cc * 128:(cc + 1) * 128, t0:t0 + n], o16[:, :n], reads=[o16r], writes=[Rc["Dsv"]])
            if mstop <= 2:
                return

            gfull = AR[:, 0:32 * 512].rearrange("p (t c) -> p t c", t=32)
            GGa = GG.ap()
            DFa = DF.ap()
            for r in range(2):
                fw.dma(sp, gfull[:, r * 16:(r + 1) * 16, :], GGa[r * NL:(r + 1) * NL, :].rearrange("(t p) c -> p t c", p=128),
                       reads=[Rc["GG"]], writes=[RA])

            def stage2(P, Pr, Q, Qr, n, dst):
                pS, pSr = t32()
                qS, qSr = t32()
                fw.op(dve, lambda h: h.tensor_copy(out=pS[:, :n], in_=P[:, :n]), reads=[Pr], writes=[pSr])
                fw.op(act, lambda h: h.copy(out=qS[:, :n], in_=Q[:, :n]), reads=[Qr], writes=[qSr])
                mm(P[:, :n], CS32[:, 0, :], pS[:, :n], True, False, [pSr, Rc["CS32"]], [Pr])
                mm(P[:, :n], CS32[:, 1, :], qS[:, :n], False, True, [qSr, Rc["CS32"]], [Pr])
                o16, o16r = t16()
                fw.op(act, lambda h: h.copy(out=o16[:, :n], in_=P[:, :n]), reads=[Pr], writes=[o16r])
                fw.dma(pool, dst, o16[:, :n], reads=[o16r], writes=[Rc["DF"]])

            for kb in range(4):
                for ti in range(32):
                    tab, tabr = ws()
                    tv = tab[:, 0:1024].rearrange("p (s k) -> p s k", s=2)
                    fw.dma(sp, tv, dft[kb, ti, :, :, :], writes=[tabr])
                    for gi in range(4):
                        mm(PS[gi][:, :], gfull[:, ti, gi * 128:(gi + 1) * 128], tv[:, 0, :], ti == 0, ti == 31, [RA, tabr], [RPS[gi]])
                        mm(PS[4 + gi][:, :], gfull[:, ti, gi * 128:(gi + 1) * 128], tv[:, 1, :], ti == 0, ti == 31, [RA, tabr], [RPS[4 + gi]],
                           sig=(True if gi == 3 else None))
                for gi in range(4):
                    stage2(PS[gi], RPS[gi], PS[4 + gi], RPS[4 + gi], 512, DFa[gi * 128:(gi + 1) * 128, kb * 512:(kb + 1) * 512])
            if not last:
                gc, gcr = ws()
                gcv = gc[:, 0:1024].rearrange("p (t c) -> p t c", t=2)
                fw.dma(sp, gcv, XGc.ap().rearrange("(t p) c -> p t c", p=128), reads=[Rc["XGc"]], writes=[gcr])
                tc_, tcr = ws()
                tcv = tc_[:, 0:1024].rearrange("p (t s k) -> p t s k", t=2, s=2)
                fw.dma(sp, tcv, dftc.rearrange("t p s k -> p t s k"), writes=[tcr])
                for gi in range(4):
                    P, Pr = PS[gi], RPS[gi]
                    Q, Qr = PS[4 + gi], RPS[4 + gi]
                    for tl in range(2):
                        mm(P[:, :256], gcv[:, tl, gi * 128:(gi + 1) * 128], tcv[:, tl, 0, :], tl == 0, tl == 1, [gcr, tcr], [Pr])
                    for tl in range(2):
                        mm(Q[:, :256], gcv[:, tl, gi * 128:(gi + 1) * 128], tcv[:, tl, 1, :], tl == 0, tl == 1, [gcr, tcr], [Qr])
                    stage2(P, Pr, Q, Qr, 256, DFa[gi * 128:(gi + 1) * 128, NL:NT])
            if mstop <= 3:
                return

            GAa = GA.ap()
            Data = Dat.ap()
            NK = SEQ + NCX
            KTb = [AR[:, b * 8704:b * 8704 + NK] for b in range(2)]
            VAb = [AR[:, b * 8704 + NK:(b + 1) * 8704].rearrange("p (t c) -> p t c", t=34) for b in range(2)]
            RKV = [Res("kv0"), Res("kv1")]
            fw.op(dve, lambda h: h.memset(VAb[0][:, :, 64:128], 1.0), reads=[RA], writes=[RA, RKV[0]])
            fw.op(dve, lambda h: h.memset(VAb[1][:, :, 0:64], 1.0), reads=[RA], writes=[RA, RKV[1]])
            qbs = tbs_c
            HW = {}
            LAG = 2

            def kv_build(hd):
                b = hd % 2
                voff = 0 if b == 0 else 64
                wsl, wslr = ws()
                wkvh = wsl[:, 0:256].rearrange("p (c n) -> p c n", c=2)
                wqh = wsl[:, 256:256 + 288].rearrange("p (c n) -> p c n", c=3)
                wqph = wsl[:, 544:544 + 288].rearrange("p (c n) -> p c n", c=3)
                fw.dma(pool, wkvh, w_ukv[l][:, hd * 128:(hd + 1) * 128].rearrange("(c p) n -> p c n", p=128), writes=[wslr])
                fw.dma(pool, wqh, w_uq[l][:, hd * 96:(hd + 1) * 96].rearrange("(c p) n -> p c n", p=128), writes=[wslr])
                fw.dma(pool, wqph, w_uqp[l][:, hd * 96:(hd + 1) * 96].rearrange("(c p) n -> p c n", p=128), writes=[wslr])
                HW[hd] = (wqh, wqph, wslr)
                yield
                for kb in range(9):
                    if kb < 8:
                        r, c0, n = kb // 4, (kb % 4) * 512, 512
                    else:
                        r, c0, n = 0, NL, NCX
                    kk0 = kb * 512
                    ck = []
                    for cc in range(2):
                        c16, c16r = t16()
                        fw.dma(sp, c16[:, :n], GAa[r * 288 + cc * 128:r * 288 + (cc + 1) * 128, c0:c0 + n], reads=[Rc["GA"]], writes=[c16r])
                        ck.append((c16, c16r))
                    fw.dma(sp, KTb[b][64:96, kk0:kk0 + n], GAa[r * 288 + 256:r * 288 + 288, c0:c0 + n], reads=[Rc["GA"]], writes=[RKV[b]])
                    pk, pkr = ps()
                    for cc in range(2):
                        mm(pk[0:64, :n], wkvh[:, cc, 0:64], ck[cc][0][:, :n], cc == 0, cc == 1, [wslr, ck[cc][1]], [pkr])
                    fw.op(dve, lambda h: h.tensor_copy(out=KTb[b][0:64, kk0:kk0 + n], in_=pk[0:64, :n]), reads=[pkr], writes=[RKV[b]])
                    pv, pvr = ps()
                    nt_ = n // 128
                    for tl in range(nt_):
                        for cc in range(2):
                            mm(pv[:, tl * 64:(tl + 1) * 64], ck[cc][0][:, tl * 128:(tl + 1) * 128], wkvh[:, cc, 64:128], cc == 0, cc == 1,
                               [wslr, ck[cc][1]], [pvr])
                    fw.op(dve, lambda h: h.tensor_copy(out=VAb[b][:, kb * 4:kb * 4 + nt_, voff:voff + 64],
                                                       in_=pv[:, 0:nt_ * 64].rearrange("p (t c) -> p t c", t=nt_)),
                          reads=[pvr], writes=[RKV[b]])
                    yield

            kvg = [None]

            def kv_step():
                if kvg[0] is not None:
                    try:
                        next(kvg[0])
                    except StopIteration:
                        kvg[0] = None

            CQ = [T16[:, 3 + cc, :] for cc in range(3)]
            RCQ = [RT16[3 + cc] for cc in range(3)]
            ropet = {}

            def q_load(hd, qi):
                bi, (t0, n) = qbs[qi]
                for cc in range(3):
                    fw.dma(sp, CQ[cc][:, :n], Dcq[cc * 128:(cc + 1) * 128, t0:t0 + n], reads=[Rc["Dcqn"]], writes=[RCQ[cc]])
                if t0 < NL:
                    a1, a1r = t32()
                    a2, a2r = t32()
                    fw.dma(sp, a1[64:96, :], rope[0, :, t0:t0 + 512], writes=[a1r])
                    fw.dma(sp, a2[64:96, :], rope[1, :, t0:t0 + 512], writes=[a2r])
                    ropet[(hd, qi)] = (a1, a1r, a2, a2r)

            def q_compute(hd, qi):
                bi, (t0, n) = qbs[qi]
                wqh, wqph, wslr = HW[hd]
                lat = t0 < NL
                pq, pqr = ps()
                for cc in range(3):
                    mm(pq[0:96, :n], wqh[:, cc, :], CQ[cc][:, :n], cc == 0, cc == 2, [wslr, RCQ[cc]], [pqr])
                q16, q16r = QT[:, qti[0] % 2, :], RQT[qti[0] % 2]
                qti[0] += 1
                if lat:
                    pp, ppr = ps()
                    for cc in range(3):
                        mm(pp[0:96, :n], wqph[:, cc, :], CQ[cc][:, :n], cc == 0, cc == 2, [wslr, RCQ[cc]], [ppr])
                    fw.op(dve, lambda h: h.tensor_copy(out=q16[0:64, :], in_=pq[0:64, :]), reads=[pqr], writes=[q16r])
                    a1, a1r, a2, a2r = ropet.pop((hd, qi))
                    fw.op(dve, lambda h: h.tensor_tensor(out=a1[64:96, :], in0=pq[64:96, :], in1=a1[64:96, :], op=ALU.mult),
                          reads=[pqr, a1r], writes=[a1r])
                    fw.op(dve, lambda h: h.tensor_tensor(out=a2[64:96, :], in0=pp[64:96, :], in1=a2[64:96, :], op=ALU.mult),
                          reads=[ppr, a2r], writes=[a2r])
                    fw.op(dve, lambda h: h.tensor_tensor(out=q16[64:96, :], in0=a1[64:96, :], in1=a2[64:96, :], op=ALU.add),
                          reads=[a1r, a2r], writes=[q16r])
                    kts = list(range(34))
                else:
                    fw.op(dve, lambda h: h.tensor_copy(out=q16[0:96, :n], in_=pq[0:96, :n]), reads=[pqr], writes=[q16r])
                    kts = [32, 33]
                return (q16, q16r, kts, t0, n)

            def finish_block(hd, po, por, t0, n):
                b = hd % 2
                orow = slice(0, 64) if b == 0 else slice(64, 128)
                drow = slice(64, 128) if b == 0 else slice(0, 64)
                rd, rdr = t32()
                fw.op(dve, lambda h: h.reciprocal(out=rd[drow, :n], in_=po[drow, :n]), reads=[por], writes=[rdr])
                rsh, rshr = t32()
                fw.op(dve, lambda h: h.tensor_copy(out=rsh[orow, :n], in_=rd[drow, :n]), reads=[rdr], writes=[rshr])
                ao, aor = t16()
                fw.op(dve, lambda h: h.tensor_tensor(out=ao[orow, :n], in0=po[orow, :n], in1=rsh[orow, :n], op=ALU.mult),
                      reads=[por, rshr], writes=[aor])
                r0 = (hd // 2) * 128 + (0 if b == 0 else 64)
                fw.dma(pool, Data[r0:r0 + 64, t0:t0 + n], ao[orow, :n], reads=[aor], writes=[Rc["Dat"]])

            wslim[0] = 4
            handover([RWS[4], RWS[5]], RPT)
            psr[0], psr[1] = 4, 2
            t16lim[0] = 3
            blocks = [(hd, qi) for hd in range(8) for qi in range(len(qbs))]
            items = []
            for bk, (hd, qi) in enumerate(blocks):
                npair = 17 if qbs[qi][1][0] < NL else 1
                items += [(bk, j, npair) for j in range(npair)]
            qinfo = {}
            for _ in kv_build(0):
                pass
            q_load(0, 0)
            qinfo[0] = q_compute(0, 0)
            q_load(*blocks[1])
            pts = {}
            for idx in range(len(items) + LAG):
                if idx < len(items):
                    bk, j, npair = items[idx]
                    hd, qi = blocks[bk]
                    b = hd % 2
                    if j == 0 and qi == 0:
                        while kvg[0] is not None:
                            kv_step()
                        if hd < 7:
                            kvg[0] = kv_build(hd + 1)
                            kv_step()
                    if j == min(2, npair - 1) and bk + 1 < len(blocks):
                        qinfo[bk + 1] = q_compute(*blocks[bk + 1])
                        if bk + 2 < len(blocks):
                            q_load(*blocks[bk + 2])
                    q16, q16r, kts, t0, n = qinfo[bk]
                    p2, p2r = psd()
                    for a in range(2):
                        kt = kts[2 * j + a]
                        mm(p2[:, a * 512:a * 512 + n], KTb[b][0:96, kt * 128:(kt + 1) * 128], q16[0:96, :n], True, True,
                           [RKV[b], q16r], [p2r[a]])
                    k = pti[0] % 4
                    pti[0] += 1
                    pt2, pt2r = PTd[k], RPT[k]
                    if n == 512:
                        fw.op(act, lambda h: h.activation(out=pt2[:, :], in_=p2[:, :], func=AF.Exp, scale=ATTN_SCALE),
                              reads=p2r, writes=[pt2r])
                    else:
                        fw.op(act, lambda h: h.activation(out=pt2[:, :].rearrange("p (a c) -> p a c", a=2)[:, :, :n],
                                                          in_=p2[:, :].rearrange("p (a c) -> p a c", a=2)[:, :, :n],
                                                          func=AF.Exp, scale=ATTN_SCALE), reads=p2r, writes=[pt2r])
                    pts[idx] = (pt2, pt2r)
                    if j % 5 == 3:
                        kv_step()
                jdx = idx - LAG
                if jdx >= 0:
                    bk2, j2, npair2 = items[jdx]
                    hd2, qi2 = blocks[bk2]
                    b2 = hd2 % 2
                    q16_, q16r_, kts2, t02, n2 = qinfo[bk2]
                    po, por = PS[6 + bk2 % 2], RPS[6 + bk2 % 2]
                    pt2, pt2r = pts.pop(jdx)
                    for a in range(2):
                        mm(po[:, :n2], VAb[b2][:, kts2[2 * j2 + a], :], pt2[:, a * 512:a * 512 + n2], j2 == 0 and a == 0,
                           j2 == npair2 - 1 and a == 1, [RKV[b2], pt2r], [por], sig=(True if a == 1 else None))
                    if j2 == npair2 - 1:
                        finish_block(hd2, po, por, t02, n2)
            t16lim[0] = NT16
            psr[0], psr[1] = 0, NROT
            handover(RPT, [RWS[4], RWS[5]])
            wslim[0] = NWS
            if mstop <= 4:
                return

            mixacc = AR[:, :].rearrange("p (c t) -> p c t", c=KC)
            RM = Res("mix")
            handover([RKV[0], RKV[1], RA], [RM, RA, RKV[0], RKV[1]])
            branches = [(Dsv.ap(), Rc["Dsv"], 3, w_pw, 96), (Dat.ap(), Rc["Dat"], 4, w_o, None), (DFa, Rc["DF"], 4, w_fo, 104)]
            first = True
            for r, (Dsrc, Dres, nch, wsrc, bcol) in enumerate(branches):
                wbg = [wload(kcview(w_bg[l], r * D + s * 256, 256), (KC, 256)) for s in range(4)]
                wr_ = [wload(wsrc[l].rearrange("(c p) n -> p c n", p=128)[:, :, s * 512:(s + 1) * 512], (nch, 512)) for s in range(2)]
                for bi, (t0, n) in tbs_c:
                    xin_ = []
                    for c in range(nch):
                        c16, c16r = t16()
                        fw.dma(sp, c16[:, :n], Dsrc[c * 128:(c + 1) * 128, t0:t0 + n], reads=[Dres], writes=[c16r])
                        xin_.append((c16, c16r))
                    for m in range(KC):
                        pgt, pgtr = ps()
                        wv, wr = wbg[m // 2]
                        for kc in range(KC):
                            mm(pgt[:, :n], wv[:, kc, (m % 2) * 128:(m % 2) * 128 + 128], uT[:, kc, t0:t0 + n], kc == 0, kc == KC - 1, [wr, Ru[bi]], [pgtr])
                        s, sr = t32()
                        fw.op(act, lambda h, s=s, pgt=pgt, n=n, r=r, m=m: h.activation(out=s[:, :n], in_=pgt[:, :n], func=AF.Sigmoid,
                                                                                     bias=vb[:, r * 8 + m:r * 8 + m + 1], scale=1.0),
                              reads=[pgtr, Rc["VT"]], writes=[sr])
                        py, pyr = ps()
                        wv2, wr2 = wr_[m // 4]
                        for c in range(nch):
                            mm(py[:, :n], wv2[:, c, (m % 4) * 128:(m % 4) * 128 + 128], xin_[c][0][:, :n], c == 0, c == nch - 1, [wr2, xin_[c][1]], [pyr])
                        bias = va[:, bcol + m:bcol + m + 1] if bcol is not None else 0.0
                        if first:
                            fw.op(dve, lambda h, py=py, s=s, n=n, m=m, t0=t0, bias=bias: h.scalar_tensor_tensor(
                                out=mixacc[:, m, t0:t0 + n], in0=py[:, :n], scalar=bias, in1=s[:, :n], op0=ALU.add, op1=ALU.mult),
                                reads=[pyr, sr, Rc["VT"]], writes=[RM])
                        else:
                            fw.op(dve, lambda h, py=py, s=s, n=n, bias=bias: h.scalar_tensor_tensor(
                                out=s[:, :n], in0=py[:, :n], scalar=bias, in1=s[:, :n], op0=ALU.add, op1=ALU.mult),
                                reads=[pyr, sr, Rc["VT"]], writes=[sr])
                            fw.op(dve, lambda h, s=s, n=n, m=m, t0=t0: h.tensor_tensor(
                                out=mixacc[:, m, t0:t0 + n], in0=mixacc[:, m, t0:t0 + n], in1=s[:, :n], op=ALU.add),
                                reads=[sr, RM], writes=[RM])
                first = False
            for mo in range(KC):
                wv, wr = wload(kcview(w_out[l], mo * 128, 128), (KC, 128))
                for bi, (t0, n) in tbs_c:
                    ts = 0 if t0 < NL else 1
                    po, por = ps()
                    for m in range(KC):
                        mm(po[:, :n], wv[:, m, :], mixacc[:, m, t0:t0 + n], m == 0, m == KC - 1, [wr, RM], [por])
                    fw.op(dve, lambda h, po=po, mo=mo, t0=t0, n=n, ts=ts: h.scalar_tensor_tensor(
                        out=hT[:, mo, t0:t0 + n], in0=po[:, :n], scalar=DER[:, 4, mo, ts:ts + 1],
                        in1=hT[:, mo, t0:t0 + n], op0=ALU.mult, op1=ALU.add),
                        reads=[por, Rh[bi], Rc["DER"]], writes=[Rh[bi]])
            handover([RM, RKV[0], RKV[1], RA], [RA, RAB[0], RAB[1]])

        obi = [0]
        qti = [0]
        tbs_all = list(enumerate(TBS))
        stage = 0
        done = False
        for l in range(DEPTH):
            last = l == DEPTH - 1
            mod_stage(l)
            norm_stage(0, tbs_all)
            ffn_stage(l, w1a, w3a, w2a, 0, tbs_all)
            stage += 1
            if stage >= stop:
                done = True
                break
            norm_stage(1, tbs_all)
            handover([RAB[0], RAB[1], RA], [RA, RAB[0], RAB[1]])
            ms = (stop - stage) if (stop - stage) < 6 else 99
            mixer_stage(l, last, ms)
            stage += 5
            if stage >= stop:
                done = True
                break
            tb2 = tbs_all if not last else tbs_all[:4]
            norm_stage(2, tb2)
            ffn_stage(l, w1b, w3b, w2b, 2, tb2)
            stage += 1
            if stage >= stop:
                done = True
                break

        vg = VT[:, 0, :]
        for bi, (t0, n) in tbs_all[:4]:
            if not done:
                pst, pstr = ps()
                for kc in range(KC):
                    sq, sqr = t16()
                    fw.op(act, lambda h, sq=sq, kc=kc: h.activation(out=sq[:, :n], in_=hT[:, kc, t0:t0 + n], func=AF.Square),
                          reads=[Rh[bi]], writes=[sqr])
                    mm(pst[:, :n], ones16[:], sq[:, :n], kc == 0, kc == KC - 1, [sqr, Rc["ones"]], [pstr], sig=True)
                rs, rsr = rstd_from((pst, pstr), n, 1.0 / D)
                for kc in range(KC):
                    fw.op(dve, lambda h, kc=kc, rs=rs: h.scalar_tensor_tensor(
                        out=hT[:, kc, t0:t0 + n], in0=hT[:, kc, t0:t0 + n], scalar=vg[:, 16 + kc:17 + kc], in1=rs[:, :n],
                        op0=ALU.mult, op1=ALU.mult), reads=[Rh[bi], rsr, Rc["VT"]], writes=[Rh[bi]])
            for tl in range(4):
                tt = (t0 // 128) + tl
                for half in range(2):
                    p, pr = ps()
                    for kk in range(4):
                        kc = half * 4 + kk
                        fw.op(pe, lambda h, p=p, kk=kk, kc=kc, tt=tt: h.transpose(out=p[:, kk * 128:(kk + 1) * 128],
                                                                                 in_=hT[:, kc, tt * 128:(tt + 1) * 128], identity=ident[:]),
                              reads=[Rh[bi], Rc["ident"]], writes=[pr], sig=(kk == 3))
                    o, orr = t32()
                    if half == 0:
                        fw.op(dve, lambda h, o=o, p=p: h.tensor_copy(out=o[:, :], in_=p[:, :]), reads=[pr], writes=[orr])
                    else:
                        fw.op(act, lambda h, o=o, p=p: h.copy(out=o[:, :], in_=p[:, :]), reads=[pr], writes=[orr])
                    fw.dma(sp, yout[tt * 128:(tt + 1) * 128, half * 512:(half + 1) * 512], o[:, :], reads=[orr], writes=[Res()])
        for ds in fw.dsems:
            if ds.count:
                fw._wait(sp, ds.sem, ds.count)
        fw.run()
    return nc


def _tables(half):
    bf = ml_dtypes.bfloat16
    t = np.arange(SEQ, dtype=np.int64)
    k = np.arange(NL, dtype=np.int64) + half * NL
    ang = 2.0 * np.pi * ((t[:, None] * k[None, :]) % SEQ).astype(np.float64) / SEQ
    tab = np.stack([np.cos(ang) / 64.0, -np.sin(ang) / 64.0], axis=1)
    tab = tab.reshape(32, 128, 2, 4, 512).transpose(3, 0, 1, 2, 4)
    dft = np.ascontiguousarray(tab).astype(bf)
    tc = np.arange(NCX, dtype=np.int64)
    angc = 2.0 * np.pi * ((tc[:, None] * tc[None, :]) % NCX).astype(np.float64) / NCX
    tabc = np.stack([np.cos(angc) / 16.0, -np.sin(angc) / 16.0], axis=1).reshape(2, 128, 2, NCX)
    dftc = np.ascontiguousarray(tabc).astype(bf)
    return dft, dftc


def _rope(half):
    tok = np.arange(NL) + half * NL
    row = (tok // 64).astype(np.float32)
    col = (tok % 64).astype(np.float32)
    inv = (1.0 / (np.float32(10000.0) ** (np.arange(8, dtype=np.float32) * np.float32(2.0) / np.float32(16)))).astype(np.float32)
    ar = row[:, None] * inv
    ac = col[:, None] * inv
    ang = np.concatenate([ar, ar, ac, ac], axis=-1).astype(np.float32)
    cos = np.cos(ang).astype(np.float32).T
    sin = np.sin(ang).astype(np.float32).T
    sign = np.ones(32, np.float32)
    sign[0:8] = -1.0
    sign[16:24] = -1.0
    return np.ascontiguousarray(np.stack([cos, sin * sign[:, None]], 0)).astype(np.float32)


_PERM = np.array([(f + 8) if (f % 16) < 8 else (f - 8) for f in range(32)])


def _prep(inputs):
    f32 = np.float32
    g = {k: np.asarray(v, dtype=f32) for k, v in inputs.items()}
    L = DEPTH
    vecs_l = np.zeros((L, 3, 128, 128), f32)
    for l in range(L):
        A = vecs_l[l, 0]
        A[0:72] = g["b_ada"][l].reshape(72, 128)
        A[72:80] = g["g_ffn1"][l].reshape(8, 128)
        A[80:88] = g["g_mix"][l].reshape(8, 128)
        A[88:96] = g["g_ffn2"][l].reshape(8, 128)
        A[96:104] = g["b_pw_conv"][l].reshape(8, 128)
        A[104:112] = g["b_fourier"][l].reshape(8, 128)
        B = vecs_l[l, 1]
        B[0:24] = g["b_bgate"][l].reshape(24, 128)
        B[24:27] = g["b_dw"][l].reshape(3, 128)
        B[27:30] = g["ln_g_conv"][l].reshape(3, 128)
        B[30:33] = g["ln_b_conv"][l].reshape(3, 128)
        B[33:36] = g["g_qnorm"][l].reshape(3, 128)
        B[36:38] = g["g_kvnorm"][l].reshape(2, 128)
        vecs_l[l, 2, 0:93] = g["w_dw"][l].reshape(31 * 3, 128)
    w_krp = np.zeros((L, D, 192), f32)
    w_krp[:, :, 64:96] = g["w_in"][:, :, 1408:1440]
    w_krp[:, :, 160:192] = g["w_in"][:, :, 1408 + _PERM]
    w_uqp = np.zeros((L, 384, 768), f32)
    for hd in range(8):
        w_uqp[:, :, hd * 96 + 64:hd * 96 + 96] = g["w_uq"][:, :, hd * 96 + 64 + _PERM]
    cm = np.arange(128)
    angc = 2.0 * np.pi * ((cm[:, None] * cm[None, :]) % 128).astype(np.float64) / 128.0
    ccsc = np.stack([np.cos(angc), np.sin(angc)], axis=1) / np.sqrt(128.0)
    ccsc = np.ascontiguousarray(ccsc).astype(f32)
    shared = {k: np.ascontiguousarray(g[k]) for k in
              ("w_ada", "w1_ffn1", "w3_ffn1", "w2_ffn1", "w1_ffn2", "w3_ffn2", "w2_ffn2", "w_in", "w_pw_conv",
               "w_uq", "w_ukv", "w_o_mla", "w_fourier", "w_bgate", "w_out")}
    shared["w_krp"] = w_krp
    shared["w_uqp"] = w_uqp
    shared["ccsc"] = ccsc
    tabs = [_tables(0), _tables(1)]
    ropes = [_rope(0), _rope(1)]
    maps = []
    for c in range(8):
        b, half = c // 2, c % 2
        vecs = np.zeros((7, 128, 128), f32)
        vecs[0, 0:8] = g["c"][b].reshape(8, 128)
        vecs[0, 8:16] = g["c_ctx"].reshape(8, 128)
        vecs[0, 16:24] = g["g_final"].reshape(8, 128)
        vecs[1:4] = vecs_l[0]
        vecs[4:7] = vecs_l[1]
        m = dict(shared)
        m["xin"] = np.ascontiguousarray(np.concatenate([g["x"][b, half * NL:(half + 1) * NL], g["ctx"][b]], 0))
        m["vecs"] = vecs
        m["rope"] = ropes[half]
        m["dft"], m["dftc"] = tabs[half]
        mk = np.zeros((128, 2), f32)
        mk[:, 0] = 1.0 if half == 1 else 0.0
        mk[:, 1] = 1.0 if half == 0 else 0.0
        m["maskd"] = mk
        maps.append(m)
    return maps


def run(inputs, stop=99, cores=8, dbg=False, ret_all=False):
    nc = build(stop, dbg)
    maps = _prep(inputs)
    res = run_bass_kernel_spmd(nc, maps[:cores], core_ids=list(range(cores)))
    if ret_all:
        return res.results
    out = np.zeros((4, SEQ, D), np.float32)
    for c in range(cores):
        b, half = c // 2, c % 2
        out[b, half * NL:(half + 1) * NL] = res.results[c]["yout"]
    return out


def kernel(**inputs):
    return run(inputs)
```

```python
import numpy as np
import ml_dtypes
from contextlib import ExitStack
import concourse.bass as bass
import concourse.mybir as mybir
from concourse.bass_utils import run_bass_kernel_spmd

F32 = mybir.dt.float32
BF16 = mybir.dt.bfloat16
ALU = mybir.AluOpType
AF = mybir.ActivationFunctionType

D = 1024
KC = 8
NL = 2048
NCX = 256
NT = NL + NCX
SEQ = 4096
DFF = 2816
NJ = DFF // 128
DEPTH = 2
EPS = 1e-6
ATTN_SCALE = 96.0 ** -0.5
TBS = [(0, 512), (512, 512), (1024, 512), (1536, 512), (2048, 256)]
SAME_ENGINE_SYNC = True
FUSE_WAIT = True


class Res:
    __slots__ = ("w", "r", "name")

    def __init__(self, name=""):
        self.w = None
        self.r = {}
        self.name = name


class Eng:
    def __init__(self, name, sem):
        self.name, self.sem = name, sem
        self.count = 0
        self.seen = {}
        self.q = []


class DmaSem:
    def __init__(self, sem):
        self.sem = sem
        self.count = 0


class _Rec:
    def __init__(self):
        self.call = None

    def __getattr__(self, name):
        def f(*a, **k):
            self.call = (name, a, k)
            return self
        return f


class FW:
    def __init__(self, nc, stack, n_dma_sems=32):
        self.nc = nc
        mk = lambda n: stack.enter_context(nc.semaphore(n))
        self.pe = Eng("pe", mk("s_pe"))
        self.act = Eng("act", mk("s_act"))
        self.dve = Eng("dve", mk("s_dve"))
        self.pool = Eng("pool", mk("s_pool"))
        self.sp = Eng("sp", mk("s_sp"))
        self.dsems_q = {"sp": [DmaSem(mk(f"s_dma{i}")) for i in range(n_dma_sems // 2)],
                        "pool": [DmaSem(mk(f"s_dmp{i}")) for i in range(n_dma_sems // 2)]}
        self.dsems = self.dsems_q["sp"] + self.dsems_q["pool"]
        self.dnext_q = {"sp": 0, "pool": 0}
        self.ccs = DmaSem(mk("s_cc"))

    def _wait(self, eng, sem, val):
        key = id(sem)
        if eng.seen.get(key, 0) >= val:
            return
        eng.q.append(lambda h, sem=sem, val=val: h.wait_ge(sem, val))
        eng.seen[key] = val

    def _need(self, eng, sem, val, pend):
        key = id(sem)
        if eng.seen.get(key, 0) >= val:
            return
        eng.seen[key] = val
        for i, (s2, v2) in enumerate(pend):
            if s2 is sem:
                pend[i] = (sem, max(val, v2))
                return
        pend.append((sem, val))

    def _flush(self, eng, pend):
        if not pend:
            return None
        for (sem, val) in pend[:-1]:
            eng.q.append(lambda h, sem=sem, val=val: h.wait_ge(sem, val))
        return pend[-1] if FUSE_WAIT else (eng.q.append(lambda h, sem=pend[-1][0], val=pend[-1][1]: h.wait_ge(sem, val)) or None)

    def _deps(self, eng, reads, writes):
        deps = []
        for r in reads:
            if r.w is not None:
                deps.append(r.w)
        for w in writes:
            if w.w is not None:
                deps.append(w.w)
            deps.extend(w.r.values())
        pend = []
        for (sem, val, src) in deps:
            if src is eng and (eng.name == "pe" or not SAME_ENGINE_SYNC):
                continue
            self._need(eng, sem, val, pend)
        return pend

    def _commit(self, ev, reads, writes):
        key = id(ev[0])
        for r in reads:
            old = r.r.get(key)
            if old is None or old[1] < ev[1]:
                r.r[key] = ev
        for w in writes:
            w.w = ev
            w.r = {}

    def op(self, eng, fn, reads=(), writes=(), sig=True):
        fz = self._flush(eng, self._deps(eng, reads, writes))
        rec = _Rec()
        fn(rec)
        name, a, k = rec.call

        def emit(h, name=name, a=a, k=k, fz=fz, sem=eng.sem, sig=sig):
            inst = getattr(h, name)(*a, **k)
            if fz is not None:
                inst._wait_ge(fz[0], fz[1])
            if sig:
                inst.then_inc(sem, 1)
        eng.q.append(emit)
        if sig:
            eng.count += 1
            ev = (eng.sem, eng.count, eng)
        else:
            ev = (eng.sem, eng.count + 1, eng)
        self._commit(ev, reads, writes)
        return ev

    def dma(self, q, out, in_, reads=(), writes=()):
        pend = self._deps(q, reads, writes)
        pool_ = self.dsems_q[q.name]
        ds = pool_[self.dnext_q[q.name]]
        self.dnext_q[q.name] = (self.dnext_q[q.name] + 1) % len(pool_)
        if ds.count:
            self._need(q, ds.sem, ds.count, pend)
        fz = self._flush(q, pend)
        ds.count += 16

        def emit(h, out=out, in_=in_, sem=ds.sem, fz=fz):
            inst = h.dma_start(out=out, in_=in_)
            if fz is not None:
                inst._wait_ge(fz[0], fz[1])
            inst.then_inc(sem, 16)
        q.q.append(emit)
        ev = (ds.sem, ds.count, None)
        self._commit(ev, reads, writes)
        return ev

    def allgather(self, src, dst, reads, writes):
        q = self.pool
        pend = self._deps(q, reads, writes)
        cs = self.ccs
        if cs.count:
            self._need(q, cs.sem, cs.count, pend)
        for (sem, val) in pend:
            q.q.append(lambda h, sem=sem, val=val: h.wait_ge(sem, val))
        cs.count += 1
        q.q.append(lambda h, src=src, dst=dst, sem=cs.sem: h.collective_compute(
            "AllGather", ALU.bypass, replica_groups=[[0, 1], [2, 3], [4, 5], [6, 7]],
            ins=[src], outs=[dst]).then_inc(sem, 1))
        ev = (cs.sem, cs.count, None)
        self._commit(ev, reads, writes)
        return ev

    def run(self):
        nc = self.nc
        with nc.Block() as block:
            @block.tensor
            def _(e):
                for f in self.pe.q:
                    f(e)

            @block.scalar
            def _(e):
                for f in self.act.q:
                    f(e)

            @block.vector
            def _(e):
                for f in self.dve.q:
                    f(e)

            @block.gpsimd
            def _(e):
                for f in self.pool.q:
                    f(e)

            @block.sync
            def _(e):
                for f in self.sp.q:
                    f(e)


def build(stop=99, dbg=False):
    nc = bass.Bass("TRN2", target_bir_lowering=False)
    di = lambda n, s, d=F32: nc.dram_tensor(n, list(s), d, kind="ExternalInput").ap()
    xin = di("xin", [NT, D])
    vecs = di("vecs", [7, 128, 128])
    rope = di("rope", [2, 32, NL])
    dft = di("dft", [4, 32, 128, 2, 512], BF16)
    dftc = di("dftc", [2, 128, 2, 256], BF16)
    ccsc = di("ccsc", [128, 2, 128])
    maskd = di("maskd", [128, 2])
    w_ada = di("w_ada", [DEPTH, D, 9 * D])
    w1a = di("w1_ffn1", [DEPTH, D, DFF]); w3a = di("w3_ffn1", [DEPTH, D, DFF]); w2a = di("w2_ffn1", [DEPTH, DFF, D])
    w1b = di("w1_ffn2", [DEPTH, D, DFF]); w3b = di("w3_ffn2", [DEPTH, D, DFF]); w2b = di("w2_ffn2", [DEPTH, DFF, D])
    w_in = di("w_in", [DEPTH, D, 1952])
    w_krp = di("w_krp", [DEPTH, D, 192])
    w_pw = di("w_pw_conv", [DEPTH, 384, D])
    w_uq = di("w_uq", [DEPTH, 384, 768])
    w_uqp = di("w_uqp", [DEPTH, 384, 768])
    w_ukv = di("w_ukv", [DEPTH, 256, 1024])
    w_o = di("w_o_mla", [DEPTH, 512, D])
    w_fo = di("w_fourier", [DEPTH, 512, D])
    w_bg = di("w_bgate", [DEPTH, D, 3 * D])
    w_out = di("w_out", [DEPTH, D, D])
    yout = nc.dram_tensor("yout", [NL, D], F32, kind="ExternalOutput").ap()

    dt_ = lambda n, s: nc.dram_tensor(n, list(s), BF16)
    XA = dt_("XA", [288, NT]); GA = dt_("GA", [576, NT])
    XG = dt_("XG", [NL, 512]); GG = dt_("GG", [2 * NL, 512]); XGc = dt_("XGc", [NCX, 512])
    XH = dt_("XH", [128, 90]); GH = dt_("GH", [256, 90])
    dd = (lambda n, s: nc.dram_tensor(n, list(s), BF16, kind="ExternalOutput")) if dbg else dt_
    Dcqn = dd("Dcqn", [3 * 128, NT]); Dsv = dd("Dsv", [3 * 128, NT])
    DF = dd("DF", [4 * 128, NT]); Dat = dd("Dat", [4 * 128, NT])

    with ExitStack() as st:
        fw = FW(nc, st)
        pe, act, dve, pool, sp = fw.pe, fw.act, fw.dve, fw.pool, fw.sp
        sb = lambda n, s, d: st.enter_context(nc.sbuf_tensor(n, list(s), d))

        hT = sb("hT", [128, KC, NT], F32)
        Rh = [Res(f"h{i}") for i in range(5)]
        UR = sb("UR", [128, KC * NT], BF16)
        uT = UR[:, :].rearrange("p (c t) -> p c t", c=KC)
        Ru = [Res(f"u{i}") for i in range(5)]
        AR = sb("AR", [128, KC * NT], BF16)
        RA = Res("A")
        RAB = [Res("A0"), Res("A1")]

        PSn = 8
        PSD = [st.enter_context(nc.psum_tensor(f"psd{i}", [128, 1024], F32)) for i in range(PSn // 2)]
        PS = [PSD[i // 2][:, (i % 2) * 512:(i % 2 + 1) * 512] for i in range(PSn)]
        RPS = [Res(f"ps{i}") for i in range(PSn)]
        psdi = [0]

        def psd():
            k = psdi[0] % 2
            psdi[0] += 1
            return PSD[k], [RPS[2 * k], RPS[2 * k + 1]]
        psi = [0]
        NROT = 6

        psr = [0, NROT]

        def ps():
            i = psr[0] + psi[0] % psr[1]
            psi[0] += 1
            return PS[i], RPS[i]

        T32 = sb("T32", [128, 8, 512], F32)
        RT32 = [Res(f"t32_{i}") for i in range(8)]
        t32i = [0]

        def t32():
            i = t32i[0] % 8
            t32i[0] += 1
            return T32[:, i, :], RT32[i]

        NT16 = 6
        T16 = sb("T16", [128, NT16, 512], BF16)
        RT16 = [Res(f"t16_{i}") for i in range(NT16)]
        t16i = [0]

        t16lim = [NT16]

        def t16():
            i = t16i[0] % t16lim[0]
            t16i[0] += 1
            return T16[:, i, :], RT16[i]

        NWS = 6
        WS = sb("WS", [128, NWS, 2048], BF16)
        RWS = [Res(f"ws{i}") for i in range(NWS)]
        wsi = [0]

        wslim = [NWS]

        def ws():
            i = wsi[0] % wslim[0]
            wsi[0] += 1
            return WS[:, i, :], RWS[i]

        PTd = [WS[:, 4 + k // 2, (k % 2) * 1024:(k % 2 + 1) * 1024] for k in range(4)]
        RPT = [Res(f"pt{k}") for k in range(4)]
        pti = [0]

        ident = sb("ident", [128, 128], F32)
        ones32 = sb("ones32", [128, 128], F32)
        ones16 = sb("ones16", [128, 128], BF16)
        cst = sb("cst", [128, 4], F32)
        VT = sb("VT", [128, 7, 128], F32)
        scb = sb("scb", [128, KC, 2], BF16)
        MOD = sb("MOD", [128, 72, 2], F32)
        DER = sb("DER", [128, 6, KC, 2], F32)
        MODn = sb("MODn", [128, 72, 2], F32)
        DERn = sb("DERn", [128, 6, KC, 2], F32)
        CS32 = sb("CS32", [128, 2, 128], F32)
        MSK = sb("MSK", [128, 2], F32)
        TT2 = sb("TT2", [128, 2, 512], F32)
        RTT2 = [Res("tt2_0"), Res("tt2_1")]
        QT = sb("QT", [128, 2, 512], BF16)
        RQT = [Res("qt0"), Res("qt1")]
        vTc = sb("vTc", [128, 3, 286], BF16)
        HT = sb("HT", [128, 2, 90], BF16)
        Rc = {k: Res(k) for k in "ident ones cst cst2 VT scb MOD DER MODn DERn CS32 MSK vTc HT XA GA XG XGc GG XH GH Dcqn Dsv DF Dat".split()}

        def mm(out, lhsT, rhs, start, stop, reads, writes, sig=None):
            fw.op(pe, lambda h: h.matmul(out, lhsT=lhsT, rhs=rhs, start=start, stop=stop),
                  reads=reads, writes=writes, sig=(stop if sig is None else sig))

        def handover(srcs, dsts):
            fw.op(dve, lambda h: h.memset(cst[:, 2:3], 0.0), reads=list(srcs), writes=list(dsts) + [Rc["cst2"]])

        fw.op(pool, lambda h: h.memset(ident[:], 1.0), writes=[Rc["ident"]])
        fw.op(pool, lambda h: h.affine_select(out=ident[:], in_=ident[:], pattern=[[-1, 128]],
                                              compare_op=ALU.is_equal, fill=0.0, base=0, channel_multiplier=1),
              reads=[Rc["ident"]], writes=[Rc["ident"]])
        fw.op(pool, lambda h: h.memset(ones32[:], 1.0), writes=[Rc["ones"]])
        fw.op(pool, lambda h: h.memset(ones16[:], 1.0), writes=[Rc["ones"]])
        fw.op(pool, lambda h: h.memset(cst[:, 0:1], EPS), writes=[Rc["cst"]])
        fw.op(pool, lambda h: h.memset(cst[:, 1:2], 0.0), writes=[Rc["cst"]])
        fw.op(pool, lambda h: h.memset(vTc[:], 0.0), writes=[Rc["vTc"]])
        fw.dma(sp, CS32[:], ccsc[:, :, :], writes=[Rc["CS32"]])
        fw.dma(sp, MSK[:], maskd[:, :], writes=[Rc["MSK"]])
        for i in range(7):
            t, r = t32()
            fw.dma(sp, t[:, 0:128], vecs[i, :, :], writes=[r])
            p, pr = ps()
            fw.op(pe, lambda h, p=p, t=t: h.transpose(out=p[:, 0:128], in_=t[:, 0:128], identity=ident[:]),
                  reads=[r, Rc["ident"]], writes=[pr])
            fw.op(dve, lambda h, p=p, i=i: h.tensor_copy(out=VT[:, i, :], in_=p[:, 0:128]), reads=[pr], writes=[Rc["VT"]])
        VG = VT[:, 0, :]
        VA = lambda l: VT[:, 1 + 3 * l, :]
        VB = lambda l: VT[:, 2 + 3 * l, :]
        VC = lambda l: VT[:, 3 + 3 * l, :]
        for t in range(2):
            fw.op(act, lambda h, t=t: h.activation(out=scb[:, :, t], in_=VG[:, 8 * t:8 * t + 8], func=AF.Silu),
                  reads=[Rc["VT"]], writes=[Rc["scb"]])

        for ti in range(NT // 128):
            bi = min(ti // 4, 4)
            for half in range(2):
                t, r = t32()
                fw.dma(sp, t, xin[ti * 128:(ti + 1) * 128, half * 512:(half + 1) * 512], writes=[r])
                p, pr = ps()
                for kk in range(4):
                    fw.op(pe, lambda h, p=p, t=t, kk=kk: h.transpose(out=p[:, kk * 128:(kk + 1) * 128],
                                                                     in_=t[:, kk * 128:(kk + 1) * 128], identity=ident[:]),
                          reads=[r, Rc["ident"]], writes=[pr], sig=(kk == 3))
                eng = dve if half == 0 else act
                if half == 0:
                    fw.op(dve, lambda h, p=p, ti=ti, half=half: h.tensor_copy(
                        out=hT[:, half * 4:half * 4 + 4, ti * 128:(ti + 1) * 128],
                        in_=p[:, :].rearrange("p (c t) -> p c t", c=4)), reads=[pr], writes=[Rh[bi]])
                else:
                    fw.op(act, lambda h, p=p, ti=ti, half=half: h.copy(
                        out=hT[:, half * 4:half * 4 + 4, ti * 128:(ti + 1) * 128],
                        in_=p[:, :].rearrange("p (c t) -> p c t", c=4)), reads=[pr], writes=[Rh[bi]])

        def wload(src_ap, shape3):
            w, wr = ws()
            a, b = shape3
            v = w[:, 0:a * b].rearrange("p (a b) -> p a b", a=a)
            fw.dma(pool, v, src_ap, writes=[wr])
            return v, wr

        def kcview(wap, c0, n):
            return wap[:, c0:c0 + n].rearrange("(kc p) n -> p kc n", p=128)

        def rstd_from(pst, n, scale):
            rs, rsr = t32()
            fw.op(act, lambda h: h.activation(out=rs[:, :n], in_=pst[0][:, :n], func=AF.Sqrt, bias=cst[:, 0:1], scale=scale),
                  reads=[pst[1], Rc["cst"]], writes=[rsr])
            fw.op(dve, lambda h: h.reciprocal(out=rs[:, :n], in_=rs[:, :n]), reads=[rsr], writes=[rsr])
            return rs, rsr

        def mod_stage(l):
            pm, pmr = ps()
            for s in range(36):
                wv, wr = wload(kcview(w_ada[l], s * 256, 256), (KC, 256))
                for jj in range(2):
                    j = s * 2 + jj
                    for kc in range(KC):
                        mm(pm[:, j * 2:j * 2 + 2], wv[:, kc, jj * 128:(jj + 1) * 128], scb[:, kc, :], kc == 0, kc == KC - 1,
                           [wr, Rc["scb"]], [pmr])
            pmv = pm[:, 0:144].rearrange("p (j t) -> p j t", t=2)
            va = VA(l)
            for t in range(2):
                fw.op(dve, lambda h, t=t: h.tensor_tensor(out=MOD[:, :, t], in0=pmv[:, :, t], in1=va[:, 0:72], op=ALU.add),
                      reads=[pmr, Rc["VT"]], writes=[Rc["MOD"]])
                for i, (gcol, n) in enumerate([(72, 1), (80, 4), (88, 7)]):
                    fw.op(dve, lambda h, t=t, i=i, n=n: h.tensor_scalar(out=DER[:, i, :, t], in0=MOD[:, n * 8:(n + 1) * 8, t],
                                                                       scalar1=1.0, scalar2=None, op0=ALU.add),
                          reads=[Rc["MOD"]], writes=[Rc["DER"]])
                    fw.op(dve, lambda h, t=t, i=i, gcol=gcol: h.tensor_tensor(out=DER[:, i, :, t], in0=DER[:, i, :, t],
                                                                              in1=va[:, gcol:gcol + 8], op=ALU.mult),
                          reads=[Rc["DER"], Rc["VT"]], writes=[Rc["DER"]])
                for i, (n, f) in enumerate([(2, 0.5), (5, 1.0), (8, 0.5)]):
                    fw.op(dve, lambda h, t=t, i=i, n=n, f=f: h.tensor_scalar(out=DER[:, 3 + i, :, t], in0=MOD[:, n * 8:(n + 1) * 8, t],
                                                                             scalar1=f, scalar2=None, op0=ALU.mult),
                          reads=[Rc["MOD"]], writes=[Rc["DER"]])

        modg = [None]

        def mod_step():
            if modg[0] is not None:
                try:
                    next(modg[0])
                except StopIteration:
                    modg[0] = None

        def mod_gen(l):
            va = VA(l)
            slots = [(WS[:, 2 + i, 0:KC * 256].rearrange("p (a b) -> p a b", a=KC), RWS[2 + i]) for i in range(2)]

            def load(s_):
                wv, wr = slots[s_ % 2]
                fw.dma(pool, wv, kcview(w_ada[l], s_ * 256, 256), writes=[wr])
            load(0)
            load(1)
            yield
            for s_ in range(36):
                wv, wr = slots[s_ % 2]
                pm, pmr = ps()
                for jj in range(2):
                    for kc in range(KC):
                        mm(pm[:, jj * 2:jj * 2 + 2], wv[:, kc, jj * 128:(jj + 1) * 128], scb[:, kc, :], kc == 0, kc == KC - 1,
                           [wr, Rc["scb"]], [pmr])
                pmv = pm[:, 0:4].rearrange("p (j t) -> p j t", t=2)
                for t in range(2):
                    fw.op(dve, lambda h: h.tensor_tensor(out=MODn[:, 2 * s_:2 * s_ + 2, t], in0=pmv[:, :, t],
                                                         in1=va[:, 2 * s_:2 * s_ + 2], op=ALU.add),
                          reads=[pmr, Rc["VT"]], writes=[Rc["MODn"]])
                if s_ + 2 < 36:
                    load(s_ + 2)
                yield
            for t in range(2):
                for i, (gcol, n) in enumerate([(72, 1), (80, 4), (88, 7)]):
                    fw.op(dve, lambda h: h.tensor_scalar(out=DERn[:, i, :, t], in0=MODn[:, n * 8:(n + 1) * 8, t],
                                                         scalar1=1.0, scalar2=None, op0=ALU.add),
                          reads=[Rc["MODn"]], writes=[Rc["DERn"]])
                    fw.op(dve, lambda h: h.tensor_tensor(out=DERn[:, i, :, t], in0=DERn[:, i, :, t],
                                                         in1=va[:, gcol:gcol + 8], op=ALU.mult),
                          reads=[Rc["DERn"], Rc["VT"]], writes=[Rc["DERn"]])
                for i, (n, f) in enumerate([(2, 0.5), (5, 1.0), (8, 0.5)]):
                    fw.op(dve, lambda h: h.tensor_scalar(out=DERn[:, 3 + i, :, t], in0=MODn[:, n * 8:(n + 1) * 8, t],
                                                         scalar1=f, scalar2=None, op0=ALU.mult),
                          reads=[Rc["MODn"]], writes=[Rc["DERn"]])

        def norm_stage(idx, tbs):
            for bi, (t0, n) in tbs:
                ts = 0 if t0 < NL else 1
                pst, pstr = ps()
                for kc in range(KC):
                    sq, sqr = t16()
                    fw.op(act, lambda h, sq=sq, kc=kc: h.activation(out=sq[:, :n], in_=hT[:, kc, t0:t0 + n], func=AF.Square),
                          reads=[Rh[bi]], writes=[sqr])
                    mm(pst[:, :n], ones16[:], sq[:, :n], kc == 0, kc == KC - 1, [sqr, Rc["ones"]], [pstr], sig=True)
                rs, rsr = rstd_from((pst, pstr), n, 1.0 / D)
                for kc in range(KC):
                    tt, ttr = TT2[:, kc % 2, :], RTT2[kc % 2]
                    fw.op(dve, lambda h, tt=tt, kc=kc: h.scalar_tensor_tensor(
                        out=tt[:, :n], in0=hT[:, kc, t0:t0 + n], scalar=DER[:, idx, kc, ts:ts + 1], in1=rs[:, :n],
                        op0=ALU.mult, op1=ALU.mult), reads=[Rh[bi], rsr, Rc["DER"]], writes=[ttr])
                    fw.op(act, lambda h, tt=tt, kc=kc: h.activation(
                        out=uT[:, kc, t0:t0 + n], in_=tt[:, :n], func=AF.Identity,
                        bias=MOD[:, 3 * idx * 8 + kc, ts:ts + 1], scale=1.0), reads=[ttr, Rc["MOD"]], writes=[Ru[bi]])

        def ffn_stage(l, w1, w3, w2, gidx, tbs):
            groups = [(0, 4), (4, 4), (8, 4), (12, 4), (16, 4), (20, 2)]
            for gi, (j0, nj) in enumerate(groups):
                half = gi % 2
                ab = AR[:, half * 4 * NT:(half + 1) * 4 * NT].rearrange("p (c t) -> p c t", c=4)
                abr = RAB[half]
                for sub in range(nj // 2):
                    c0 = (j0 + sub * 2) * 128
                    w1v, w1r = wload(kcview(w1[l], c0, 256), (KC, 256))
                    w3v, w3r = wload(kcview(w3[l], c0, 256), (KC, 256))
                    for jj in range(2):
                        ja = sub * 2 + jj
                        for bi, (t0, n) in tbs:
                            p1, p1r = ps()
                            p3, p3r = ps()
                            for kc in range(KC):
                                mm(p1[:, :n], w1v[:, kc, jj * 128:(jj + 1) * 128], uT[:, kc, t0:t0 + n], kc == 0, kc == KC - 1,
                                   [w1r, Ru[bi]], [p1r])
                            for kc in range(KC):
                                mm(p3[:, :n], w3v[:, kc, jj * 128:(jj + 1) * 128], uT[:, kc, t0:t0 + n], kc == 0, kc == KC - 1,
                                   [w3r, Ru[bi]], [p3r])
                            s, sr = t32()
                            fw.op(act, lambda h, s=s, p1=p1, n=n: h.activation(out=s[:, :n], in_=p1[:, :n], func=AF.Silu),
                                  reads=[p1r], writes=[sr])
                            fw.op(dve, lambda h, s=s, p3=p3, n=n, ja=ja, t0=t0, ab=ab: h.tensor_tensor(
                                out=ab[:, ja, t0:t0 + n], in0=s[:, :n], in1=p3[:, :n], op=ALU.mult),
                                reads=[sr, p3r], writes=[abr])
                w2v = []
                for sub in range(nj // 2):
                    r0 = (j0 + sub * 2) * 128
                    w2v.append(wload(w2[l, r0:r0 + 256, :].rearrange("(j p) n -> p j n", p=128), (2, D)))
                for bi, (t0, n) in tbs:
                    ts = 0 if t0 < NL else 1
                    for m in range(KC):
                        po, por = ps()
                        for ja in range(nj):
                            wv, wr = w2v[ja // 2]
                            mm(po[:, :n], wv[:, ja % 2, m * 128:(m + 1) * 128], ab[:, ja, t0:t0 + n], ja == 0, ja == nj - 1,
                               [wr, abr], [por])
                        fw.op(dve, lambda h, po=po, m=m, t0=t0, n=n, ts=ts: h.scalar_tensor_tensor(
                            out=hT[:, m, t0:t0 + n], in0=po[:, :n], scalar=DER[:, 3 + gidx, m, ts:ts + 1],
                            in1=hT[:, m, t0:t0 + n], op0=ALU.mult, op1=ALU.add),
                            reads=[por, Rh[bi], Rc["DER"]], writes=[Rh[bi]])

        def mixer_stage(l, last, mstop):
            tbs_all = list(enumerate(TBS))
            tbs_c = tbs_all if not last else tbs_all[:4]
            vb = VB(l)
            vc = VC(l)
            va = VA(l)
            vTl = AR[:, 0:3 * 2078].rearrange("p (c t) -> p c t", c=3)
            DG = AR[:, 6234:6234 + 93 * 128].rearrange("p (j m) -> p j m", j=93)

            wc = [wload(kcview(w_in[l], s * 256, 256), (KC, 256)) for s in range(3)]
            for bi, (t0, n) in tbs_c:
                for cc in range(3):
                    pa, par = ps()
                    pg, pgr = ps()
                    ca, cg = cc * 128, 384 + cc * 128
                    wa, war = wc[ca // 256]
                    wg, wgr = wc[cg // 256]
                    for kc in range(KC):
                        mm(pa[:, :n], wa[:, kc, ca % 256:ca % 256 + 128], uT[:, kc, t0:t0 + n], kc == 0, kc == KC - 1, [war, Ru[bi]], [par])
                    for kc in range(KC):
                        mm(pg[:, :n], wg[:, kc, cg % 256:cg % 256 + 128], uT[:, kc, t0:t0 + n], kc == 0, kc == KC - 1, [wgr, Ru[bi]], [pgr])
                    s, sr = t32()
                    fw.op(act, lambda h, s=s, pg=pg, n=n: h.activation(out=s[:, :n], in_=pg[:, :n], func=AF.Sigmoid), reads=[pgr], writes=[sr])
                    if t0 < NL:
                        fw.op(dve, lambda h, s=s, pa=pa, n=n, cc=cc, t0=t0: h.tensor_tensor(
                            out=vTl[:, cc, 15 + t0:15 + t0 + n], in0=s[:, :n], in1=pa[:, :n], op=ALU.mult), reads=[sr, par], writes=[RA])
                    else:
                        fw.op(dve, lambda h, s=s, pa=pa, n=n, cc=cc: h.tensor_tensor(
                            out=vTc[:, cc, 15:15 + n], in0=s[:, :n], in1=pa[:, :n], op=ALU.mult), reads=[sr, par], writes=[Rc["vTc"]])
            XHv = XH.ap().rearrange("p (c t) -> p c t", c=3)
            fw.dma(pool, XHv[:, :, 0:15], vTl[:, :, 15:30], reads=[RA], writes=[Rc["XH"]])
            fw.dma(pool, XHv[:, :, 15:30], vTl[:, :, 15 + NL - 15:15 + NL], reads=[RA], writes=[Rc["XH"]])

            wq = [wload(kcview(w_in[l], 768, 256), (KC, 256)), wload(kcview(w_in[l], 1024, 128), (KC, 128))]
            wkv = wload(kcview(w_in[l], 1152, 256), (KC, 256))
            wkr = wload(kcview(w_krp[l], 0, 192), (KC, 192))
            XAa = XA.ap()
            Dcq = Dcqn.ap()
            for bi, (t0, n) in tbs_all:
                lat = t0 < NL
                for (nch, wsel, gcol, dst, need) in ((3, "q", 33, Dcq, (lat or not last)), (2, "kv", 36, XAa, True)):
                    if not need:
                        continue
                    pcs = []
                    for cc in range(nch):
                        pq, pqr = ps()
                        if wsel == "q":
                            wv, wr = wq[0] if cc < 2 else wq[1]
                            col = (cc % 2) * 128 if cc < 2 else 0
                        else:
                            wv, wr = wkv
                            col = cc * 128
                        for kc in range(KC):
                            mm(pq[:, :n], wv[:, kc, col:col + 128], uT[:, kc, t0:t0 + n], kc == 0, kc == KC - 1, [wr, Ru[bi]], [pqr])
                        pcs.append((pq, pqr))
                    pst, pstr = ps()
                    for cc in range(nch):
                        sq, sqr = t16()
                        fw.op(act, lambda h, sq=sq, pq=pcs[cc][0], n=n: h.activation(out=sq[:, :n], in_=pq[:, :n], func=AF.Square),
                              reads=[pcs[cc][1]], writes=[sqr])
                        mm(pst[:, :n], ones16[:], sq[:, :n], cc == 0, cc == nch - 1, [sqr, Rc["ones"]], [pstr])
                    rs, rsr = rstd_from((pst, pstr), n, 1.0 / (128 * nch))
                    for cc in range(nch):
                        o16, o16r = t16()
                        fw.op(dve, lambda h, o16=o16, pq=pcs[cc][0], n=n, cc=cc, gcol=gcol, rs=rs: h.scalar_tensor_tensor(
                            out=o16[:, :n], in0=pq[:, :n], scalar=vb[:, gcol + cc:gcol + cc + 1], in1=rs[:, :n],
                            op0=ALU.mult, op1=ALU.mult), reads=[pcs[cc][1], rsr, Rc["VT"]], writes=[o16r])
                        fw.dma(pool, dst[cc * 128:(cc + 1) * 128, t0:t0 + n], o16[:, :n], reads=[o16r],
                               writes=[Rc["Dcqn"] if wsel == "q" else Rc["XA"]])
                pk, pkr = ps()
                pp, ppr = ps()
                for kc in range(KC):
                    mm(pk[0:96, :n], wkr[0][:, kc, 0:96], uT[:, kc, t0:t0 + n], kc == 0, kc == KC - 1, [wkr[1], Ru[bi]], [pkr])
                for kc in range(KC):
                    mm(pp[0:96, :n], wkr[0][:, kc, 96:192], uT[:, kc, t0:t0 + n], kc == 0, kc == KC - 1, [wkr[1], Ru[bi]], [ppr])
                o16, o16r = t16()
                if lat:
                    a1, a1r = t32()
                    a2, a2r = t32()
                    fw.dma(sp, a1[64:96, :], rope[0, :, t0:t0 + 512], writes=[a1r])
                    fw.dma(sp, a2[64:96, :], rope[1, :, t0:t0 + 512], writes=[a2r])
                    fw.op(dve, lambda h, a1=a1, pk=pk: h.tensor_tensor(out=a1[64:96, :], in0=pk[64:96, :], in1=a1[64:96, :], op=ALU.mult),
                          reads=[pkr, a1r], writes=[a1r])
                    fw.op(dve, lambda h, a2=a2, pp=pp: h.tensor_tensor(out=a2[64:96, :], in0=pp[64:96, :], in1=a2[64:96, :], op=ALU.mult),
                          reads=[ppr, a2r], writes=[a2r])
                    fw.op(dve, lambda h, a1=a1, a2=a2, o16=o16: h.tensor_tensor(out=o16[64:96, :], in0=a1[64:96, :], in1=a2[64:96, :], op=ALU.add),
                          reads=[a1r, a2r], writes=[o16r])
                else:
                    fw.op(dve, lambda h, o16=o16, pk=pk, n=n: h.tensor_copy(out=o16[64:96, :n], in_=pk[64:96, :n]), reads=[pkr], writes=[o16r])
                fw.dma(pool, XAa[256:288, t0:t0 + n], o16[64:96, :n], reads=[o16r], writes=[Rc["XA"]])

            wf = [wload(kcview(w_in[l], 1440 + s * 256, 256), (KC, 256)) for s in range(2)]
            XGa = XG.ap()
            ntt = 18 if not last else 16
            for tt in range(ntt):
                bi = min(tt // 4, 4)
                pgm, pgr = ps()
                for s in range(2):
                    for kc in range(KC):
                        mm(pgm[:, s * 256:(s + 1) * 256], uT[:, kc, tt * 128:(tt + 1) * 128], wf[s][0][:, kc, :], kc == 0, kc == KC - 1,
                           [wf[s][1], Ru[bi]], [pgr])
                o16, o16r = t16()
                if tt % 2 == 0:
                    fw.op(dve, lambda h, o16=o16, pgm=pgm: h.tensor_copy(out=o16[:, :], in_=pgm[:, :]), reads=[pgr], writes=[o16r])
                else:
                    fw.op(act, lambda h, o16=o16, pgm=pgm: h.copy(out=o16[:, :], in_=pgm[:, :]), reads=[pgr], writes=[o16r])
                if tt < 16:
                    fw.dma(pool, XGa[tt * 128:(tt + 1) * 128, :], o16[:, :], reads=[o16r], writes=[Rc["XG"]])
                else:
                    fw.dma(pool, XGc.ap()[(tt - 16) * 128:(tt - 15) * 128, :], o16[:, :], reads=[o16r], writes=[Rc["XGc"]])

            fw.allgather(XH.ap(), GH.ap(), [Rc["XH"]], [Rc["GH"]])
            fw.allgather(XA.ap(), GA.ap(), [Rc["XA"]], [Rc["GA"]])
            fw.allgather(XG.ap(), GG.ap(), [Rc["XG"]], [Rc["GG"]])
            if mstop <= 1:
                return

            GHa = GH.ap()
            fw.dma(sp, HT[:, :, :], GHa.rearrange("(r p) f -> p r f", p=128), reads=[Rc["GH"]], writes=[Rc["HT"]])
            HTv = HT[:, :, :].rearrange("p r (c t) -> p r c t", c=3)
            fw.op(dve, lambda h: h.tensor_scalar(out=vTl[:, :, 0:15], in0=HTv[:, 0, :, 15:30], scalar1=MSK[:, 0:1], scalar2=None, op0=ALU.mult),
                  reads=[Rc["HT"], Rc["MSK"]], writes=[RA])
            fw.op(dve, lambda h: h.tensor_scalar(out=vTl[:, :, 15 + NL:30 + NL], in0=HTv[:, 1, :, 0:15], scalar1=MSK[:, 1:2], scalar2=None, op0=ALU.mult),
                  reads=[Rc["HT"], Rc["MSK"]], writes=[RA])
            for idx in range(93):
                fw.op(dve, lambda h, idx=idx: h.tensor_scalar(out=DG[:, idx, :], in0=ident[:], scalar1=vc[:, idx:idx + 1], scalar2=None, op0=ALU.mult),
                      reads=[Rc["ident"], Rc["VT"]], writes=[RA])
            Dsva = Dsv.ap()
            for bi, (t0, n) in tbs_c:
                lat = t0 < NL
                cos_ = []
                for cc in range(3):
                    pc, pcr = ps()
                    for j in range(31):
                        rhs = vTl[:, cc, t0 + j:t0 + j + n] if lat else vTc[:, cc, j:j + n]
                        mm(pc[:, :n], DG[:, j * 3 + cc, :], rhs, j == 0, j == 30, [RA] if lat else [RA, Rc["vTc"]], [pcr])
                    co, cor = t32()
                    fw.op(act, lambda h, co=co, pc=pc, n=n, cc=cc: h.activation(out=co[:, :n], in_=pc[:, :n], func=AF.Identity,
                                                                              bias=vb[:, 24 + cc:25 + cc], scale=1.0),
                          reads=[pcr, Rc["VT"]], writes=[cor])
                    cos_.append((co, cor))
                pss, pssr = ps()
                psq, psqr = ps()
                for cc in range(3):
                    mm(pss[:, :n], ones32[:], cos_[cc][0][:, :n], cc == 0, cc == 2, [cos_[cc][1], Rc["ones"]], [pssr])
                for cc in range(3):
                    sq, sqr = t16()
                    fw.op(act, lambda h, sq=sq, co=cos_[cc][0], n=n: h.activation(out=sq[:, :n], in_=co[:, :n], func=AF.Square),
                          reads=[cos_[cc][1]], writes=[sqr])
                    mm(psq[:, :n], ones16[:], sq[:, :n], cc == 0, cc == 2, [sqr, Rc["ones"]], [psqr])
                mu, mur = t32()
                fw.op(dve, lambda h, mu=mu, pss=pss, n=n: h.tensor_scalar(out=mu[:, :n], in0=pss[:, :n], scalar1=1.0 / 384, scalar2=None, op0=ALU.mult),
                      reads=[pssr], writes=[mur])
                m2, m2r = t32()
                fw.op(dve, lambda h, mu=mu, m2=m2, n=n: h.tensor_tensor(out=m2[:, :n], in0=mu[:, :n], in1=mu[:, :n], op=ALU.mult), reads=[mur], writes=[m2r])
                fw.op(dve, lambda h, m2=m2, psq=psq, n=n: h.scalar_tensor_tensor(out=m2[:, :n], in0=psq[:, :n], scalar=1.0 / 384, in1=m2[:, :n],
                                                                               op0=ALU.mult, op1=ALU.subtract), reads=[psqr, m2r], writes=[m2r])
                fw.op(act, lambda h, m2=m2, n=n: h.activation(out=m2[:, :n], in_=m2[:, :n], func=AF.Sqrt, bias=cst[:, 0:1], scale=1.0),
                      reads=[m2r, Rc["cst"]], writes=[m2r])
                fw.op(dve, lambda h, m2=m2, n=n: h.reciprocal(out=m2[:, :n], in_=m2[:, :n]), reads=[m2r], writes=[m2r])
                for cc in range(3):
                    co, cor = cos_[cc]
                    fw.op(dve, lambda h, co=co, mu=mu, n=n: h.tensor_tensor(out=co[:, :n], in0=co[:, :n], in1=mu[:, :n], op=ALU.subtract),
                          reads=[cor, mur], writes=[cor])
                    fw.op(dve, lambda h, co=co, m2=m2, n=n: h.tensor_tensor(out=co[:, :n], in0=co[:, :n], in1=m2[:, :n], op=ALU.mult),
                          reads=[cor, m2r], writes=[cor])
                    o16, o16r = t16()
                    fw.op(act, lambda h, co=co, o16=o16, n=n, cc=cc: h.activation(out=o16[:, :n], in_=co[:, :n], func=AF.Silu,
                                                                                bias=vb[:, 30 + cc:31 + cc], scale=vb[:, 27 + cc:28 + cc]),
                          reads=[cor, Rc["VT"]], writes=[o16r])
                    fw.dma(pool, Dsva[cc * 128:(cc + 1) * 128, t0:t0 + n], o16[:, :n], reads=[o16r], writes=[Rc["Dsv"]])
            if mstop <= 2:
                return

            gfull = AR[:, 0:32 * 512].rearrange("p (t c) -> p t c", t=32)
            GGa = GG.ap()
            DFa = DF.ap()
            for r in range(2):
                fw.dma(sp, gfull[:, r * 16:(r + 1) * 16, :], GGa[r * NL:(r + 1) * NL, :].rearrange("(t p) c -> p t c", p=128),
                       reads=[Rc["GG"]], writes=[RA])

            def stage2(P, Pr, Q, Qr, n, dst):
                pS, pSr = t32()
                qS, qSr = t32()
                fw.op(dve, lambda h: h.tensor_copy(out=pS[:, :n], in_=P[:, :n]), reads=[Pr], writes=[pSr])
                fw.op(act, lambda h: h.copy(out=qS[:, :n], in_=Q[:, :n]), reads=[Qr], writes=[qSr])
                mm(P[:, :n], CS32[:, 0, :], pS[:, :n], True, False, [pSr, Rc["CS32"]], [Pr])
                mm(P[:, :n], CS32[:, 1, :], qS[:, :n], False, True, [qSr, Rc["CS32"]], [Pr])
                o16, o16r = t16()
                fw.op(act, lambda h: h.copy(out=o16[:, :n], in_=P[:, :n]), reads=[Pr], writes=[o16r])
                fw.dma(pool, dst, o16[:, :n], reads=[o16r], writes=[Rc["DF"]])

            for kb in range(4):
                for ti in range(32):
                    tab, tabr = ws()
                    tv = tab[:, 0:1024].rearrange("p (s k) -> p s k", s=2)
                    fw.dma(sp, tv, dft[kb, ti, :, :, :], writes=[tabr])
                    for gi in range(4):
                        mm(PS[gi][:, :], gfull[:, ti, gi * 128:(gi + 1) * 128], tv[:, 0, :], ti == 0, ti == 31, [RA, tabr], [RPS[gi]])
                        mm(PS[4 + gi][:, :], gfull[:, ti, gi * 128:(gi + 1) * 128], tv[:, 1, :], ti == 0, ti == 31, [RA, tabr], [RPS[4 + gi]],
                           sig=(True if gi == 3 else None))
                for gi in range(4):
                    stage2(PS[gi], RPS[gi], PS[4 + gi], RPS[4 + gi], 512, DFa[gi * 128:(gi + 1) * 128, kb * 512:(kb + 1) * 512])
            if not last:
                gc, gcr = ws()
                gcv = gc[:, 0:1024].rearrange("p (t c) -> p t c", t=2)
                fw.dma(sp, gcv, XGc.ap().rearrange("(t p) c -> p t c", p=128), reads=[Rc["XGc"]], writes=[gcr])
                tc_, tcr = ws()
                tcv = tc_[:, 0:1024].rearrange("p (t s k) -> p t s k", t=2, s=2)
                fw.dma(sp, tcv, dftc.rearrange("t p s k -> p t s k"), writes=[tcr])
                for gi in range(4):
                    P, Pr = PS[gi], RPS[gi]
                    Q, Qr = PS[4 + gi], RPS[4 + gi]
                    for tl in range(2):
                        mm(P[:, :256], gcv[:, tl, gi * 128:(gi + 1) * 128], tcv[:, tl, 0, :], tl == 0, tl == 1, [gcr, tcr], [Pr])
                    for tl in range(2):
                        mm(Q[:, :256], gcv[:, tl, gi * 128:(gi + 1) * 128], tcv[:, tl, 1, :], tl == 0, tl == 1, [gcr, tcr], [Qr])
                    stage2(P, Pr, Q, Qr, 256, DFa[gi * 128:(gi + 1) * 128, NL:NT])
            if mstop <= 3:
                return

            GAa = GA.ap()
            Data = Dat.ap()
            NK = SEQ + NCX
            KTb = [AR[:, b * 8704:b * 8704 + NK] for b in range(2)]
            VAb = [AR[:, b * 8704 + NK:(b + 1) * 8704].rearrange("p (t c) -> p t c", t=34) for b in range(2)]
            RKV = [Res("kv0"), Res("kv1")]
            fw.op(dve, lambda h: h.memset(VAb[0][:, :, 64:128], 1.0), reads=[RA], writes=[RA, RKV[0]])
            fw.op(dve, lambda h: h.memset(VAb[1][:, :, 0:64], 1.0), reads=[RA], writes=[RA, RKV[1]])
            qbs = tbs_c
            HW = {}
            LAG = 2

            def kv_build(hd):
                b = hd % 2
                voff = 0 if b == 0 else 64
                wsl, wslr = ws()
                wkvh = wsl[:, 0:256].rearrange("p (c n) -> p c n", c=2)
                wqh = wsl[:, 256:256 + 288].rearrange("p (c n) -> p c n", c=3)
                wqph = wsl[:, 544:544 + 288].rearrange("p (c n) -> p c n", c=3)
                fw.dma(pool, wkvh, w_ukv[l][:, hd * 128:(hd + 1) * 128].rearrange("(c p) n -> p c n", p=128), writes=[wslr])
                fw.dma(pool, wqh, w_uq[l][:, hd * 96:(hd + 1) * 96].rearrange("(c p) n -> p c n", p=128), writes=[wslr])
                fw.dma(pool, wqph, w_uqp[l][:, hd * 96:(hd + 1) * 96].rearrange("(c p) n -> p c n", p=128), writes=[wslr])
                HW[hd] = (wqh, wqph, wslr)
                yield
                for kb in range(9):
                    if kb < 8:
                        r, c0, n = kb // 4, (kb % 4) * 512, 512
                    else:
                        r, c0, n = 0, NL, NCX
                    kk0 = kb * 512
                    ck = []
                    for cc in range(2):
                        c16, c16r = t16()
                        fw.dma(sp, c16[:, :n], GAa[r * 288 + cc * 128:r * 288 + (cc + 1) * 128, c0:c0 + n], reads=[Rc["GA"]], writes=[c16r])
                        ck.append((c16, c16r))
                    fw.dma(sp, KTb[b][64:96, kk0:kk0 + n], GAa[r * 288 + 256:r * 288 + 288, c0:c0 + n], reads=[Rc["GA"]], writes=[RKV[b]])
                    pk, pkr = ps()
                    for cc in range(2):
                        mm(pk[0:64, :n], wkvh[:, cc, 0:64], ck[cc][0][:, :n], cc == 0, cc == 1, [wslr, ck[cc][1]], [pkr])
                    fw.op(dve, lambda h: h.tensor_copy(out=KTb[b][0:64, kk0:kk0 + n], in_=pk[0:64, :n]), reads=[pkr], writes=[RKV[b]])
                    pv, pvr = ps()
                    nt_ = n // 128
                    for tl in range(nt_):
                        for cc in range(2):
                            mm(pv[:, tl * 64:(tl + 1) * 64], ck[cc][0][:, tl * 128:(tl + 1) * 128], wkvh[:, cc, 64:128], cc == 0, cc == 1,
                               [wslr, ck[cc][1]], [pvr])
                    fw.op(dve, lambda h: h.tensor_copy(out=VAb[b][:, kb * 4:kb * 4 + nt_, voff:voff + 64],
                                                       in_=pv[:, 0:nt_ * 64].rearrange("p (t c) -> p t c", t=nt_)),
                          reads=[pvr], writes=[RKV[b]])
                    yield

            kvg = [None]

            def kv_step():
                if kvg[0] is not None:
                    try:
                        next(kvg[0])
                    except StopIteration:
                        kvg[0] = None

            CQ = [T16[:, 3 + cc, :] for cc in range(3)]
            RCQ = [RT16[3 + cc] for cc in range(3)]
            ropet = {}

            def q_load(hd, qi):
                bi, (t0, n) = qbs[qi]
                for cc in range(3):
                    fw.dma(sp, CQ[cc][:, :n], Dcq[cc * 128:(cc + 1) * 128, t0:t0 + n], reads=[Rc["Dcqn"]], writes=[RCQ[cc]])
                if t0 < NL:
                    a1, a1r = t32()
                    a2, a2r = t32()
                    fw.dma(sp, a1[64:96, :], rope[0, :, t0:t0 + 512], writes=[a1r])
                    fw.dma(sp, a2[64:96, :], rope[1, :, t0:t0 + 512], writes=[a2r])
                    ropet[(hd, qi)] = (a1, a1r, a2, a2r)

            def q_compute(hd, qi):
                bi, (t0, n) = qbs[qi]
                wqh, wqph, wslr = HW[hd]
                lat = t0 < NL
                pq, pqr = ps()
                for cc in range(3):
                    mm(pq[0:96, :n], wqh[:, cc, :], CQ[cc][:, :n], cc == 0, cc == 2, [wslr, RCQ[cc]], [pqr])
                q16, q16r = QT[:, qti[0] % 2, :], RQT[qti[0] % 2]
                qti[0] += 1
                if lat:
                    pp, ppr = ps()
                    for cc in range(3):
                        mm(pp[0:96, :n], wqph[:, cc, :], CQ[cc][:, :n], cc == 0, cc == 2, [wslr, RCQ[cc]], [ppr])
                    fw.op(dve, lambda h: h.tensor_copy(out=q16[0:64, :], in_=pq[0:64, :]), reads=[pqr], writes=[q16r])
                    a1, a1r, a2, a2r = ropet.pop((hd, qi))
                    fw.op(dve, lambda h: h.tensor_tensor(out=a1[64:96, :], in0=pq[64:96, :], in1=a1[64:96, :], op=ALU.mult),
                          reads=[pqr, a1r], writes=[a1r])
                    fw.op(dve, lambda h: h.tensor_tensor(out=a2[64:96, :], in0=pp[64:96, :], in1=a2[64:96, :], op=ALU.mult),
                          reads=[ppr, a2r], writes=[a2r])
                    fw.op(dve, lambda h: h.tensor_tensor(out=q16[64:96, :], in0=a1[64:96, :], in1=a2[64:96, :], op=ALU.add),
                          reads=[a1r, a2r], writes=[q16r])
                    kts = list(range(34))
                else:
                    fw.op(dve, lambda h: h.tensor_copy(out=q16[0:96, :n], in_=pq[0:96, :n]), reads=[pqr], writes=[q16r])
                    kts = [32, 33]
                return (q16, q16r, kts, t0, n)

            def finish_block(hd, po, por, t0, n):
                b = hd % 2
                orow = slice(0, 64) if b == 0 else slice(64, 128)
                drow = slice(64, 128) if b == 0 else slice(0, 64)
                rd, rdr = t32()
                fw.op(dve, lambda h: h.reciprocal(out=rd[drow, :n], in_=po[drow, :n]), reads=[por], writes=[rdr])
                rsh, rshr = t32()
                fw.op(dve, lambda h: h.tensor_copy(out=rsh[orow, :n], in_=rd[drow, :n]), reads=[rdr], writes=[rshr])
                ao, aor = t16()
                fw.op(dve, lambda h: h.tensor_tensor(out=ao[orow, :n], in0=po[orow, :n], in1=rsh[orow, :n], op=ALU.mult),
                      reads=[por, rshr], writes=[aor])
                r0 = (hd // 2) * 128 + (0 if b == 0 else 64)
                fw.dma(pool, Data[r0:r0 + 64, t0:t0 + n], ao[orow, :n], reads=[aor], writes=[Rc["Dat"]])

            wslim[0] = 2 if modg[0] is not None else 4
            handover([RWS[4], RWS[5]], RPT)
            psr[0], psr[1] = 4, 2
            t16lim[0] = 3
            mod_step()
            blocks = [(hd, qi) for hd in range(8) for qi in range(len(qbs))]
            items = []
            for bk, (hd, qi) in enumerate(blocks):
                npair = 17 if qbs[qi][1][0] < NL else 1
                items += [(bk, j, npair) for j in range(npair)]
            qinfo = {}
            for _ in kv_build(0):
                pass
            q_load(0, 0)
            qinfo[0] = q_compute(0, 0)
            q_load(*blocks[1])
            pts = {}
            for idx in range(len(items) + LAG):
                if idx < len(items):
                    bk, j, npair = items[idx]
                    hd, qi = blocks[bk]
                    b = hd % 2
                    if j == 0 and qi == 0:
                        while kvg[0] is not None:
                            kv_step()
                        if hd < 7:
                            kvg[0] = kv_build(hd + 1)
                            kv_step()
                    if j == min(2, npair - 1) and bk + 1 < len(blocks):
                        qinfo[bk + 1] = q_compute(*blocks[bk + 1])
                        if bk + 2 < len(blocks):
                            q_load(*blocks[bk + 2])
                    q16, q16r, kts, t0, n = qinfo[bk]
                    p2, p2r = psd()
                    for a in range(2):
                        kt = kts[2 * j + a]
                        mm(p2[:, a * 512:a * 512 + n], KTb[b][0:96, kt * 128:(kt + 1) * 128], q16[0:96, :n], True, True,
                           [RKV[b], q16r], [p2r[a]])
                    k = pti[0] % 4
                    pti[0] += 1
                    pt2, pt2r = PTd[k], RPT[k]
                    if n == 512:
                        fw.op(act, lambda h: h.activation(out=pt2[:, :], in_=p2[:, :], func=AF.Exp, scale=ATTN_SCALE),
                              reads=p2r, writes=[pt2r])
                    else:
                        fw.op(act, lambda h: h.activation(out=pt2[:, :].rearrange("p (a c) -> p a c", a=2)[:, :, :n],
                                                          in_=p2[:, :].rearrange("p (a c) -> p a c", a=2)[:, :, :n],
                                                          func=AF.Exp, scale=ATTN_SCALE), reads=p2r, writes=[pt2r])
                    pts[idx] = (pt2, pt2r)
                    if j % 5 == 3:
                        kv_step()
                    if j % 5 == 1:
                        mod_step()
                jdx = idx - LAG
                if jdx >= 0:
                    bk2, j2, npair2 = items[jdx]
                    hd2, qi2 = blocks[bk2]
                    b2 = hd2 % 2
                    q16_, q16r_, kts2, t02, n2 = qinfo[bk2]
                    po, por = PS[6 + bk2 % 2], RPS[6 + bk2 % 2]
                    pt2, pt2r = pts.pop(jdx)
                    for a in range(2):
                        mm(po[:, :n2], VAb[b2][:, kts2[2 * j2 + a], :], pt2[:, a * 512:a * 512 + n2], j2 == 0 and a == 0,
                           j2 == npair2 - 1 and a == 1, [RKV[b2], pt2r], [por])
                    if j2 == npair2 - 1:
                        finish_block(hd2, po, por, t02, n2)
            t16lim[0] = NT16
            while modg[0] is not None:
                mod_step()
            psr[0], psr[1] = 0, NROT
            handover(RPT, [RWS[4], RWS[5]])
            wslim[0] = NWS
            if mstop <= 4:
                return

            mixacc = AR[:, :].rearrange("p (c t) -> p c t", c=KC)
            RM = Res("mix")
            handover([RKV[0], RKV[1], RA], [RM, RA, RKV[0], RKV[1]])
            branches = [(Dsv.ap(), Rc["Dsv"], 3, w_pw, 96), (Dat.ap(), Rc["Dat"], 4, w_o, None), (DFa, Rc["DF"], 4, w_fo, 104)]
            first = True
            for r, (Dsrc, Dres, nch, wsrc, bcol) in enumerate(branches):
                wbg = [wload(kcview(w_bg[l], r * D + s * 256, 256), (KC, 256)) for s in range(4)]
                wr_ = [wload(wsrc[l].rearrange("(c p) n -> p c n", p=128)[:, :, s * 512:(s + 1) * 512], (nch, 512)) for s in range(2)]
                for bi, (t0, n) in tbs_c:
                    xin_ = []
                    for c in range(nch):
                        c16, c16r = t16()
                        fw.dma(sp, c16[:, :n], Dsrc[c * 128:(c + 1) * 128, t0:t0 + n], reads=[Dres], writes=[c16r])
                        xin_.append((c16, c16r))
                    for m in range(KC):
                        pgt, pgtr = ps()
                        wv, wr = wbg[m // 2]
                        for kc in range(KC):
                            mm(pgt[:, :n], wv[:, kc, (m % 2) * 128:(m % 2) * 128 + 128], uT[:, kc, t0:t0 + n], kc == 0, kc == KC - 1, [wr, Ru[bi]], [pgtr])
                        s, sr = t32()
                        fw.op(act, lambda h, s=s, pgt=pgt, n=n, r=r, m=m: h.activation(out=s[:, :n], in_=pgt[:, :n], func=AF.Sigmoid,
                                                                                     bias=vb[:, r * 8 + m:r * 8 + m + 1], scale=1.0),
                              reads=[pgtr, Rc["VT"]], writes=[sr])
                        py, pyr = ps()
                        wv2, wr2 = wr_[m // 4]
                        for c in range(nch):
                            mm(py[:, :n], wv2[:, c, (m % 4) * 128:(m % 4) * 128 + 128], xin_[c][0][:, :n], c == 0, c == nch - 1, [wr2, xin_[c][1]], [pyr])
                        bias = va[:, bcol + m:bcol + m + 1] if bcol is not None else 0.0
                        if first:
                            fw.op(dve, lambda h, py=py, s=s, n=n, m=m, t0=t0, bias=bias: h.scalar_tensor_tensor(
                                out=mixacc[:, m, t0:t0 + n], in0=py[:, :n], scalar=bias, in1=s[:, :n], op0=ALU.add, op1=ALU.mult),
                                reads=[pyr, sr, Rc["VT"]], writes=[RM])
                        else:
                            fw.op(dve, lambda h, py=py, s=s, n=n, bias=bias: h.scalar_tensor_tensor(
                                out=s[:, :n], in0=py[:, :n], scalar=bias, in1=s[:, :n], op0=ALU.add, op1=ALU.mult),
                                reads=[pyr, sr, Rc["VT"]], writes=[sr])
                            fw.op(dve, lambda h, s=s, n=n, m=m, t0=t0: h.tensor_tensor(
                                out=mixacc[:, m, t0:t0 + n], in0=mixacc[:, m, t0:t0 + n], in1=s[:, :n], op=ALU.add),
                                reads=[sr, RM], writes=[RM])
                first = False
            for mo in range(KC):
                wv, wr = wload(kcview(w_out[l], mo * 128, 128), (KC, 128))
                for bi, (t0, n) in tbs_c:
                    ts = 0 if t0 < NL else 1
                    po, por = ps()
                    for m in range(KC):
                        mm(po[:, :n], wv[:, m, :], mixacc[:, m, t0:t0 + n], m == 0, m == KC - 1, [wr, RM], [por])
                    fw.op(dve, lambda h, po=po, mo=mo, t0=t0, n=n, ts=ts: h.scalar_tensor_tensor(
                        out=hT[:, mo, t0:t0 + n], in0=po[:, :n], scalar=DER[:, 4, mo, ts:ts + 1],
                        in1=hT[:, mo, t0:t0 + n], op0=ALU.mult, op1=ALU.add),
                        reads=[por, Rh[bi], Rc["DER"]], writes=[Rh[bi]])
            handover([RM, RKV[0], RKV[1], RA], [RA, RAB[0], RAB[1]])

        obi = [0]
        qti = [0]
        tbs_all = list(enumerate(TBS))
        stage = 0
        done = False
        for l in range(DEPTH):
            last = l == DEPTH - 1
            if l == 0 or stop < 99:
                mod_stage(l)
            else:
                fw.op(dve, lambda h: h.tensor_copy(out=MOD[:, :, :], in_=MODn[:, :, :]), reads=[Rc["MODn"]], writes=[Rc["MOD"]])
                fw.op(dve, lambda h: h.tensor_copy(out=DER[:, :, :, :], in_=DERn[:, :, :, :]), reads=[Rc["DERn"]], writes=[Rc["DER"]])
            norm_stage(0, tbs_all)
            ffn_stage(l, w1a, w3a, w2a, 0, tbs_all)
            stage += 1
            if stage >= stop:
                done = True
                break
            norm_stage(1, tbs_all)
            handover([RAB[0], RAB[1], RA], [RA, RAB[0], RAB[1]])
            ms = (stop - stage) if (stop - stage) < 6 else 99
            if l == 0 and stop >= 99:
                modg[0] = mod_gen(1)
            mixer_stage(l, last, ms)
            stage += 5
            if stage >= stop:
                done = True
                break
            tb2 = tbs_all if not last else tbs_all[:4]
            norm_stage(2, tb2)
            ffn_stage(l, w1b, w3b, w2b, 2, tb2)
            stage += 1
            if stage >= stop:
                done = True
                break

        vg = VT[:, 0, :]
        for bi, (t0, n) in tbs_all[:4]:
            if not done:
                pst, pstr = ps()
                for kc in range(KC):
                    sq, sqr = t16()
                    fw.op(act, lambda h, sq=sq, kc=kc: h.activation(out=sq[:, :n], in_=hT[:, kc, t0:t0 + n], func=AF.Square),
                          reads=[Rh[bi]], writes=[sqr])
                    mm(pst[:, :n], ones16[:], sq[:, :n], kc == 0, kc == KC - 1, [sqr, Rc["ones"]], [pstr], sig=True)
                rs, rsr = rstd_from((pst, pstr), n, 1.0 / D)
                for kc in range(KC):
                    fw.op(dve, lambda h, kc=kc, rs=rs: h.scalar_tensor_tensor(
                        out=hT[:, kc, t0:t0 + n], in0=hT[:, kc, t0:t0 + n], scalar=vg[:, 16 + kc:17 + kc], in1=rs[:, :n],
                        op0=ALU.mult, op1=ALU.mult), reads=[Rh[bi], rsr, Rc["VT"]], writes=[Rh[bi]])
            for tl in range(4):
                tt = (t0 // 128) + tl
                for half in range(2):
                    p, pr = ps()
                    for kk in range(4):
                        kc = half * 4 + kk
                        fw.op(pe, lambda h, p=p, kk=kk, kc=kc, tt=tt: h.transpose(out=p[:, kk * 128:(kk + 1) * 128],
                                                                                 in_=hT[:, kc, tt * 128:(tt + 1) * 128], identity=ident[:]),
                              reads=[Rh[bi], Rc["ident"]], writes=[pr], sig=(kk == 3))
                    o, orr = t32()
                    if half == 0:
                        fw.op(dve, lambda h, o=o, p=p: h.tensor_copy(out=o[:, :], in_=p[:, :]), reads=[pr], writes=[orr])
                    else:
                        fw.op(act, lambda h, o=o, p=p: h.copy(out=o[:, :], in_=p[:, :]), reads=[pr], writes=[orr])
                    fw.dma(sp, yout[tt * 128:(tt + 1) * 128, half * 512:(half + 1) * 512], o[:, :], reads=[orr], writes=[Res()])
        for ds in fw.dsems:
            if ds.count:
                fw._wait(sp, ds.sem, ds.count)
        fw.run()
    return nc


def _tables(half):
    bf = ml_dtypes.bfloat16
    t = np.arange(SEQ, dtype=np.int64)
    k = np.arange(NL, dtype=np.int64) + half * NL
    ang = 2.0 * np.pi * ((t[:, None] * k[None, :]) % SEQ).astype(np.float64) / SEQ
    tab = np.stack([np.cos(ang) / 64.0, -np.sin(ang) / 64.0], axis=1)
    tab = tab.reshape(32, 128, 2, 4, 512).transpose(3, 0, 1, 2, 4)
    dft = np.ascontiguousarray(tab).astype(bf)
    tc = np.arange(NCX, dtype=np.int64)
    angc = 2.0 * np.pi * ((tc[:, None] * tc[None, :]) % NCX).astype(np.float64) / NCX
    tabc = np.stack([np.cos(angc) / 16.0, -np.sin(angc) / 16.0], axis=1).reshape(2, 128, 2, NCX)
    dftc = np.ascontiguousarray(tabc).astype(bf)
    return dft, dftc


def _rope(half):
    tok = np.arange(NL) + half * NL
    row = (tok // 64).astype(np.float32)
    col = (tok % 64).astype(np.float32)
    inv = (1.0 / (np.float32(10000.0) ** (np.arange(8, dtype=np.float32) * np.float32(2.0) / np.float32(16)))).astype(np.float32)
    ar = row[:, None] * inv
    ac = col[:, None] * inv
    ang = np.concatenate([ar, ar, ac, ac], axis=-1).astype(np.float32)
    cos = np.cos(ang).astype(np.float32).T
    sin = np.sin(ang).astype(np.float32).T
    sign = np.ones(32, np.float32)
    sign[0:8] = -1.0
    sign[16:24] = -1.0
    return np.ascontiguousarray(np.stack([cos, sin * sign[:, None]], 0)).astype(np.float32)


_PERM = np.array([(f + 8) if (f % 16) < 8 else (f - 8) for f in range(32)])


def _prep(inputs):
    f32 = np.float32
    g = {k: np.asarray(v, dtype=f32) for k, v in inputs.items()}
    L = DEPTH
    vecs_l = np.zeros((L, 3, 128, 128), f32)
    for l in range(L):
        A = vecs_l[l, 0]
        A[0:72] = g["b_ada"][l].reshape(72, 128)
        A[72:80] = g["g_ffn1"][l].reshape(8, 128)
        A[80:88] = g["g_mix"][l].reshape(8, 128)
        A[88:96] = g["g_ffn2"][l].reshape(8, 128)
        A[96:104] = g["b_pw_conv"][l].reshape(8, 128)
        A[104:112] = g["b_fourier"][l].reshape(8, 128)
        B = vecs_l[l, 1]
        B[0:24] = g["b_bgate"][l].reshape(24, 128)
        B[24:27] = g["b_dw"][l].reshape(3, 128)
        B[27:30] = g["ln_g_conv"][l].reshape(3, 128)
        B[30:33] = g["ln_b_conv"][l].reshape(3, 128)
        B[33:36] = g["g_qnorm"][l].reshape(3, 128)
        B[36:38] = g["g_kvnorm"][l].reshape(2, 128)
        vecs_l[l, 2, 0:93] = g["w_dw"][l].reshape(31 * 3, 128)
    w_krp = np.zeros((L, D, 192), f32)
    w_krp[:, :, 64:96] = g["w_in"][:, :, 1408:1440]
    w_krp[:, :, 160:192] = g["w_in"][:, :, 1408 + _PERM]
    w_uqp = np.zeros((L, 384, 768), f32)
    for hd in range(8):
        w_uqp[:, :, hd * 96 + 64:hd * 96 + 96] = g["w_uq"][:, :, hd * 96 + 64 + _PERM]
    cm = np.arange(128)
    angc = 2.0 * np.pi * ((cm[:, None] * cm[None, :]) % 128).astype(np.float64) / 128.0
    ccsc = np.stack([np.cos(angc), np.sin(angc)], axis=1) / np.sqrt(128.0)
    ccsc = np.ascontiguousarray(ccsc).astype(f32)
    shared = {k: np.ascontiguousarray(g[k]) for k in
              ("w_ada", "w1_ffn1", "w3_ffn1", "w2_ffn1", "w1_ffn2", "w3_ffn2", "w2_ffn2", "w_in", "w_pw_conv",
               "w_uq", "w_ukv", "w_o_mla", "w_fourier", "w_bgate", "w_out")}
    shared["w_krp"] = w_krp
    shared["w_uqp"] = w_uqp
    shared["ccsc"] = ccsc
    tabs = [_tables(0), _tables(1)]
    ropes = [_rope(0), _rope(1)]
    maps = []
    for c in range(8):
        b, half = c // 2, c % 2
        vecs = np.zeros((7, 128, 128), f32)
        vecs[0, 0:8] = g["c"][b].reshape(8, 128)
        vecs[0, 8:16] = g["c_ctx"].reshape(8, 128)
        vecs[0, 16:24] = g["g_final"].reshape(8, 128)
        vecs[1:4] = vecs_l[0]
        vecs[4:7] = vecs_l[1]
        m = dict(shared)
        m["xin"] = np.ascontiguousarray(np.concatenate([g["x"][b, half * NL:(half + 1) * NL], g["ctx"][b]], 0))
        m["vecs"] = vecs
        m["rope"] = ropes[half]
        m["dft"], m["dftc"] = tabs[half]
        mk = np.zeros((128, 2), f32)
        mk[:, 0] = 1.0 if half == 1 else 0.0
        mk[:, 1] = 1.0 if half == 0 else 0.0
        m["maskd"] = mk
        maps.append(m)
    return maps


def run(inputs, stop=99, cores=8, dbg=False, ret_all=False):
    nc = build(stop, dbg)
    maps = _prep(inputs)
    res = run_bass_kernel_spmd(nc, maps[:cores], core_ids=list(range(cores)))
    if ret_all:
        return res.results
    out = np.zeros((4, SEQ, D), np.float32)
    for c in range(cores):
        b, half = c // 2, c % 2
        out[b, half * NL:(half + 1) * NL] = res.results[c]["yout"]
    return out


def kernel(**inputs):
    return run(inputs)
```
